# Optimizing a Trainium2 kernel written in Bass

```python
import math
import jax
import jax.numpy as jnp
from jax import lax
import numpy as np

D_MODEL = 1024
BATCH = 8
SEQ = 4096
DEPTH = 4

CHUNK = 64
Q_BLOCK = 128
EPS = 1e-6
MEM_LEN = 256

SSD_D_INNER = 1024
SSD_HEAD_DIM = 64
SSD_N_HEADS = SSD_D_INNER // SSD_HEAD_DIM
SSD_N_GROUPS = 4
SSD_HEADS_PER_GROUP = SSD_N_HEADS // SSD_N_GROUPS
SSD_D_STATE = 128
SSD_CONV = 4
SSD_XBC = SSD_D_INNER + 2 * SSD_N_GROUPS * SSD_D_STATE

CONV_D = 1024
CONV_K = 31

MLA_HEADS = 8
MLA_Q_RANK = 384
MLA_KV_RANK = 256
MLA_NOPE = 128
MLA_ROPE = 64
MLA_V = 128
MLA_QK = MLA_NOPE + MLA_ROPE
ROPE_THETA = 10000.0

N_BRANCH = 3
IN_SIZES = (SSD_D_INNER, SSD_XBC, SSD_N_HEADS, 2 * CONV_D, MLA_Q_RANK, MLA_KV_RANK + MLA_ROPE, N_BRANCH * D_MODEL)
IN_WIDTH = sum(IN_SIZES)

X_HEADS = 4
X_HEAD_DIM = D_MODEL // X_HEADS

FFN_HIDDEN = -(-(8 * D_MODEL) // (3 * 256)) * 256

kernel_name = 'hybrid_ssd_conformer_mla_block'


def split_cols(x, sizes):
    parts, start = [], 0
    for n in sizes:
        parts.append(x[..., start:start + n])
        start += n
    return parts


def rms_norm(x, g):
    xf = x.astype(jnp.float32)
    y = xf * lax.rsqrt(jnp.mean(xf * xf, axis=-1, keepdims=True) + EPS)
    return (y * g.astype(jnp.float32)).astype(x.dtype)


def layer_norm(x, g, b):
    xf = x.astype(jnp.float32)
    mu = jnp.mean(xf, axis=-1, keepdims=True)
    xc = xf - mu
    var = jnp.mean(xc * xc, axis=-1, keepdims=True)
    y = xc * lax.rsqrt(var + EPS) * g.astype(jnp.float32) + b.astype(jnp.float32)
    return y.astype(x.dtype)


def causal_depthwise_conv(x, w, b):
    k, c = w.shape
    y = lax.conv_general_dilated(x, w[:, None, :].astype(x.dtype), window_strides=(1,), padding=[(k - 1, 0)],
                                 dimension_numbers=('NWC', 'WIO', 'NWC'), feature_group_count=c)
    return y + b


def rope_tables(positions, dim):
    inv = ROPE_THETA ** (-jnp.arange(0, dim, 2, dtype=jnp.float32) / dim)
    ang = positions.astype(jnp.float32)[..., None] * inv
    return jnp.cos(ang), jnp.sin(ang)


def apply_rope(x, cos, sin):
    x1, x2 = jnp.split(x.astype(jnp.float32), 2, axis=-1)
    return jnp.concatenate([x1 * cos - x2 * sin, x2 * cos + x1 * sin], axis=-1).astype(x.dtype)


def segsum(a):
    l = a.shape[-1]
    cs = jnp.cumsum(a, axis=-1)
    seg = cs[..., :, None] - cs[..., None, :]
    mask = jnp.tril(jnp.ones((l, l), dtype=bool))
    return jnp.where(mask, seg, -jnp.inf)


def ssd_chunked_scan(xh, dt, A, Bg, Cg):
    b, s, G, R, P = xh.shape
    N = Bg.shape[-1]
    nc = s // CHUNK
    X = (xh * dt[..., None]).reshape(b, nc, CHUNK, G, R, P)
    a = (dt * A).reshape(b, nc, CHUNK, G, R).transpose(0, 3, 4, 1, 2)
    Bc = Bg.reshape(b, nc, CHUNK, G, N)
    Cc = Cg.reshape(b, nc, CHUNK, G, N)
    a_cs = jnp.cumsum(a, axis=-1)
    decay = jnp.exp(segsum(a))
    cb = jnp.einsum('bclgn,bcsgn->bgcls', Cc, Bc)
    y_diag = jnp.einsum('bgrcls,bcsgrp->bclgrp', cb[:, :, None] * decay, X)
    decay_states = jnp.exp(a_cs[..., -1:] - a_cs)
    states = jnp.einsum('bclgn,bgrcl,bclgrp->bcgrpn', Bc, decay_states, X)
    chunk_decay = jnp.exp(a_cs[..., -1])

    def step(h, inp):
        st, dec = inp
        return h * dec[..., None, None] + st, h

    h0 = jnp.zeros((b, G, R, P, N), dtype=X.dtype)
    _, prev = lax.scan(step, h0, (states.transpose(1, 0, 2, 3, 4, 5), chunk_decay.transpose(3, 0, 1, 2)))
    y_off = jnp.einsum('bclgn,cbgrpn,bgrcl->bclgrp', Cc, prev, jnp.exp(a_cs))
    return (y_diag + y_off).reshape(b, s, G, R, P)


def ssd_mixer(z, xbc, dt_raw, conv_w, conv_b, dt_bias, a_log, d_skip, norm_g, w_proj):
    b, s, _ = z.shape
    G, R, P, N = SSD_N_GROUPS, SSD_HEADS_PER_GROUP, SSD_HEAD_DIM, SSD_D_STATE
    xbc = jax.nn.silu(causal_depthwise_conv(xbc, conv_w, conv_b))
    xs, bm, cm = split_cols(xbc, (SSD_D_INNER, G * N, G * N))
    xh = xs.astype(jnp.float32).reshape(b, s, G, R, P)
    Bg = bm.astype(jnp.float32).reshape(b, s, G, N)
    Cg = cm.astype(jnp.float32).reshape(b, s, G, N)
    dt = jax.nn.softplus(dt_raw.astype(jnp.float32) + dt_bias.astype(jnp.float32)).reshape(b, s, G, R)
    A = -jnp.exp(a_log.astype(jnp.float32)).reshape(G, R)
    y = ssd_chunked_scan(xh, dt, A, Bg, Cg) + d_skip.astype(jnp.float32).reshape(G, R)[..., None] * xh
    y = y.reshape(b, s, SSD_D_INNER) * jax.nn.silu(z.astype(jnp.float32))
    yg = y.reshape(b, s, G, SSD_D_INNER // G)
    yg = yg * lax.rsqrt(jnp.mean(yg * yg, axis=-1, keepdims=True) + EPS)
    y = yg.reshape(b, s, SSD_D_INNER) * norm_g.astype(jnp.float32)
    return y.astype(z.dtype) @ w_proj


def conformer_conv_module(glu_in, dw_w, dw_b, ln_g, ln_b, w_pw):
    a, g = jnp.split(glu_in, 2, axis=-1)
    v = a * jax.nn.sigmoid(g)
    v = causal_depthwise_conv(v, dw_w, dw_b)
    v = jax.nn.silu(layer_norm(v, ln_g, ln_b))
    return v @ w_pw


def block_causal_attention(q, k, v):
    b, s, h, dq = q.shape
    dv = v.shape[-1]
    nb = s // Q_BLOCK
    scale = dq ** -0.5
    kt = k.transpose(0, 2, 1, 3)
    vt = v.transpose(0, 2, 1, 3)
    k_chunk = jnp.arange(s) // CHUNK
    qb = q.reshape(b, nb, Q_BLOCK, h, dq).transpose(1, 0, 3, 2, 4)

    def one_block(args):
        qblk, i = args
        q_chunk = (i * Q_BLOCK + jnp.arange(Q_BLOCK)) // CHUNK
        sc = jnp.einsum('bhqd,bhkd->bhqk', qblk, kt, preferred_element_type=jnp.float32) * scale
        sc = jnp.where(k_chunk[None, :] <= q_chunk[:, None], sc, -jnp.inf)
        p = jax.nn.softmax(sc, axis=-1)
        return jnp.einsum('bhqk,bhkd->bhqd', p.astype(v.dtype), vt)

    o = lax.map(one_block, (qb, jnp.arange(nb)))
    return o.transpose(1, 0, 3, 2, 4).reshape(b, s, h, dv)


def mla_mixer(q_lat, kv_lat, cos, sin, q_a_g, w_q_b, kv_a_g, w_kv_b, q_norm_g, k_norm_g, w_o):
    b, s, _ = q_lat.shape
    q = (rms_norm(q_lat, q_a_g) @ w_q_b).reshape(b, s, MLA_HEADS, MLA_QK)
    c_kv, k_rope = kv_lat[..., :MLA_KV_RANK], kv_lat[..., MLA_KV_RANK:]
    kv = (rms_norm(c_kv, kv_a_g) @ w_kv_b).reshape(b, s, MLA_HEADS, MLA_NOPE + MLA_V)
    k_nope, v = kv[..., :MLA_NOPE], kv[..., MLA_NOPE:]
    q_nope = rms_norm(q[..., :MLA_NOPE], q_norm_g[:MLA_NOPE])
    q_rope = apply_rope(rms_norm(q[..., MLA_NOPE:], q_norm_g[MLA_NOPE:]), cos[:, :, None], sin[:, :, None])
    k_nope = rms_norm(k_nope, k_norm_g[:MLA_NOPE])
    k_rope = apply_rope(rms_norm(k_rope, k_norm_g[MLA_NOPE:]), cos, sin)
    qf = jnp.concatenate([q_nope, q_rope], axis=-1)
    kf = jnp.concatenate([k_nope, jnp.broadcast_to(k_rope[:, :, None], (b, s, MLA_HEADS, MLA_ROPE))], axis=-1)
    o = block_causal_attention(qf, kf, v)
    return o.reshape(b, s, MLA_HEADS * MLA_V) @ w_o


def memory_cross_attention(h, mem_n, w_q, w_kv, q_norm_g, k_norm_g, w_o):
    b, s, _ = h.shape
    m = mem_n.shape[1]
    q = rms_norm((h @ w_q).reshape(b, s, X_HEADS, X_HEAD_DIM), q_norm_g)
    kv = (mem_n @ w_kv).reshape(b, m, 2, X_HEADS, X_HEAD_DIM)
    k = rms_norm(kv[:, :, 0], k_norm_g)
    v = kv[:, :, 1]
    sc = jnp.einsum('bqhd,bkhd->bhqk', q, k, preferred_element_type=jnp.float32) * (X_HEAD_DIM ** -0.5)
    p = jax.nn.softmax(sc, axis=-1)
    o = jnp.einsum('bhqk,bkhd->bqhd', p.astype(v.dtype), v)
    return o.reshape(b, s, D_MODEL) @ w_o


def swiglu_ffn(h, w_in, w_out):
    gate, up = jnp.split(h @ w_in, 2, axis=-1)
    return (jax.nn.silu(gate) * up) @ w_out


def setup_inputs(seed: int = 0) -> dict:
    key = jax.random.key(seed)
    ks = iter(jax.random.split(key, 64))
    f32 = jnp.float32
    L = DEPTH

    def normal(shape, std):
        return jax.random.normal(next(ks), shape, f32) * std

    def dense(shape, fan_in, scale=1.0):
        return normal(shape, scale * fan_in ** -0.5)

    def gain(shape):
        return 1.0 + normal(shape, 0.02)

    def small(shape):
        return normal(shape, 0.02)

    out_scale = 0.5
    x = normal((BATCH, SEQ, D_MODEL), 1.0)
    mem = normal((BATCH, MEM_LEN, D_MODEL), 1.0)
    start = jax.random.randint(next(ks), (BATCH, 1), 0, 100000, dtype=jnp.int32)
    positions = start + jnp.arange(SEQ, dtype=jnp.int32)[None, :]
    dt0 = jnp.exp(jax.random.uniform(next(ks), (L, SSD_N_HEADS), f32, math.log(1e-3), math.log(1e-1)))
    ssd_dt_bias = dt0 + jnp.log(-jnp.expm1(-dt0))
    ssd_a_log = jnp.log(jax.random.uniform(next(ks), (L, SSD_N_HEADS), f32, 1.0, 16.0))
    return {
        'x': x,
        'mem': mem,
        'positions': positions,
        'mix_norm_g': gain((L, D_MODEL)),
        'w_in': dense((L, D_MODEL, IN_WIDTH), D_MODEL),
        'ssd_conv_w': dense((L, SSD_CONV, SSD_XBC), SSD_CONV),
        'ssd_conv_b': small((L, SSD_XBC)),
        'ssd_dt_bias': ssd_dt_bias,
        'ssd_a_log': ssd_a_log,
        'ssd_d': gain((L, SSD_N_HEADS)),
        'ssd_norm_g': gain((L, SSD_D_INNER)),
        'ssd_w_out': dense((L, SSD_D_INNER, D_MODEL), SSD_D_INNER),
        'conv_dw_w': dense((L, CONV_K, CONV_D), CONV_K),
        'conv_dw_b': small((L, CONV_D)),
        'conv_ln_g': gain((L, CONV_D)),
        'conv_ln_b': small((L, CONV_D)),
        'conv_w_out': dense((L, CONV_D, D_MODEL), CONV_D),
        'mla_q_a_g': gain((L, MLA_Q_RANK)),
        'mla_w_q_b': dense((L, MLA_Q_RANK, MLA_HEADS * MLA_QK), MLA_Q_RANK),
        'mla_kv_a_g': gain((L, MLA_KV_RANK)),
        'mla_w_kv_b': dense((L, MLA_KV_RANK, MLA_HEADS * (MLA_NOPE + MLA_V)), MLA_KV_RANK),
        'mla_q_norm_g': gain((L, MLA_QK)),
        'mla_k_norm_g': gain((L, MLA_QK)),
        'mla_w_o': dense((L, MLA_HEADS * MLA_V, D_MODEL), MLA_HEADS * MLA_V),
        'gate_b': small((L, N_BRANCH, D_MODEL)),
        'w_out': dense((L, D_MODEL, D_MODEL), D_MODEL, out_scale),
        'xattn_norm_g': gain((L, D_MODEL)),
        'mem_norm_g': gain((L, D_MODEL)),
        'xattn_w_q': dense((L, D_MODEL, D_MODEL), D_MODEL),
        'xattn_w_kv': dense((L, D_MODEL, 2 * D_MODEL), D_MODEL),
        'xattn_q_norm_g': gain((L, X_HEAD_DIM)),
        'xattn_k_norm_g': gain((L, X_HEAD_DIM)),
        'xattn_w_o': dense((L, D_MODEL, D_MODEL), D_MODEL, out_scale),
        'ffn_norm_g': gain((L, D_MODEL)),
        'ffn_w_in': dense((L, D_MODEL, 2 * FFN_HIDDEN), D_MODEL),
        'ffn_w_out': dense((L, FFN_HIDDEN, D_MODEL), FFN_HIDDEN, out_scale),
    }


def reference(x, mem, positions, mix_norm_g, w_in, ssd_conv_w, ssd_conv_b, ssd_dt_bias, ssd_a_log, ssd_d,
              ssd_norm_g, ssd_w_out, conv_dw_w, conv_dw_b, conv_ln_g, conv_ln_b, conv_w_out, mla_q_a_g,
              mla_w_q_b, mla_kv_a_g, mla_w_kv_b, mla_q_norm_g, mla_k_norm_g, mla_w_o, gate_b, w_out,
              xattn_norm_g, mem_norm_g, xattn_w_q, xattn_w_kv, xattn_q_norm_g, xattn_k_norm_g, xattn_w_o,
              ffn_norm_g, ffn_w_in, ffn_w_out):
    b, s, _ = x.shape
    cos, sin = rope_tables(positions, MLA_ROPE)
    for l in range(DEPTH):
        u = rms_norm(x, mix_norm_g[l])
        z, xbc, dt_raw, glu_in, q_lat, kv_lat, gate_logits = split_cols(u @ w_in[l], IN_SIZES)
        y_ssd = ssd_mixer(z, xbc, dt_raw, ssd_conv_w[l], ssd_conv_b[l], ssd_dt_bias[l], ssd_a_log[l],
                          ssd_d[l], ssd_norm_g[l], ssd_w_out[l])
        y_conv = conformer_conv_module(glu_in, conv_dw_w[l], conv_dw_b[l], conv_ln_g[l], conv_ln_b[l],
                                       conv_w_out[l])
        y_mla = mla_mixer(q_lat, kv_lat, cos, sin, mla_q_a_g[l], mla_w_q_b[l], mla_kv_a_g[l], mla_w_kv_b[l],
                          mla_q_norm_g[l], mla_k_norm_g[l], mla_w_o[l])
        gates = jax.nn.sigmoid((gate_logits + gate_b[l].reshape(-1)).astype(jnp.float32))
        gates = gates.astype(x.dtype).reshape(b, s, N_BRANCH, D_MODEL)
        merged = gates[:, :, 0] * y_ssd + gates[:, :, 1] * y_conv + gates[:, :, 2] * y_mla
        x = x + merged @ w_out[l]
        x = x + memory_cross_attention(rms_norm(x, xattn_norm_g[l]), rms_norm(mem, mem_norm_g[l]),
                                       xattn_w_q[l], xattn_w_kv[l], xattn_q_norm_g[l], xattn_k_norm_g[l],
                                       xattn_w_o[l])
        x = x + swiglu_ffn(rms_norm(x, ffn_norm_g[l]), ffn_w_in[l], ffn_w_out[l])
    return x
```

```python
import contextlib
import numpy as np
import concourse.bass as bass
import concourse.mybir as mybir

F32 = mybir.dt.float32
BF16 = mybir.dt.bfloat16
I32 = mybir.dt.int32
AF = mybir.ActivationFunctionType
ALU = mybir.AluOpType
AX = mybir.AxisListType

ENGS = ("pe", "act", "dve", "pool", "sp")
NDMA_SEMS = 12


class H:
    __slots__ = ("w", "r")

    def __init__(self):
        self.w = None
        self.r = []


class V:
    __slots__ = ("ap", "hs")

    def __init__(self, ap, hs):
        self.ap = ap
        self.hs = hs if isinstance(hs, (list, tuple)) else [hs]


class Eng:
    def __init__(self, name, idx):
        self.name = name
        self.idx = idx
        self.n = 0
        self.seen = [0] * len(ENGS)
        self.seen_dma = {}
        self.ops = []
        self.dma_count = 0


class Prog:
    def __init__(self):
        self.nc = bass.Bass("TRN2", target_bir_lowering=False)
        self.stack = contextlib.ExitStack()
        self.eng = {n: Eng(n, i) for i, n in enumerate(ENGS)}
        self.snaps = {}
        self.ntens = 0
        self.nwaits = 0

    def sb(self, shape, dtype, name=None):
        self.ntens += 1
        t = self.stack.enter_context(self.nc.sbuf_tensor(name or f"sb{self.ntens}", list(shape), dtype))
        return t

    def ps(self, shape, dtype, name=None):
        self.ntens += 1
        t = self.stack.enter_context(self.nc.psum_tensor(name or f"ps{self.ntens}", list(shape), dtype))
        return t

    def dram(self, name, shape, dtype, kind="Internal"):
        return self.nc.dram_tensor(name, list(shape), dtype, kind=kind).ap()

    def _deps(self, reads, writes):
        deps = {}

        def add(tok):
            k, v = tok
            if deps.get(k, 0) < v:
                deps[k] = v
        for h in reads:
            if h.w is not None:
                add(h.w)
        for h in writes:
            if h.w is not None:
                add(h.w)
            for t in h.r:
                add(t)
        return deps

    def _waits(self, e, deps):
        waits = []
        for k, v in deps.items():
            if isinstance(k, int):
                if k == e.idx and e.name == "pe":
                    continue
                if e.seen[k] >= v:
                    continue
                waits.append((k, v))
            else:
                if e.seen_dma.get(k, 0) >= v:
                    continue
                waits.append((k, v))
        for k, v in waits:
            if isinstance(k, int):
                if e.seen[k] < v:
                    e.seen[k] = v
            else:
                e.seen_dma[k] = v
            snap = self.snaps.get((k, v))
            if snap is not None:
                for i in range(len(ENGS)):
                    if e.seen[i] < snap[i]:
                        e.seen[i] = snap[i]
        return waits

    def op(self, engname, fn, reads, writes):
        e = self.eng[engname]
        rh = [h for v in reads for h in v.hs]
        wh = [h for v in writes for h in v.hs]
        deps = self._deps(rh, wh)
        waits = self._waits(e, deps)
        e.n += 1
        tok = (e.idx, e.n)
        self.snaps[tok] = tuple(e.seen)
        e.ops.append((waits, fn, None))
        self.nwaits += len(waits)
        for h in rh:
            h.r.append(tok)
        for h in wh:
            h.w = tok
            h.r = []
        return tok

    def dma(self, qname, out, in_, **kw):
        e = self.eng[qname]
        rh = list(in_.hs)
        wh = list(out.hs)
        deps = self._deps(rh, wh)
        k = e.dma_count % NDMA_SEMS
        rnd = e.dma_count // NDMA_SEMS
        e.dma_count += 1
        key = (qname, k)
        if rnd > 0:
            if deps.get(key, 0) < 16 * rnd:
                deps[key] = 16 * rnd
        waits = self._waits(e, deps)
        tok = (key, 16 * (rnd + 1))
        self.snaps[tok] = tuple(e.seen)
        oap, iap = out.ap, in_.ap
        e.ops.append((waits, lambda eng: eng.dma_start(out=oap, in_=iap, **kw), key))
        self.nwaits += len(waits)
        for h in rh:
            h.r.append(tok)
        for h in wh:
            h.w = tok
            h.r = []
        return tok

    def mm(self, out, lhsT, rhs, start=True, stop=True):
        o, l, r = out.ap, lhsT.ap, rhs.ap
        return self.op("pe", lambda e: e.matmul(o, l, r, start=start, stop=stop), [lhsT, rhs], [out])

    def transpose(self, out, in_, ident):
        o, i, d = out.ap, in_.ap, ident.ap
        return self.op("pe", lambda e: e.transpose(o, i, d), [in_, ident], [out])

    def act(self, out, in_, func, bias=None, scale=None, eng="act", accum=None):
        o, i = out.ap, in_.ap
        reads = [in_]
        kw = {}
        if bias is not None:
            if isinstance(bias, V):
                reads.append(bias)
                kw["bias"] = bias.ap
            else:
                kw["bias"] = bias
        if scale is not None:
            if isinstance(scale, V):
                reads.append(scale)
                kw["scale"] = scale.ap
            else:
                kw["scale"] = scale
        writes = [out]
        if accum is not None:
            writes.append(accum)
            kw["accum_out"] = accum.ap
        return self.op("act", lambda e: e.activation(o, i, func, **kw), reads, writes)

    def tt(self, out, a, b, op, eng="dve"):
        o, x, y = out.ap, a.ap, b.ap
        return self.op(eng, lambda e: e.tensor_tensor(o, x, y, op), [a, b], [out])

    def ts(self, out, a, s1, op0, s2=None, op1=None, eng="dve", accum=None):
        o, x = out.ap, a.ap
        reads = [a]
        if isinstance(s1, V):
            reads.append(s1)
            s1 = s1.ap
        if isinstance(s2, V):
            reads.append(s2)
            s2 = s2.ap
        writes = [out]
        kw = {}
        if accum is not None:
            writes.append(accum)
            kw["accum_out"] = accum.ap
        if op1 is None:
            return self.op(eng, lambda e: e.tensor_scalar(o, x, s1, None, op0, **kw), reads, writes)
        return self.op(eng, lambda e: e.tensor_scalar(o, x, s1, s2, op0, op1, **kw), reads, writes)

    def stt(self, out, a, s, b, op0, op1, eng="dve"):
        o, x, y = out.ap, a.ap, b.ap
        reads = [a, b]
        if isinstance(s, V):
            reads.append(s)
            s = s.ap
        return self.op(eng, lambda e: e.scalar_tensor_tensor(o, x, s, y, op0, op1), reads, [out])

    def copy(self, out, in_, eng="dve"):
        o, i = out.ap, in_.ap
        if eng == "act":
            return self.op("act", lambda e: e.copy(o, i), [in_], [out])
        return self.op(eng, lambda e: e.tensor_copy(o, i), [in_], [out])

    def memset(self, out, val, eng="dve"):
        o = out.ap
        return self.op(eng, lambda e: e.memset(o, val), [], [out])

    def recip(self, out, in_):
        o, i = out.ap, in_.ap
        return self.op("dve", lambda e: e.reciprocal(o, i), [in_], [out])

    def finish(self, final_tokens):
        nc = self.nc
        sems = {}
        for i, n in enumerate(ENGS):
            sems[i] = self.stack.enter_context(nc.semaphore(f"s_{n}"))
        for n in ENGS:
            e = self.eng[n]
            if e.dma_count:
                for k in range(min(NDMA_SEMS, e.dma_count)):
                    sems[(n, k)] = self.stack.enter_context(nc.semaphore(f"d_{n}{k}"))
        sp = self.eng["sp"]
        fin = {}
        for k, v in final_tokens:
            if fin.get(k, 0) < v:
                fin[k] = v
        sp.ops.append((list(fin.items()), None, None))
        block = self.stack.enter_context(nc.Block())
        hw = {"pe": block.tensor, "act": block.scalar, "dve": block.vector, "pool": block.gpsimd, "sp": block.sync}

        def make(e):
            def body(eng):
                own = sems[e.idx]
                for waits, fn, dkey in e.ops:
                    for k, v in waits:
                        eng.wait_ge(sems[k], v)
                    if fn is None:
                        continue
                    ins = fn(eng)
                    if dkey is None:
                        ins.then_inc(own, 1)
                    else:
                        ins.then_inc(sems[dkey], 16)
            return body
        for n in ENGS:
            e = self.eng[n]
            if e.ops:
                hw[n](make(e))
        self.stack.close()
        return nc


S = 4096; D = 1024; T = 512; NT = 8; NCH = 32
EPS = 1e-6
OFF_Z, OFF_XBC, OFF_DT, OFF_GA, OFF_GG, OFF_QL, OFF_KV, OFF_KR, OFF_GATE = 0, 1024, 3072, 3088, 4112, 5136, 5520, 5776, 5840
C_MIXG, C_XATG, C_FFNG, C_MEMG, C_SCW, C_SCB, C_CDW, C_CDB, C_LNG, C_LNB = 0, 8, 16, 24, 32, 96, 112, 360, 368, 376
C_QAG, C_KVAG, C_QNN, C_QNR, C_QNRS, C_KNN, C_KNR, C_KNRS, C_GATEB, C_XQG, C_XKG, NPP = 384, 387, 389, 390, 391, 392, 393, 394, 395, 419, 421, 424
B_DTB, B_ALOG, B_DSK, B_SNG, NPB = 0, 16, 32, 48, 1072
K_TRI, K_GT, K_ID, K_INV, K_MASK, NCST = 0, 128, 256, 384, 385, 385 + 2048


class Arena:
    def __init__(self, big, base, limit):
        self.big, self.off, self.limit = big, base, limit

    def alloc(self, n, dtype, parts=128):
        nb = n * (4 if dtype in (F32, I32) else 2)
        nb = (nb + 63) // 64 * 64
        ne = nb // 2
        assert self.off + ne <= self.limit, ("SBUF arena overflow", self.off + ne, self.limit)
        ap = self.big[:, self.off:self.off + ne]
        self.off += ne
        if dtype != BF16:
            ap = ap.bitcast(dtype)
        return ap[:, 0:n]


def build(L=4, dbg=False, only=None):
    P = Prog(); nc = P.nc
    kind_s = "ExternalOutput" if dbg else "Internal"
    din = lambda n, s, dt=F32: P.dram(n, s, dt, kind="ExternalInput")
    xT_d = din("xT", [D, S]); memT_d = din("memT", [D, 256]); pos_d = din("pos", [1, S], I32)
    cst_d = din("cst", [128, NCST]); pp_d = din("pp", [4, 128, NPP]); pb_d = din("pb", [4, NPB])
    w_in_d = din("w_in", [4, D, 8912]); ssd_wo_d = din("ssd_w_out", [4, D, D]); conv_wo_d = din("conv_w_out", [4, D, D])
    wqb_d = din("mla_w_q_b", [4, 384, 1536]); wkvb_d = din("mla_w_kv_b", [4, 256, 2048]); mla_wo_d = din("mla_w_o", [4, D, D])
    wout_d = din("w_out", [4, D, D]); xwq_d = din("xattn_w_q", [4, D, D]); xwkv_d = din("xattn_w_kv", [4, D, 2048])
    xwo_d = din("xattn_w_o", [4, D, D]); fwi_d = din("ffn_w_in", [4, D, 5632]); fwo_d = din("ffn_w_out", [4, 2816, D])
    out_d = P.dram("outT", [D, S], F32, kind="ExternalOutput")
    sc = lambda n, s, dt=F32: P.dram(n, s, dt, kind=kind_s)
    xres_d = sc("xres", [D, S]); sz_d = sc("sz", [S, D]); xbc_d = sc("xbc", [2048, S], BF16); cv_d = sc("cv", [D, S])
    lat_d = sc("lat", [768, S]); ys_d = sc("ys", [D, S], BF16); yc_d = sc("yc", [D, S], BF16)
    q_d = sc("qs", [8, 192, S], BF16); k_d = sc("ks", [8, 128, S], BF16); kr_d = sc("krs", [64, S], BF16)
    v_d = sc("vs", [S, D], BF16); o_d = sc("os", [D, S], BF16); fp_d = sc("fpart", [D, S]); rope_d = sc("rope", [128, S])
    hd = {}

    def dh(name, t):
        k = (name, t)
        if k not in hd:
            hd[k] = H()
        return hd[k]

    def dall(name, n=NT):
        return [dh(name, t) for t in range(n)]

    big = P.sb([128, 103000], BF16, name="big")
    banks = [P.ps([128, 512], F32, name=f"bank{i}") for i in range(8)]
    bh = [H() for _ in range(8)]

    def PS(i, n=512, parts=128, dt=F32):
        ap = banks[i][:]
        if dt == BF16:
            ap = ap.bitcast(BF16)
        return V(ap[0:parts, 0:n], bh[i])

    pers = Arena(big, 0, 8000)
    cst = pers.alloc(K_MASK, F32); h_cst = H()
    tri = V(cst[:, K_TRI:K_TRI + 128], h_cst); gt = V(cst[:, K_GT:K_GT + 128], h_cst)
    inv_c = V(cst[:, K_INV:K_INV + 1], h_cst)
    ident = pers.alloc(128, BF16); h_id = H(); identv = V(ident, h_id)
    ones_b = pers.alloc(128, BF16); h_ob = H(); onesb = V(ones_b, h_ob)
    ones_f = pers.alloc(128, F32); h_of = H(); onesf = V(ones_f, h_of)
    masks = pers.alloc(2048, BF16); h_mk = H()
    epsc = pers.alloc(1, F32); h_eps = H(); epsv = V(epsc, h_eps)
    ppt = pers.alloc(NPP, F32); h_pp = H()
    pbt = pers.alloc(NPB, F32); h_pb = H()
    abc = pers.alloc(16, F32); h_abc = H()
    gsc = pers.alloc(8, F32); h_gsc = H()
    ARENA0 = pers.off

    def pcol(c, n=1, parts=128):
        return V(ppt[0:parts, c:c + n], h_pp)

    def barrier():
        toks = {}
        for n in ENGS:
            e = P.eng[n]
            if e.n:
                toks[e.idx] = e.n
            for k in range(min(NDMA_SEMS, e.dma_count)):
                rnd = (e.dma_count - 1 - k) // NDMA_SEMS
                toks[(n, k)] = 16 * (rnd + 1)
        for n in ENGS:
            e = P.eng[n]
            waits = P._waits(e, dict(toks))
            if waits:
                e.ops.append((waits, None, None))

    P.dma("sp", V(cst, h_cst), V(cst_d[:, 0:K_MASK], H()))
    P.dma("pool", V(ident, h_id), V(cst_d[:, K_ID:K_ID + 128], H()))
    P.dma("pool", V(masks, h_mk), V(cst_d[:, K_MASK:K_MASK + 2048], H()))
    P.memset(onesb, 1.0); P.memset(onesf, 1.0); P.memset(epsv, EPS)

    def rope_tables():
        A = Arena(big, ARENA0, 103000)
        for t in range(NT):
            sl = slice(t * T, (t + 1) * T)
            pi_ = A.alloc(T, I32) if t == 0 else rope_tables.bufs[0]
            if t == 0:
                rope_tables.bufs = [pi_] + [A.alloc(T, F32) for _ in range(5)] + [A.alloc(T, I32)]
            pi_, pf, ang, kf, r, rc, ki = rope_tables.bufs
            hs = [H() for _ in range(7)]
            pi_v, pf_v, ang_v, kf_v, r_v, rc_v, ki_v = [V(a[0:64], h) for a, h in zip(rope_tables.bufs, hs)]
            P.dma("sp", pi_v, V(pos_d[0, sl].partition_broadcast(64), H()))
            P.copy(pf_v, pi_v)
            P.ts(ang_v, pf_v, V(cst[0:64, K_INV:K_INV + 1], h_cst), ALU.mult)
            P.ts(kf_v, ang_v, float(1.0 / (2 * np.pi)), ALU.mult)
            P.copy(ki_v, kf_v)
            P.copy(kf_v, ki_v)
            C1 = 6.28125; C2 = float(np.float32(2 * np.pi - 6.28125))
            P.stt(r_v, kf_v, -C1, ang_v, ALU.mult, ALU.add)
            P.stt(r_v, kf_v, -C2, r_v, ALU.mult, ALU.add)
            P.ts(r_v, r_v, 3.1415925, ALU.min, -3.1415925, ALU.max)
            P.ts(rc_v, r_v, float(np.pi / 2), ALU.is_gt, float(-2 * np.pi), ALU.mult)
            P.stt(rc_v, r_v, float(np.pi / 2), rc_v, ALU.add, ALU.add)
            P.ts(rc_v, rc_v, 3.1415925, ALU.min, -3.1415925, ALU.max)
            P.act(rc_v, rc_v, AF.Sin)
            P.act(r_v, r_v, AF.Sin)
            P.ts(V(r[0:32], hs[4]), V(r[0:32], hs[4]), -1.0, ALU.mult)
            P.dma("sp", V(rope_d[0:64, sl], dh("rope", t)), rc_v)
            P.dma("sp", V(rope_d[64:128, sl], dh("rope", t)), r_v)
            barrier()
    rope_tables()
    barrier()

    def wload(dst, src, hdst):
        return P.dma("pool", V(dst, hdst), V(src, H()))

    def fmview(d, t):
        return d.rearrange("(k p) s -> p k s", p=128)[:, :, t * T:(t + 1) * T]

    def xnorm(xt, hx, gcol, outb, hout, sq, hsq, rs, hrs, bank, nk=8, width=T, inv_n=1.0 / D):
        P.act(V(sq, hsq), V(xt, hx), AF.Square)
        for k in range(nk):
            P.mm(PS(bank, width), onesb, V(sq[:, k, :], hsq), start=(k == 0), stop=(k == nk - 1))
        P.act(V(rs, hrs), PS(bank, width), AF.Ln, bias=epsv, scale=inv_n)
        P.act(V(rs, hrs), V(rs, hrs), AF.Exp, scale=-0.5)
        for k in range(nk):
            P.stt(V(outb[:, k, :], hout), V(xt[:, k, :], hx), pcol(gcol + k), V(rs, hrs), ALU.mult, ALU.mult)

    dtraw = pers.alloc(NCH * 16, F32).rearrange("p (c h) -> p c h", c=NCH); h_dtraw = H()
    ARENA0 = pers.off
    final_toks = []

    def phase_A(l):
        xsrc = xT_d if l == 0 else xres_d
        xsn = "xT" if l == 0 else "xres"
        barrier()
        P.dma("sp", V(ppt, h_pp), V(pp_d[l], H()))
        P.dma("sp", V(pbt, h_pb), V(pb_d[l, :].partition_broadcast(128), H()))
        P.act(V(abc, h_abc), V(pbt[:, B_ALOG:B_ALOG + 16], h_pb), AF.Exp)
        P.ts(V(abc, h_abc), V(abc, h_abc), -1.0, ALU.mult)
        P.ts(V(gsc[:, 0:3], h_gsc), V(ppt[:, C_QNN:C_QNN + 3], h_pp), float(192 ** -0.5), ALU.mult)
        P.ts(V(gsc[:, 3:5], h_gsc), V(ppt[:, C_XQG:C_XQG + 2], h_pp), float(256 ** -0.5), ALU.mult)

        A = Arena(big, ARENA0, 103000)
        uT = A.alloc(8 * S, BF16).rearrange("p (k s) -> p k s", k=8); hu = [H() for _ in range(NT)]
        A1 = A.off
        xt = [A.alloc(8 * T, F32).rearrange("p (k s) -> p k s", k=8) for _ in range(2)]; hxt = [H(), H()]
        sq = A.alloc(8 * T, BF16).rearrange("p (k s) -> p k s", k=8); hsq = H()
        rs = A.alloc(T, F32); hrs = H()
        for t in range(NT):
            b = t % 2
            P.dma("sp", V(xt[b], hxt[b]), V(fmview(xsrc, t), dh(xsn, t)))
            xnorm(xt[b], hxt[b], C_MIXG, uT[:, :, t * T:(t + 1) * T], hu[t], sq, hsq, rs, hrs, 0)
        barrier()
        A.off = A1
        wz = A.alloc(8 * 1040, BF16).rearrange("p (k n) -> p k n", k=8); hwz = H()
        wload(wz[:, :, 0:1024], w_in_d[l].rearrange("(k p) n -> p k n", p=128)[:, :, OFF_Z:OFF_Z + 1024], hwz)
        wload(wz[:, :, 1024:1040], w_in_d[l].rearrange("(k p) n -> p k n", p=128)[:, :, OFF_DT:OFF_DT + 16], hwz)
        szb = [A.alloc(1024, F32) for _ in range(2)]; hszb = [H(), H()]
        for c in range(NCH):
            t = c // 4; b = c % 2
            us = lambda k: V(uT[:, k, c * 128:(c + 1) * 128], hu[t])
            for half in range(2):
                for k in range(8):
                    P.mm(PS(1 + half), us(k), V(wz[:, k, half * 512:(half + 1) * 512], hwz), start=(k == 0), stop=(k == 7))
            for k in range(8):
                P.mm(PS(3, 16), us(k), V(wz[:, k, 1024:1040], hwz), start=(k == 0), stop=(k == 7))
            for half in range(2):
                P.act(V(szb[b][:, half * 512:(half + 1) * 512], hszb[b]), PS(1 + half), AF.Silu)
            P.copy(V(dtraw[:, c, :], h_dtraw), PS(3, 16))
            P.dma("pool", V(sz_d[c * 128:(c + 1) * 128, :], dh("sz", t)), V(szb[b], hszb[b]))
        barrier()
        A.off = A1
        wg = [A.alloc(8 * 512, BF16).rearrange("p (k n) -> p k n", k=8) for _ in range(2)]; hwg = [H(), H()]
        pre = [A.alloc(S + 32, F32) for _ in range(2)]; hpre = [H(), H()]
        acc = [A.alloc(S, F32) for _ in range(2)]; hacc = [H(), H()]
        xo = [A.alloc(S, BF16) for _ in range(2)]; hxo = [H(), H()]
        for b in range(2):
            P.memset(V(pre[b][:, 0:32], hpre[b]), 0.0)
        win_v = w_in_d[l].rearrange("(k p) n -> p k n", p=128)
        for j in range(16):
            g4 = j // 4; wb = g4 % 2; b = j % 2
            if j % 4 == 0:
                wload(wg[wb], win_v[:, :, OFF_XBC + g4 * 512:OFF_XBC + (g4 + 1) * 512], hwg[wb])
            for t in range(NT):
                bank = 1 + (t % 2)
                for k in range(8):
                    P.mm(PS(bank), V(wg[wb][:, k, (j % 4) * 128:(j % 4 + 1) * 128], hwg[wb]),
                         V(uT[:, k, t * T:(t + 1) * T], hu[t]), start=(k == 0), stop=(k == 7))
                P.copy(V(pre[b][:, 32 + t * T:32 + (t + 1) * T], hpre[b]), PS(bank), eng="act")
            P.ts(V(acc[b], hacc[b]), V(pre[b][:, 29:29 + S], hpre[b]), pcol(C_SCW + j * 4 + 0), ALU.mult, pcol(C_SCB + j), ALU.add)
            for tap in range(1, 4):
                P.stt(V(acc[b], hacc[b]), V(pre[b][:, 29 + tap:29 + tap + S], hpre[b]), pcol(C_SCW + j * 4 + tap),
                      V(acc[b], hacc[b]), ALU.mult, ALU.add)
            P.act(V(xo[b], hxo[b]), V(acc[b], hacc[b]), AF.Silu)
            P.dma("pool", V(xbc_d[j * 128:(j + 1) * 128, :], dall("xbc")), V(xo[b], hxo[b]))
        barrier()
        A.off = A1
        wa = [A.alloc(8 * 128, BF16).rearrange("p (k n) -> p k n", k=8) for _ in range(2)]; hwa = [H(), H()]
        wgt = [A.alloc(8 * 128, BF16).rearrange("p (k n) -> p k n", k=8) for _ in range(2)]; hwgt = [H(), H()]
        vb = [A.alloc(S + 32, BF16) for _ in range(2)]; hvb = [[H() for _ in range(NT)] for _ in range(2)]; hvz = [H(), H()]
        dg = [A.alloc(31 * 128, BF16).rearrange("p (j n) -> p j n", j=31) for _ in range(2)]; hdg = [H(), H()]
        acc = [A.alloc(S, F32) for _ in range(2)]; hacc = [H(), H()]
        sg = [A.alloc(T, F32) for _ in range(2)]; hsg = [H(), H()]
        for b in range(2):
            P.memset(V(vb[b][:, 0:32], hvz[b]), 0.0)
        for j in range(8):
            b = j % 2
            wload(wa[b], win_v[:, :, OFF_GA + j * 128:OFF_GA + (j + 1) * 128], hwa[b])
            wload(wgt[b], win_v[:, :, OFF_GG + j * 128:OFF_GG + (j + 1) * 128], hwgt[b])
            for tap in range(31):
                P.ts(V(dg[b][:, tap, :], hdg[b]), identv, pcol(C_CDW + j * 31 + tap), ALU.mult)

            def glu(t):
                sb_ = t % 2
                for k in range(8):
                    P.mm(PS(1 + sb_), V(wa[b][:, k, :], hwa[b]), V(uT[:, k, t * T:(t + 1) * T], hu[t]), start=(k == 0), stop=(k == 7))
                for k in range(8):
                    P.mm(PS(3 + sb_), V(wgt[b][:, k, :], hwgt[b]), V(uT[:, k, t * T:(t + 1) * T], hu[t]), start=(k == 0), stop=(k == 7))
                P.act(V(sg[sb_], hsg[sb_]), PS(3 + sb_), AF.Sigmoid)
                P.tt(V(vb[b][:, 32 + t * T:32 + (t + 1) * T], hvb[b][t]), PS(1 + sb_), V(sg[sb_], hsg[sb_]), ALU.mult)

            def conv(t):
                cbk = 5 + t % 2
                rd_h = [hvb[b][t]] + ([hvb[b][t - 1]] if t > 0 else [hvz[b]])
                for tap in range(31):
                    o0 = 2 + tap + t * T
                    P.mm(PS(cbk), V(dg[b][:, tap, :], hdg[b]), V(vb[b][:, o0:o0 + T], rd_h), start=(tap == 0), stop=(tap == 30))
                P.act(V(acc[b][:, t * T:(t + 1) * T], hacc[b]), PS(cbk), AF.Identity, bias=pcol(C_CDB + j))
            glu(0)
            for t in range(NT):
                if t + 1 < NT:
                    glu(t + 1)
                conv(t)
            P.dma("pool", V(cv_d[j * 128:(j + 1) * 128, :], dall("cv")), V(acc[b], hacc[b]))
        barrier()
        A.off = A1
        wl = A.alloc(8 * 768, BF16).rearrange("p (k n) -> p k n", k=8); hwl = H()
        wload(wl[:, :, 0:704], win_v[:, :, OFF_QL:OFF_QL + 704], hwl)
        wload(wl[:, :, 704:736], win_v[:, :, OFF_KR + 32:OFF_KR + 64], hwl)
        wload(wl[:, :, 736:768], win_v[:, :, OFF_KR:OFF_KR + 32], hwl)
        lo = [A.alloc(T, F32) for _ in range(2)]; hlo = [H(), H()]
        segs = [(0, 128), (128, 128), (256, 128), (384, 128), (512, 128), (640, 64), (704, 64)]
        i = 0
        for t in range(NT):
            for (c0, m) in segs:
                b = i % 2; i += 1
                for k in range(8):
                    P.mm(PS(1 + b, T, m), V(wl[:, k, c0:c0 + m], hwl), V(uT[:, k, t * T:(t + 1) * T], hu[t]), start=(k == 0), stop=(k == 7))
                P.copy(V(lo[b][0:m], hlo[b]), PS(1 + b, T, m), eng="act")
                P.dma("pool", V(lat_d[c0:c0 + m, t * T:(t + 1) * T], dh("lat", t)), V(lo[b][0:m], hlo[b]))
        barrier()
    def phase_S(l):
        A = Arena(big, ARENA0, 103000)
        xbt = [A.alloc(16 * T, BF16).rearrange("p (k s) -> p k s", k=16) for _ in range(2)]; hxb = [H(), H()]
        szt = [A.alloc(1024, F32) for _ in range(2)]; hszt = [H(), H()]
        hst = A.alloc(1024, F32); h_hst = H()
        hbf = A.alloc(1024, BF16); h_hbf = H()
        D2 = lambda n, dt: ([A.alloc(n, dt) for _ in range(2)], [H(), H()])
        sm2, h_sm2 = D2(64, F32)
        ew2, h_ew2 = D2(48, F32)
        Xb2, h_Xb2 = D2(1024, BF16)
        Xw2, h_Xw2 = D2(1024, BF16)
        tD2, h_tD2 = D2(1024, F32)
        Bt2, h_Bt2 = D2(512, BF16)
        MT2, h_MT2 = D2(2048, BF16)
        cbm = A.alloc(512, F32); h_cbm = H()
        AG = A.alloc(2048, F32); h_AG = H()
        dec = [A.alloc(512, F32) for _ in range(2)]; h_dec = [H(), H()]
        yg = A.alloc(1024, F32); h_yg = H()
        t1 = [A.alloc(512, F32) for _ in range(2)]; h_t1 = [H(), H()]
        ssq = A.alloc(8, F32); h_ssq = H()
        junk = A.alloc(256, F32); h_junk = H()
        ynb = A.alloc(1024, BF16); h_ynb = H()
        yst = [A.alloc(8 * T, BF16).rearrange("p (k s) -> p k s", k=8) for _ in range(2)]; h_yst = [H(), H()]
        P.memset(V(hst, h_hst), 0.0); P.memset(V(hbf, h_hbf), 0.0)
        dtb = V(pbt[:, B_DTB:B_DTB + 16], h_pb)
        sng = V(pbt[:, B_SNG:B_SNG + 1024], h_pb)
        xview = xbc_d.rearrange("(k p) s -> p k s", p=128)
        SMALL = lambda lo, n: V(banks[2][:, 256 + lo:256 + lo + n], bh[2])

        def bc16(ap, lo, n):
            return ap[:, lo:lo + n].unsqueeze(2).broadcast_to([128, n, 64])
        v3 = lambda ap: ap.rearrange("p (h d) -> p h d", h=16)

        def stage1(c):
            t = c // 4; s_ = c % 4; tb = t % 2; cb_ = c % 2
            cs = slice(s_ * 128, (s_ + 1) * 128)
            if s_ == 0:
                P.dma("sp", V(xbt[tb], hxb[tb]), V(xview[:, :, t * T:(t + 1) * T], dall("xbc")))
            P.dma("sp", V(szt[cb_], hszt[cb_]), V(sz_d[c * 128:(c + 1) * 128, :], dh("sz", t)))
            xb_ = xbt[tb]; hx_ = hxb[tb]
            sm, h_sm, ew, h_ew = sm2[cb_], h_sm2[cb_], ew2[cb_], h_ew2[cb_]
            Xb, h_Xb, Xw, h_Xw, tD, h_tD = Xb2[cb_], h_Xb2[cb_], Xw2[cb_], h_Xw2[cb_], tD2[cb_], h_tD2[cb_]
            Btm, h_Btm, MT, h_MT = Bt2[cb_], h_Bt2[cb_], MT2[cb_], h_MT2[cb_]
            for k in range(8):
                P.transpose(V(banks[1][:].bitcast(BF16)[:, k * 128:(k + 1) * 128], bh[1]), V(xb_[:, k, cs], hx_), identv)
            for g in range(4):
                P.transpose(V(banks[2][:].bitcast(BF16)[:, g * 128:(g + 1) * 128], bh[2]), V(xb_[:, 8 + g, cs], hx_), identv)
            for g in range(4):
                P.mm(V(banks[3][:, g * 128:(g + 1) * 128], bh[3]), V(xb_[:, 8 + g, cs], hx_), V(xb_[:, 12 + g, cs], hx_))
            P.tt(V(sm[:, 0:16], h_sm), V(dtraw[:, c, :], h_dtraw), dtb, ALU.add)
            P.act(V(sm[:, 0:16], h_sm), V(sm[:, 0:16], h_sm), AF.Exp)
            P.act(V(sm[:, 0:16], h_sm), V(sm[:, 0:16], h_sm), AF.Ln, bias=1.0)
            P.tt(V(sm[:, 16:32], h_sm), V(sm[:, 0:16], h_sm), V(abc, h_abc), ALU.mult)
            av = V(sm[:, 16:32], h_sm)
            P.mm(SMALL(0, 16), tri, av); P.mm(SMALL(16, 16), gt, av); P.mm(SMALL(32, 16), onesf, av)
            P.act(V(ew, h_ew), SMALL(0, 48), AF.Exp)
            P.tt(V(sm[:, 32:48], h_sm), V(sm[:, 0:16], h_sm), V(ew[:, 16:32], h_ew), ALU.mult)
            P.tt(V(AG.rearrange("p (h s) -> p h s", h=16), h_AG),
                 V(cst[:, K_GT:K_GT + 128].unsqueeze(1).broadcast_to([128, 16, 128]), h_cst),
                 V(sm[:, 16:32].unsqueeze(2).broadcast_to([128, 16, 128]), h_sm), ALU.mult)
            P.tt(V(cbm.rearrange("p (g l) -> p g l", g=4), h_cbm), V(banks[3][:].rearrange("p (g l) -> p g l", g=4), bh[3]),
                 V(cst[:, K_TRI:K_TRI + 128].unsqueeze(1).broadcast_to([128, 4, 128]), h_cst), ALU.mult)
            xsT = banks[1][:].bitcast(BF16)[:, 0:1024].rearrange("p (h d) -> p h d", h=16)
            P.tt(V(v3(Xb), h_Xb), V(xsT, bh[1]), V(bc16(sm, 0, 16), h_sm), ALU.mult)
            P.tt(V(v3(Xw), h_Xw), V(xsT, bh[1]), V(bc16(sm, 32, 16), h_sm), ALU.mult)
            P.tt(V(v3(tD), h_tD), V(xsT, bh[1]), V(bc16(pbt, B_DSK, 16), h_pb), ALU.mult)
            P.copy(V(Btm, h_Btm), V(banks[2][:].bitcast(BF16)[:, 0:512], bh[2]), eng="act")
            for g in range(4):
                db = g % 2
                for r in range(4):
                    hd_ = g * 4 + r
                    P.mm(V(banks[4][:, r * 128:(r + 1) * 128], bh[4]), V(AG[:, hd_ * 128:(hd_ + 1) * 128], h_AG), tri)
                P.act(V(dec[db], h_dec[db]), PS(4), AF.Exp)
                P.tt(V(MT[:, g * 512:(g + 1) * 512].rearrange("p (r l) -> p r l", r=4), h_MT), V(dec[db].rearrange("p (r l) -> p r l", r=4), h_dec[db]),
                     V(cbm[:, g * 128:(g + 1) * 128].unsqueeze(1).broadcast_to([128, 4, 128]), h_cbm), ALU.mult)

        def stage2(c):
            t = c // 4; s_ = c % 4; tb = t % 2; cb_ = c % 2
            cs = slice(s_ * 128, (s_ + 1) * 128)
            xb_ = xbt[tb]; hx_ = hxb[tb]
            ew, h_ew = ew2[cb_], h_ew2[cb_]
            Xb, h_Xb, Xw, h_Xw, tD, h_tD = Xb2[cb_], h_Xb2[cb_], Xw2[cb_], h_Xw2[cb_], tD2[cb_], h_tD2[cb_]
            Btm, h_Btm, MT, h_MT = Bt2[cb_], h_Bt2[cb_], MT2[cb_], h_MT2[cb_]
            for hd_ in range(16):
                ybank = 5 + hd_ // 8
                col = (hd_ % 8) * 64
                P.mm(V(banks[ybank][:, col:col + 64], bh[ybank]), V(MT[:, hd_ * 128:(hd_ + 1) * 128], h_MT),
                     V(Xb[:, hd_ * 64:(hd_ + 1) * 64], h_Xb))
            for hf in range(2):
                ybank = 5 + hf
                for gg in range(2):
                    g = hf * 2 + gg
                    P.mm(V(banks[7][:, gg * 256:(gg + 1) * 256], bh[7]), V(xb_[:, 12 + g, cs], hx_), V(hbf[:, g * 256:(g + 1) * 256], h_hbf))
                t1_ = t1[hf]; ht1 = h_t1[hf]
                P.tt(V(t1_.rearrange("p (h d) -> p h d", h=8), ht1), V(banks[7][:].rearrange("p (h d) -> p h d", h=8), bh[7]), V(bc16(ew, hf * 8, 8), h_ew), ALU.mult)
                P.tt(V(t1_, ht1), V(t1_, ht1), PS(ybank), ALU.add)
                P.tt(V(t1_, ht1), V(t1_, ht1), V(tD[:, hf * 512:(hf + 1) * 512], h_tD), ALU.add, eng="pool")
                P.tt(V(yg[:, hf * 512:(hf + 1) * 512], h_yg), V(t1_, ht1), V(szt[cb_][:, hf * 512:(hf + 1) * 512], hszt[cb_]), ALU.mult)
            for hf in range(2):
                for gg in range(2):
                    g = hf * 2 + gg
                    P.mm(V(banks[7][:, gg * 256:(gg + 1) * 256], bh[7]), V(Btm[:, g * 128:(g + 1) * 128], h_Btm), V(Xw[:, g * 256:(g + 1) * 256], h_Xw))
                hv = V(hst[:, hf * 512:(hf + 1) * 512].rearrange("p (h d) -> p h d", h=8), h_hst)
                P.tt(hv, hv, V(bc16(ew, 32 + hf * 8, 8), h_ew), ALU.mult)
                P.tt(V(hst[:, hf * 512:(hf + 1) * 512], h_hst), V(hst[:, hf * 512:(hf + 1) * 512], h_hst), PS(7), ALU.add)
            P.copy(V(hbf, h_hbf), V(hst, h_hst), eng="act")
            for g in range(4):
                P.act(V(junk, h_junk), V(yg[:, g * 256:(g + 1) * 256], h_yg), AF.Square, accum=V(ssq[:, g:g + 1], h_ssq))
            P.act(V(ssq[:, 4:8], h_ssq), V(ssq[:, 0:4], h_ssq), AF.Ln, bias=epsv, scale=1.0 / 256)
            P.act(V(ssq[:, 4:8], h_ssq), V(ssq[:, 4:8], h_ssq), AF.Exp, scale=-0.5)
            P.tt(V(yg.rearrange("p (g d) -> p g d", g=4), h_yg), V(yg.rearrange("p (g d) -> p g d", g=4), h_yg),
                 V(ssq[:, 4:8].unsqueeze(2).broadcast_to([128, 4, 256]), h_ssq), ALU.mult)
            P.tt(V(ynb, h_ynb), V(yg, h_yg), sng, ALU.mult)
            for k in range(8):
                P.transpose(V(banks[0][:].bitcast(BF16)[:, k * 128:(k + 1) * 128], bh[0]), V(ynb[:, k * 128:(k + 1) * 128], h_ynb), identv)
            P.copy(V(yst[tb][:, :, cs], h_yst[tb]), V(banks[0][:].bitcast(BF16)[:, 0:1024].rearrange("p (k s) -> p k s", k=8), bh[0]), eng="act")
            if s_ == 3:
                P.dma("pool", V(ys_d.rearrange("(k p) s -> p k s", p=128)[:, :, t * T:(t + 1) * T], dh("ys", t)), V(yst[tb], h_yst[tb]))

        stage1(0)
        for c in range(NCH):
            if c + 1 < NCH:
                stage1(c + 1)
            stage2(c)
        barrier()

    def phase_C(l):
        A = Arena(big, ARENA0, 103000)
        ct = [A.alloc(8 * T, F32).rearrange("p (k s) -> p k s", k=8) for _ in range(2)]; hct = [H(), H()]
        cb16 = A.alloc(8 * T, BF16).rearrange("p (k s) -> p k s", k=8); hcb = H()
        sq = A.alloc(8 * T, BF16).rearrange("p (k s) -> p k s", k=8); hsq = H()
        mean = A.alloc(T, F32); hmean = H()
        var = A.alloc(T, F32); hvar = H()
        tmp = A.alloc(T, F32); htmp = H()
        yo = [A.alloc(8 * T, BF16).rearrange("p (k s) -> p k s", k=8) for _ in range(2)]; hyo = [H(), H()]
        for t in range(NT):
            b = t % 2
            P.dma("sp", V(ct[b], hct[b]), V(fmview(cv_d, t), dall("cv")))
            P.copy(V(cb16, hcb), V(ct[b], hct[b]), eng="pool")
            P.act(V(sq, hsq), V(ct[b], hct[b]), AF.Square)
            for k in range(8):
                P.mm(PS(0), onesb, V(cb16[:, k, :], hcb), start=(k == 0), stop=(k == 7))
            for k in range(8):
                P.mm(PS(1), onesb, V(sq[:, k, :], hsq), start=(k == 0), stop=(k == 7))
            P.ts(V(mean, hmean), PS(0), 1.0 / D, ALU.mult)
            P.tt(V(tmp, htmp), V(mean, hmean), V(mean, hmean), ALU.mult)
            P.stt(V(var, hvar), PS(1), 1.0 / D, V(tmp, htmp), ALU.mult, ALU.subtract)
            P.act(V(var, hvar), V(var, hvar), AF.Ln, bias=epsv)
            P.act(V(var, hvar), V(var, hvar), AF.Exp, scale=-0.5)
            for k in range(8):
                P.tt(V(tmp, htmp), V(ct[b][:, k, :], hct[b]), V(mean, hmean), ALU.subtract)
                P.tt(V(tmp, htmp), V(tmp, htmp), V(var, hvar), ALU.mult)
                P.act(V(yo[b][:, k, :], hyo[b]), V(tmp, htmp), AF.Silu, bias=pcol(C_LNB + k), scale=pcol(C_LNG + k))
            P.dma("pool", V(fmview(yc_d, t), dh("yc", t)), V(yo[b], hyo[b]))
        barrier()
    def phase_M1(l):
        A = Arena(big, ARENA0, 103000)
        wq = A.alloc(3 * 1536, BF16).rearrange("p (k n) -> p k n", k=3); hwq = H()
        wqs = A.alloc(3 * 512, BF16).rearrange("p (k n) -> p k n", k=3); hwqs = H()
        wkn = A.alloc(2 * 1024, BF16).rearrange("p (k n) -> p k n", k=2); hwkn = H()
        wv = A.alloc(2 * 1024, BF16).rearrange("p (k n) -> p k n", k=2); hwv = H()
        wload(wq, wqb_d[l].rearrange("(k p) n -> p k n", p=128), hwq)
        qv = wqb_d[l].rearrange("(k p) (h c) -> p k h c", p=128, h=8)
        wqs4 = wqs.rearrange("p k (h c) -> p k h c", h=8)
        for k in range(3):
            wload(wqs4[:, k, :, 0:32], qv[:, k, :, 160:192], hwqs)
            wload(wqs4[:, k, :, 32:64], qv[:, k, :, 128:160], hwqs)
        kv4 = wkvb_d[l].rearrange("(k p) (h c) -> p k h c", p=128, h=8)
        for k in range(2):
            wload(wkn.rearrange("p k (h c) -> p k h c", h=8)[:, k], kv4[:, k, :, 0:128], hwkn)
            wload(wv.rearrange("p k (h c) -> p k h c", h=8)[:, k], kv4[:, k, :, 128:256], hwv)
        lt = [A.alloc(6 * T, F32).rearrange("p (k s) -> p k s", k=6) for _ in range(2)]; hlt = [H(), H()]
        rp = [A.alloc(T, F32) for _ in range(2)]; hrp = [H(), H()]
        rp2 = [A.alloc(T, F32) for _ in range(2)]; hrp2 = [H(), H()]
        sq = A.alloc(3 * T, BF16).rearrange("p (k s) -> p k s", k=3); hsq = H()
        rs = A.alloc(T, F32); hrs = H()
        qn = A.alloc(3 * T, BF16).rearrange("p (k s) -> p k s", k=3); hqn = H()
        kvn = A.alloc(2 * T, BF16).rearrange("p (k s) -> p k s", k=2); hkvn = H()
        sqh = [A.alloc(T, BF16) for _ in range(3)]; hsqh = [H() for _ in range(3)]
        rsh = [A.alloc(T, F32) for _ in range(3)]; hrsh = [H() for _ in range(3)]
        krb = A.alloc(T, F32); hkrb = H()
        ob = [A.alloc(T, BF16) for _ in range(6)]; hob = [H() for _ in range(6)]
        ta = A.alloc(T, F32); hta = H()
        tb_ = A.alloc(T, F32); htb = H()
        vb = [A.alloc(1024, BF16) for _ in range(2)]; hvb = [H(), H()]
        latv = lat_d.rearrange("(k p) s -> p k s", p=128)
        oi = 0
        for t in range(NT):
            b = t % 2; sl = slice(t * T, (t + 1) * T)
            P.dma("sp", V(lt[b], hlt[b]), V(latv[:, :, sl], dh("lat", t)))
            P.dma("sp", V(lt[b][0:64, 5, :], hlt[b]), V(lat_d[704:768, sl], dh("lat", t)))
            P.dma("sp", V(rp[b][0:64], hrp[b]), V(rope_d[0:64, sl], dh("rope", t)))
            P.dma("sp", V(rp2[b][0:64], hrp2[b]), V(rope_d[64:128, sl], dh("rope", t)))
            L_ = lt[b]; hL = hlt[b]
            cosv = V(rp[b][0:64], hrp[b]); sinv = V(rp2[b][0:64], hrp2[b])
            xnorm(L_[:, 0:3, :], hL, C_QAG, qn, hqn, sq, hsq, rs, hrs, 0, nk=3, inv_n=1.0 / 384)
            xnorm(L_[:, 3:5, :], hL, C_KVAG, kvn, hkvn, sq[:, 0:2, :], hsq, rs, hrs, 0, nk=2, inv_n=1.0 / 256)

            for h in range(8):
                for k in range(3):
                    P.mm(PS(1), V(wq[:, k, h * 192:h * 192 + 128], hwq), V(qn[:, k, :], hqn), start=(k == 0), stop=(k == 2))
                for k in range(3):
                    P.mm(PS(2, T, 64), V(wq[:, k, h * 192 + 128:h * 192 + 192], hwq), V(qn[:, k, :], hqn), start=(k == 0), stop=(k == 2))
                for k in range(3):
                    P.mm(PS(3, T, 64), V(wqs[:, k, h * 64:(h + 1) * 64], hwqs), V(qn[:, k, :], hqn), start=(k == 0), stop=(k == 2))
                for k in range(2):
                    P.mm(PS(4), V(wkn[:, k, h * 128:(h + 1) * 128], hwkn), V(kvn[:, k, :], hkvn), start=(k == 0), stop=(k == 1))
                specs = [(1, 128, 1.0 / 128, 5), (2, 64, 1.0 / 64, 6), (4, 128, 1.0 / 128, 7)]
                for i, (bank, parts, inv_n, nb) in enumerate(specs):
                    P.act(V(sqh[i][0:parts], hsqh[i]), PS(bank, T, parts), AF.Square)
                for i, (bank, parts, inv_n, nb) in enumerate(specs):
                    P.mm(PS(nb, T, parts), V(ones_b[0:parts, 0:parts], h_ob), V(sqh[i][0:parts], hsqh[i]))
                for i, (bank, parts, inv_n, nb) in enumerate(specs):
                    P.act(V(rsh[i][0:parts], hrsh[i]), PS(nb, T, parts), AF.Ln, bias=V(epsc[0:parts], h_eps), scale=inv_n)
                    P.act(V(rsh[i][0:parts], hrsh[i]), V(rsh[i][0:parts], hrsh[i]), AF.Exp, scale=-0.5)
                o1 = oi % 6; o2 = (oi + 1) % 6; o3 = (oi + 2) % 6; oi += 3
                P.stt(V(ob[o1], hob[o1]), PS(1), V(gsc[:, 0:1], h_gsc), V(rsh[0], hrsh[0]), ALU.mult, ALU.mult)
                P.stt(V(ob[o3], hob[o3]), PS(4), pcol(C_KNN), V(rsh[2], hrsh[2]), ALU.mult, ALU.mult)
                P.stt(V(ta[0:64], hta), PS(2, T, 64), V(gsc[0:64, 1:2], h_gsc), cosv, ALU.mult, ALU.mult)
                P.stt(V(tb_[0:64], htb), PS(3, T, 64), V(gsc[0:64, 2:3], h_gsc), sinv, ALU.mult, ALU.mult)
                P.tt(V(ta[0:64], hta), V(ta[0:64], hta), V(tb_[0:64], htb), ALU.add)
                P.tt(V(ob[o2][0:64], hob[o2]), V(ta[0:64], hta), V(rsh[1][0:64], hrsh[1]), ALU.mult)
                P.dma("pool", V(q_d[h, 0:128, sl], dh("q", t)), V(ob[o1], hob[o1]))
                P.dma("pool", V(q_d[h, 128:192, sl], dh("q", t)), V(ob[o2][0:64], hob[o2]))
                P.dma("pool", V(k_d[h, :, sl], dh("k", t)), V(ob[o3], hob[o3]))
            P.dma("sp", V(krb[0:64], hkrb), V(lat_d[640:704, sl], dh("lat", t)))
            krv = V(krb[0:64], hkrb); krsv = V(L_[0:64, 5, :], hL)
            P.act(V(sqh[1][0:64], hsqh[1]), krv, AF.Square)
            P.mm(PS(7, T, 64), V(ones_b[0:64, 0:64], h_ob), V(sqh[1][0:64], hsqh[1]))
            P.act(V(rsh[1][0:64], hrsh[1]), PS(7, T, 64), AF.Ln, bias=V(epsc[0:64], h_eps), scale=1.0 / 64)
            P.act(V(rsh[1][0:64], hrsh[1]), V(rsh[1][0:64], hrsh[1]), AF.Exp, scale=-0.5)
            P.stt(V(ta[0:64], hta), krv, pcol(C_KNR, 1, 64), cosv, ALU.mult, ALU.mult)
            P.stt(V(tb_[0:64], htb), krsv, pcol(C_KNRS, 1, 64), sinv, ALU.mult, ALU.mult)
            P.tt(V(ta[0:64], hta), V(ta[0:64], hta), V(tb_[0:64], htb), ALU.add)
            o = oi % 6; oi += 1
            P.tt(V(ob[o][0:64], hob[o]), V(ta[0:64], hta), V(rsh[1][0:64], hrsh[1]), ALU.mult)
            P.dma("pool", V(kr_d[:, sl], dh("kr", t)), V(ob[o][0:64], hob[o]))
            for s_ in range(4):
                vb_ = s_ % 2
                for half in range(2):
                    for k in range(2):
                        P.mm(PS(2 + half), V(kvn[:, k, s_ * 128:(s_ + 1) * 128], hkvn), V(wv[:, k, half * 512:(half + 1) * 512], hwv),
                             start=(k == 0), stop=(k == 1))
                    P.copy(V(vb[vb_][:, half * 512:(half + 1) * 512], hvb[vb_]), PS(2 + half), eng="act")
                r0 = t * T + s_ * 128
                P.dma("pool", V(v_d[r0:r0 + 128, :], dh("v", t)), V(vb[vb_], hvb[vb_]))
        barrier()

    def phase_M2(l):
        A = Arena(big, ARENA0, 103000)
        krt = A.alloc(S, BF16); hkr = H()
        P.memset(V(krt[64:128], hkr), 0.0)
        P.dma("sp", V(krt[0:64], hkr), V(kr_d, dall("kr")))
        kt_ = [A.alloc(S, BF16) for _ in range(2)]; hkt = [H(), H()]
        qnt = [A.alloc(S, BF16) for _ in range(2)]; hqn = [H(), H()]
        qrt = [A.alloc(S, BF16) for _ in range(2)]; hqr = [H(), H()]
        for b_ in range(2):
            P.memset(V(qrt[b_][64:128], hqr[b_]), 0.0)
        vt = [A.alloc(NCH * 128, BF16).rearrange("p (c d) -> p c d", c=NCH) for _ in range(2)]; hvt = [H(), H()]
        pt = [A.alloc(T, BF16) for _ in range(4)]; hpt = [H() for _ in range(4)]
        pd = [A.alloc(T, BF16) for _ in range(4)]; hpd = [H() for _ in range(4)]
        for j in range(4):
            P.memset(V(pd[j], hpd[j]), 0.0)
        dacc = [A.alloc(T, F32) for _ in range(2)]; hdacc = [H(), H()]
        rd = A.alloc(T, F32); hrd = H()
        dhi = A.alloc(T, BF16); hdhi = H()
        dlo = A.alloc(T, BF16); hdlo = H()
        oo = [A.alloc(T, BF16) for _ in range(2)]; hoo = [H(), H()]
        mk = masks.rearrange("p (j q) -> p j q", j=4)
        pi = 0
        for h in range(8):
            b = h % 2
            P.dma("sp", V(kt_[b], hkt[b]), V(k_d[h], dall("k")))
            P.dma("sp", V(qnt[b], hqn[b]), V(q_d[h, 0:128, :], dall("q")))
            P.dma("sp", V(qrt[b][0:64], hqr[b]), V(q_d[h, 128:192, :], dall("q")))
            P.dma("sp", V(vt[b], hvt[b]), V(v_d.rearrange("(c p) d -> p c d", p=128)[:, :, h * 128:(h + 1) * 128], dall("v")))
            for qt in range(NT):
                qs = slice(qt * T, (qt + 1) * T)
                nk = 4 * qt + 4
                ob_ = 2 + (qt % 2)
                da = qt % 2

                def st(kt):
                    sbk = kt % 2
                    ks = slice(kt * 128, (kt + 1) * 128)
                    P.mm(PS(sbk), V(kt_[b][:, ks], hkt[b]), V(qnt[b][:, qs], hqn[b]), start=True, stop=False)
                    P.mm(PS(sbk), V(krt[:, ks], hkr), V(qrt[b][:, qs], hqr[b]), start=False, stop=True)
                st(0)
                for kt in range(nk):
                    if kt + 1 < nk:
                        st(kt + 1)
                    if kt >= 4 * qt:
                        j = kt - 4 * qt
                        pv_ = V(pd[j], hpd[j])
                        bk = banks[kt % 2]
                        P.act(V(pd[j][0:64, 128 * j:T], hpd[j]), V(bk[0:64, 128 * j:T], bh[kt % 2]), AF.Exp)
                        P.act(V(pd[j][64:128, 128 * j + 64:T], hpd[j]), V(bk[64:128, 128 * j + 64:T], bh[kt % 2]), AF.Exp)
                    else:
                        p_ = pi % 4; pi += 1
                        pv_ = V(pt[p_], hpt[p_])
                        P.act(pv_, PS(kt % 2), AF.Exp)
                    P.mm(PS(ob_), V(vt[b][:, kt, :], hvt[b]), pv_, start=(kt == 0), stop=(kt == nk - 1))
                    if kt % 2 == 1:
                        P.mm(PS(4 + da), onesb, pv_, start=(kt == 1), stop=False)
                    elif kt == 0:
                        P.copy(V(dacc[da], hdacc[da]), pv_)
                    else:
                        P.tt(V(dacc[da], hdacc[da]), V(dacc[da], hdacc[da]), pv_, ALU.add)
                P.copy(V(dhi, hdhi), V(dacc[da], hdacc[da]))
                P.tt(V(dlo, hdlo), V(dacc[da], hdacc[da]), V(dhi, hdhi), ALU.subtract)
                P.mm(PS(4 + da), onesb, V(dhi, hdhi), start=False, stop=False)
                P.mm(PS(4 + da), onesb, V(dlo, hdlo), start=False, stop=True)
                P.act(V(rd, hrd), PS(4 + da), AF.Ln)
                P.act(V(rd, hrd), V(rd, hrd), AF.Exp, scale=-1.0)
                o = qt % 2
                P.tt(V(oo[o], hoo[o]), PS(ob_), V(rd, hrd), ALU.mult)
                P.dma("pool", V(o_d[h * 128:(h + 1) * 128, qs], dh("o", qt)), V(oo[o], hoo[o]))
        barrier()
    def phase_G(l):
        xsrc = xT_d if l == 0 else xres_d
        xsn = "xT" if l == 0 else "xres"
        A = Arena(big, ARENA0, 103000)
        wgate = A.alloc(8 * 3072, BF16).rearrange("p (k n) -> p k n", k=8); hwg = H()
        wbr = [A.alloc(8 * 1024, BF16).rearrange("p (k n) -> p k n", k=8) for _ in range(3)]; hwb = [H() for _ in range(3)]
        wo = A.alloc(8 * 1024, BF16).rearrange("p (k n) -> p k n", k=8); hwo = H()
        kp = lambda d: d.rearrange("(k p) n -> p k n", p=128)
        for b3 in range(3):
            wload(wgate[:, :, b3 * 1024:(b3 + 1) * 1024], kp(w_in_d[l])[:, :, OFF_GATE + b3 * 1024:OFF_GATE + (b3 + 1) * 1024], hwg)
        for b3, wd in enumerate((ssd_wo_d, conv_wo_d, mla_wo_d)):
            wload(wbr[b3], kp(wd[l]), hwb[b3])
        wload(wo, kp(wout_d[l]), hwo)
        xt = A.alloc(8 * T, F32).rearrange("p (k s) -> p k s", k=8); hxt = H()
        ut = A.alloc(8 * T, BF16).rearrange("p (k s) -> p k s", k=8); hut = H()
        sq = A.alloc(8 * T, BF16).rearrange("p (k s) -> p k s", k=8); hsq = H()
        rs = A.alloc(T, F32); hrs = H()
        br = [A.alloc(8 * T, BF16).rearrange("p (k s) -> p k s", k=8) for _ in range(3)]; hbr = [H() for _ in range(3)]
        mg = A.alloc(8 * T, BF16).rearrange("p (k s) -> p k s", k=8); hmg = H()
        gt_ = [A.alloc(T, F32) for _ in range(2)]; hgt = [H(), H()]
        macc = A.alloc(T, F32); hmacc = H()
        srcs = [(ys_d, "ys"), (yc_d, "yc"), (o_d, "o")]
        for t in range(NT):
            P.dma("sp", V(xt, hxt), V(fmview(xsrc, t), dh(xsn, t)))
            for b3, (d_, nm) in enumerate(srcs):
                P.dma("sp", V(br[b3], hbr[b3]), V(fmview(d_, t), dall(nm)))
            xnorm(xt, hxt, C_MIXG, ut, hut, sq, hsq, rs, hrs, 0)
            gi = 0
            for m in range(8):
                for b3 in range(3):
                    gb = 1 + gi % 2; yb = 3 + gi % 2; g2 = gi % 2; gi += 1
                    for k in range(8):
                        P.mm(PS(gb), V(wgate[:, k, b3 * 1024 + m * 128:b3 * 1024 + (m + 1) * 128], hwg), V(ut[:, k, :], hut), start=(k == 0), stop=(k == 7))
                    for k in range(8):
                        P.mm(PS(yb), V(wbr[b3][:, k, m * 128:(m + 1) * 128], hwb[b3]), V(br[b3][:, k, :], hbr[b3]), start=(k == 0), stop=(k == 7))
                    P.act(V(gt_[g2], hgt[g2]), PS(gb), AF.Sigmoid, bias=pcol(C_GATEB + b3 * 8 + m))
                    if b3 == 0:
                        P.tt(V(macc, hmacc), V(gt_[g2], hgt[g2]), PS(yb), ALU.mult)
                    else:
                        P.tt(V(gt_[g2], hgt[g2]), V(gt_[g2], hgt[g2]), PS(yb), ALU.mult)
                        if b3 == 1:
                            P.tt(V(macc, hmacc), V(macc, hmacc), V(gt_[g2], hgt[g2]), ALU.add)
                        else:
                            P.tt(V(mg[:, m, :], hmg), V(macc, hmacc), V(gt_[g2], hgt[g2]), ALU.add)
            for n in range(8):
                ob_ = 5 + n % 2
                for m in range(8):
                    P.mm(PS(ob_), V(wo[:, m, n * 128:(n + 1) * 128], hwo), V(mg[:, m, :], hmg), start=(m == 0), stop=(m == 7))
                P.tt(V(xt[:, n, :], hxt), V(xt[:, n, :], hxt), PS(ob_), ALU.add)
            P.dma("pool", V(fmview(xres_d, t), dh("xres", t)), V(xt, hxt))
        barrier()

    def phase_X(l):
        A = Arena(big, ARENA0, 103000)
        kp = lambda d: d.rearrange("(k p) n -> p k n", p=128)
        wkv = A.alloc(8 * 2048, BF16).rearrange("p (k n) -> p k n", k=8); hwkv = H()
        wload(wkv[:, :, 0:1024], kp(xwkv_d[l])[:, :, 0:1024], hwkv)
        wload(wkv[:, :, 1024:2048], kp(xwkv_d[l])[:, :, 1024:2048], hwkv)
        mt = A.alloc(8 * 256, F32).rearrange("p (k s) -> p k s", k=8); hmt = H()
        mn = A.alloc(8 * 256, BF16).rearrange("p (k s) -> p k s", k=8); hmn = H()
        sqm = A.alloc(8 * 256, BF16).rearrange("p (k s) -> p k s", k=8); hsqm = H()
        rsm = A.alloc(256, F32); hrsm = H()
        KX = A.alloc(8 * 256, BF16).rearrange("p (c s) -> p c s", c=8); hKX = H()
        VX = A.alloc(2 * 1024, BF16).rearrange("p (m d) -> p m d", m=2); hVX = H()
        sq2 = A.alloc(2 * 256, BF16).rearrange("p (c s) -> p c s", c=2); hsq2 = H()
        P.dma("sp", V(mt, hmt), V(memT_d.rearrange("(k p) s -> p k s", p=128), H()))
        xnorm(mt, hmt, C_MEMG, mn, hmn, sqm, hsqm, rsm, hrsm, 0, nk=8, width=256)
        for hx in range(4):
            for c in range(2):
                for k in range(8):
                    P.mm(PS(1 + c, 256), V(wkv[:, k, hx * 256 + c * 128:hx * 256 + (c + 1) * 128], hwkv), V(mn[:, k, :], hmn), start=(k == 0), stop=(k == 7))
                P.act(V(sq2[:, c, :], hsq2), PS(1 + c, 256), AF.Square)
            for c in range(2):
                P.mm(PS(3, 256), onesb, V(sq2[:, c, :], hsq2), start=(c == 0), stop=(c == 1))
            P.act(V(rsm, hrsm), PS(3, 256), AF.Ln, bias=epsv, scale=1.0 / 256)
            P.act(V(rsm, hrsm), V(rsm, hrsm), AF.Exp, scale=-0.5)
            for c in range(2):
                P.stt(V(KX[:, hx * 2 + c, :], hKX), PS(1 + c, 256), pcol(C_XKG + c), V(rsm, hrsm), ALU.mult, ALU.mult)
        for m in range(2):
            for half in range(2):
                for k in range(8):
                    P.mm(PS(4 + half), V(mn[:, k, m * 128:(m + 1) * 128], hmn), V(wkv[:, k, 1024 + half * 512:1024 + (half + 1) * 512], hwkv), start=(k == 0), stop=(k == 7))
                P.copy(V(VX[:, m, half * 512:(half + 1) * 512], hVX), PS(4 + half), eng="act")
        barrier()
        A2 = Arena(big, ARENA0, 103000)
        KX2 = A2.alloc(8 * 256, BF16).rearrange("p (c s) -> p c s", c=8); hKX2 = H()
        VX2 = A2.alloc(2 * 1024, BF16).rearrange("p (m d) -> p m d", m=2); hVX2 = H()
        P.copy(V(KX2, hKX2), V(KX, hKX)); P.copy(V(VX2, hVX2), V(VX, hVX))
        barrier()
        A = A2
        wq_ = A.alloc(8 * 1024, BF16).rearrange("p (k n) -> p k n", k=8); hwq = H()
        wo_ = A.alloc(8 * 1024, BF16).rearrange("p (k n) -> p k n", k=8); hwo = H()
        wload(wq_, kp(xwq_d[l]), hwq); wload(wo_, kp(xwo_d[l]), hwo)
        xt = [A.alloc(8 * T, F32).rearrange("p (k s) -> p k s", k=8) for _ in range(2)]; hxt = [H(), H()]
        ht2 = [A.alloc(8 * T, BF16).rearrange("p (k s) -> p k s", k=8) for _ in range(2)]; hht2 = [H(), H()]
        sq = A.alloc(8 * T, BF16).rearrange("p (k s) -> p k s", k=8); hsq = H()
        rs = A.alloc(T, F32); hrs = H()
        sqq = A.alloc(2 * T, BF16).rearrange("p (c s) -> p c s", c=2); hsqq = H()
        rsq = A.alloc(T, F32); hrsq = H()
        qx = A.alloc(2 * T, BF16).rearrange("p (c s) -> p c s", c=2); hqx = H()
        pt = [A.alloc(T, BF16) for _ in range(2)]; hpt = [H(), H()]
        rd = A.alloc(T, F32); hrd = H()
        ox = A.alloc(8 * T, BF16).rearrange("p (k s) -> p k s", k=8); hox = H()
        def prep(t):
            b = t % 2
            P.dma("sp", V(xt[b], hxt[b]), V(fmview(xres_d, t), dh("xres", t)))
            xnorm(xt[b], hxt[b], C_XATG, ht2[b], hht2[b], sq, hsq, rs, hrs, 0)
        prep(0)
        for t in range(NT):
            b = t % 2
            ht = ht2[b]; hht = hht2[b]
            if t + 1 < NT:
                prep(t + 1)
            for hx in range(4):
                for c in range(2):
                    for k in range(8):
                        P.mm(PS(1 + c), V(wq_[:, k, hx * 256 + c * 128:hx * 256 + (c + 1) * 128], hwq), V(ht[:, k, :], hht), start=(k == 0), stop=(k == 7))
                    P.act(V(sqq[:, c, :], hsqq), PS(1 + c), AF.Square)
                for c in range(2):
                    P.mm(PS(3), onesb, V(sqq[:, c, :], hsqq), start=(c == 0), stop=(c == 1))
                P.act(V(rsq, hrsq), PS(3), AF.Ln, bias=epsv, scale=1.0 / 256)
                P.act(V(rsq, hrsq), V(rsq, hrsq), AF.Exp, scale=-0.5)
                for c in range(2):
                    P.stt(V(qx[:, c, :], hqx), PS(1 + c), V(gsc[:, 3 + c:4 + c], h_gsc), V(rsq, hrsq), ALU.mult, ALU.mult)
                for m in range(2):
                    for c in range(2):
                        P.mm(PS(4 + m), V(KX2[:, hx * 2 + c, m * 128:(m + 1) * 128], hKX2), V(qx[:, c, :], hqx), start=(c == 0), stop=(c == 1))
                    P.act(V(pt[m], hpt[m]), PS(4 + m), AF.Exp)
                for m in range(2):
                    P.mm(PS(3), onesb, V(pt[m], hpt[m]), start=(m == 0), stop=(m == 1))
                P.act(V(rd, hrd), PS(3), AF.Ln)
                P.act(V(rd, hrd), V(rd, hrd), AF.Exp, scale=-1.0)
                for c2 in range(2):
                    for m in range(2):
                        P.mm(PS(6 + c2), V(VX2[:, m, hx * 256 + c2 * 128:hx * 256 + (c2 + 1) * 128], hVX2), V(pt[m], hpt[m]), start=(m == 0), stop=(m == 1))
                    P.tt(V(ox[:, hx * 2 + c2, :], hox), PS(6 + c2), V(rd, hrd), ALU.mult)
            for n in range(8):
                ob_ = 1 + n % 2
                for k in range(8):
                    P.mm(PS(ob_), V(wo_[:, k, n * 128:(n + 1) * 128], hwo), V(ox[:, k, :], hox), start=(k == 0), stop=(k == 7))
                P.tt(V(xt[b][:, n, :], hxt[b]), V(xt[b][:, n, :], hxt[b]), PS(ob_), ALU.add)
            P.dma("pool", V(fmview(xres_d, t), dh("xres", t)), V(xt[b], hxt[b]))
        barrier()

    def phase_F(l, last):
        kp = lambda d: d.rearrange("(k p) n -> p k n", p=128)
        toks = []
        for hh in range(2):
            A = Arena(big, ARENA0, 103000)
            w1g = A.alloc(8 * 1408, BF16).rearrange("p (k n) -> p k n", k=8); hw1g = H()
            w1u = A.alloc(8 * 1408, BF16).rearrange("p (k n) -> p k n", k=8); hw1u = H()
            w2 = A.alloc(11 * 1024, BF16).rearrange("p (k n) -> p k n", k=11); hw2 = H()
            wload(w1g, kp(fwi_d[l])[:, :, hh * 1408:(hh + 1) * 1408], hw1g)
            wload(w1u, kp(fwi_d[l])[:, :, 2816 + hh * 1408:2816 + (hh + 1) * 1408], hw1u)
            wload(w2, fwo_d[l, hh * 1408:(hh + 1) * 1408, :].rearrange("(k p) n -> p k n", p=128), hw2)
            xt = [A.alloc(8 * T, F32).rearrange("p (k s) -> p k s", k=8) for _ in range(2)]; hxt = [H(), H()]
            pt_ = A.alloc(8 * T, F32).rearrange("p (k s) -> p k s", k=8); hpt_ = H()
            ht2 = [A.alloc(8 * T, BF16).rearrange("p (k s) -> p k s", k=8) for _ in range(2)]; hht2 = [H(), H()]
            sq = A.alloc(8 * T, BF16).rearrange("p (k s) -> p k s", k=8); hsq = H()
            rs = A.alloc(T, F32); hrs = H()
            ac = A.alloc(11 * T, BF16).rearrange("p (k s) -> p k s", k=11); hac = H()
            sg = [A.alloc(T, F32) for _ in range(2)]; hsg = [H(), H()]
            def prep(t):
                b = t % 2
                P.dma("sp", V(xt[b], hxt[b]), V(fmview(xres_d, t), dh("xres", t)))
                xnorm(xt[b], hxt[b], C_FFNG, ht2[b], hht2[b], sq, hsq, rs, hrs, 0)
            prep(0)
            for t in range(NT):
                b = t % 2
                ht = ht2[b]; hht = hht2[b]
                if t + 1 < NT:
                    prep(t + 1)
                if hh == 1:
                    P.dma("sp", V(pt_, hpt_), V(fmview(fp_d, t), dh("fp", t)))
                for j in range(11):
                    s2 = j % 2
                    for k in range(8):
                        P.mm(PS(1 + s2), V(w1g[:, k, j * 128:(j + 1) * 128], hw1g), V(ht[:, k, :], hht), start=(k == 0), stop=(k == 7))
                    for k in range(8):
                        P.mm(PS(3 + s2), V(w1u[:, k, j * 128:(j + 1) * 128], hw1u), V(ht[:, k, :], hht), start=(k == 0), stop=(k == 7))
                    P.act(V(sg[s2], hsg[s2]), PS(1 + s2), AF.Silu)
                    P.tt(V(ac[:, j, :], hac), V(sg[s2], hsg[s2]), PS(3 + s2), ALU.mult)
                for n in range(8):
                    ob_ = 5 + n % 2
                    for j in range(11):
                        P.mm(PS(ob_), V(w2[:, j, n * 128:(n + 1) * 128], hw2), V(ac[:, j, :], hac), start=(j == 0), stop=(j == 10))
                    if hh == 0:
                        P.copy(V(xt[b][:, n, :], hxt[b]), PS(ob_), eng="act")
                    else:
                        P.tt(V(xt[b][:, n, :], hxt[b]), V(xt[b][:, n, :], hxt[b]), PS(ob_), ALU.add)
                        P.tt(V(xt[b][:, n, :], hxt[b]), V(xt[b][:, n, :], hxt[b]), V(pt_[:, n, :], hpt_), ALU.add)
                if hh == 0:
                    P.dma("pool", V(fmview(fp_d, t), dh("fp", t)), V(xt[b], hxt[b]))
                else:
                    dst = out_d if last else xres_d
                    tk = P.dma("pool", V(fmview(dst, t), dh("out" if last else "xres", t)), V(xt[b], hxt[b]))
                    toks.append(tk)
            barrier()
        return toks

    phases = {"A": phase_A, "S": phase_S, "C": phase_C, "M1": phase_M1, "M2": phase_M2, "G": phase_G, "X": phase_X}
    order = ["A", "S", "C", "M1", "M2", "G", "X", "F"]
    P.marks = []
    for l in range(L):
        for ph in order:
            if only is not None and ph not in only:
                continue
            P.marks.append((l, ph, P.eng["dve"].n))
            if ph == "F":
                final_toks = phase_F(l, l == L - 1)
            else:
                phases[ph](l)
    P.finish(final_toks)
    return P


from concourse.bass_utils import run_bass_kernel_spmd


def _fm(v, nch):
    return np.ascontiguousarray(np.asarray(v, np.float32).reshape(nch, 128).T)


def _consts():
    c = np.zeros((128, NCST), np.float32)
    i = np.arange(128)
    c[:, K_TRI:K_TRI + 128] = (i[:, None] <= i[None, :])
    c[:, K_GT:K_GT + 128] = (i[:, None] > i[None, :])
    c[:, K_ID:K_ID + 128] = np.eye(128)
    f = np.arange(32)
    inv = (10000.0 ** (-(2 * f).astype(np.float32) / np.float32(64))).astype(np.float32)
    c[0:32, K_INV] = inv; c[32:64, K_INV] = inv
    q = np.arange(512)
    for j in range(4):
        c[:, K_MASK + j * 512:K_MASK + (j + 1) * 512] = ((2 * j + i[:, None] // 64) <= (q[None, :] // 64))
    return c


def _pack_params(inp):
    L = 4
    pp = np.zeros((L, 128, NPP), np.float32)
    pb = np.zeros((L, NPB), np.float32)
    for l in range(L):
        p = pp[l]
        p[:, C_MIXG:C_MIXG + 8] = _fm(inp["mix_norm_g"][l], 8)
        p[:, C_XATG:C_XATG + 8] = _fm(inp["xattn_norm_g"][l], 8)
        p[:, C_FFNG:C_FFNG + 8] = _fm(inp["ffn_norm_g"][l], 8)
        p[:, C_MEMG:C_MEMG + 8] = _fm(inp["mem_norm_g"][l], 8)
        p[:, C_SCW:C_SCW + 64] = np.asarray(inp["ssd_conv_w"][l]).T.reshape(16, 128, 4).transpose(1, 0, 2).reshape(128, 64)
        p[:, C_SCB:C_SCB + 16] = _fm(inp["ssd_conv_b"][l], 16)
        p[:, C_CDW:C_CDW + 248] = np.asarray(inp["conv_dw_w"][l]).T.reshape(8, 128, 31).transpose(1, 0, 2).reshape(128, 248)
        p[:, C_CDB:C_CDB + 8] = _fm(inp["conv_dw_b"][l], 8)
        p[:, C_LNG:C_LNG + 8] = _fm(inp["conv_ln_g"][l], 8)
        p[:, C_LNB:C_LNB + 8] = _fm(inp["conv_ln_b"][l], 8)
        p[:, C_QAG:C_QAG + 3] = _fm(inp["mla_q_a_g"][l], 3)
        p[:, C_KVAG:C_KVAG + 2] = _fm(inp["mla_kv_a_g"][l], 2)
        for (cn, cr, crs, g) in ((C_QNN, C_QNR, C_QNRS, inp["mla_q_norm_g"][l]), (C_KNN, C_KNR, C_KNRS, inp["mla_k_norm_g"][l])):
            g = np.asarray(g, np.float32)
            p[:, cn] = g[0:128]
            p[0:64, cr] = g[128:192]
            p[0:64, crs] = np.concatenate([g[160:192], g[128:160]])
        p[:, C_GATEB:C_GATEB + 24] = np.asarray(inp["gate_b"][l], np.float32).reshape(24, 128).T
        p[:, C_XQG:C_XQG + 2] = _fm(inp["xattn_q_norm_g"][l], 2)
        p[:, C_XKG:C_XKG + 2] = _fm(inp["xattn_k_norm_g"][l], 2)
        pb[l, B_DTB:B_DTB + 16] = inp["ssd_dt_bias"][l]
        pb[l, B_ALOG:B_ALOG + 16] = inp["ssd_a_log"][l]
        pb[l, B_DSK:B_DSK + 16] = inp["ssd_d"][l]
        pb[l, B_SNG:B_SNG + 1024] = inp["ssd_norm_g"][l]
    return pp, pb


WNAMES = ["w_in", "ssd_w_out", "conv_w_out", "mla_w_q_b", "mla_w_kv_b", "mla_w_o", "w_out", "xattn_w_q", "xattn_w_kv",
          "xattn_w_o", "ffn_w_in", "ffn_w_out"]


def make_in_maps(inp, cores):
    pp, pb = _pack_params(inp)
    cst = _consts()
    shared = {n: np.ascontiguousarray(np.asarray(inp[n], np.float32)) for n in WNAMES}
    shared.update(cst=cst, pp=pp, pb=pb)
    maps = []
    for b in cores:
        m = dict(shared)
        m["xT"] = np.ascontiguousarray(np.asarray(inp["x"][b], np.float32).T)
        m["memT"] = np.ascontiguousarray(np.asarray(inp["mem"][b], np.float32).T)
        m["pos"] = np.ascontiguousarray(np.asarray(inp["positions"][b], np.int32)[None, :])
        maps.append(m)
    return maps


_CACHE = {}


def kernel(**inputs):
    if "P" not in _CACHE:
        _CACHE["P"] = build(L=4)
    P = _CACHE["P"]
    maps = make_in_maps(inputs, list(range(8)))
    res = run_bass_kernel_spmd(P.nc, maps, core_ids=list(range(8)))
    out = np.stack([np.ascontiguousarray(r["outT"].T) for r in res.results], axis=0)
    return out.astype(np.float32)
```

```python
import contextlib
import numpy as np
import concourse.bass as bass
import concourse.mybir as mybir

F32 = mybir.dt.float32
BF16 = mybir.dt.bfloat16
I32 = mybir.dt.int32
AF = mybir.ActivationFunctionType
ALU = mybir.AluOpType
AX = mybir.AxisListType

ENGS = ("pe", "act", "dve", "pool", "sp")
NDMA_SEMS = 12


class H:
    __slots__ = ("w", "r")

    def __init__(self):
        self.w = None
        self.r = []


class V:
    __slots__ = ("ap", "hs")

    def __init__(self, ap, hs):
        self.ap = ap
        self.hs = hs if isinstance(hs, (list, tuple)) else [hs]


class Eng:
    def __init__(self, name, idx):
        self.name = name
        self.idx = idx
        self.n = 0
        self.seen = [0] * len(ENGS)
        self.seen_dma = {}
        self.ops = []
        self.dma_count = 0


class Prog:
    def __init__(self):
        self.nc = bass.Bass("TRN2", target_bir_lowering=False)
        self.stack = contextlib.ExitStack()
        self.eng = {n: Eng(n, i) for i, n in enumerate(ENGS)}
        self.snaps = {}
        self.ntens = 0
        self.nwaits = 0

    def sb(self, shape, dtype, name=None):
        self.ntens += 1
        t = self.stack.enter_context(self.nc.sbuf_tensor(name or f"sb{self.ntens}", list(shape), dtype))
        return t

    def ps(self, shape, dtype, name=None):
        self.ntens += 1
        t = self.stack.enter_context(self.nc.psum_tensor(name or f"ps{self.ntens}", list(shape), dtype))
        return t

    def dram(self, name, shape, dtype, kind="Internal"):
        return self.nc.dram_tensor(name, list(shape), dtype, kind=kind).ap()

    def _deps(self, reads, writes):
        deps = {}

        def add(tok):
            k, v = tok
            if deps.get(k, 0) < v:
                deps[k] = v
        for h in reads:
            if h.w is not None:
                add(h.w)
        for h in writes:
            if h.w is not None:
                add(h.w)
            for t in h.r:
                add(t)
        return deps

    def _waits(self, e, deps):
        waits = []
        for k, v in deps.items():
            if isinstance(k, int):
                if k == e.idx and e.name == "pe":
                    continue
                if e.seen[k] >= v:
                    continue
                waits.append((k, v))
            else:
                if e.seen_dma.get(k, 0) >= v:
                    continue
                waits.append((k, v))
        for k, v in waits:
            if isinstance(k, int):
                if e.seen[k] < v:
                    e.seen[k] = v
            else:
                e.seen_dma[k] = v
            snap = self.snaps.get((k, v))
            if snap is not None:
                for i in range(len(ENGS)):
                    if e.seen[i] < snap[i]:
                        e.seen[i] = snap[i]
        return waits

    def capture(self):
        self._cap = []
        return self._cap

    def end_capture(self):
        self._cap = None

    def interleave(self, la, lb):
        self._cap = None
        na, nb = len(la), len(lb)
        ia = ib = 0
        while ia < na or ib < nb:
            if ib >= nb or (ia < na and ia * nb <= ib * na):
                kind, args, kw = la[ia]; ia += 1
            else:
                kind, args, kw = lb[ib]; ib += 1
            (self.op if kind == "op" else self.dma)(*args, **kw)

    def op(self, engname, fn, reads, writes):
        if getattr(self, "_cap", None) is not None:
            self._cap.append(("op", (engname, fn, reads, writes), {}))
            return None
        e = self.eng[engname]
        rh = [h for v in reads for h in v.hs]
        wh = [h for v in writes for h in v.hs]
        deps = self._deps(rh, wh)
        waits = self._waits(e, deps)
        e.n += 1
        tok = (e.idx, e.n)
        self.snaps[tok] = tuple(e.seen)
        e.ops.append((waits, fn, None))
        self.nwaits += len(waits)
        for h in rh:
            h.r.append(tok)
        for h in wh:
            h.w = tok
            h.r = []
        return tok

    def dma(self, qname, out, in_, **kw):
        if getattr(self, "_cap", None) is not None:
            self._cap.append(("dma", (qname, out, in_), kw))
            return None
        e = self.eng[qname]
        rh = list(in_.hs)
        wh = list(out.hs)
        deps = self._deps(rh, wh)
        k = e.dma_count % NDMA_SEMS
        rnd = e.dma_count // NDMA_SEMS
        e.dma_count += 1
        key = (qname, k)
        if rnd > 0:
            if deps.get(key, 0) < 16 * rnd:
                deps[key] = 16 * rnd
        waits = self._waits(e, deps)
        tok = (key, 16 * (rnd + 1))
        self.snaps[tok] = tuple(e.seen)
        oap, iap = out.ap, in_.ap
        e.ops.append((waits, lambda eng: eng.dma_start(out=oap, in_=iap, **kw), key))
        self.nwaits += len(waits)
        for h in rh:
            h.r.append(tok)
        for h in wh:
            h.w = tok
            h.r = []
        return tok

    def mm(self, out, lhsT, rhs, start=True, stop=True):
        o, l, r = out.ap, lhsT.ap, rhs.ap
        return self.op("pe", lambda e: e.matmul(o, l, r, start=start, stop=stop), [lhsT, rhs], [out])

    def transpose(self, out, in_, ident):
        o, i, d = out.ap, in_.ap, ident.ap
        return self.op("pe", lambda e: e.transpose(o, i, d), [in_, ident], [out])

    def act(self, out, in_, func, bias=None, scale=None, eng="act", accum=None):
        o, i = out.ap, in_.ap
        reads = [in_]
        kw = {}
        if bias is not None:
            if isinstance(bias, V):
                reads.append(bias)
                kw["bias"] = bias.ap
            else:
                kw["bias"] = bias
        if scale is not None:
            if isinstance(scale, V):
                reads.append(scale)
                kw["scale"] = scale.ap
            else:
                kw["scale"] = scale
        writes = [out]
        if accum is not None:
            writes.append(accum)
            kw["accum_out"] = accum.ap
        return self.op("act", lambda e: e.activation(o, i, func, **kw), reads, writes)

    def tt(self, out, a, b, op, eng="dve"):
        o, x, y = out.ap, a.ap, b.ap
        return self.op(eng, lambda e: e.tensor_tensor(o, x, y, op), [a, b], [out])

    def ts(self, out, a, s1, op0, s2=None, op1=None, eng="dve", accum=None):
        o, x = out.ap, a.ap
        reads = [a]
        if isinstance(s1, V):
            reads.append(s1)
            s1 = s1.ap
        if isinstance(s2, V):
            reads.append(s2)
            s2 = s2.ap
        writes = [out]
        kw = {}
        if accum is not None:
            writes.append(accum)
            kw["accum_out"] = accum.ap
        if op1 is None:
            return self.op(eng, lambda e: e.tensor_scalar(o, x, s1, None, op0, **kw), reads, writes)
        return self.op(eng, lambda e: e.tensor_scalar(o, x, s1, s2, op0, op1, **kw), reads, writes)

    def stt(self, out, a, s, b, op0, op1, eng="dve"):
        o, x, y = out.ap, a.ap, b.ap
        reads = [a, b]
        if isinstance(s, V):
            reads.append(s)
            s = s.ap
        return self.op(eng, lambda e: e.scalar_tensor_tensor(o, x, s, y, op0, op1), reads, [out])

    def copy(self, out, in_, eng="dve"):
        o, i = out.ap, in_.ap
        if eng == "act":
            return self.op("act", lambda e: e.copy(o, i), [in_], [out])
        return self.op(eng, lambda e: e.tensor_copy(o, i), [in_], [out])

    def memset(self, out, val, eng="dve"):
        o = out.ap
        return self.op(eng, lambda e: e.memset(o, val), [], [out])

    def recip(self, out, in_):
        o, i = out.ap, in_.ap
        return self.op("dve", lambda e: e.reciprocal(o, i), [in_], [out])

    def finish(self, final_tokens):
        nc = self.nc
        sems = {}
        for i, n in enumerate(ENGS):
            sems[i] = self.stack.enter_context(nc.semaphore(f"s_{n}"))
        for n in ENGS:
            e = self.eng[n]
            if e.dma_count:
                for k in range(min(NDMA_SEMS, e.dma_count)):
                    sems[(n, k)] = self.stack.enter_context(nc.semaphore(f"d_{n}{k}"))
        sp = self.eng["sp"]
        fin = {}
        for k, v in final_tokens:
            if fin.get(k, 0) < v:
                fin[k] = v
        sp.ops.append((list(fin.items()), None, None))
        block = self.stack.enter_context(nc.Block())
        hw = {"pe": block.tensor, "act": block.scalar, "dve": block.vector, "pool": block.gpsimd, "sp": block.sync}

        def make(e):
            def body(eng):
                own = sems[e.idx]
                for waits, fn, dkey in e.ops:
                    for k, v in waits:
                        eng.wait_ge(sems[k], v)
                    if fn is None:
                        continue
                    ins = fn(eng)
                    if dkey is None:
                        ins.then_inc(own, 1)
                    else:
                        ins.then_inc(sems[dkey], 16)
            return body
        for n in ENGS:
            e = self.eng[n]
            if e.ops:
                hw[n](make(e))
        self.stack.close()
        return nc


S = 4096; D = 1024; T = 512; NT = 8; NCH = 32
EPS = 1e-6
OFF_Z, OFF_XBC, OFF_DT, OFF_GA, OFF_GG, OFF_QL, OFF_KV, OFF_KR, OFF_GATE = 0, 1024, 3072, 3088, 4112, 5136, 5520, 5776, 5840
C_MIXG, C_XATG, C_FFNG, C_MEMG, C_SCW, C_SCB, C_CDW, C_CDB, C_LNG, C_LNB = 0, 8, 16, 24, 32, 96, 112, 360, 368, 376
C_QAG, C_KVAG, C_QNN, C_QNR, C_QNRS, C_KNN, C_KNR, C_KNRS, C_GATEB, C_XQG, C_XKG, NPP = 384, 387, 389, 390, 391, 392, 393, 394, 395, 419, 421, 424
B_DTB, B_ALOG, B_DSK, B_SNG, NPB = 0, 16, 32, 48, 1072
K_TRI, K_GT, K_ID, K_INV, K_MASK, NCST = 0, 128, 256, 384, 385, 385 + 2048


class Arena:
    def __init__(self, big, base, limit):
        self.big, self.off, self.limit = big, base, limit

    def alloc(self, n, dtype, parts=128):
        nb = n * (4 if dtype in (F32, I32) else 2)
        nb = (nb + 63) // 64 * 64
        ne = nb // 2
        assert self.off + ne <= self.limit, ("SBUF arena overflow", self.off + ne, self.limit)
        ap = self.big[:, self.off:self.off + ne]
        self.off += ne
        if dtype != BF16:
            ap = ap.bitcast(dtype)
        return ap[:, 0:n]


def build(L=4, dbg=False, only=None):
    P = Prog(); nc = P.nc
    kind_s = "ExternalOutput" if dbg else "Internal"
    din = lambda n, s, dt=F32: P.dram(n, s, dt, kind="ExternalInput")
    xT_d = din("xT", [D, S]); memT_d = din("memT", [D, 256]); pos_d = din("pos", [1, S], I32)
    cst_d = din("cst", [128, NCST]); pp_d = din("pp", [4, 128, NPP]); pb_d = din("pb", [4, NPB])
    w_in_d = din("w_in", [4, D, 8912]); ssd_wo_d = din("ssd_w_out", [4, D, D]); conv_wo_d = din("conv_w_out", [4, D, D])
    wqb_d = din("mla_w_q_b", [4, 384, 1536]); wkvb_d = din("mla_w_kv_b", [4, 256, 2048]); mla_wo_d = din("mla_w_o", [4, D, D])
    wout_d = din("w_out", [4, D, D]); xwq_d = din("xattn_w_q", [4, D, D]); xwkv_d = din("xattn_w_kv", [4, D, 2048])
    xwo_d = din("xattn_w_o", [4, D, D]); fwi_d = din("ffn_w_in", [4, D, 5632]); fwo_d = din("ffn_w_out", [4, 2816, D])
    out_d = P.dram("outT", [D, S], F32, kind="ExternalOutput")
    sc = lambda n, s, dt=F32: P.dram(n, s, dt, kind=kind_s)
    xres_d = sc("xres", [D, S]); sz_d = sc("sz", [S, D]); xbc_d = sc("xbc", [2048, S], BF16); cv_d = sc("cv", [D, S])
    lat_d = sc("lat", [768, S]); ys_d = sc("ys", [D, S], BF16); yc_d = sc("yc", [D, S], BF16)
    q_d = sc("qs", [8, 192, S], BF16); k_d = sc("ks", [8, 128, S], BF16); kr_d = sc("krs", [64, S], BF16)
    v_d = sc("vs", [S, D], BF16); o_d = sc("os", [D, S], BF16); fp_d = sc("fpart", [D, S]); rope_d = sc("rope", [128, S])
    hd = {}

    def dh(name, t):
        k = (name, t)
        if k not in hd:
            hd[k] = H()
        return hd[k]

    def dall(name, n=NT):
        return [dh(name, t) for t in range(n)]

    big = P.sb([128, 103000], BF16, name="big")
    banks = [P.ps([128, 512], F32, name=f"bank{i}") for i in range(8)]
    bh = [H() for _ in range(8)]

    def PS(i, n=512, parts=128, dt=F32):
        ap = banks[i][:]
        if dt == BF16:
            ap = ap.bitcast(BF16)
        return V(ap[0:parts, 0:n], bh[i])

    pers = Arena(big, 0, 8000)
    cst = pers.alloc(K_MASK, F32); h_cst = H()
    tri = V(cst[:, K_TRI:K_TRI + 128], h_cst); gt = V(cst[:, K_GT:K_GT + 128], h_cst)
    inv_c = V(cst[:, K_INV:K_INV + 1], h_cst)
    ident = pers.alloc(128, BF16); h_id = H(); identv = V(ident, h_id)
    ones_b = pers.alloc(128, BF16); h_ob = H(); onesb = V(ones_b, h_ob)
    ones_f = pers.alloc(128, F32); h_of = H(); onesf = V(ones_f, h_of)
    masks = pers.alloc(2048, BF16); h_mk = H()
    epsc = pers.alloc(1, F32); h_eps = H(); epsv = V(epsc, h_eps)
    ppt = pers.alloc(NPP, F32); h_pp = H()
    pbt = pers.alloc(NPB, F32); h_pb = H()
    abc = pers.alloc(16, F32); h_abc = H()
    gsc = pers.alloc(8, F32); h_gsc = H()
    ARENA0 = pers.off

    def pcol(c, n=1, parts=128):
        return V(ppt[0:parts, c:c + n], h_pp)

    def barrier():
        toks = {}
        for n in ENGS:
            e = P.eng[n]
            if e.n:
                toks[e.idx] = e.n
            for k in range(min(NDMA_SEMS, e.dma_count)):
                rnd = (e.dma_count - 1 - k) // NDMA_SEMS
                toks[(n, k)] = 16 * (rnd + 1)
        for n in ENGS:
            e = P.eng[n]
            waits = P._waits(e, dict(toks))
            if waits:
                e.ops.append((waits, None, None))

    P.dma("sp", V(cst, h_cst), V(cst_d[:, 0:K_MASK], H()))
    P.dma("pool", V(ident, h_id), V(cst_d[:, K_ID:K_ID + 128], H()))
    P.dma("pool", V(masks, h_mk), V(cst_d[:, K_MASK:K_MASK + 2048], H()))
    P.memset(onesb, 1.0); P.memset(onesf, 1.0); P.memset(epsv, EPS)

    def rope_tables():
        A = Arena(big, ARENA0, 103000)
        for t in range(NT):
            sl = slice(t * T, (t + 1) * T)
            pi_ = A.alloc(T, I32) if t == 0 else rope_tables.bufs[0]
            if t == 0:
                rope_tables.bufs = [pi_] + [A.alloc(T, F32) for _ in range(5)] + [A.alloc(T, I32)]
            pi_, pf, ang, kf, r, rc, ki = rope_tables.bufs
            hs = [H() for _ in range(7)]
            pi_v, pf_v, ang_v, kf_v, r_v, rc_v, ki_v = [V(a[0:64], h) for a, h in zip(rope_tables.bufs, hs)]
            P.dma("sp", pi_v, V(pos_d[0, sl].partition_broadcast(64), H()))
            P.copy(pf_v, pi_v)
            P.ts(ang_v, pf_v, V(cst[0:64, K_INV:K_INV + 1], h_cst), ALU.mult)
            P.ts(kf_v, ang_v, float(1.0 / (2 * np.pi)), ALU.mult)
            P.copy(ki_v, kf_v)
            P.copy(kf_v, ki_v)
            C1 = 6.28125; C2 = float(np.float32(2 * np.pi - 6.28125))
            P.stt(r_v, kf_v, -C1, ang_v, ALU.mult, ALU.add)
            P.stt(r_v, kf_v, -C2, r_v, ALU.mult, ALU.add)
            P.ts(r_v, r_v, 3.1415925, ALU.min, -3.1415925, ALU.max)
            P.ts(rc_v, r_v, float(np.pi / 2), ALU.is_gt, float(-2 * np.pi), ALU.mult)
            P.stt(rc_v, r_v, float(np.pi / 2), rc_v, ALU.add, ALU.add)
            P.ts(rc_v, rc_v, 3.1415925, ALU.min, -3.1415925, ALU.max)
            P.act(rc_v, rc_v, AF.Sin)
            P.act(r_v, r_v, AF.Sin)
            P.ts(V(r[0:32], hs[4]), V(r[0:32], hs[4]), -1.0, ALU.mult)
            P.dma("sp", V(rope_d[0:64, sl], dh("rope", t)), rc_v)
            P.dma("sp", V(rope_d[64:128, sl], dh("rope", t)), r_v)
            barrier()
    rope_tables()
    barrier()

    def wload(dst, src, hdst):
        return P.dma("pool", V(dst, hdst), V(src, H()))

    def fmview(d, t):
        return d.rearrange("(k p) s -> p k s", p=128)[:, :, t * T:(t + 1) * T]

    def xnorm(xt, hx, gcol, outb, hout, sq, hsq, rs, hrs, bank, nk=8, width=T, inv_n=1.0 / D):
        P.act(V(sq, hsq), V(xt, hx), AF.Square)
        for k in range(nk):
            P.mm(PS(bank, width), onesb, V(sq[:, k, :], hsq), start=(k == 0), stop=(k == nk - 1))
        P.act(V(rs, hrs), PS(bank, width), AF.Ln, bias=epsv, scale=inv_n)
        P.act(V(rs, hrs), V(rs, hrs), AF.Exp, scale=-0.5)
        for k in range(nk):
            P.stt(V(outb[:, k, :], hout), V(xt[:, k, :], hx), pcol(gcol + k), V(rs, hrs), ALU.mult, ALU.mult)

    dtraw = pers.alloc(NCH * 16, F32).rearrange("p (c h) -> p c h", c=NCH); h_dtraw = H()
    ARENA0 = pers.off
    final_toks = []

    def phase_A(l):
        xsrc = xT_d if l == 0 else xres_d
        xsn = "xT" if l == 0 else "xres"
        barrier()
        P.dma("sp", V(ppt, h_pp), V(pp_d[l], H()))
        P.dma("sp", V(pbt, h_pb), V(pb_d[l, :].partition_broadcast(128), H()))
        P.act(V(abc, h_abc), V(pbt[:, B_ALOG:B_ALOG + 16], h_pb), AF.Exp)
        P.ts(V(abc, h_abc), V(abc, h_abc), -1.0, ALU.mult)
        P.ts(V(gsc[:, 0:3], h_gsc), V(ppt[:, C_QNN:C_QNN + 3], h_pp), float(192 ** -0.5), ALU.mult)
        P.ts(V(gsc[:, 3:5], h_gsc), V(ppt[:, C_XQG:C_XQG + 2], h_pp), float(256 ** -0.5), ALU.mult)

        A = Arena(big, ARENA0, 103000)
        uT = A.alloc(8 * S, BF16).rearrange("p (k s) -> p k s", k=8); hu = [H() for _ in range(NT)]
        A1 = A.off
        xt = [A.alloc(8 * T, F32).rearrange("p (k s) -> p k s", k=8) for _ in range(2)]; hxt = [H(), H()]
        sq = A.alloc(8 * T, BF16).rearrange("p (k s) -> p k s", k=8); hsq = H()
        rs = A.alloc(T, F32); hrs = H()
        for t in range(NT):
            b = t % 2
            P.dma("sp", V(xt[b], hxt[b]), V(fmview(xsrc, t), dh(xsn, t)))
            xnorm(xt[b], hxt[b], C_MIXG, uT[:, :, t * T:(t + 1) * T], hu[t], sq, hsq, rs, hrs, 0)
        barrier()
        A.off = A1
        wz = A.alloc(8 * 1040, BF16).rearrange("p (k n) -> p k n", k=8); hwz = H()
        wload(wz[:, :, 0:1024], w_in_d[l].rearrange("(k p) n -> p k n", p=128)[:, :, OFF_Z:OFF_Z + 1024], hwz)
        wload(wz[:, :, 1024:1040], w_in_d[l].rearrange("(k p) n -> p k n", p=128)[:, :, OFF_DT:OFF_DT + 16], hwz)
        szb = [A.alloc(1024, F32) for _ in range(2)]; hszb = [H(), H()]
        for c in range(NCH):
            t = c // 4; b = c % 2
            us = lambda k: V(uT[:, k, c * 128:(c + 1) * 128], hu[t])
            for half in range(2):
                for k in range(8):
                    P.mm(PS(1 + half), us(k), V(wz[:, k, half * 512:(half + 1) * 512], hwz), start=(k == 0), stop=(k == 7))
            for k in range(8):
                P.mm(PS(3, 16), us(k), V(wz[:, k, 1024:1040], hwz), start=(k == 0), stop=(k == 7))
            for half in range(2):
                P.act(V(szb[b][:, half * 512:(half + 1) * 512], hszb[b]), PS(1 + half), AF.Silu)
            P.copy(V(dtraw[:, c, :], h_dtraw), PS(3, 16))
            P.dma("pool", V(sz_d[c * 128:(c + 1) * 128, :], dh("sz", t)), V(szb[b], hszb[b]))
        barrier()
        A.off = A1
        wg = [A.alloc(8 * 512, BF16).rearrange("p (k n) -> p k n", k=8) for _ in range(2)]; hwg = [H(), H()]
        pre = [A.alloc(S + 32, F32) for _ in range(2)]; hpre = [H(), H()]
        acc = [A.alloc(S, F32) for _ in range(2)]; hacc = [H(), H()]
        xo = [A.alloc(S, BF16) for _ in range(2)]; hxo = [H(), H()]
        for b in range(2):
            P.memset(V(pre[b][:, 0:32], hpre[b]), 0.0)
        win_v = w_in_d[l].rearrange("(k p) n -> p k n", p=128)
        wload(wg[0], win_v[:, :, OFF_XBC:OFF_XBC + 512], hwg[0])
        pend = None
        for j in range(16):
            g4 = j // 4; wb = g4 % 2; b = j % 2
            if j % 4 == 0 and g4 + 1 < 4:
                wload(wg[1 - wb], win_v[:, :, OFF_XBC + (g4 + 1) * 512:OFF_XBC + (g4 + 2) * 512], hwg[1 - wb])
            for t in range(NT):
                bank = 1 + (t % 2)
                for k in range(8):
                    P.mm(PS(bank), V(wg[wb][:, k, (j % 4) * 128:(j % 4 + 1) * 128], hwg[wb]),
                         V(uT[:, k, t * T:(t + 1) * T], hu[t]), start=(k == 0), stop=(k == 7))
                P.copy(V(pre[b][:, 32 + t * T:32 + (t + 1) * T], hpre[b]), PS(bank), eng="act")
            P.ts(V(acc[b], hacc[b]), V(pre[b][:, 29:29 + S], hpre[b]), pcol(C_SCW + j * 4 + 0), ALU.mult, pcol(C_SCB + j), ALU.add)
            for tap in range(1, 4):
                P.stt(V(acc[b], hacc[b]), V(pre[b][:, 29 + tap:29 + tap + S], hpre[b]), pcol(C_SCW + j * 4 + tap),
                      V(acc[b], hacc[b]), ALU.mult, ALU.add)
            if pend is not None:
                pend()

            def fin(b=b, j=j):
                P.act(V(xo[b], hxo[b]), V(acc[b], hacc[b]), AF.Silu)
                P.dma("pool", V(xbc_d[j * 128:(j + 1) * 128, :], dall("xbc")), V(xo[b], hxo[b]))
            pend = fin
        pend()
        barrier()
        A.off = A1
        wa = [A.alloc(8 * 128, BF16).rearrange("p (k n) -> p k n", k=8) for _ in range(2)]; hwa = [H(), H()]
        wgt = [A.alloc(8 * 128, BF16).rearrange("p (k n) -> p k n", k=8) for _ in range(2)]; hwgt = [H(), H()]
        vb = [A.alloc(S + 32, BF16) for _ in range(2)]; hvb = [[H() for _ in range(NT)] for _ in range(2)]; hvz = [H(), H()]
        dg = [A.alloc(31 * 128, BF16).rearrange("p (j n) -> p j n", j=31) for _ in range(2)]; hdg = [H(), H()]
        acc = [A.alloc(S, F32) for _ in range(2)]; hacc = [H(), H()]
        sg = [A.alloc(T, F32) for _ in range(2)]; hsg = [H(), H()]
        for b in range(2):
            P.memset(V(vb[b][:, 0:32], hvz[b]), 0.0)
        def prepj(j):
            b = j % 2
            wload(wa[b], win_v[:, :, OFF_GA + j * 128:OFF_GA + (j + 1) * 128], hwa[b])
            wload(wgt[b], win_v[:, :, OFF_GG + j * 128:OFF_GG + (j + 1) * 128], hwgt[b])
            for tap in range(31):
                P.ts(V(dg[b][:, tap, :], hdg[b]), identv, pcol(C_CDW + j * 31 + tap), ALU.mult)
        prepj(0)
        for j in range(8):
            b = j % 2
            if j + 1 < 8:
                prepj(j + 1)

            def glu(t):
                sb_ = t % 2
                for k in range(8):
                    P.mm(PS(1 + sb_), V(wa[b][:, k, :], hwa[b]), V(uT[:, k, t * T:(t + 1) * T], hu[t]), start=(k == 0), stop=(k == 7))
                for k in range(8):
                    P.mm(PS(3 + sb_), V(wgt[b][:, k, :], hwgt[b]), V(uT[:, k, t * T:(t + 1) * T], hu[t]), start=(k == 0), stop=(k == 7))
                P.act(V(sg[sb_], hsg[sb_]), PS(3 + sb_), AF.Sigmoid)
                P.tt(V(vb[b][:, 32 + t * T:32 + (t + 1) * T], hvb[b][t]), PS(1 + sb_), V(sg[sb_], hsg[sb_]), ALU.mult)

            def conv(t):
                cbk = 5 + t % 2
                rd_h = [hvb[b][t]] + ([hvb[b][t - 1]] if t > 0 else [hvz[b]])
                for tap in range(31):
                    o0 = 2 + tap + t * T
                    P.mm(PS(cbk), V(dg[b][:, tap, :], hdg[b]), V(vb[b][:, o0:o0 + T], rd_h), start=(tap == 0), stop=(tap == 30))
                P.act(V(acc[b][:, t * T:(t + 1) * T], hacc[b]), PS(cbk), AF.Identity, bias=pcol(C_CDB + j))
            glu(0)
            for t in range(NT):
                if t + 1 < NT:
                    glu(t + 1)
                conv(t)
            P.dma("pool", V(cv_d[j * 128:(j + 1) * 128, :], dall("cv")), V(acc[b], hacc[b]))
        barrier()
        A.off = A1
        wl = A.alloc(8 * 768, BF16).rearrange("p (k n) -> p k n", k=8); hwl = H()
        wload(wl[:, :, 0:704], win_v[:, :, OFF_QL:OFF_QL + 704], hwl)
        wload(wl[:, :, 704:736], win_v[:, :, OFF_KR + 32:OFF_KR + 64], hwl)
        wload(wl[:, :, 736:768], win_v[:, :, OFF_KR:OFF_KR + 32], hwl)
        lo = [A.alloc(T, F32) for _ in range(2)]; hlo = [H(), H()]
        segs = [(0, 128), (128, 128), (256, 128), (384, 128), (512, 128), (640, 64), (704, 64)]
        i = 0
        for t in range(NT):
            for (c0, m) in segs:
                b = i % 2; i += 1
                for k in range(8):
                    P.mm(PS(1 + b, T, m), V(wl[:, k, c0:c0 + m], hwl), V(uT[:, k, t * T:(t + 1) * T], hu[t]), start=(k == 0), stop=(k == 7))
                P.copy(V(lo[b][0:m], hlo[b]), PS(1 + b, T, m), eng="act")
                P.dma("pool", V(lat_d[c0:c0 + m, t * T:(t + 1) * T], dh("lat", t)), V(lo[b][0:m], hlo[b]))
        barrier()
    def phase_S(l):
        A = Arena(big, ARENA0, 103000)
        xbt = [A.alloc(16 * T, BF16).rearrange("p (k s) -> p k s", k=16) for _ in range(2)]; hxb = [H(), H()]
        szt = [A.alloc(1024, F32) for _ in range(2)]; hszt = [H(), H()]
        hst = A.alloc(1024, F32); h_hst = H()
        hbf = A.alloc(1024, BF16); h_hbf = H()
        D2 = lambda n, dt: ([A.alloc(n, dt) for _ in range(2)], [H(), H()])
        sm2, h_sm2 = D2(64, F32)
        ew2, h_ew2 = D2(48, F32)
        Xb2, h_Xb2 = D2(1024, BF16)
        Xw2, h_Xw2 = D2(1024, BF16)
        tD2, h_tD2 = D2(1024, F32)
        Bt2, h_Bt2 = D2(512, BF16)
        MT2, h_MT2 = D2(2048, BF16)
        cbm = A.alloc(512, F32); h_cbm = H()
        AG = A.alloc(2048, F32); h_AG = H()
        dec = [A.alloc(512, F32) for _ in range(2)]; h_dec = [H(), H()]
        yg = A.alloc(1024, F32); h_yg = H()
        t1 = [A.alloc(512, F32) for _ in range(2)]; h_t1 = [H(), H()]
        ssq = A.alloc(8, F32); h_ssq = H()
        junk = A.alloc(256, F32); h_junk = H()
        ynb = A.alloc(1024, BF16); h_ynb = H()
        yst = [A.alloc(8 * T, BF16).rearrange("p (k s) -> p k s", k=8) for _ in range(2)]; h_yst = [H(), H()]
        P.memset(V(hst, h_hst), 0.0); P.memset(V(hbf, h_hbf), 0.0)
        dtb = V(pbt[:, B_DTB:B_DTB + 16], h_pb)
        sng = V(pbt[:, B_SNG:B_SNG + 1024], h_pb)
        xview = xbc_d.rearrange("(k p) s -> p k s", p=128)
        SMALL = lambda lo, n: V(banks[2][:, 256 + lo:256 + lo + n], bh[2])

        def bc16(ap, lo, n):
            return ap[:, lo:lo + n].unsqueeze(2).broadcast_to([128, n, 64])
        v3 = lambda ap: ap.rearrange("p (h d) -> p h d", h=16)

        def stage1(c):
            t = c // 4; s_ = c % 4; tb = t % 2; cb_ = c % 2
            cs = slice(s_ * 128, (s_ + 1) * 128)
            if s_ == 0:
                P.dma("sp", V(xbt[tb], hxb[tb]), V(xview[:, :, t * T:(t + 1) * T], dall("xbc")))
            P.dma("sp", V(szt[cb_], hszt[cb_]), V(sz_d[c * 128:(c + 1) * 128, :], dh("sz", t)))
            xb_ = xbt[tb]; hx_ = hxb[tb]
            sm, h_sm, ew, h_ew = sm2[cb_], h_sm2[cb_], ew2[cb_], h_ew2[cb_]
            Xb, h_Xb, Xw, h_Xw, tD, h_tD = Xb2[cb_], h_Xb2[cb_], Xw2[cb_], h_Xw2[cb_], tD2[cb_], h_tD2[cb_]
            Btm, h_Btm, MT, h_MT = Bt2[cb_], h_Bt2[cb_], MT2[cb_], h_MT2[cb_]
            for k in range(8):
                P.transpose(V(banks[1][:].bitcast(BF16)[:, k * 128:(k + 1) * 128], bh[1]), V(xb_[:, k, cs], hx_), identv)
            for g in range(4):
                P.transpose(V(banks[2][:].bitcast(BF16)[:, g * 128:(g + 1) * 128], bh[2]), V(xb_[:, 8 + g, cs], hx_), identv)
            for g in range(4):
                P.mm(V(banks[3][:, g * 128:(g + 1) * 128], bh[3]), V(xb_[:, 8 + g, cs], hx_), V(xb_[:, 12 + g, cs], hx_))
            P.tt(V(sm[:, 0:16], h_sm), V(dtraw[:, c, :], h_dtraw), dtb, ALU.add)
            P.act(V(sm[:, 0:16], h_sm), V(sm[:, 0:16], h_sm), AF.Exp)
            P.act(V(sm[:, 0:16], h_sm), V(sm[:, 0:16], h_sm), AF.Ln, bias=1.0)
            P.tt(V(sm[:, 16:32], h_sm), V(sm[:, 0:16], h_sm), V(abc, h_abc), ALU.mult)
            av = V(sm[:, 16:32], h_sm)
            P.mm(SMALL(0, 16), tri, av); P.mm(SMALL(16, 16), gt, av); P.mm(SMALL(32, 16), onesf, av)
            P.act(V(ew, h_ew), SMALL(0, 48), AF.Exp)
            P.tt(V(sm[:, 32:48], h_sm), V(sm[:, 0:16], h_sm), V(ew[:, 16:32], h_ew), ALU.mult)
            P.tt(V(AG.rearrange("p (h s) -> p h s", h=16), h_AG),
                 V(cst[:, K_GT:K_GT + 128].unsqueeze(1).broadcast_to([128, 16, 128]), h_cst),
                 V(sm[:, 16:32].unsqueeze(2).broadcast_to([128, 16, 128]), h_sm), ALU.mult)
            P.tt(V(cbm.rearrange("p (g l) -> p g l", g=4), h_cbm), V(banks[3][:].rearrange("p (g l) -> p g l", g=4), bh[3]),
                 V(cst[:, K_TRI:K_TRI + 128].unsqueeze(1).broadcast_to([128, 4, 128]), h_cst), ALU.mult)
            xsT = banks[1][:].bitcast(BF16)[:, 0:1024].rearrange("p (h d) -> p h d", h=16)
            P.tt(V(v3(Xb), h_Xb), V(xsT, bh[1]), V(bc16(sm, 0, 16), h_sm), ALU.mult)
            P.tt(V(v3(Xw), h_Xw), V(xsT, bh[1]), V(bc16(sm, 32, 16), h_sm), ALU.mult)
            P.tt(V(v3(tD), h_tD), V(xsT, bh[1]), V(bc16(pbt, B_DSK, 16), h_pb), ALU.mult)
            P.copy(V(Btm, h_Btm), V(banks[2][:].bitcast(BF16)[:, 0:512], bh[2]), eng="act")
            for g in range(4):
                db = g % 2
                for r in range(4):
                    hd_ = g * 4 + r
                    P.mm(V(banks[4][:, r * 128:(r + 1) * 128], bh[4]), V(AG[:, hd_ * 128:(hd_ + 1) * 128], h_AG), tri)
                P.act(V(dec[db], h_dec[db]), PS(4), AF.Exp)
                P.tt(V(MT[:, g * 512:(g + 1) * 512].rearrange("p (r l) -> p r l", r=4), h_MT), V(dec[db].rearrange("p (r l) -> p r l", r=4), h_dec[db]),
                     V(cbm[:, g * 128:(g + 1) * 128].unsqueeze(1).broadcast_to([128, 4, 128]), h_cbm), ALU.mult)

        def stage2(c):
            t = c // 4; s_ = c % 4; tb = t % 2; cb_ = c % 2
            cs = slice(s_ * 128, (s_ + 1) * 128)
            xb_ = xbt[tb]; hx_ = hxb[tb]
            ew, h_ew = ew2[cb_], h_ew2[cb_]
            Xb, h_Xb, Xw, h_Xw, tD, h_tD = Xb2[cb_], h_Xb2[cb_], Xw2[cb_], h_Xw2[cb_], tD2[cb_], h_tD2[cb_]
            Btm, h_Btm, MT, h_MT = Bt2[cb_], h_Bt2[cb_], MT2[cb_], h_MT2[cb_]
            for hd_ in range(16):
                ybank = 5 + hd_ // 8
                col = (hd_ % 8) * 64
                P.mm(V(banks[ybank][:, col:col + 64], bh[ybank]), V(MT[:, hd_ * 128:(hd_ + 1) * 128], h_MT),
                     V(Xb[:, hd_ * 64:(hd_ + 1) * 64], h_Xb))
            for hf in range(2):
                ybank = 5 + hf
                for gg in range(2):
                    g = hf * 2 + gg
                    P.mm(V(banks[7][:, gg * 256:(gg + 1) * 256], bh[7]), V(xb_[:, 12 + g, cs], hx_), V(hbf[:, g * 256:(g + 1) * 256], h_hbf))
                t1_ = t1[hf]; ht1 = h_t1[hf]
                P.tt(V(t1_.rearrange("p (h d) -> p h d", h=8), ht1), V(banks[7][:].rearrange("p (h d) -> p h d", h=8), bh[7]), V(bc16(ew, hf * 8, 8), h_ew), ALU.mult)
                P.tt(V(t1_, ht1), V(t1_, ht1), PS(ybank), ALU.add)
                P.tt(V(t1_, ht1), V(t1_, ht1), V(tD[:, hf * 512:(hf + 1) * 512], h_tD), ALU.add, eng="pool")
                P.tt(V(yg[:, hf * 512:(hf + 1) * 512], h_yg), V(t1_, ht1), V(szt[cb_][:, hf * 512:(hf + 1) * 512], hszt[cb_]), ALU.mult)
            for hf in range(2):
                for gg in range(2):
                    g = hf * 2 + gg
                    P.mm(V(banks[7][:, gg * 256:(gg + 1) * 256], bh[7]), V(Btm[:, g * 128:(g + 1) * 128], h_Btm), V(Xw[:, g * 256:(g + 1) * 256], h_Xw))
                hv = V(hst[:, hf * 512:(hf + 1) * 512].rearrange("p (h d) -> p h d", h=8), h_hst)
                P.tt(hv, hv, V(bc16(ew, 32 + hf * 8, 8), h_ew), ALU.mult)
                P.tt(V(hst[:, hf * 512:(hf + 1) * 512], h_hst), V(hst[:, hf * 512:(hf + 1) * 512], h_hst), PS(7), ALU.add)
            P.copy(V(hbf, h_hbf), V(hst, h_hst), eng="act")
            for g in range(4):
                P.act(V(junk, h_junk), V(yg[:, g * 256:(g + 1) * 256], h_yg), AF.Square, accum=V(ssq[:, g:g + 1], h_ssq))
            P.act(V(ssq[:, 4:8], h_ssq), V(ssq[:, 0:4], h_ssq), AF.Ln, bias=epsv, scale=1.0 / 256)
            P.act(V(ssq[:, 4:8], h_ssq), V(ssq[:, 4:8], h_ssq), AF.Exp, scale=-0.5)
            P.tt(V(yg.rearrange("p (g d) -> p g d", g=4), h_yg), V(yg.rearrange("p (g d) -> p g d", g=4), h_yg),
                 V(ssq[:, 4:8].unsqueeze(2).broadcast_to([128, 4, 256]), h_ssq), ALU.mult)
            P.tt(V(ynb, h_ynb), V(yg, h_yg), sng, ALU.mult)
            for k in range(8):
                P.transpose(V(banks[0][:].bitcast(BF16)[:, k * 128:(k + 1) * 128], bh[0]), V(ynb[:, k * 128:(k + 1) * 128], h_ynb), identv)
            P.copy(V(yst[tb][:, :, cs], h_yst[tb]), V(banks[0][:].bitcast(BF16)[:, 0:1024].rearrange("p (k s) -> p k s", k=8), bh[0]), eng="act")
            if s_ == 3:
                P.dma("pool", V(ys_d.rearrange("(k p) s -> p k s", p=128)[:, :, t * T:(t + 1) * T], dh("ys", t)), V(yst[tb], h_yst[tb]))

        stage1(0)
        for c in range(NCH):
            la = P.capture()
            if c + 1 < NCH:
                stage1(c + 1)
            lb = P.capture()
            stage2(c)
            P.interleave(la, lb)
        barrier()

    def phase_C(l):
        A = Arena(big, ARENA0, 103000)
        ct = [A.alloc(8 * T, F32).rearrange("p (k s) -> p k s", k=8) for _ in range(2)]; hct = [H(), H()]
        cb16 = A.alloc(8 * T, BF16).rearrange("p (k s) -> p k s", k=8); hcb = H()
        sq = A.alloc(8 * T, BF16).rearrange("p (k s) -> p k s", k=8); hsq = H()
        mean = A.alloc(T, F32); hmean = H()
        var = A.alloc(T, F32); hvar = H()
        tmp = A.alloc(T, F32); htmp = H()
        yo = [A.alloc(8 * T, BF16).rearrange("p (k s) -> p k s", k=8) for _ in range(2)]; hyo = [H(), H()]
        for t in range(NT):
            b = t % 2
            P.dma("sp", V(ct[b], hct[b]), V(fmview(cv_d, t), dall("cv")))
            P.copy(V(cb16, hcb), V(ct[b], hct[b]), eng="act")
            P.act(V(sq, hsq), V(ct[b], hct[b]), AF.Square)
            for k in range(8):
                P.mm(PS(0), onesb, V(cb16[:, k, :], hcb), start=(k == 0), stop=(k == 7))
            for k in range(8):
                P.mm(PS(1), onesb, V(sq[:, k, :], hsq), start=(k == 0), stop=(k == 7))
            P.ts(V(mean, hmean), PS(0), 1.0 / D, ALU.mult)
            P.tt(V(tmp, htmp), V(mean, hmean), V(mean, hmean), ALU.mult)
            P.stt(V(var, hvar), PS(1), 1.0 / D, V(tmp, htmp), ALU.mult, ALU.subtract)
            P.act(V(var, hvar), V(var, hvar), AF.Ln, bias=epsv)
            P.act(V(var, hvar), V(var, hvar), AF.Exp, scale=-0.5)
            for k in range(8):
                P.tt(V(tmp, htmp), V(ct[b][:, k, :], hct[b]), V(mean, hmean), ALU.subtract)
                P.tt(V(tmp, htmp), V(tmp, htmp), V(var, hvar), ALU.mult)
                P.act(V(yo[b][:, k, :], hyo[b]), V(tmp, htmp), AF.Silu, bias=pcol(C_LNB + k), scale=pcol(C_LNG + k))
            P.dma("pool", V(fmview(yc_d, t), dh("yc", t)), V(yo[b], hyo[b]))
        barrier()
    def phase_M1(l):
        A = Arena(big, ARENA0, 103000)
        wq = A.alloc(3 * 1536, BF16).rearrange("p (k n) -> p k n", k=3); hwq = H()
        wqs = A.alloc(3 * 512, BF16).rearrange("p (k n) -> p k n", k=3); hwqs = H()
        wkn = A.alloc(2 * 1024, BF16).rearrange("p (k n) -> p k n", k=2); hwkn = H()
        wv = A.alloc(2 * 1024, BF16).rearrange("p (k n) -> p k n", k=2); hwv = H()
        wload(wq, wqb_d[l].rearrange("(k p) n -> p k n", p=128), hwq)
        qv = wqb_d[l].rearrange("(k p) (h c) -> p k h c", p=128, h=8)
        wqs4 = wqs.rearrange("p k (h c) -> p k h c", h=8)
        for k in range(3):
            wload(wqs4[:, k, :, 0:32], qv[:, k, :, 160:192], hwqs)
            wload(wqs4[:, k, :, 32:64], qv[:, k, :, 128:160], hwqs)
        kv4 = wkvb_d[l].rearrange("(k p) (h c) -> p k h c", p=128, h=8)
        for k in range(2):
            wload(wkn.rearrange("p k (h c) -> p k h c", h=8)[:, k], kv4[:, k, :, 0:128], hwkn)
            wload(wv.rearrange("p k (h c) -> p k h c", h=8)[:, k], kv4[:, k, :, 128:256], hwv)
        lt = [A.alloc(6 * T, F32).rearrange("p (k s) -> p k s", k=6) for _ in range(2)]; hlt = [H(), H()]
        rp = [A.alloc(T, F32) for _ in range(2)]; hrp = [H(), H()]
        rp2 = [A.alloc(T, F32) for _ in range(2)]; hrp2 = [H(), H()]
        sq = A.alloc(3 * T, BF16).rearrange("p (k s) -> p k s", k=3); hsq = H()
        rs = A.alloc(T, F32); hrs = H()
        qn = A.alloc(3 * T, BF16).rearrange("p (k s) -> p k s", k=3); hqn = H()
        kvn = A.alloc(2 * T, BF16).rearrange("p (k s) -> p k s", k=2); hkvn = H()
        sqh = [A.alloc(T, BF16) for _ in range(3)]; hsqh = [H() for _ in range(3)]
        rsh = [A.alloc(T, F32) for _ in range(3)]; hrsh = [H() for _ in range(3)]
        krb = A.alloc(T, F32); hkrb = H()
        ob = [A.alloc(T, BF16) for _ in range(6)]; hob = [H() for _ in range(6)]
        ta = A.alloc(T, F32); hta = H()
        tb_ = A.alloc(T, F32); htb = H()
        vb = [A.alloc(1024, BF16) for _ in range(2)]; hvb = [H(), H()]
        latv = lat_d.rearrange("(k p) s -> p k s", p=128)
        oi = 0
        for t in range(NT):
            b = t % 2; sl = slice(t * T, (t + 1) * T)
            P.dma("sp", V(lt[b], hlt[b]), V(latv[:, :, sl], dh("lat", t)))
            P.dma("sp", V(lt[b][0:64, 5, :], hlt[b]), V(lat_d[704:768, sl], dh("lat", t)))
            P.dma("sp", V(rp[b][0:64], hrp[b]), V(rope_d[0:64, sl], dh("rope", t)))
            P.dma("sp", V(rp2[b][0:64], hrp2[b]), V(rope_d[64:128, sl], dh("rope", t)))
            L_ = lt[b]; hL = hlt[b]
            cosv = V(rp[b][0:64], hrp[b]); sinv = V(rp2[b][0:64], hrp2[b])
            xnorm(L_[:, 0:3, :], hL, C_QAG, qn, hqn, sq, hsq, rs, hrs, 0, nk=3, inv_n=1.0 / 384)
            xnorm(L_[:, 3:5, :], hL, C_KVAG, kvn, hkvn, sq[:, 0:2, :], hsq, rs, hrs, 0, nk=2, inv_n=1.0 / 256)

            for h in range(8):
                for k in range(3):
                    P.mm(PS(1), V(wq[:, k, h * 192:h * 192 + 128], hwq), V(qn[:, k, :], hqn), start=(k == 0), stop=(k == 2))
                for k in range(3):
                    P.mm(PS(2, T, 64), V(wq[:, k, h * 192 + 128:h * 192 + 192], hwq), V(qn[:, k, :], hqn), start=(k == 0), stop=(k == 2))
                for k in range(3):
                    P.mm(PS(3, T, 64), V(wqs[:, k, h * 64:(h + 1) * 64], hwqs), V(qn[:, k, :], hqn), start=(k == 0), stop=(k == 2))
                for k in range(2):
                    P.mm(PS(4), V(wkn[:, k, h * 128:(h + 1) * 128], hwkn), V(kvn[:, k, :], hkvn), start=(k == 0), stop=(k == 1))
                specs = [(1, 128, 1.0 / 128, 5), (2, 64, 1.0 / 64, 6), (4, 128, 1.0 / 128, 7)]
                for i, (bank, parts, inv_n, nb) in enumerate(specs):
                    P.act(V(sqh[i][0:parts], hsqh[i]), PS(bank, T, parts), AF.Square)
                for i, (bank, parts, inv_n, nb) in enumerate(specs):
                    P.mm(PS(nb, T, parts), V(ones_b[0:parts, 0:parts], h_ob), V(sqh[i][0:parts], hsqh[i]))
                for i, (bank, parts, inv_n, nb) in enumerate(specs):
                    P.act(V(rsh[i][0:parts], hrsh[i]), PS(nb, T, parts), AF.Ln, bias=V(epsc[0:parts], h_eps), scale=inv_n)
                    P.act(V(rsh[i][0:parts], hrsh[i]), V(rsh[i][0:parts], hrsh[i]), AF.Exp, scale=-0.5)
                o1 = oi % 6; o2 = (oi + 1) % 6; o3 = (oi + 2) % 6; oi += 3
                P.stt(V(ob[o1], hob[o1]), PS(1), V(gsc[:, 0:1], h_gsc), V(rsh[0], hrsh[0]), ALU.mult, ALU.mult)
                P.stt(V(ob[o3], hob[o3]), PS(4), pcol(C_KNN), V(rsh[2], hrsh[2]), ALU.mult, ALU.mult)
                P.stt(V(ta[0:64], hta), PS(2, T, 64), V(gsc[0:64, 1:2], h_gsc), cosv, ALU.mult, ALU.mult)
                P.stt(V(tb_[0:64], htb), PS(3, T, 64), V(gsc[0:64, 2:3], h_gsc), sinv, ALU.mult, ALU.mult)
                P.tt(V(ta[0:64], hta), V(ta[0:64], hta), V(tb_[0:64], htb), ALU.add)
                P.tt(V(ob[o2][0:64], hob[o2]), V(ta[0:64], hta), V(rsh[1][0:64], hrsh[1]), ALU.mult)
                P.dma("pool", V(q_d[h, 0:128, sl], dh("q", t)), V(ob[o1], hob[o1]))
                P.dma("pool", V(q_d[h, 128:192, sl], dh("q", t)), V(ob[o2][0:64], hob[o2]))
                P.dma("pool", V(k_d[h, :, sl], dh("k", t)), V(ob[o3], hob[o3]))
            P.dma("sp", V(krb[0:64], hkrb), V(lat_d[640:704, sl], dh("lat", t)))
            krv = V(krb[0:64], hkrb); krsv = V(L_[0:64, 5, :], hL)
            P.act(V(sqh[1][0:64], hsqh[1]), krv, AF.Square)
            P.mm(PS(7, T, 64), V(ones_b[0:64, 0:64], h_ob), V(sqh[1][0:64], hsqh[1]))
            P.act(V(rsh[1][0:64], hrsh[1]), PS(7, T, 64), AF.Ln, bias=V(epsc[0:64], h_eps), scale=1.0 / 64)
            P.act(V(rsh[1][0:64], hrsh[1]), V(rsh[1][0:64], hrsh[1]), AF.Exp, scale=-0.5)
            P.stt(V(ta[0:64], hta), krv, pcol(C_KNR, 1, 64), cosv, ALU.mult, ALU.mult)
            P.stt(V(tb_[0:64], htb), krsv, pcol(C_KNRS, 1, 64), sinv, ALU.mult, ALU.mult)
            P.tt(V(ta[0:64], hta), V(ta[0:64], hta), V(tb_[0:64], htb), ALU.add)
            o = oi % 6; oi += 1
            P.tt(V(ob[o][0:64], hob[o]), V(ta[0:64], hta), V(rsh[1][0:64], hrsh[1]), ALU.mult)
            P.dma("pool", V(kr_d[:, sl], dh("kr", t)), V(ob[o][0:64], hob[o]))
            for s_ in range(4):
                vb_ = s_ % 2
                for half in range(2):
                    for k in range(2):
                        P.mm(PS(2 + half), V(kvn[:, k, s_ * 128:(s_ + 1) * 128], hkvn), V(wv[:, k, half * 512:(half + 1) * 512], hwv),
                             start=(k == 0), stop=(k == 1))
                    P.copy(V(vb[vb_][:, half * 512:(half + 1) * 512], hvb[vb_]), PS(2 + half), eng="act")
                r0 = t * T + s_ * 128
                P.dma("pool", V(v_d[r0:r0 + 128, :], dh("v", t)), V(vb[vb_], hvb[vb_]))
        barrier()

    def phase_M2(l):
        A = Arena(big, ARENA0, 103000)
        krt = A.alloc(S, BF16); hkr = H()
        P.memset(V(krt[64:128], hkr), 0.0)
        P.dma("sp", V(krt[0:64], hkr), V(kr_d, dall("kr")))
        kt_ = [A.alloc(S, BF16) for _ in range(2)]; hkt = [H(), H()]
        qnt = [A.alloc(S, BF16) for _ in range(2)]; hqn = [H(), H()]
        qrt = [A.alloc(S, BF16) for _ in range(2)]; hqr = [H(), H()]
        for b_ in range(2):
            P.memset(V(qrt[b_][64:128], hqr[b_]), 0.0)
        vt = [A.alloc(NCH * 128, BF16).rearrange("p (c d) -> p c d", c=NCH) for _ in range(2)]; hvt = [H(), H()]
        pt = [A.alloc(T, BF16) for _ in range(4)]; hpt = [H() for _ in range(4)]
        pd = [A.alloc(T, BF16) for _ in range(4)]; hpd = [H() for _ in range(4)]
        for j in range(4):
            P.memset(V(pd[j], hpd[j]), 0.0)
        dacc = [A.alloc(T, F32) for _ in range(2)]; hdacc = [H(), H()]
        rd = A.alloc(T, F32); hrd = H()
        dhi = A.alloc(T, BF16); hdhi = H()
        dlo = A.alloc(T, BF16); hdlo = H()
        oo = [A.alloc(T, BF16) for _ in range(2)]; hoo = [H(), H()]
        mk = masks.rearrange("p (j q) -> p j q", j=4)
        pi = 0
        for h in range(8):
            b = h % 2
            P.dma("sp", V(kt_[b], hkt[b]), V(k_d[h], dall("k")))
            P.dma("sp", V(qnt[b], hqn[b]), V(q_d[h, 0:128, :], dall("q")))
            P.dma("sp", V(qrt[b][0:64], hqr[b]), V(q_d[h, 128:192, :], dall("q")))
            P.dma("sp", V(vt[b], hvt[b]), V(v_d.rearrange("(c p) d -> p c d", p=128)[:, :, h * 128:(h + 1) * 128], dall("v")))
            for qt in range(NT):
                qs = slice(qt * T, (qt + 1) * T)
                nk = 4 * qt + 4
                ob_ = 2 + (qt % 2)
                da = qt % 2

                def st(kt):
                    sbk = kt % 2
                    ks = slice(kt * 128, (kt + 1) * 128)
                    P.mm(PS(sbk), V(kt_[b][:, ks], hkt[b]), V(qnt[b][:, qs], hqn[b]), start=True, stop=False)
                    P.mm(PS(sbk), V(krt[:, ks], hkr), V(qrt[b][:, qs], hqr[b]), start=False, stop=True)
                st(0)
                for kt in range(nk):
                    if kt + 1 < nk:
                        st(kt + 1)
                    if kt >= 4 * qt:
                        j = kt - 4 * qt
                        pv_ = V(pd[j], hpd[j])
                        bk = banks[kt % 2]
                        P.act(V(pd[j][0:64, 128 * j:T], hpd[j]), V(bk[0:64, 128 * j:T], bh[kt % 2]), AF.Exp)
                        P.act(V(pd[j][64:128, 128 * j + 64:T], hpd[j]), V(bk[64:128, 128 * j + 64:T], bh[kt % 2]), AF.Exp)
                    else:
                        p_ = pi % 4; pi += 1
                        pv_ = V(pt[p_], hpt[p_])
                        P.act(pv_, PS(kt % 2), AF.Exp)
                    P.mm(PS(ob_), V(vt[b][:, kt, :], hvt[b]), pv_, start=(kt == 0), stop=(kt == nk - 1))
                    if kt % 2 == 1:
                        P.mm(PS(4 + da), onesb, pv_, start=(kt == 1), stop=False)
                    elif kt == 0:
                        P.copy(V(dacc[da], hdacc[da]), pv_)
                    else:
                        P.tt(V(dacc[da], hdacc[da]), V(dacc[da], hdacc[da]), pv_, ALU.add)
                P.copy(V(dhi, hdhi), V(dacc[da], hdacc[da]))
                P.tt(V(dlo, hdlo), V(dacc[da], hdacc[da]), V(dhi, hdhi), ALU.subtract)
                P.mm(PS(4 + da), onesb, V(dhi, hdhi), start=False, stop=False)
                P.mm(PS(4 + da), onesb, V(dlo, hdlo), start=False, stop=True)
                P.act(V(rd, hrd), PS(4 + da), AF.Ln)
                P.act(V(rd, hrd), V(rd, hrd), AF.Exp, scale=-1.0)
                o = qt % 2
                P.tt(V(oo[o], hoo[o]), PS(ob_), V(rd, hrd), ALU.mult)
                P.dma("pool", V(o_d[h * 128:(h + 1) * 128, qs], dh("o", qt)), V(oo[o], hoo[o]))
        barrier()
    def phase_G(l):
        xsrc = xT_d if l == 0 else xres_d
        xsn = "xT" if l == 0 else "xres"
        A = Arena(big, ARENA0, 103000)
        wgate = A.alloc(8 * 3072, BF16).rearrange("p (k n) -> p k n", k=8); hwg = H()
        wbr = [A.alloc(8 * 1024, BF16).rearrange("p (k n) -> p k n", k=8) for _ in range(3)]; hwb = [H() for _ in range(3)]
        wo = A.alloc(8 * 1024, BF16).rearrange("p (k n) -> p k n", k=8); hwo = H()
        kp = lambda d: d.rearrange("(k p) n -> p k n", p=128)
        for b3 in range(3):
            wload(wgate[:, :, b3 * 1024:(b3 + 1) * 1024], kp(w_in_d[l])[:, :, OFF_GATE + b3 * 1024:OFF_GATE + (b3 + 1) * 1024], hwg)
        for b3, wd in enumerate((ssd_wo_d, conv_wo_d, mla_wo_d)):
            wload(wbr[b3], kp(wd[l]), hwb[b3])
        wload(wo, kp(wout_d[l]), hwo)
        xt = A.alloc(8 * T, F32).rearrange("p (k s) -> p k s", k=8); hxt = H()
        ut = A.alloc(8 * T, BF16).rearrange("p (k s) -> p k s", k=8); hut = H()
        sq = A.alloc(8 * T, BF16).rearrange("p (k s) -> p k s", k=8); hsq = H()
        rs = A.alloc(T, F32); hrs = H()
        br = [A.alloc(8 * T, BF16).rearrange("p (k s) -> p k s", k=8) for _ in range(3)]; hbr = [H() for _ in range(3)]
        mg = A.alloc(8 * T, BF16).rearrange("p (k s) -> p k s", k=8); hmg = H()
        gt_ = [A.alloc(T, F32) for _ in range(2)]; hgt = [H(), H()]
        macc = A.alloc(T, F32); hmacc = H()
        srcs = [(ys_d, "ys"), (yc_d, "yc"), (o_d, "o")]
        for t in range(NT):
            P.dma("sp", V(xt, hxt), V(fmview(xsrc, t), dh(xsn, t)))
            for b3, (d_, nm) in enumerate(srcs):
                P.dma("sp", V(br[b3], hbr[b3]), V(fmview(d_, t), dall(nm)))
            xnorm(xt, hxt, C_MIXG, ut, hut, sq, hsq, rs, hrs, 0)
            gi = 0
            for m in range(8):
                for b3 in range(3):
                    gb = 1 + gi % 2; yb = 3 + gi % 2; g2 = gi % 2; gi += 1
                    for k in range(8):
                        P.mm(PS(gb), V(wgate[:, k, b3 * 1024 + m * 128:b3 * 1024 + (m + 1) * 128], hwg), V(ut[:, k, :], hut), start=(k == 0), stop=(k == 7))
                    for k in range(8):
                        P.mm(PS(yb), V(wbr[b3][:, k, m * 128:(m + 1) * 128], hwb[b3]), V(br[b3][:, k, :], hbr[b3]), start=(k == 0), stop=(k == 7))
                    P.act(V(gt_[g2], hgt[g2]), PS(gb), AF.Sigmoid, bias=pcol(C_GATEB + b3 * 8 + m))
                    if b3 == 0:
                        P.tt(V(macc, hmacc), V(gt_[g2], hgt[g2]), PS(yb), ALU.mult)
                    else:
                        P.tt(V(gt_[g2], hgt[g2]), V(gt_[g2], hgt[g2]), PS(yb), ALU.mult)
                        if b3 == 1:
                            P.tt(V(macc, hmacc), V(macc, hmacc), V(gt_[g2], hgt[g2]), ALU.add)
                        else:
                            P.tt(V(mg[:, m, :], hmg), V(macc, hmacc), V(gt_[g2], hgt[g2]), ALU.add)
            for n in range(8):
                ob_ = 5 + n % 2
                for m in range(8):
                    P.mm(PS(ob_), V(wo[:, m, n * 128:(n + 1) * 128], hwo), V(mg[:, m, :], hmg), start=(m == 0), stop=(m == 7))
                P.tt(V(xt[:, n, :], hxt), V(xt[:, n, :], hxt), PS(ob_), ALU.add)
            P.dma("pool", V(fmview(xres_d, t), dh("xres", t)), V(xt, hxt))
        barrier()

    def phase_X(l):
        A = Arena(big, ARENA0, 103000)
        kp = lambda d: d.rearrange("(k p) n -> p k n", p=128)
        wkv = A.alloc(8 * 2048, BF16).rearrange("p (k n) -> p k n", k=8); hwkv = H()
        wload(wkv[:, :, 0:1024], kp(xwkv_d[l])[:, :, 0:1024], hwkv)
        wload(wkv[:, :, 1024:2048], kp(xwkv_d[l])[:, :, 1024:2048], hwkv)
        mt = A.alloc(8 * 256, F32).rearrange("p (k s) -> p k s", k=8); hmt = H()
        mn = A.alloc(8 * 256, BF16).rearrange("p (k s) -> p k s", k=8); hmn = H()
        sqm = A.alloc(8 * 256, BF16).rearrange("p (k s) -> p k s", k=8); hsqm = H()
        rsm = A.alloc(256, F32); hrsm = H()
        KX = A.alloc(8 * 256, BF16).rearrange("p (c s) -> p c s", c=8); hKX = H()
        VX = A.alloc(2 * 1024, BF16).rearrange("p (m d) -> p m d", m=2); hVX = H()
        sq2 = A.alloc(2 * 256, BF16).rearrange("p (c s) -> p c s", c=2); hsq2 = H()
        P.dma("sp", V(mt, hmt), V(memT_d.rearrange("(k p) s -> p k s", p=128), H()))
        xnorm(mt, hmt, C_MEMG, mn, hmn, sqm, hsqm, rsm, hrsm, 0, nk=8, width=256)
        for hx in range(4):
            for c in range(2):
                for k in range(8):
                    P.mm(PS(1 + c, 256), V(wkv[:, k, hx * 256 + c * 128:hx * 256 + (c + 1) * 128], hwkv), V(mn[:, k, :], hmn), start=(k == 0), stop=(k == 7))
                P.act(V(sq2[:, c, :], hsq2), PS(1 + c, 256), AF.Square)
            for c in range(2):
                P.mm(PS(3, 256), onesb, V(sq2[:, c, :], hsq2), start=(c == 0), stop=(c == 1))
            P.act(V(rsm, hrsm), PS(3, 256), AF.Ln, bias=epsv, scale=1.0 / 256)
            P.act(V(rsm, hrsm), V(rsm, hrsm), AF.Exp, scale=-0.5)
            for c in range(2):
                P.stt(V(KX[:, hx * 2 + c, :], hKX), PS(1 + c, 256), pcol(C_XKG + c), V(rsm, hrsm), ALU.mult, ALU.mult)
        for m in range(2):
            for half in range(2):
                for k in range(8):
                    P.mm(PS(4 + half), V(mn[:, k, m * 128:(m + 1) * 128], hmn), V(wkv[:, k, 1024 + half * 512:1024 + (half + 1) * 512], hwkv), start=(k == 0), stop=(k == 7))
                P.copy(V(VX[:, m, half * 512:(half + 1) * 512], hVX), PS(4 + half), eng="act")
        barrier()
        A2 = Arena(big, ARENA0, 103000)
        KX2 = A2.alloc(8 * 256, BF16).rearrange("p (c s) -> p c s", c=8); hKX2 = H()
        VX2 = A2.alloc(2 * 1024, BF16).rearrange("p (m d) -> p m d", m=2); hVX2 = H()
        P.copy(V(KX2, hKX2), V(KX, hKX)); P.copy(V(VX2, hVX2), V(VX, hVX))
        barrier()
        A = A2
        wq_ = A.alloc(8 * 1024, BF16).rearrange("p (k n) -> p k n", k=8); hwq = H()
        wo_ = A.alloc(8 * 1024, BF16).rearrange("p (k n) -> p k n", k=8); hwo = H()
        wload(wq_, kp(xwq_d[l]), hwq); wload(wo_, kp(xwo_d[l]), hwo)
        xt = [A.alloc(8 * T, F32).rearrange("p (k s) -> p k s", k=8) for _ in range(3)]; hxt = [H(), H(), H()]
        ht2 = [A.alloc(8 * T, BF16).rearrange("p (k s) -> p k s", k=8) for _ in range(2)]; hht2 = [H(), H()]
        sq = A.alloc(8 * T, BF16).rearrange("p (k s) -> p k s", k=8); hsq = H()
        rs = A.alloc(T, F32); hrs = H()
        sqq = A.alloc(2 * T, BF16).rearrange("p (c s) -> p c s", c=2); hsqq = H()
        rsq = A.alloc(T, F32); hrsq = H()
        qx = A.alloc(2 * T, BF16).rearrange("p (c s) -> p c s", c=2); hqx = H()
        pt = [A.alloc(T, BF16) for _ in range(2)]; hpt = [H(), H()]
        rd = A.alloc(T, F32); hrd = H()
        ox = A.alloc(8 * T, BF16).rearrange("p (k s) -> p k s", k=8); hox = H()
        def prep(t):
            b = t % 2; b3 = t % 3
            P.dma("sp", V(xt[b3], hxt[b3]), V(fmview(xres_d, t), dh("xres", t)))
            xnorm(xt[b3], hxt[b3], C_XATG, ht2[b], hht2[b], sq, hsq, rs, hrs, 0)
        prep(0)
        for t in range(NT):
            b = t % 2
            ht = ht2[b]; hht = hht2[b]
            b = t % 3
            if t + 1 < NT:
                prep(t + 1)
            for hx in range(4):
                for c in range(2):
                    for k in range(8):
                        P.mm(PS(1 + c), V(wq_[:, k, hx * 256 + c * 128:hx * 256 + (c + 1) * 128], hwq), V(ht[:, k, :], hht), start=(k == 0), stop=(k == 7))
                    P.act(V(sqq[:, c, :], hsqq), PS(1 + c), AF.Square)
                for c in range(2):
                    P.mm(PS(3), onesb, V(sqq[:, c, :], hsqq), start=(c == 0), stop=(c == 1))
                P.act(V(rsq, hrsq), PS(3), AF.Ln, bias=epsv, scale=1.0 / 256)
                P.act(V(rsq, hrsq), V(rsq, hrsq), AF.Exp, scale=-0.5)
                for c in range(2):
                    P.stt(V(qx[:, c, :], hqx), PS(1 + c), V(gsc[:, 3 + c:4 + c], h_gsc), V(rsq, hrsq), ALU.mult, ALU.mult)
                for m in range(2):
                    for c in range(2):
                        P.mm(PS(4 + m), V(KX2[:, hx * 2 + c, m * 128:(m + 1) * 128], hKX2), V(qx[:, c, :], hqx), start=(c == 0), stop=(c == 1))
                    P.act(V(pt[m], hpt[m]), PS(4 + m), AF.Exp)
                for m in range(2):
                    P.mm(PS(3), onesb, V(pt[m], hpt[m]), start=(m == 0), stop=(m == 1))
                P.act(V(rd, hrd), PS(3), AF.Ln)
                P.act(V(rd, hrd), V(rd, hrd), AF.Exp, scale=-1.0)
                for c2 in range(2):
                    for m in range(2):
                        P.mm(PS(6 + c2), V(VX2[:, m, hx * 256 + c2 * 128:hx * 256 + (c2 + 1) * 128], hVX2), V(pt[m], hpt[m]), start=(m == 0), stop=(m == 1))
                    P.tt(V(ox[:, hx * 2 + c2, :], hox), PS(6 + c2), V(rd, hrd), ALU.mult)
            for n in range(8):
                ob_ = 1 + n % 2
                for k in range(8):
                    P.mm(PS(ob_), V(wo_[:, k, n * 128:(n + 1) * 128], hwo), V(ox[:, k, :], hox), start=(k == 0), stop=(k == 7))
                P.tt(V(xt[b][:, n, :], hxt[b]), V(xt[b][:, n, :], hxt[b]), PS(ob_), ALU.add)
            P.dma("pool", V(fmview(xres_d, t), dh("xres", t)), V(xt[b], hxt[b]))
        barrier()

    def phase_F(l, last):
        kp = lambda d: d.rearrange("(k p) n -> p k n", p=128)
        toks = []
        for hh in range(2):
            A = Arena(big, ARENA0, 103000)
            w1g = A.alloc(8 * 1408, BF16).rearrange("p (k n) -> p k n", k=8); hw1g = H()
            w1u = A.alloc(8 * 1408, BF16).rearrange("p (k n) -> p k n", k=8); hw1u = H()
            w2 = A.alloc(11 * 1024, BF16).rearrange("p (k n) -> p k n", k=11); hw2 = H()
            wload(w1g, kp(fwi_d[l])[:, :, hh * 1408:(hh + 1) * 1408], hw1g)
            wload(w1u, kp(fwi_d[l])[:, :, 2816 + hh * 1408:2816 + (hh + 1) * 1408], hw1u)
            wload(w2, fwo_d[l, hh * 1408:(hh + 1) * 1408, :].rearrange("(k p) n -> p k n", p=128), hw2)
            xt = [A.alloc(8 * T, F32).rearrange("p (k s) -> p k s", k=8) for _ in range(3)]; hxt = [H(), H(), H()]
            pt_ = A.alloc(8 * T, F32).rearrange("p (k s) -> p k s", k=8); hpt_ = H()
            ht2 = [A.alloc(8 * T, BF16).rearrange("p (k s) -> p k s", k=8) for _ in range(2)]; hht2 = [H(), H()]
            sq = A.alloc(8 * T, BF16).rearrange("p (k s) -> p k s", k=8); hsq = H()
            rs = A.alloc(T, F32); hrs = H()
            ac = A.alloc(11 * T, BF16).rearrange("p (k s) -> p k s", k=11); hac = H()
            sg = [A.alloc(T, F32) for _ in range(2)]; hsg = [H(), H()]
            def prep(t):
                b = t % 2; b3 = t % 3
                P.dma("sp", V(xt[b3], hxt[b3]), V(fmview(xres_d, t), dh("xres", t)))
                xnorm(xt[b3], hxt[b3], C_FFNG, ht2[b], hht2[b], sq, hsq, rs, hrs, 0)
            prep(0)
            for t in range(NT):
                b = t % 2
                ht = ht2[b]; hht = hht2[b]
                b = t % 3
                if t + 1 < NT:
                    prep(t + 1)
                if hh == 1:
                    P.dma("sp", V(pt_, hpt_), V(fmview(fp_d, t), dh("fp", t)))
                for j in range(11):
                    s2 = j % 2
                    for k in range(8):
                        P.mm(PS(1 + s2), V(w1g[:, k, j * 128:(j + 1) * 128], hw1g), V(ht[:, k, :], hht), start=(k == 0), stop=(k == 7))
                    for k in range(8):
                        P.mm(PS(3 + s2), V(w1u[:, k, j * 128:(j + 1) * 128], hw1u), V(ht[:, k, :], hht), start=(k == 0), stop=(k == 7))
                    P.act(V(sg[s2], hsg[s2]), PS(1 + s2), AF.Silu)
                    P.tt(V(ac[:, j, :], hac), V(sg[s2], hsg[s2]), PS(3 + s2), ALU.mult)
                for n in range(8):
                    ob_ = 5 + n % 2
                    for j in range(11):
                        P.mm(PS(ob_), V(w2[:, j, n * 128:(n + 1) * 128], hw2), V(ac[:, j, :], hac), start=(j == 0), stop=(j == 10))
                    if hh == 0:
                        P.copy(V(xt[b][:, n, :], hxt[b]), PS(ob_), eng="act")
                    else:
                        P.tt(V(xt[b][:, n, :], hxt[b]), V(xt[b][:, n, :], hxt[b]), PS(ob_), ALU.add)
                        P.tt(V(xt[b][:, n, :], hxt[b]), V(xt[b][:, n, :], hxt[b]), V(pt_[:, n, :], hpt_), ALU.add)
                if hh == 0:
                    P.dma("pool", V(fmview(fp_d, t), dh("fp", t)), V(xt[b], hxt[b]))
                else:
                    dst = out_d if last else xres_d
                    tk = P.dma("pool", V(fmview(dst, t), dh("out" if last else "xres", t)), V(xt[b], hxt[b]))
                    toks.append(tk)
            barrier()
        return toks

    phases = {"A": phase_A, "S": phase_S, "C": phase_C, "M1": phase_M1, "M2": phase_M2, "G": phase_G, "X": phase_X}
    order = ["A", "S", "C", "M1", "M2", "G", "X", "F"]
    P.marks = []
    for l in range(L):
        for ph in order:
            if only is not None and ph not in only:
                continue
            P.marks.append((l, ph, P.eng["dve"].n))
            if ph == "F":
                final_toks = phase_F(l, l == L - 1)
            else:
                phases[ph](l)
    P.finish(final_toks)
    return P


from concourse.bass_utils import run_bass_kernel_spmd


def _fm(v, nch):
    return np.ascontiguousarray(np.asarray(v, np.float32).reshape(nch, 128).T)


def _consts():
    c = np.zeros((128, NCST), np.float32)
    i = np.arange(128)
    c[:, K_TRI:K_TRI + 128] = (i[:, None] <= i[None, :])
    c[:, K_GT:K_GT + 128] = (i[:, None] > i[None, :])
    c[:, K_ID:K_ID + 128] = np.eye(128)
    f = np.arange(32)
    inv = (10000.0 ** (-(2 * f).astype(np.float32) / np.float32(64))).astype(np.float32)
    c[0:32, K_INV] = inv; c[32:64, K_INV] = inv
    q = np.arange(512)
    for j in range(4):
        c[:, K_MASK + j * 512:K_MASK + (j + 1) * 512] = ((2 * j + i[:, None] // 64) <= (q[None, :] // 64))
    return c


def _pack_params(inp):
    L = 4
    pp = np.zeros((L, 128, NPP), np.float32)
    pb = np.zeros((L, NPB), np.float32)
    for l in range(L):
        p = pp[l]
        p[:, C_MIXG:C_MIXG + 8] = _fm(inp["mix_norm_g"][l], 8)
        p[:, C_XATG:C_XATG + 8] = _fm(inp["xattn_norm_g"][l], 8)
        p[:, C_FFNG:C_FFNG + 8] = _fm(inp["ffn_norm_g"][l], 8)
        p[:, C_MEMG:C_MEMG + 8] = _fm(inp["mem_norm_g"][l], 8)
        p[:, C_SCW:C_SCW + 64] = np.asarray(inp["ssd_conv_w"][l]).T.reshape(16, 128, 4).transpose(1, 0, 2).reshape(128, 64)
        p[:, C_SCB:C_SCB + 16] = _fm(inp["ssd_conv_b"][l], 16)
        p[:, C_CDW:C_CDW + 248] = np.asarray(inp["conv_dw_w"][l]).T.reshape(8, 128, 31).transpose(1, 0, 2).reshape(128, 248)
        p[:, C_CDB:C_CDB + 8] = _fm(inp["conv_dw_b"][l], 8)
        p[:, C_LNG:C_LNG + 8] = _fm(inp["conv_ln_g"][l], 8)
        p[:, C_LNB:C_LNB + 8] = _fm(inp["conv_ln_b"][l], 8)
        p[:, C_QAG:C_QAG + 3] = _fm(inp["mla_q_a_g"][l], 3)
        p[:, C_KVAG:C_KVAG + 2] = _fm(inp["mla_kv_a_g"][l], 2)
        for (cn, cr, crs, g) in ((C_QNN, C_QNR, C_QNRS, inp["mla_q_norm_g"][l]), (C_KNN, C_KNR, C_KNRS, inp["mla_k_norm_g"][l])):
            g = np.asarray(g, np.float32)
            p[:, cn] = g[0:128]
            p[0:64, cr] = g[128:192]
            p[0:64, crs] = np.concatenate([g[160:192], g[128:160]])
        p[:, C_GATEB:C_GATEB + 24] = np.asarray(inp["gate_b"][l], np.float32).reshape(24, 128).T
        p[:, C_XQG:C_XQG + 2] = _fm(inp["xattn_q_norm_g"][l], 2)
        p[:, C_XKG:C_XKG + 2] = _fm(inp["xattn_k_norm_g"][l], 2)
        pb[l, B_DTB:B_DTB + 16] = inp["ssd_dt_bias"][l]
        pb[l, B_ALOG:B_ALOG + 16] = inp["ssd_a_log"][l]
        pb[l, B_DSK:B_DSK + 16] = inp["ssd_d"][l]
        pb[l, B_SNG:B_SNG + 1024] = inp["ssd_norm_g"][l]
    return pp, pb


WNAMES = ["w_in", "ssd_w_out", "conv_w_out", "mla_w_q_b", "mla_w_kv_b", "mla_w_o", "w_out", "xattn_w_q", "xattn_w_kv",
          "xattn_w_o", "ffn_w_in", "ffn_w_out"]


def make_in_maps(inp, cores):
    pp, pb = _pack_params(inp)
    cst = _consts()
    shared = {n: np.ascontiguousarray(np.asarray(inp[n], np.float32)) for n in WNAMES}
    shared.update(cst=cst, pp=pp, pb=pb)
    maps = []
    for b in cores:
        m = dict(shared)
        m["xT"] = np.ascontiguousarray(np.asarray(inp["x"][b], np.float32).T)
        m["memT"] = np.ascontiguousarray(np.asarray(inp["mem"][b], np.float32).T)
        m["pos"] = np.ascontiguousarray(np.asarray(inp["positions"][b], np.int32)[None, :])
        maps.append(m)
    return maps


_CACHE = {}


def kernel(**inputs):
    if "P" not in _CACHE:
        _CACHE["P"] = build(L=4)
    P = _CACHE["P"]
    maps = make_in_maps(inputs, list(range(8)))
    res = run_bass_kernel_spmd(P.nc, maps, core_ids=list(range(8)))
    out = np.stack([np.ascontiguousarray(r["outT"].T) for r in res.results], axis=0)
    return out.astype(np.float32)
```

```python
import contextlib
import numpy as np
import concourse.bass as bass
import concourse.mybir as mybir

F32 = mybir.dt.float32
BF16 = mybir.dt.bfloat16
I32 = mybir.dt.int32
AF = mybir.ActivationFunctionType
ALU = mybir.AluOpType
AX = mybir.AxisListType

ENGS = ("pe", "act", "dve", "pool", "sp")
NDMA_SEMS = 12


class H:
    __slots__ = ("w", "r")

    def __init__(self):
        self.w = None
        self.r = []


class V:
    __slots__ = ("ap", "hs")

    def __init__(self, ap, hs):
        self.ap = ap
        self.hs = hs if isinstance(hs, (list, tuple)) else [hs]


class Eng:
    def __init__(self, name, idx):
        self.name = name
        self.idx = idx
        self.n = 0
        self.seen = [0] * len(ENGS)
        self.seen_dma = {}
        self.ops = []
        self.dma_count = 0


class Prog:
    def __init__(self):
        self.nc = bass.Bass("TRN2", target_bir_lowering=False)
        self.stack = contextlib.ExitStack()
        self.eng = {n: Eng(n, i) for i, n in enumerate(ENGS)}
        self.snaps = {}
        self.ntens = 0
        self.nwaits = 0

    def sb(self, shape, dtype, name=None):
        self.ntens += 1
        t = self.stack.enter_context(self.nc.sbuf_tensor(name or f"sb{self.ntens}", list(shape), dtype))
        return t

    def ps(self, shape, dtype, name=None):
        self.ntens += 1
        t = self.stack.enter_context(self.nc.psum_tensor(name or f"ps{self.ntens}", list(shape), dtype))
        return t

    def dram(self, name, shape, dtype, kind="Internal"):
        return self.nc.dram_tensor(name, list(shape), dtype, kind=kind).ap()

    def _deps(self, reads, writes):
        deps = {}

        def add(tok):
            k, v = tok
            if deps.get(k, 0) < v:
                deps[k] = v
        for h in reads:
            if h.w is not None:
                add(h.w)
        for h in writes:
            if h.w is not None:
                add(h.w)
            for t in h.r:
                add(t)
        return deps

    def _waits(self, e, deps):
        waits = []
        for k, v in deps.items():
            if isinstance(k, int):
                if k == e.idx and e.name == "pe":
                    continue
                if e.seen[k] >= v:
                    continue
                waits.append((k, v))
            else:
                if e.seen_dma.get(k, 0) >= v:
                    continue
                waits.append((k, v))
        for k, v in waits:
            if isinstance(k, int):
                if e.seen[k] < v:
                    e.seen[k] = v
            else:
                e.seen_dma[k] = v
            snap = self.snaps.get((k, v))
            if snap is not None:
                for i in range(len(ENGS)):
                    if e.seen[i] < snap[i]:
                        e.seen[i] = snap[i]
        return waits

    def capture(self):
        self._cap = []
        return self._cap

    def end_capture(self):
        self._cap = None

    def interleave(self, la, lb):
        self._cap = None
        na, nb = len(la), len(lb)
        ia = ib = 0
        while ia < na or ib < nb:
            if ib >= nb or (ia < na and ia * nb <= ib * na):
                kind, args, kw = la[ia]; ia += 1
            else:
                kind, args, kw = lb[ib]; ib += 1
            (self.op if kind == "op" else self.dma)(*args, **kw)

    def op(self, engname, fn, reads, writes):
        if getattr(self, "_cap", None) is not None:
            self._cap.append(("op", (engname, fn, reads, writes), {}))
            return None
        e = self.eng[engname]
        rh = [h for v in reads for h in v.hs]
        wh = [h for v in writes for h in v.hs]
        deps = self._deps(rh, wh)
        waits = self._waits(e, deps)
        e.n += 1
        tok = (e.idx, e.n)
        self.snaps[tok] = tuple(e.seen)
        e.ops.append((waits, fn, None))
        self.nwaits += len(waits)
        for h in rh:
            h.r.append(tok)
        for h in wh:
            h.w = tok
            h.r = []
        return tok

    def dma(self, qname, out, in_, **kw):
        if getattr(self, "_cap", None) is not None:
            self._cap.append(("dma", (qname, out, in_), kw))
            return None
        e = self.eng[qname]
        rh = list(in_.hs)
        wh = list(out.hs)
        deps = self._deps(rh, wh)
        k = e.dma_count % NDMA_SEMS
        rnd = e.dma_count // NDMA_SEMS
        e.dma_count += 1
        key = (qname, k)
        if rnd > 0:
            if deps.get(key, 0) < 16 * rnd:
                deps[key] = 16 * rnd
        waits = self._waits(e, deps)
        tok = (key, 16 * (rnd + 1))
        self.snaps[tok] = tuple(e.seen)
        oap, iap = out.ap, in_.ap
        e.ops.append((waits, lambda eng: eng.dma_start(out=oap, in_=iap, **kw), key))
        self.nwaits += len(waits)
        for h in rh:
            h.r.append(tok)
        for h in wh:
            h.w = tok
            h.r = []
        return tok

    def mm(self, out, lhsT, rhs, start=True, stop=True):
        o, l, r = out.ap, lhsT.ap, rhs.ap
        return self.op("pe", lambda e: e.matmul(o, l, r, start=start, stop=stop), [lhsT, rhs], [out])

    def transpose(self, out, in_, ident):
        o, i, d = out.ap, in_.ap, ident.ap
        return self.op("pe", lambda e: e.transpose(o, i, d), [in_, ident], [out])

    def act(self, out, in_, func, bias=None, scale=None, eng="act", accum=None):
        o, i = out.ap, in_.ap
        reads = [in_]
        kw = {}
        if bias is not None:
            if isinstance(bias, V):
                reads.append(bias)
                kw["bias"] = bias.ap
            else:
                kw["bias"] = bias
        if scale is not None:
            if isinstance(scale, V):
                reads.append(scale)
                kw["scale"] = scale.ap
            else:
                kw["scale"] = scale
        writes = [out]
        if accum is not None:
            writes.append(accum)
            kw["accum_out"] = accum.ap
        return self.op("act", lambda e: e.activation(o, i, func, **kw), reads, writes)

    def tt(self, out, a, b, op, eng="dve"):
        o, x, y = out.ap, a.ap, b.ap
        return self.op(eng, lambda e: e.tensor_tensor(o, x, y, op), [a, b], [out])

    def ts(self, out, a, s1, op0, s2=None, op1=None, eng="dve", accum=None):
        o, x = out.ap, a.ap
        reads = [a]
        if isinstance(s1, V):
            reads.append(s1)
            s1 = s1.ap
        if isinstance(s2, V):
            reads.append(s2)
            s2 = s2.ap
        writes = [out]
        kw = {}
        if accum is not None:
            writes.append(accum)
            kw["accum_out"] = accum.ap
        if op1 is None:
            return self.op(eng, lambda e: e.tensor_scalar(o, x, s1, None, op0, **kw), reads, writes)
        return self.op(eng, lambda e: e.tensor_scalar(o, x, s1, s2, op0, op1, **kw), reads, writes)

    def stt(self, out, a, s, b, op0, op1, eng="dve"):
        o, x, y = out.ap, a.ap, b.ap
        reads = [a, b]
        if isinstance(s, V):
            reads.append(s)
            s = s.ap
        return self.op(eng, lambda e: e.scalar_tensor_tensor(o, x, s, y, op0, op1), reads, [out])

    def copy(self, out, in_, eng="dve"):
        o, i = out.ap, in_.ap
        if eng == "act":
            return self.op("act", lambda e: e.copy(o, i), [in_], [out])
        return self.op(eng, lambda e: e.tensor_copy(o, i), [in_], [out])

    def memset(self, out, val, eng="dve"):
        o = out.ap
        return self.op(eng, lambda e: e.memset(o, val), [], [out])

    def recip(self, out, in_):
        o, i = out.ap, in_.ap
        return self.op("dve", lambda e: e.reciprocal(o, i), [in_], [out])

    def finish(self, final_tokens):
        nc = self.nc
        sems = {}
        for i, n in enumerate(ENGS):
            sems[i] = self.stack.enter_context(nc.semaphore(f"s_{n}"))
        for n in ENGS:
            e = self.eng[n]
            if e.dma_count:
                for k in range(min(NDMA_SEMS, e.dma_count)):
                    sems[(n, k)] = self.stack.enter_context(nc.semaphore(f"d_{n}{k}"))
        sp = self.eng["sp"]
        fin = {}
        for k, v in final_tokens:
            if fin.get(k, 0) < v:
                fin[k] = v
        sp.ops.append((list(fin.items()), None, None))
        block = self.stack.enter_context(nc.Block())
        hw = {"pe": block.tensor, "act": block.scalar, "dve": block.vector, "pool": block.gpsimd, "sp": block.sync}

        def make(e):
            def body(eng):
                own = sems[e.idx]
                for waits, fn, dkey in e.ops:
                    for k, v in waits:
                        eng.wait_ge(sems[k], v)
                    if fn is None:
                        continue
                    ins = fn(eng)
                    if dkey is None:
                        ins.then_inc(own, 1)
                    else:
                        ins.then_inc(sems[dkey], 16)
            return body
        for n in ENGS:
            e = self.eng[n]
            if e.ops:
                hw[n](make(e))
        self.stack.close()
        return nc


S = 4096; D = 1024; T = 512; NT = 8; NCH = 32
EPS = 1e-6
OFF_Z, OFF_XBC, OFF_DT, OFF_GA, OFF_GG, OFF_QL, OFF_KV, OFF_KR, OFF_GATE = 0, 1024, 3072, 3088, 4112, 5136, 5520, 5776, 5840
C_MIXG, C_XATG, C_FFNG, C_MEMG, C_SCW, C_SCB, C_CDW, C_CDB, C_LNG, C_LNB = 0, 8, 16, 24, 32, 96, 112, 360, 368, 376
C_QAG, C_KVAG, C_QNN, C_QNR, C_QNRS, C_KNN, C_KNR, C_KNRS, C_GATEB, C_XQG, C_XKG, NPP = 384, 387, 389, 390, 391, 392, 393, 394, 395, 419, 421, 424
B_DTB, B_ALOG, B_DSK, B_SNG, NPB = 0, 16, 32, 48, 1072
K_TRI, K_GT, K_ID, K_INV, K_MASK, NCST = 0, 128, 256, 384, 385, 385 + 2048


class Arena:
    def __init__(self, big, base, limit):
        self.big, self.off, self.limit = big, base, limit

    def alloc(self, n, dtype, parts=128):
        nb = n * (4 if dtype in (F32, I32) else 2)
        nb = (nb + 63) // 64 * 64
        ne = nb // 2
        assert self.off + ne <= self.limit, ("SBUF arena overflow", self.off + ne, self.limit)
        ap = self.big[:, self.off:self.off + ne]
        self.off += ne
        if dtype != BF16:
            ap = ap.bitcast(dtype)
        return ap[:, 0:n]


def build(L=4, dbg=False, only=None):
    P = Prog(); nc = P.nc
    kind_s = "ExternalOutput" if dbg else "Internal"
    din = lambda n, s, dt=F32: P.dram(n, s, dt, kind="ExternalInput")
    xT_d = din("xT", [D, S]); memT_d = din("memT", [D, 256]); pos_d = din("pos", [1, S], I32)
    cst_d = din("cst", [128, NCST]); pp_d = din("pp", [4, 128, NPP]); pb_d = din("pb", [4, NPB])
    w_in_d = din("w_in", [4, D, 8912]); ssd_wo_d = din("ssd_w_out", [4, D, D]); conv_wo_d = din("conv_w_out", [4, D, D])
    wqb_d = din("mla_w_q_b", [4, 384, 1536]); wkvb_d = din("mla_w_kv_b", [4, 256, 2048]); mla_wo_d = din("mla_w_o", [4, D, D])
    wout_d = din("w_out", [4, D, D]); xwq_d = din("xattn_w_q", [4, D, D]); xwkv_d = din("xattn_w_kv", [4, D, 2048])
    xwo_d = din("xattn_w_o", [4, D, D]); fwi_d = din("ffn_w_in", [4, D, 5632]); fwo_d = din("ffn_w_out", [4, 2816, D])
    out_d = P.dram("outT", [D, S], F32, kind="ExternalOutput")
    sc = lambda n, s, dt=F32: P.dram(n, s, dt, kind=kind_s)
    xres_d = sc("xres", [D, S]); sz_d = sc("sz", [S, D]); xbc_d = sc("xbc", [2048, S], BF16); cv_d = sc("cv", [D, S])
    lat_d = sc("lat", [768, S]); ys_d = sc("ys", [D, S], BF16); yc_d = sc("yc", [D, S], BF16)
    q_d = sc("qs", [8, 192, S], BF16); k_d = sc("ks", [8, 128, S], BF16); kr_d = sc("krs", [64, S], BF16)
    v_d = sc("vs", [S, D], BF16); o_d = sc("os", [D, S], BF16); fp_d = sc("fpart", [D, S]); rope_d = sc("rope", [128, S])
    hd = {}

    def dh(name, t):
        k = (name, t)
        if k not in hd:
            hd[k] = H()
        return hd[k]

    def dall(name, n=NT):
        return [dh(name, t) for t in range(n)]

    big = P.sb([128, 103000], BF16, name="big")
    banks = [P.ps([128, 512], F32, name=f"bank{i}") for i in range(8)]
    bh = [H() for _ in range(8)]

    def PS(i, n=512, parts=128, dt=F32):
        ap = banks[i][:]
        if dt == BF16:
            ap = ap.bitcast(BF16)
        return V(ap[0:parts, 0:n], bh[i])

    pers = Arena(big, 0, 8000)
    cst = pers.alloc(K_MASK, F32); h_cst = H()
    tri = V(cst[:, K_TRI:K_TRI + 128], h_cst); gt = V(cst[:, K_GT:K_GT + 128], h_cst)
    inv_c = V(cst[:, K_INV:K_INV + 1], h_cst)
    ident = pers.alloc(128, BF16); h_id = H(); identv = V(ident, h_id)
    ones_b = pers.alloc(128, BF16); h_ob = H(); onesb = V(ones_b, h_ob)
    ones_f = pers.alloc(128, F32); h_of = H(); onesf = V(ones_f, h_of)
    masks = pers.alloc(2048, BF16); h_mk = H()
    epsc = pers.alloc(1, F32); h_eps = H(); epsv = V(epsc, h_eps)
    ppt = pers.alloc(NPP, F32); h_pp = H()
    pbt = pers.alloc(NPB, F32); h_pb = H()
    abc = pers.alloc(16, F32); h_abc = H()
    gsc = pers.alloc(8, F32); h_gsc = H()
    ARENA0 = pers.off

    def pcol(c, n=1, parts=128):
        return V(ppt[0:parts, c:c + n], h_pp)

    def barrier():
        toks = {}
        for n in ENGS:
            e = P.eng[n]
            if e.n:
                toks[e.idx] = e.n
            for k in range(min(NDMA_SEMS, e.dma_count)):
                rnd = (e.dma_count - 1 - k) // NDMA_SEMS
                toks[(n, k)] = 16 * (rnd + 1)
        for n in ENGS:
            e = P.eng[n]
            waits = P._waits(e, dict(toks))
            if waits:
                e.ops.append((waits, None, None))

    P.dma("sp", V(cst, h_cst), V(cst_d[:, 0:K_MASK], H()))
    P.dma("pool", V(ident, h_id), V(cst_d[:, K_ID:K_ID + 128], H()))
    P.dma("pool", V(masks, h_mk), V(cst_d[:, K_MASK:K_MASK + 2048], H()))
    P.memset(onesb, 1.0); P.memset(onesf, 1.0); P.memset(epsv, EPS)

    def rope_tables():
        A = Arena(big, ARENA0, 103000)
        for t in range(NT):
            sl = slice(t * T, (t + 1) * T)
            pi_ = A.alloc(T, I32) if t == 0 else rope_tables.bufs[0]
            if t == 0:
                rope_tables.bufs = [pi_] + [A.alloc(T, F32) for _ in range(5)] + [A.alloc(T, I32)]
            pi_, pf, ang, kf, r, rc, ki = rope_tables.bufs
            hs = [H() for _ in range(7)]
            pi_v, pf_v, ang_v, kf_v, r_v, rc_v, ki_v = [V(a[0:64], h) for a, h in zip(rope_tables.bufs, hs)]
            P.dma("sp", pi_v, V(pos_d[0, sl].partition_broadcast(64), H()))
            P.copy(pf_v, pi_v)
            P.ts(ang_v, pf_v, V(cst[0:64, K_INV:K_INV + 1], h_cst), ALU.mult)
            P.ts(kf_v, ang_v, float(1.0 / (2 * np.pi)), ALU.mult)
            P.copy(ki_v, kf_v)
            P.copy(kf_v, ki_v)
            C1 = 6.28125; C2 = float(np.float32(2 * np.pi - 6.28125))
            P.stt(r_v, kf_v, -C1, ang_v, ALU.mult, ALU.add)
            P.stt(r_v, kf_v, -C2, r_v, ALU.mult, ALU.add)
            P.ts(r_v, r_v, 3.1415925, ALU.min, -3.1415925, ALU.max)
            P.ts(rc_v, r_v, float(np.pi / 2), ALU.is_gt, float(-2 * np.pi), ALU.mult)
            P.stt(rc_v, r_v, float(np.pi / 2), rc_v, ALU.add, ALU.add)
            P.ts(rc_v, rc_v, 3.1415925, ALU.min, -3.1415925, ALU.max)
            P.act(rc_v, rc_v, AF.Sin)
            P.act(r_v, r_v, AF.Sin)
            P.ts(V(r[0:32], hs[4]), V(r[0:32], hs[4]), -1.0, ALU.mult)
            P.dma("sp", V(rope_d[0:64, sl], dh("rope", t)), rc_v)
            P.dma("sp", V(rope_d[64:128, sl], dh("rope", t)), r_v)
            barrier()
    rope_tables()
    barrier()

    def wload(dst, src, hdst):
        return P.dma("pool", V(dst, hdst), V(src, H()))

    def fmview(d, t):
        return d.rearrange("(k p) s -> p k s", p=128)[:, :, t * T:(t + 1) * T]

    def xnorm(xt, hx, gcol, outb, hout, sq, hsq, rs, hrs, bank, nk=8, width=T, inv_n=1.0 / D):
        P.act(V(sq, hsq), V(xt, hx), AF.Square)
        for k in range(nk):
            P.mm(PS(bank, width), onesb, V(sq[:, k, :], hsq), start=(k == 0), stop=(k == nk - 1))
        P.act(V(rs, hrs), PS(bank, width), AF.Ln, bias=epsv, scale=inv_n)
        P.act(V(rs, hrs), V(rs, hrs), AF.Exp, scale=-0.5)
        for k in range(nk):
            P.stt(V(outb[:, k, :], hout), V(xt[:, k, :], hx), pcol(gcol + k), V(rs, hrs), ALU.mult, ALU.mult)

    dtraw = pers.alloc(NCH * 16, F32).rearrange("p (c h) -> p c h", c=NCH); h_dtraw = H()
    ARENA0 = pers.off
    final_toks = []

    def phase_A(l):
        xsrc = xT_d if l == 0 else xres_d
        xsn = "xT" if l == 0 else "xres"
        barrier()
        P.dma("sp", V(ppt, h_pp), V(pp_d[l], H()))
        P.dma("sp", V(pbt, h_pb), V(pb_d[l, :].partition_broadcast(128), H()))
        P.act(V(abc, h_abc), V(pbt[:, B_ALOG:B_ALOG + 16], h_pb), AF.Exp)
        P.ts(V(abc, h_abc), V(abc, h_abc), -1.0, ALU.mult)
        P.ts(V(gsc[:, 0:3], h_gsc), V(ppt[:, C_QNN:C_QNN + 3], h_pp), float(192 ** -0.5), ALU.mult)
        P.ts(V(gsc[:, 3:5], h_gsc), V(ppt[:, C_XQG:C_XQG + 2], h_pp), float(256 ** -0.5), ALU.mult)

        A = Arena(big, ARENA0, 103000)
        uT = A.alloc(8 * S, BF16).rearrange("p (k s) -> p k s", k=8); hu = [H() for _ in range(NT)]
        A1 = A.off
        xt = [A.alloc(8 * T, F32).rearrange("p (k s) -> p k s", k=8) for _ in range(2)]; hxt = [H(), H()]
        sq = A.alloc(8 * T, BF16).rearrange("p (k s) -> p k s", k=8); hsq = H()
        rs = A.alloc(T, F32); hrs = H()
        for t in range(NT):
            b = t % 2
            P.dma("sp", V(xt[b], hxt[b]), V(fmview(xsrc, t), dh(xsn, t)))
            xnorm(xt[b], hxt[b], C_MIXG, uT[:, :, t * T:(t + 1) * T], hu[t], sq, hsq, rs, hrs, 0)
        barrier()
        A.off = A1
        wz = A.alloc(8 * 1040, BF16).rearrange("p (k n) -> p k n", k=8); hwz = H()
        wload(wz[:, :, 0:1024], w_in_d[l].rearrange("(k p) n -> p k n", p=128)[:, :, OFF_Z:OFF_Z + 1024], hwz)
        wload(wz[:, :, 1024:1040], w_in_d[l].rearrange("(k p) n -> p k n", p=128)[:, :, OFF_DT:OFF_DT + 16], hwz)
        szb = [A.alloc(1024, F32) for _ in range(2)]; hszb = [H(), H()]
        for c in range(NCH):
            t = c // 4; b = c % 2
            us = lambda k: V(uT[:, k, c * 128:(c + 1) * 128], hu[t])
            for half in range(2):
                for k in range(8):
                    P.mm(PS(1 + half), us(k), V(wz[:, k, half * 512:(half + 1) * 512], hwz), start=(k == 0), stop=(k == 7))
            for k in range(8):
                P.mm(PS(3, 16), us(k), V(wz[:, k, 1024:1040], hwz), start=(k == 0), stop=(k == 7))
            for half in range(2):
                P.act(V(szb[b][:, half * 512:(half + 1) * 512], hszb[b]), PS(1 + half), AF.Silu)
            P.copy(V(dtraw[:, c, :], h_dtraw), PS(3, 16))
            P.dma("pool", V(sz_d[c * 128:(c + 1) * 128, :], dh("sz", t)), V(szb[b], hszb[b]))
        barrier()
        A.off = A1
        wg = [A.alloc(8 * 512, BF16).rearrange("p (k n) -> p k n", k=8) for _ in range(2)]; hwg = [H(), H()]
        pre = [A.alloc(S + 32, F32) for _ in range(2)]; hpre = [H(), H()]
        acc = [A.alloc(S, F32) for _ in range(2)]; hacc = [H(), H()]
        xo = [A.alloc(S, BF16) for _ in range(2)]; hxo = [H(), H()]
        for b in range(2):
            P.memset(V(pre[b][:, 0:32], hpre[b]), 0.0)
        win_v = w_in_d[l].rearrange("(k p) n -> p k n", p=128)
        wload(wg[0], win_v[:, :, OFF_XBC:OFF_XBC + 512], hwg[0])
        pend = None
        for j in range(16):
            g4 = j // 4; wb = g4 % 2; b = j % 2
            if j % 4 == 0 and g4 + 1 < 4:
                wload(wg[1 - wb], win_v[:, :, OFF_XBC + (g4 + 1) * 512:OFF_XBC + (g4 + 2) * 512], hwg[1 - wb])
            for t in range(NT):
                bank = 1 + (t % 2)
                for k in range(8):
                    P.mm(PS(bank), V(wg[wb][:, k, (j % 4) * 128:(j % 4 + 1) * 128], hwg[wb]),
                         V(uT[:, k, t * T:(t + 1) * T], hu[t]), start=(k == 0), stop=(k == 7))
                P.copy(V(pre[b][:, 32 + t * T:32 + (t + 1) * T], hpre[b]), PS(bank), eng="act")
            P.ts(V(acc[b], hacc[b]), V(pre[b][:, 29:29 + S], hpre[b]), pcol(C_SCW + j * 4 + 0), ALU.mult, pcol(C_SCB + j), ALU.add)
            for tap in range(1, 4):
                P.stt(V(acc[b], hacc[b]), V(pre[b][:, 29 + tap:29 + tap + S], hpre[b]), pcol(C_SCW + j * 4 + tap),
                      V(acc[b], hacc[b]), ALU.mult, ALU.add)
            if pend is not None:
                pend()

            def fin(b=b, j=j):
                P.act(V(xo[b], hxo[b]), V(acc[b], hacc[b]), AF.Silu)
                P.dma("pool", V(xbc_d[j * 128:(j + 1) * 128, :], dall("xbc")), V(xo[b], hxo[b]))
            pend = fin
        pend()
        barrier()
        A.off = A1
        wa = [A.alloc(8 * 128, BF16).rearrange("p (k n) -> p k n", k=8) for _ in range(2)]; hwa = [H(), H()]
        wgt = [A.alloc(8 * 128, BF16).rearrange("p (k n) -> p k n", k=8) for _ in range(2)]; hwgt = [H(), H()]
        vb = [A.alloc(S + 32, BF16) for _ in range(2)]; hvb = [[H() for _ in range(NT)] for _ in range(2)]; hvz = [H(), H()]
        dg = [A.alloc(31 * 128, BF16).rearrange("p (j n) -> p j n", j=31) for _ in range(2)]; hdg = [H(), H()]
        acc = [A.alloc(S, F32) for _ in range(2)]; hacc = [H(), H()]
        sg = [A.alloc(T, F32) for _ in range(2)]; hsg = [H(), H()]
        for b in range(2):
            P.memset(V(vb[b][:, 0:32], hvz[b]), 0.0)
        def prepj(j):
            b = j % 2
            wload(wa[b], win_v[:, :, OFF_GA + j * 128:OFF_GA + (j + 1) * 128], hwa[b])
            wload(wgt[b], win_v[:, :, OFF_GG + j * 128:OFF_GG + (j + 1) * 128], hwgt[b])
            for tap in range(31):
                P.ts(V(dg[b][:, tap, :], hdg[b]), identv, pcol(C_CDW + j * 31 + tap), ALU.mult)
        prepj(0)
        for j in range(8):
            b = j % 2
            if j + 1 < 8:
                prepj(j + 1)

            def glu(t):
                sb_ = t % 2
                for k in range(8):
                    P.mm(PS(1 + sb_), V(wa[b][:, k, :], hwa[b]), V(uT[:, k, t * T:(t + 1) * T], hu[t]), start=(k == 0), stop=(k == 7))
                for k in range(8):
                    P.mm(PS(3 + sb_), V(wgt[b][:, k, :], hwgt[b]), V(uT[:, k, t * T:(t + 1) * T], hu[t]), start=(k == 0), stop=(k == 7))
                P.act(V(sg[sb_], hsg[sb_]), PS(3 + sb_), AF.Sigmoid)
                P.tt(V(vb[b][:, 32 + t * T:32 + (t + 1) * T], hvb[b][t]), PS(1 + sb_), V(sg[sb_], hsg[sb_]), ALU.mult)

            def conv(t):
                cbk = 5 + t % 2
                rd_h = [hvb[b][t]] + ([hvb[b][t - 1]] if t > 0 else [hvz[b]])
                for tap in range(31):
                    o0 = 2 + tap + t * T
                    P.mm(PS(cbk), V(dg[b][:, tap, :], hdg[b]), V(vb[b][:, o0:o0 + T], rd_h), start=(tap == 0), stop=(tap == 30))
                P.act(V(acc[b][:, t * T:(t + 1) * T], hacc[b]), PS(cbk), AF.Identity, bias=pcol(C_CDB + j))
            glu(0)
            for t in range(NT):
                if t + 1 < NT:
                    glu(t + 1)
                conv(t)
            P.dma("pool", V(cv_d[j * 128:(j + 1) * 128, :], dall("cv")), V(acc[b], hacc[b]))
        barrier()
        A.off = A1
        wl = A.alloc(8 * 768, BF16).rearrange("p (k n) -> p k n", k=8); hwl = H()
        wload(wl[:, :, 0:704], win_v[:, :, OFF_QL:OFF_QL + 704], hwl)
        wload(wl[:, :, 704:736], win_v[:, :, OFF_KR + 32:OFF_KR + 64], hwl)
        wload(wl[:, :, 736:768], win_v[:, :, OFF_KR:OFF_KR + 32], hwl)
        lo = [A.alloc(T, F32) for _ in range(2)]; hlo = [H(), H()]
        segs = [(0, 128), (128, 128), (256, 128), (384, 128), (512, 128), (640, 64), (704, 64)]
        i = 0
        for t in range(NT):
            for (c0, m) in segs:
                b = i % 2; i += 1
                for k in range(8):
                    P.mm(PS(1 + b, T, m), V(wl[:, k, c0:c0 + m], hwl), V(uT[:, k, t * T:(t + 1) * T], hu[t]), start=(k == 0), stop=(k == 7))
                P.copy(V(lo[b][0:m], hlo[b]), PS(1 + b, T, m), eng="act")
                P.dma("pool", V(lat_d[c0:c0 + m, t * T:(t + 1) * T], dh("lat", t)), V(lo[b][0:m], hlo[b]))
        barrier()
    def phase_S(l):
        A = Arena(big, ARENA0, 103000)
        xbt = [A.alloc(16 * T, BF16).rearrange("p (k s) -> p k s", k=16) for _ in range(2)]; hxb = [H(), H()]
        szt = [A.alloc(1024, F32) for _ in range(2)]; hszt = [H(), H()]
        hst = A.alloc(1024, F32); h_hst = H()
        hbf = A.alloc(1024, BF16); h_hbf = H()
        D2 = lambda n, dt: ([A.alloc(n, dt) for _ in range(2)], [H(), H()])
        sm2, h_sm2 = D2(64, F32)
        ew2, h_ew2 = D2(48, F32)
        Xb2, h_Xb2 = D2(1024, BF16)
        Xw2, h_Xw2 = D2(1024, BF16)
        tD2, h_tD2 = D2(1024, F32)
        Bt2, h_Bt2 = D2(512, BF16)
        MT2, h_MT2 = D2(2048, BF16)
        cbm = A.alloc(512, F32); h_cbm = H()
        AG = A.alloc(2048, F32); h_AG = H()
        dec = [A.alloc(512, F32) for _ in range(2)]; h_dec = [H(), H()]
        yg = A.alloc(1024, F32); h_yg = H()
        t1 = [A.alloc(512, F32) for _ in range(2)]; h_t1 = [H(), H()]
        ssq = A.alloc(8, F32); h_ssq = H()
        junk = A.alloc(256, F32); h_junk = H()
        ynb = A.alloc(1024, BF16); h_ynb = H()
        yst = [A.alloc(8 * T, BF16).rearrange("p (k s) -> p k s", k=8) for _ in range(2)]; h_yst = [H(), H()]
        P.memset(V(hst, h_hst), 0.0); P.memset(V(hbf, h_hbf), 0.0)
        dtb = V(pbt[:, B_DTB:B_DTB + 16], h_pb)
        sng = V(pbt[:, B_SNG:B_SNG + 1024], h_pb)
        xview = xbc_d.rearrange("(k p) s -> p k s", p=128)
        SMALL = lambda lo, n: V(banks[2][:, 256 + lo:256 + lo + n], bh[2])

        def bc16(ap, lo, n):
            return ap[:, lo:lo + n].unsqueeze(2).broadcast_to([128, n, 64])
        v3 = lambda ap: ap.rearrange("p (h d) -> p h d", h=16)

        def stage1(c):
            t = c // 4; s_ = c % 4; tb = t % 2; cb_ = c % 2
            cs = slice(s_ * 128, (s_ + 1) * 128)
            if s_ == 0:
                P.dma("sp", V(xbt[tb], hxb[tb]), V(xview[:, :, t * T:(t + 1) * T], dall("xbc")))
            P.dma("sp", V(szt[cb_], hszt[cb_]), V(sz_d[c * 128:(c + 1) * 128, :], dh("sz", t)))
            xb_ = xbt[tb]; hx_ = hxb[tb]
            sm, h_sm, ew, h_ew = sm2[cb_], h_sm2[cb_], ew2[cb_], h_ew2[cb_]
            Xb, h_Xb, Xw, h_Xw, tD, h_tD = Xb2[cb_], h_Xb2[cb_], Xw2[cb_], h_Xw2[cb_], tD2[cb_], h_tD2[cb_]
            Btm, h_Btm, MT, h_MT = Bt2[cb_], h_Bt2[cb_], MT2[cb_], h_MT2[cb_]
            for k in range(8):
                P.transpose(V(banks[1][:].bitcast(BF16)[:, k * 128:(k + 1) * 128], bh[1]), V(xb_[:, k, cs], hx_), identv)
            for g in range(4):
                P.transpose(V(banks[2][:].bitcast(BF16)[:, g * 128:(g + 1) * 128], bh[2]), V(xb_[:, 8 + g, cs], hx_), identv)
            for g in range(4):
                P.mm(V(banks[3][:, g * 128:(g + 1) * 128], bh[3]), V(xb_[:, 8 + g, cs], hx_), V(xb_[:, 12 + g, cs], hx_))
            P.tt(V(sm[:, 0:16], h_sm), V(dtraw[:, c, :], h_dtraw), dtb, ALU.add)
            P.act(V(sm[:, 0:16], h_sm), V(sm[:, 0:16], h_sm), AF.Exp)
            P.act(V(sm[:, 0:16], h_sm), V(sm[:, 0:16], h_sm), AF.Ln, bias=1.0)
            P.tt(V(sm[:, 16:32], h_sm), V(sm[:, 0:16], h_sm), V(abc, h_abc), ALU.mult)
            av = V(sm[:, 16:32], h_sm)
            P.mm(SMALL(0, 16), tri, av); P.mm(SMALL(16, 16), gt, av); P.mm(SMALL(32, 16), onesf, av)
            P.act(V(ew, h_ew), SMALL(0, 48), AF.Exp)
            P.tt(V(sm[:, 32:48], h_sm), V(sm[:, 0:16], h_sm), V(ew[:, 16:32], h_ew), ALU.mult)
            P.tt(V(AG.rearrange("p (h s) -> p h s", h=16), h_AG),
                 V(cst[:, K_GT:K_GT + 128].unsqueeze(1).broadcast_to([128, 16, 128]), h_cst),
                 V(sm[:, 16:32].unsqueeze(2).broadcast_to([128, 16, 128]), h_sm), ALU.mult)
            P.tt(V(cbm.rearrange("p (g l) -> p g l", g=4), h_cbm), V(banks[3][:].rearrange("p (g l) -> p g l", g=4), bh[3]),
                 V(cst[:, K_TRI:K_TRI + 128].unsqueeze(1).broadcast_to([128, 4, 128]), h_cst), ALU.mult)
            xsT = banks[1][:].bitcast(BF16)[:, 0:1024].rearrange("p (h d) -> p h d", h=16)
            P.tt(V(v3(Xb), h_Xb), V(xsT, bh[1]), V(bc16(sm, 0, 16), h_sm), ALU.mult)
            P.tt(V(v3(Xw), h_Xw), V(xsT, bh[1]), V(bc16(sm, 32, 16), h_sm), ALU.mult)
            P.tt(V(v3(tD), h_tD), V(xsT, bh[1]), V(bc16(pbt, B_DSK, 16), h_pb), ALU.mult)
            P.copy(V(Btm, h_Btm), V(banks[2][:].bitcast(BF16)[:, 0:512], bh[2]), eng="act")
            for g in range(4):
                db = g % 2
                for r in range(4):
                    hd_ = g * 4 + r
                    P.mm(V(banks[4][:, r * 128:(r + 1) * 128], bh[4]), V(AG[:, hd_ * 128:(hd_ + 1) * 128], h_AG), tri)
                P.act(V(dec[db], h_dec[db]), PS(4), AF.Exp)
                P.tt(V(MT[:, g * 512:(g + 1) * 512].rearrange("p (r l) -> p r l", r=4), h_MT), V(dec[db].rearrange("p (r l) -> p r l", r=4), h_dec[db]),
                     V(cbm[:, g * 128:(g + 1) * 128].unsqueeze(1).broadcast_to([128, 4, 128]), h_cbm), ALU.mult)

        def stage2(c):
            t = c // 4; s_ = c % 4; tb = t % 2; cb_ = c % 2
            cs = slice(s_ * 128, (s_ + 1) * 128)
            xb_ = xbt[tb]; hx_ = hxb[tb]
            ew, h_ew = ew2[cb_], h_ew2[cb_]
            Xb, h_Xb, Xw, h_Xw, tD, h_tD = Xb2[cb_], h_Xb2[cb_], Xw2[cb_], h_Xw2[cb_], tD2[cb_], h_tD2[cb_]
            Btm, h_Btm, MT, h_MT = Bt2[cb_], h_Bt2[cb_], MT2[cb_], h_MT2[cb_]
            for hd_ in range(16):
                ybank = 5 + hd_ // 8
                col = (hd_ % 8) * 64
                P.mm(V(banks[ybank][:, col:col + 64], bh[ybank]), V(MT[:, hd_ * 128:(hd_ + 1) * 128], h_MT),
                     V(Xb[:, hd_ * 64:(hd_ + 1) * 64], h_Xb))
            for hf in range(2):
                ybank = 5 + hf
                for gg in range(2):
                    g = hf * 2 + gg
                    P.mm(V(banks[7][:, gg * 256:(gg + 1) * 256], bh[7]), V(xb_[:, 12 + g, cs], hx_), V(hbf[:, g * 256:(g + 1) * 256], h_hbf))
                t1_ = t1[hf]; ht1 = h_t1[hf]
                P.tt(V(t1_.rearrange("p (h d) -> p h d", h=8), ht1), V(banks[7][:].rearrange("p (h d) -> p h d", h=8), bh[7]), V(bc16(ew, hf * 8, 8), h_ew), ALU.mult)
                P.tt(V(t1_, ht1), V(t1_, ht1), PS(ybank), ALU.add)
                P.tt(V(t1_, ht1), V(t1_, ht1), V(tD[:, hf * 512:(hf + 1) * 512], h_tD), ALU.add, eng="pool")
                P.tt(V(yg[:, hf * 512:(hf + 1) * 512], h_yg), V(t1_, ht1), V(szt[cb_][:, hf * 512:(hf + 1) * 512], hszt[cb_]), ALU.mult)
            for hf in range(2):
                for gg in range(2):
                    g = hf * 2 + gg
                    P.mm(V(banks[7][:, gg * 256:(gg + 1) * 256], bh[7]), V(Btm[:, g * 128:(g + 1) * 128], h_Btm), V(Xw[:, g * 256:(g + 1) * 256], h_Xw))
                hv = V(hst[:, hf * 512:(hf + 1) * 512].rearrange("p (h d) -> p h d", h=8), h_hst)
                P.tt(hv, hv, V(bc16(ew, 32 + hf * 8, 8), h_ew), ALU.mult)
                P.tt(V(hst[:, hf * 512:(hf + 1) * 512], h_hst), V(hst[:, hf * 512:(hf + 1) * 512], h_hst), PS(7), ALU.add)
            P.copy(V(hbf, h_hbf), V(hst, h_hst), eng="act")
            for g in range(4):
                P.act(V(junk, h_junk), V(yg[:, g * 256:(g + 1) * 256], h_yg), AF.Square, accum=V(ssq[:, g:g + 1], h_ssq))
            P.act(V(ssq[:, 4:8], h_ssq), V(ssq[:, 0:4], h_ssq), AF.Ln, bias=epsv, scale=1.0 / 256)
            P.act(V(ssq[:, 4:8], h_ssq), V(ssq[:, 4:8], h_ssq), AF.Exp, scale=-0.5)
            P.tt(V(yg.rearrange("p (g d) -> p g d", g=4), h_yg), V(yg.rearrange("p (g d) -> p g d", g=4), h_yg),
                 V(ssq[:, 4:8].unsqueeze(2).broadcast_to([128, 4, 256]), h_ssq), ALU.mult)
            P.tt(V(ynb, h_ynb), V(yg, h_yg), sng, ALU.mult)
            for k in range(8):
                P.transpose(V(banks[0][:].bitcast(BF16)[:, k * 128:(k + 1) * 128], bh[0]), V(ynb[:, k * 128:(k + 1) * 128], h_ynb), identv)
            P.copy(V(yst[tb][:, :, cs], h_yst[tb]), V(banks[0][:].bitcast(BF16)[:, 0:1024].rearrange("p (k s) -> p k s", k=8), bh[0]), eng="act")
            if s_ == 3:
                P.dma("pool", V(ys_d.rearrange("(k p) s -> p k s", p=128)[:, :, t * T:(t + 1) * T], dh("ys", t)), V(yst[tb], h_yst[tb]))

        stage1(0)
        for c in range(NCH):
            la = P.capture()
            if c + 1 < NCH:
                stage1(c + 1)
            lb = P.capture()
            stage2(c)
            P.interleave(la, lb)
        barrier()

    def phase_C(l, base=None, bk=(0, 1), do_barrier=True):
        A = Arena(big, ARENA0 if base is None else base, 103000)
        ct = [A.alloc(8 * T, F32).rearrange("p (k s) -> p k s", k=8) for _ in range(2)]; hct = [H(), H()]
        cb16 = A.alloc(8 * T, BF16).rearrange("p (k s) -> p k s", k=8); hcb = H()
        sq = A.alloc(8 * T, BF16).rearrange("p (k s) -> p k s", k=8); hsq = H()
        mean = A.alloc(T, F32); hmean = H()
        var = A.alloc(T, F32); hvar = H()
        tmp = A.alloc(T, F32); htmp = H()
        yo = [A.alloc(8 * T, BF16).rearrange("p (k s) -> p k s", k=8) for _ in range(2)]; hyo = [H(), H()]
        for t in range(NT):
            b = t % 2
            P.dma("sp", V(ct[b], hct[b]), V(fmview(cv_d, t), dall("cv")))
            P.copy(V(cb16, hcb), V(ct[b], hct[b]), eng="act")
            P.act(V(sq, hsq), V(ct[b], hct[b]), AF.Square)
            for k in range(8):
                P.mm(PS(bk[0]), onesb, V(cb16[:, k, :], hcb), start=(k == 0), stop=(k == 7))
            for k in range(8):
                P.mm(PS(bk[1]), onesb, V(sq[:, k, :], hsq), start=(k == 0), stop=(k == 7))
            P.ts(V(mean, hmean), PS(bk[0]), 1.0 / D, ALU.mult)
            P.tt(V(tmp, htmp), V(mean, hmean), V(mean, hmean), ALU.mult)
            P.stt(V(var, hvar), PS(bk[1]), 1.0 / D, V(tmp, htmp), ALU.mult, ALU.subtract)
            P.act(V(var, hvar), V(var, hvar), AF.Ln, bias=epsv)
            P.act(V(var, hvar), V(var, hvar), AF.Exp, scale=-0.5)
            for k in range(8):
                P.tt(V(tmp, htmp), V(ct[b][:, k, :], hct[b]), V(mean, hmean), ALU.subtract)
                P.tt(V(tmp, htmp), V(tmp, htmp), V(var, hvar), ALU.mult)
                P.act(V(yo[b][:, k, :], hyo[b]), V(tmp, htmp), AF.Silu, bias=pcol(C_LNB + k), scale=pcol(C_LNG + k))
            P.dma("pool", V(fmview(yc_d, t), dh("yc", t)), V(yo[b], hyo[b]))
        if do_barrier:
            barrier()
    def phase_M1(l):
        A = Arena(big, ARENA0, 103000)
        wq = A.alloc(3 * 1536, BF16).rearrange("p (k n) -> p k n", k=3); hwq = H()
        wqs = A.alloc(3 * 512, BF16).rearrange("p (k n) -> p k n", k=3); hwqs = H()
        wkn = A.alloc(2 * 1024, BF16).rearrange("p (k n) -> p k n", k=2); hwkn = H()
        wv = A.alloc(2 * 1024, BF16).rearrange("p (k n) -> p k n", k=2); hwv = H()
        wload(wq, wqb_d[l].rearrange("(k p) n -> p k n", p=128), hwq)
        qv = wqb_d[l].rearrange("(k p) (h c) -> p k h c", p=128, h=8)
        wqs4 = wqs.rearrange("p k (h c) -> p k h c", h=8)
        for k in range(3):
            wload(wqs4[:, k, :, 0:32], qv[:, k, :, 160:192], hwqs)
            wload(wqs4[:, k, :, 32:64], qv[:, k, :, 128:160], hwqs)
        kv4 = wkvb_d[l].rearrange("(k p) (h c) -> p k h c", p=128, h=8)
        for k in range(2):
            wload(wkn.rearrange("p k (h c) -> p k h c", h=8)[:, k], kv4[:, k, :, 0:128], hwkn)
            wload(wv.rearrange("p k (h c) -> p k h c", h=8)[:, k], kv4[:, k, :, 128:256], hwv)
        lt = [A.alloc(6 * T, F32).rearrange("p (k s) -> p k s", k=6) for _ in range(2)]; hlt = [H(), H()]
        rp = [A.alloc(T, F32) for _ in range(2)]; hrp = [H(), H()]
        rp2 = [A.alloc(T, F32) for _ in range(2)]; hrp2 = [H(), H()]
        sq = A.alloc(3 * T, BF16).rearrange("p (k s) -> p k s", k=3); hsq = H()
        rs = A.alloc(T, F32); hrs = H()
        qn = A.alloc(3 * T, BF16).rearrange("p (k s) -> p k s", k=3); hqn = H()
        kvn = A.alloc(2 * T, BF16).rearrange("p (k s) -> p k s", k=2); hkvn = H()
        sqh = [A.alloc(T, BF16) for _ in range(3)]; hsqh = [H() for _ in range(3)]
        rsh = [A.alloc(T, F32) for _ in range(3)]; hrsh = [H() for _ in range(3)]
        krb = A.alloc(T, F32); hkrb = H()
        ob = [A.alloc(T, BF16) for _ in range(6)]; hob = [H() for _ in range(6)]
        ta = A.alloc(T, F32); hta = H()
        tb_ = A.alloc(T, F32); htb = H()
        vb = [A.alloc(1024, BF16) for _ in range(2)]; hvb = [H(), H()]
        latv = lat_d.rearrange("(k p) s -> p k s", p=128)
        oi = 0
        for t in range(NT):
            b = t % 2; sl = slice(t * T, (t + 1) * T)
            P.dma("sp", V(lt[b], hlt[b]), V(latv[:, :, sl], dh("lat", t)))
            P.dma("sp", V(lt[b][0:64, 5, :], hlt[b]), V(lat_d[704:768, sl], dh("lat", t)))
            P.dma("sp", V(rp[b][0:64], hrp[b]), V(rope_d[0:64, sl], dh("rope", t)))
            P.dma("sp", V(rp2[b][0:64], hrp2[b]), V(rope_d[64:128, sl], dh("rope", t)))
            L_ = lt[b]; hL = hlt[b]
            cosv = V(rp[b][0:64], hrp[b]); sinv = V(rp2[b][0:64], hrp2[b])
            xnorm(L_[:, 0:3, :], hL, C_QAG, qn, hqn, sq, hsq, rs, hrs, 0, nk=3, inv_n=1.0 / 384)
            xnorm(L_[:, 3:5, :], hL, C_KVAG, kvn, hkvn, sq[:, 0:2, :], hsq, rs, hrs, 0, nk=2, inv_n=1.0 / 256)

            for h in range(8):
                for k in range(3):
                    P.mm(PS(1), V(wq[:, k, h * 192:h * 192 + 128], hwq), V(qn[:, k, :], hqn), start=(k == 0), stop=(k == 2))
                for k in range(3):
                    P.mm(PS(2, T, 64), V(wq[:, k, h * 192 + 128:h * 192 + 192], hwq), V(qn[:, k, :], hqn), start=(k == 0), stop=(k == 2))
                for k in range(3):
                    P.mm(PS(3, T, 64), V(wqs[:, k, h * 64:(h + 1) * 64], hwqs), V(qn[:, k, :], hqn), start=(k == 0), stop=(k == 2))
                for k in range(2):
                    P.mm(PS(4), V(wkn[:, k, h * 128:(h + 1) * 128], hwkn), V(kvn[:, k, :], hkvn), start=(k == 0), stop=(k == 1))
                specs = [(1, 128, 1.0 / 128, 5), (2, 64, 1.0 / 64, 6), (4, 128, 1.0 / 128, 7)]
                for i, (bank, parts, inv_n, nb) in enumerate(specs):
                    P.act(V(sqh[i][0:parts], hsqh[i]), PS(bank, T, parts), AF.Square)
                for i, (bank, parts, inv_n, nb) in enumerate(specs):
                    P.mm(PS(nb, T, parts), V(ones_b[0:parts, 0:parts], h_ob), V(sqh[i][0:parts], hsqh[i]))
                for i, (bank, parts, inv_n, nb) in enumerate(specs):
                    P.act(V(rsh[i][0:parts], hrsh[i]), PS(nb, T, parts), AF.Ln, bias=V(epsc[0:parts], h_eps), scale=inv_n)
                    P.act(V(rsh[i][0:parts], hrsh[i]), V(rsh[i][0:parts], hrsh[i]), AF.Exp, scale=-0.5)
                o1 = oi % 6; o2 = (oi + 1) % 6; o3 = (oi + 2) % 6; oi += 3
                P.stt(V(ob[o1], hob[o1]), PS(1), V(gsc[:, 0:1], h_gsc), V(rsh[0], hrsh[0]), ALU.mult, ALU.mult)
                P.stt(V(ob[o3], hob[o3]), PS(4), pcol(C_KNN), V(rsh[2], hrsh[2]), ALU.mult, ALU.mult)
                P.stt(V(ta[0:64], hta), PS(2, T, 64), V(gsc[0:64, 1:2], h_gsc), cosv, ALU.mult, ALU.mult)
                P.stt(V(tb_[0:64], htb), PS(3, T, 64), V(gsc[0:64, 2:3], h_gsc), sinv, ALU.mult, ALU.mult)
                P.tt(V(ta[0:64], hta), V(ta[0:64], hta), V(tb_[0:64], htb), ALU.add)
                P.tt(V(ob[o2][0:64], hob[o2]), V(ta[0:64], hta), V(rsh[1][0:64], hrsh[1]), ALU.mult)
                P.dma("pool", V(q_d[h, 0:128, sl], dh("q", t)), V(ob[o1], hob[o1]))
                P.dma("pool", V(q_d[h, 128:192, sl], dh("q", t)), V(ob[o2][0:64], hob[o2]))
                P.dma("pool", V(k_d[h, :, sl], dh("k", t)), V(ob[o3], hob[o3]))
            P.dma("sp", V(krb[0:64], hkrb), V(lat_d[640:704, sl], dh("lat", t)))
            krv = V(krb[0:64], hkrb); krsv = V(L_[0:64, 5, :], hL)
            P.act(V(sqh[1][0:64], hsqh[1]), krv, AF.Square)
            P.mm(PS(7, T, 64), V(ones_b[0:64, 0:64], h_ob), V(sqh[1][0:64], hsqh[1]))
            P.act(V(rsh[1][0:64], hrsh[1]), PS(7, T, 64), AF.Ln, bias=V(epsc[0:64], h_eps), scale=1.0 / 64)
            P.act(V(rsh[1][0:64], hrsh[1]), V(rsh[1][0:64], hrsh[1]), AF.Exp, scale=-0.5)
            P.stt(V(ta[0:64], hta), krv, pcol(C_KNR, 1, 64), cosv, ALU.mult, ALU.mult)
            P.stt(V(tb_[0:64], htb), krsv, pcol(C_KNRS, 1, 64), sinv, ALU.mult, ALU.mult)
            P.tt(V(ta[0:64], hta), V(ta[0:64], hta), V(tb_[0:64], htb), ALU.add)
            o = oi % 6; oi += 1
            P.tt(V(ob[o][0:64], hob[o]), V(ta[0:64], hta), V(rsh[1][0:64], hrsh[1]), ALU.mult)
            P.dma("pool", V(kr_d[:, sl], dh("kr", t)), V(ob[o][0:64], hob[o]))
            for s_ in range(4):
                vb_ = s_ % 2
                for half in range(2):
                    for k in range(2):
                        P.mm(PS(2 + half), V(kvn[:, k, s_ * 128:(s_ + 1) * 128], hkvn), V(wv[:, k, half * 512:(half + 1) * 512], hwv),
                             start=(k == 0), stop=(k == 1))
                    P.copy(V(vb[vb_][:, half * 512:(half + 1) * 512], hvb[vb_]), PS(2 + half), eng="act")
                r0 = t * T + s_ * 128
                P.dma("pool", V(v_d[r0:r0 + 128, :], dh("v", t)), V(vb[vb_], hvb[vb_]))
        barrier()

    def phase_M2(l, do_barrier=True):
        A = Arena(big, ARENA0, 103000)
        krt = A.alloc(S, BF16); hkr = H()
        P.memset(V(krt[64:128], hkr), 0.0)
        P.dma("sp", V(krt[0:64], hkr), V(kr_d, dall("kr")))
        kt_ = [A.alloc(S, BF16) for _ in range(2)]; hkt = [H(), H()]
        qnt = [A.alloc(S, BF16) for _ in range(2)]; hqn = [H(), H()]
        qrt = [A.alloc(S, BF16) for _ in range(2)]; hqr = [H(), H()]
        for b_ in range(2):
            P.memset(V(qrt[b_][64:128], hqr[b_]), 0.0)
        vt = [A.alloc(NCH * 128, BF16).rearrange("p (c d) -> p c d", c=NCH) for _ in range(2)]; hvt = [H(), H()]
        pt = [A.alloc(T, BF16) for _ in range(4)]; hpt = [H() for _ in range(4)]
        pd = [A.alloc(T, BF16) for _ in range(4)]; hpd = [H() for _ in range(4)]
        for j in range(4):
            P.memset(V(pd[j], hpd[j]), 0.0)
        dacc = [A.alloc(T, F32) for _ in range(2)]; hdacc = [H(), H()]
        rd = A.alloc(T, F32); hrd = H()
        dhi = A.alloc(T, BF16); hdhi = H()
        dlo = A.alloc(T, BF16); hdlo = H()
        oo = [A.alloc(T, BF16) for _ in range(2)]; hoo = [H(), H()]
        mk = masks.rearrange("p (j q) -> p j q", j=4)
        pi = 0
        for h in range(8):
            b = h % 2
            P.dma("sp", V(kt_[b], hkt[b]), V(k_d[h], dall("k")))
            P.dma("sp", V(qnt[b], hqn[b]), V(q_d[h, 0:128, :], dall("q")))
            P.dma("sp", V(qrt[b][0:64], hqr[b]), V(q_d[h, 128:192, :], dall("q")))
            P.dma("sp", V(vt[b], hvt[b]), V(v_d.rearrange("(c p) d -> p c d", p=128)[:, :, h * 128:(h + 1) * 128], dall("v")))
            for qt in range(NT):
                qs = slice(qt * T, (qt + 1) * T)
                nk = 4 * qt + 4
                ob_ = 2 + (qt % 2)
                da = qt % 2

                def st(kt):
                    sbk = kt % 2
                    ks = slice(kt * 128, (kt + 1) * 128)
                    P.mm(PS(sbk), V(kt_[b][:, ks], hkt[b]), V(qnt[b][:, qs], hqn[b]), start=True, stop=False)
                    P.mm(PS(sbk), V(krt[:, ks], hkr), V(qrt[b][:, qs], hqr[b]), start=False, stop=True)
                st(0)
                for kt in range(nk):
                    if kt + 1 < nk:
                        st(kt + 1)
                    if kt >= 4 * qt:
                        j = kt - 4 * qt
                        pv_ = V(pd[j], hpd[j])
                        bk = banks[kt % 2]
                        P.act(V(pd[j][0:64, 128 * j:T], hpd[j]), V(bk[0:64, 128 * j:T], bh[kt % 2]), AF.Exp)
                        P.act(V(pd[j][64:128, 128 * j + 64:T], hpd[j]), V(bk[64:128, 128 * j + 64:T], bh[kt % 2]), AF.Exp)
                    else:
                        p_ = pi % 4; pi += 1
                        pv_ = V(pt[p_], hpt[p_])
                        P.act(pv_, PS(kt % 2), AF.Exp)
                    P.mm(PS(ob_), V(vt[b][:, kt, :], hvt[b]), pv_, start=(kt == 0), stop=(kt == nk - 1))
                    if kt % 2 == 1:
                        P.mm(PS(4 + da), onesb, pv_, start=(kt == 1), stop=False)
                    elif kt == 0:
                        P.copy(V(dacc[da], hdacc[da]), pv_)
                    else:
                        P.tt(V(dacc[da], hdacc[da]), V(dacc[da], hdacc[da]), pv_, ALU.add)
                P.copy(V(dhi, hdhi), V(dacc[da], hdacc[da]))
                P.tt(V(dlo, hdlo), V(dacc[da], hdacc[da]), V(dhi, hdhi), ALU.subtract)
                P.mm(PS(4 + da), onesb, V(dhi, hdhi), start=False, stop=False)
                P.mm(PS(4 + da), onesb, V(dlo, hdlo), start=False, stop=True)
                P.act(V(rd, hrd), PS(4 + da), AF.Ln)
                P.act(V(rd, hrd), V(rd, hrd), AF.Exp, scale=-1.0)
                o = qt % 2
                P.tt(V(oo[o], hoo[o]), PS(ob_), V(rd, hrd), ALU.mult)
                P.dma("pool", V(o_d[h * 128:(h + 1) * 128, qs], dh("o", qt)), V(oo[o], hoo[o]))
        if do_barrier:
            barrier()
        return A.off
    def phase_G(l):
        xsrc = xT_d if l == 0 else xres_d
        xsn = "xT" if l == 0 else "xres"
        A = Arena(big, ARENA0, 103000)
        wgate = A.alloc(8 * 3072, BF16).rearrange("p (k n) -> p k n", k=8); hwg = H()
        wbr = [A.alloc(8 * 1024, BF16).rearrange("p (k n) -> p k n", k=8) for _ in range(3)]; hwb = [H() for _ in range(3)]
        wo = A.alloc(8 * 1024, BF16).rearrange("p (k n) -> p k n", k=8); hwo = H()
        kp = lambda d: d.rearrange("(k p) n -> p k n", p=128)
        for b3 in range(3):
            wload(wgate[:, :, b3 * 1024:(b3 + 1) * 1024], kp(w_in_d[l])[:, :, OFF_GATE + b3 * 1024:OFF_GATE + (b3 + 1) * 1024], hwg)
        for b3, wd in enumerate((ssd_wo_d, conv_wo_d, mla_wo_d)):
            wload(wbr[b3], kp(wd[l]), hwb[b3])
        wload(wo, kp(wout_d[l]), hwo)
        TG = 256; NTG = S // TG
        fmg = lambda d, t: d.rearrange("(k p) s -> p k s", p=128)[:, :, t * TG:(t + 1) * TG]
        xt3 = [A.alloc(8 * TG, F32).rearrange("p (k s) -> p k s", k=8) for _ in range(3)]; hxt3 = [H(), H(), H()]
        ut2 = [A.alloc(8 * TG, BF16).rearrange("p (k s) -> p k s", k=8) for _ in range(2)]; hut2 = [H(), H()]
        sq = A.alloc(8 * TG, BF16).rearrange("p (k s) -> p k s", k=8); hsq = H()
        rs = A.alloc(TG, F32); hrs = H()
        br2 = [[A.alloc(8 * TG, BF16).rearrange("p (k s) -> p k s", k=8) for _ in range(3)] for _ in range(2)]
        hbr2 = [[H() for _ in range(3)] for _ in range(2)]
        mg = A.alloc(8 * TG, BF16).rearrange("p (k s) -> p k s", k=8); hmg = H()
        gt_ = [A.alloc(TG, F32) for _ in range(2)]; hgt = [H(), H()]
        macc = A.alloc(TG, F32); hmacc = H()
        srcs = [(ys_d, "ys"), (yc_d, "yc"), (o_d, "o")]

        def prep(t):
            b = t % 2; b3x = t % 3
            P.dma("sp", V(xt3[b3x], hxt3[b3x]), V(fmg(xsrc, t), dh(xsn, t // 2)))
            for b3, (d_, nm) in enumerate(srcs):
                P.dma("sp", V(br2[b][b3], hbr2[b][b3]), V(fmg(d_, t), dall(nm)))
            xnorm(xt3[b3x], hxt3[b3x], C_MIXG, ut2[b], hut2[b], sq, hsq, rs, hrs, 0, width=TG)
        prep(0)
        for t in range(NTG):
            b = t % 2; b3x = t % 3
            xt = xt3[b3x]; hxt = hxt3[b3x]; ut = ut2[b]; hut = hut2[b]; br = br2[b]; hbr = hbr2[b]
            if t + 1 < NTG:
                prep(t + 1)
            gi = 0
            for m in range(8):
                for b3 in range(3):
                    gb = 1 + gi % 2; yb = 3 + gi % 2; g2 = gi % 2; gi += 1
                    for k in range(8):
                        P.mm(PS(gb, TG), V(wgate[:, k, b3 * 1024 + m * 128:b3 * 1024 + (m + 1) * 128], hwg), V(ut[:, k, :], hut), start=(k == 0), stop=(k == 7))
                    for k in range(8):
                        P.mm(PS(yb, TG), V(wbr[b3][:, k, m * 128:(m + 1) * 128], hwb[b3]), V(br[b3][:, k, :], hbr[b3]), start=(k == 0), stop=(k == 7))
                    P.act(V(gt_[g2], hgt[g2]), PS(gb, TG), AF.Sigmoid, bias=pcol(C_GATEB + b3 * 8 + m))
                    if b3 == 0:
                        P.tt(V(macc, hmacc), V(gt_[g2], hgt[g2]), PS(yb, TG), ALU.mult)
                    else:
                        P.tt(V(gt_[g2], hgt[g2]), V(gt_[g2], hgt[g2]), PS(yb, TG), ALU.mult)
                        if b3 == 1:
                            P.tt(V(macc, hmacc), V(macc, hmacc), V(gt_[g2], hgt[g2]), ALU.add)
                        else:
                            P.tt(V(mg[:, m, :], hmg), V(macc, hmacc), V(gt_[g2], hgt[g2]), ALU.add)
            for n in range(8):
                ob_ = 5 + n % 2
                for m in range(8):
                    P.mm(PS(ob_, TG), V(wo[:, m, n * 128:(n + 1) * 128], hwo), V(mg[:, m, :], hmg), start=(m == 0), stop=(m == 7))
                P.tt(V(xt[:, n, :], hxt), V(xt[:, n, :], hxt), PS(ob_, TG), ALU.add)
            P.dma("pool", V(fmg(xres_d, t), dh("xres", t // 2)), V(xt, hxt))
        barrier()

    def phase_X(l):
        A = Arena(big, ARENA0, 103000)
        kp = lambda d: d.rearrange("(k p) n -> p k n", p=128)
        wkv = A.alloc(8 * 2048, BF16).rearrange("p (k n) -> p k n", k=8); hwkv = H()
        wload(wkv[:, :, 0:1024], kp(xwkv_d[l])[:, :, 0:1024], hwkv)
        wload(wkv[:, :, 1024:2048], kp(xwkv_d[l])[:, :, 1024:2048], hwkv)
        mt = A.alloc(8 * 256, F32).rearrange("p (k s) -> p k s", k=8); hmt = H()
        mn = A.alloc(8 * 256, BF16).rearrange("p (k s) -> p k s", k=8); hmn = H()
        sqm = A.alloc(8 * 256, BF16).rearrange("p (k s) -> p k s", k=8); hsqm = H()
        rsm = A.alloc(256, F32); hrsm = H()
        KX = A.alloc(8 * 256, BF16).rearrange("p (c s) -> p c s", c=8); hKX = H()
        VX = A.alloc(2 * 1024, BF16).rearrange("p (m d) -> p m d", m=2); hVX = H()
        sq2 = A.alloc(2 * 256, BF16).rearrange("p (c s) -> p c s", c=2); hsq2 = H()
        P.dma("sp", V(mt, hmt), V(memT_d.rearrange("(k p) s -> p k s", p=128), H()))
        xnorm(mt, hmt, C_MEMG, mn, hmn, sqm, hsqm, rsm, hrsm, 0, nk=8, width=256)
        for hx in range(4):
            for c in range(2):
                for k in range(8):
                    P.mm(PS(1 + c, 256), V(wkv[:, k, hx * 256 + c * 128:hx * 256 + (c + 1) * 128], hwkv), V(mn[:, k, :], hmn), start=(k == 0), stop=(k == 7))
                P.act(V(sq2[:, c, :], hsq2), PS(1 + c, 256), AF.Square)
            for c in range(2):
                P.mm(PS(3, 256), onesb, V(sq2[:, c, :], hsq2), start=(c == 0), stop=(c == 1))
            P.act(V(rsm, hrsm), PS(3, 256), AF.Ln, bias=epsv, scale=1.0 / 256)
            P.act(V(rsm, hrsm), V(rsm, hrsm), AF.Exp, scale=-0.5)
            for c in range(2):
                P.stt(V(KX[:, hx * 2 + c, :], hKX), PS(1 + c, 256), pcol(C_XKG + c), V(rsm, hrsm), ALU.mult, ALU.mult)
        for m in range(2):
            for half in range(2):
                for k in range(8):
                    P.mm(PS(4 + half), V(mn[:, k, m * 128:(m + 1) * 128], hmn), V(wkv[:, k, 1024 + half * 512:1024 + (half + 1) * 512], hwkv), start=(k == 0), stop=(k == 7))
                P.copy(V(VX[:, m, half * 512:(half + 1) * 512], hVX), PS(4 + half), eng="act")
        barrier()
        A2 = Arena(big, ARENA0, 103000)
        KX2 = A2.alloc(8 * 256, BF16).rearrange("p (c s) -> p c s", c=8); hKX2 = H()
        VX2 = A2.alloc(2 * 1024, BF16).rearrange("p (m d) -> p m d", m=2); hVX2 = H()
        P.copy(V(KX2, hKX2), V(KX, hKX)); P.copy(V(VX2, hVX2), V(VX, hVX))
        barrier()
        A = A2
        wq_ = A.alloc(8 * 1024, BF16).rearrange("p (k n) -> p k n", k=8); hwq = H()
        wo_ = A.alloc(8 * 1024, BF16).rearrange("p (k n) -> p k n", k=8); hwo = H()
        wload(wq_, kp(xwq_d[l]), hwq); wload(wo_, kp(xwo_d[l]), hwo)
        xt = [A.alloc(8 * T, F32).rearrange("p (k s) -> p k s", k=8) for _ in range(3)]; hxt = [H(), H(), H()]
        ht2 = [A.alloc(8 * T, BF16).rearrange("p (k s) -> p k s", k=8) for _ in range(2)]; hht2 = [H(), H()]
        sq = A.alloc(8 * T, BF16).rearrange("p (k s) -> p k s", k=8); hsq = H()
        rs = A.alloc(T, F32); hrs = H()
        sqq = A.alloc(2 * T, BF16).rearrange("p (c s) -> p c s", c=2); hsqq = H()
        rsq = A.alloc(T, F32); hrsq = H()
        qx = A.alloc(2 * T, BF16).rearrange("p (c s) -> p c s", c=2); hqx = H()
        pt = [A.alloc(T, BF16) for _ in range(2)]; hpt = [H(), H()]
        rd = A.alloc(T, F32); hrd = H()
        ox = A.alloc(8 * T, BF16).rearrange("p (k s) -> p k s", k=8); hox = H()
        def prep(t):
            b = t % 2; b3 = t % 3
            P.dma("sp", V(xt[b3], hxt[b3]), V(fmview(xres_d, t), dh("xres", t)))
            xnorm(xt[b3], hxt[b3], C_XATG, ht2[b], hht2[b], sq, hsq, rs, hrs, 0)
        prep(0)
        for t in range(NT):
            b = t % 2
            ht = ht2[b]; hht = hht2[b]
            b = t % 3
            if t + 1 < NT:
                prep(t + 1)
            for hx in range(4):
                for c in range(2):
                    for k in range(8):
                        P.mm(PS(1 + c), V(wq_[:, k, hx * 256 + c * 128:hx * 256 + (c + 1) * 128], hwq), V(ht[:, k, :], hht), start=(k == 0), stop=(k == 7))
                    P.act(V(sqq[:, c, :], hsqq), PS(1 + c), AF.Square)
                for c in range(2):
                    P.mm(PS(3), onesb, V(sqq[:, c, :], hsqq), start=(c == 0), stop=(c == 1))
                P.act(V(rsq, hrsq), PS(3), AF.Ln, bias=epsv, scale=1.0 / 256)
                P.act(V(rsq, hrsq), V(rsq, hrsq), AF.Exp, scale=-0.5)
                for c in range(2):
                    P.stt(V(qx[:, c, :], hqx), PS(1 + c), V(gsc[:, 3 + c:4 + c], h_gsc), V(rsq, hrsq), ALU.mult, ALU.mult)
                for m in range(2):
                    for c in range(2):
                        P.mm(PS(4 + m), V(KX2[:, hx * 2 + c, m * 128:(m + 1) * 128], hKX2), V(qx[:, c, :], hqx), start=(c == 0), stop=(c == 1))
                    P.act(V(pt[m], hpt[m]), PS(4 + m), AF.Exp)
                for m in range(2):
                    P.mm(PS(3), onesb, V(pt[m], hpt[m]), start=(m == 0), stop=(m == 1))
                P.act(V(rd, hrd), PS(3), AF.Ln)
                P.act(V(rd, hrd), V(rd, hrd), AF.Exp, scale=-1.0)
                for c2 in range(2):
                    for m in range(2):
                        P.mm(PS(6 + c2), V(VX2[:, m, hx * 256 + c2 * 128:hx * 256 + (c2 + 1) * 128], hVX2), V(pt[m], hpt[m]), start=(m == 0), stop=(m == 1))
                    P.tt(V(ox[:, hx * 2 + c2, :], hox), PS(6 + c2), V(rd, hrd), ALU.mult)
            for n in range(8):
                ob_ = 1 + n % 2
                for k in range(8):
                    P.mm(PS(ob_), V(wo_[:, k, n * 128:(n + 1) * 128], hwo), V(ox[:, k, :], hox), start=(k == 0), stop=(k == 7))
                P.tt(V(xt[b][:, n, :], hxt[b]), V(xt[b][:, n, :], hxt[b]), PS(ob_), ALU.add)
            P.dma("pool", V(fmview(xres_d, t), dh("xres", t)), V(xt[b], hxt[b]))
        barrier()

    def phase_F(l, last):
        kp = lambda d: d.rearrange("(k p) n -> p k n", p=128)
        toks = []
        for hh in range(2):
            A = Arena(big, ARENA0, 103000)
            w1g = A.alloc(8 * 1408, BF16).rearrange("p (k n) -> p k n", k=8); hw1g = H()
            w1u = A.alloc(8 * 1408, BF16).rearrange("p (k n) -> p k n", k=8); hw1u = H()
            w2 = A.alloc(11 * 1024, BF16).rearrange("p (k n) -> p k n", k=11); hw2 = H()
            wload(w1g, kp(fwi_d[l])[:, :, hh * 1408:(hh + 1) * 1408], hw1g)
            wload(w1u, kp(fwi_d[l])[:, :, 2816 + hh * 1408:2816 + (hh + 1) * 1408], hw1u)
            wload(w2, fwo_d[l, hh * 1408:(hh + 1) * 1408, :].rearrange("(k p) n -> p k n", p=128), hw2)
            xt = [A.alloc(8 * T, F32).rearrange("p (k s) -> p k s", k=8) for _ in range(3)]; hxt = [H(), H(), H()]
            pt_ = A.alloc(8 * T, F32).rearrange("p (k s) -> p k s", k=8); hpt_ = H()
            ht2 = [A.alloc(8 * T, BF16).rearrange("p (k s) -> p k s", k=8) for _ in range(2)]; hht2 = [H(), H()]
            sq = A.alloc(8 * T, BF16).rearrange("p (k s) -> p k s", k=8); hsq = H()
            rs = A.alloc(T, F32); hrs = H()
            ac = A.alloc(11 * T, BF16).rearrange("p (k s) -> p k s", k=11); hac = H()
            sg = [A.alloc(T, F32) for _ in range(2)]; hsg = [H(), H()]
            def prep(t):
                b = t % 2; b3 = t % 3
                P.dma("sp", V(xt[b3], hxt[b3]), V(fmview(xres_d, t), dh("xres", t)))
                xnorm(xt[b3], hxt[b3], C_FFNG, ht2[b], hht2[b], sq, hsq, rs, hrs, 0)
            prep(0)
            for t in range(NT):
                b = t % 2
                ht = ht2[b]; hht = hht2[b]
                b = t % 3
                if t + 1 < NT:
                    prep(t + 1)
                if hh == 1:
                    P.dma("sp", V(pt_, hpt_), V(fmview(fp_d, t), dh("fp", t)))
                for j in range(11):
                    s2 = j % 2
                    for k in range(8):
                        P.mm(PS(1 + s2), V(w1g[:, k, j * 128:(j + 1) * 128], hw1g), V(ht[:, k, :], hht), start=(k == 0), stop=(k == 7))
                    for k in range(8):
                        P.mm(PS(3 + s2), V(w1u[:, k, j * 128:(j + 1) * 128], hw1u), V(ht[:, k, :], hht), start=(k == 0), stop=(k == 7))
                    P.act(V(sg[s2], hsg[s2]), PS(1 + s2), AF.Silu)
                    P.tt(V(ac[:, j, :], hac), V(sg[s2], hsg[s2]), PS(3 + s2), ALU.mult)
                for n in range(8):
                    ob_ = 5 + n % 2
                    for j in range(11):
                        P.mm(PS(ob_), V(w2[:, j, n * 128:(n + 1) * 128], hw2), V(ac[:, j, :], hac), start=(j == 0), stop=(j == 10))
                    if hh == 0:
                        P.copy(V(xt[b][:, n, :], hxt[b]), PS(ob_), eng="act")
                    else:
                        P.tt(V(xt[b][:, n, :], hxt[b]), V(xt[b][:, n, :], hxt[b]), PS(ob_), ALU.add)
                        P.tt(V(xt[b][:, n, :], hxt[b]), V(xt[b][:, n, :], hxt[b]), V(pt_[:, n, :], hpt_), ALU.add)
                if hh == 0:
                    P.dma("pool", V(fmview(fp_d, t), dh("fp", t)), V(xt[b], hxt[b]))
                else:
                    dst = out_d if last else xres_d
                    tk = P.dma("pool", V(fmview(dst, t), dh("out" if last else "xres", t)), V(xt[b], hxt[b]))
                    toks.append(tk)
            barrier()
        return toks

    def phase_M2C(l):
        la = P.capture()
        end = phase_M2(l, do_barrier=False)
        lb = P.capture()
        phase_C(l, base=end, bk=(6, 7), do_barrier=False)
        P.interleave(la, lb)
        barrier()

    phases = {"A": phase_A, "S": phase_S, "C": phase_C, "M1": phase_M1, "M2": phase_M2, "G": phase_G, "X": phase_X, "M2C": phase_M2C}
    order = ["A", "S", "M1", "M2C", "G", "X", "F"]
    P.marks = []
    for l in range(L):
        for ph in order:
            if only is not None and ph not in only:
                continue
            P.marks.append((l, ph, P.eng["dve"].n))
            if ph == "F":
                final_toks = phase_F(l, l == L - 1)
            else:
                phases[ph](l)
    P.finish(final_toks)
    return P


from concourse.bass_utils import run_bass_kernel_spmd


def _fm(v, nch):
    return np.ascontiguousarray(np.asarray(v, np.float32).reshape(nch, 128).T)


def _consts():
    c = np.zeros((128, NCST), np.float32)
    i = np.arange(128)
    c[:, K_TRI:K_TRI + 128] = (i[:, None] <= i[None, :])
    c[:, K_GT:K_GT + 128] = (i[:, None] > i[None, :])
    c[:, K_ID:K_ID + 128] = np.eye(128)
    f = np.arange(32)
    inv = (10000.0 ** (-(2 * f).astype(np.float32) / np.float32(64))).astype(np.float32)
    c[0:32, K_INV] = inv; c[32:64, K_INV] = inv
    q = np.arange(512)
    for j in range(4):
        c[:, K_MASK + j * 512:K_MASK + (j + 1) * 512] = ((2 * j + i[:, None] // 64) <= (q[None, :] // 64))
    return c


def _pack_params(inp):
    L = 4
    pp = np.zeros((L, 128, NPP), np.float32)
    pb = np.zeros((L, NPB), np.float32)
    for l in range(L):
        p = pp[l]
        p[:, C_MIXG:C_MIXG + 8] = _fm(inp["mix_norm_g"][l], 8)
        p[:, C_XATG:C_XATG + 8] = _fm(inp["xattn_norm_g"][l], 8)
        p[:, C_FFNG:C_FFNG + 8] = _fm(inp["ffn_norm_g"][l], 8)
        p[:, C_MEMG:C_MEMG + 8] = _fm(inp["mem_norm_g"][l], 8)
        p[:, C_SCW:C_SCW + 64] = np.asarray(inp["ssd_conv_w"][l]).T.reshape(16, 128, 4).transpose(1, 0, 2).reshape(128, 64)
        p[:, C_SCB:C_SCB + 16] = _fm(inp["ssd_conv_b"][l], 16)
        p[:, C_CDW:C_CDW + 248] = np.asarray(inp["conv_dw_w"][l]).T.reshape(8, 128, 31).transpose(1, 0, 2).reshape(128, 248)
        p[:, C_CDB:C_CDB + 8] = _fm(inp["conv_dw_b"][l], 8)
        p[:, C_LNG:C_LNG + 8] = _fm(inp["conv_ln_g"][l], 8)
        p[:, C_LNB:C_LNB + 8] = _fm(inp["conv_ln_b"][l], 8)
        p[:, C_QAG:C_QAG + 3] = _fm(inp["mla_q_a_g"][l], 3)
        p[:, C_KVAG:C_KVAG + 2] = _fm(inp["mla_kv_a_g"][l], 2)
        for (cn, cr, crs, g) in ((C_QNN, C_QNR, C_QNRS, inp["mla_q_norm_g"][l]), (C_KNN, C_KNR, C_KNRS, inp["mla_k_norm_g"][l])):
            g = np.asarray(g, np.float32)
            p[:, cn] = g[0:128]
            p[0:64, cr] = g[128:192]
            p[0:64, crs] = np.concatenate([g[160:192], g[128:160]])
        p[:, C_GATEB:C_GATEB + 24] = np.asarray(inp["gate_b"][l], np.float32).reshape(24, 128).T
        p[:, C_XQG:C_XQG + 2] = _fm(inp["xattn_q_norm_g"][l], 2)
        p[:, C_XKG:C_XKG + 2] = _fm(inp["xattn_k_norm_g"][l], 2)
        pb[l, B_DTB:B_DTB + 16] = inp["ssd_dt_bias"][l]
        pb[l, B_ALOG:B_ALOG + 16] = inp["ssd_a_log"][l]
        pb[l, B_DSK:B_DSK + 16] = inp["ssd_d"][l]
        pb[l, B_SNG:B_SNG + 1024] = inp["ssd_norm_g"][l]
    return pp, pb


WNAMES = ["w_in", "ssd_w_out", "conv_w_out", "mla_w_q_b", "mla_w_kv_b", "mla_w_o", "w_out", "xattn_w_q", "xattn_w_kv",
          "xattn_w_o", "ffn_w_in", "ffn_w_out"]


def make_in_maps(inp, cores):
    pp, pb = _pack_params(inp)
    cst = _consts()
    shared = {n: np.ascontiguousarray(np.asarray(inp[n], np.float32)) for n in WNAMES}
    shared.update(cst=cst, pp=pp, pb=pb)
    maps = []
    for b in cores:
        m = dict(shared)
        m["xT"] = np.ascontiguousarray(np.asarray(inp["x"][b], np.float32).T)
        m["memT"] = np.ascontiguousarray(np.asarray(inp["mem"][b], np.float32).T)
        m["pos"] = np.ascontiguousarray(np.asarray(inp["positions"][b], np.int32)[None, :])
        maps.append(m)
    return maps


_CACHE = {}


def kernel(**inputs):
    if "P" not in _CACHE:
        _CACHE["P"] = build(L=4)
    P = _CACHE["P"]
    maps = make_in_maps(inputs, list(range(8)))
    res = run_bass_kernel_spmd(P.nc, maps, core_ids=list(range(8)))
    out = np.stack([np.ascontiguousarray(r["outT"].T) for r in res.results], axis=0)
    return out.astype(np.float32)
```

```python
import contextlib
import numpy as np
import concourse.bass as bass
import concourse.mybir as mybir

F32 = mybir.dt.float32
BF16 = mybir.dt.bfloat16
I32 = mybir.dt.int32
AF = mybir.ActivationFunctionType
ALU = mybir.AluOpType
AX = mybir.AxisListType

ENGS = ("pe", "act", "dve", "pool", "sp")
NDMA_SEMS = 12


class H:
    __slots__ = ("w", "r")

    def __init__(self):
        self.w = None
        self.r = []


class V:
    __slots__ = ("ap", "hs")

    def __init__(self, ap, hs):
        self.ap = ap
        self.hs = hs if isinstance(hs, (list, tuple)) else [hs]


class Eng:
    def __init__(self, name, idx):
        self.name = name
        self.idx = idx
        self.n = 0
        self.seen = [0] * len(ENGS)
        self.seen_dma = {}
        self.ops = []
        self.dma_count = 0


class Prog:
    def __init__(self):
        self.nc = bass.Bass("TRN2", target_bir_lowering=False)
        self.stack = contextlib.ExitStack()
        self.eng = {n: Eng(n, i) for i, n in enumerate(ENGS)}
        self.snaps = {}
        self.ntens = 0
        self.nwaits = 0

    def sb(self, shape, dtype, name=None):
        self.ntens += 1
        t = self.stack.enter_context(self.nc.sbuf_tensor(name or f"sb{self.ntens}", list(shape), dtype))
        return t

    def ps(self, shape, dtype, name=None):
        self.ntens += 1
        t = self.stack.enter_context(self.nc.psum_tensor(name or f"ps{self.ntens}", list(shape), dtype))
        return t

    def dram(self, name, shape, dtype, kind="Internal"):
        return self.nc.dram_tensor(name, list(shape), dtype, kind=kind).ap()

    def _deps(self, reads, writes):
        deps = {}

        def add(tok):
            k, v = tok
            if deps.get(k, 0) < v:
                deps[k] = v
        for h in reads:
            if h.w is not None:
                add(h.w)
        for h in writes:
            if h.w is not None:
                add(h.w)
            for t in h.r:
                add(t)
        return deps

    def _waits(self, e, deps):
        waits = []
        for k, v in deps.items():
            if isinstance(k, int):
                if k == e.idx and e.name == "pe":
                    continue
                if e.seen[k] >= v:
                    continue
                waits.append((k, v))
            else:
                if e.seen_dma.get(k, 0) >= v:
                    continue
                waits.append((k, v))
        for k, v in waits:
            if isinstance(k, int):
                if e.seen[k] < v:
                    e.seen[k] = v
            else:
                e.seen_dma[k] = v
            snap = self.snaps.get((k, v))
            if snap is not None:
                for i in range(len(ENGS)):
                    if e.seen[i] < snap[i]:
                        e.seen[i] = snap[i]
        return waits

    def capture(self):
        self._cap = []
        return self._cap

    def end_capture(self):
        self._cap = None

    def interleave(self, la, lb):
        self._cap = None
        na, nb = len(la), len(lb)
        ia = ib = 0
        while ia < na or ib < nb:
            if ib >= nb or (ia < na and ia * nb <= ib * na):
                kind, args, kw = la[ia]; ia += 1
            else:
                kind, args, kw = lb[ib]; ib += 1
            (self.op if kind == "op" else self.dma)(*args, **kw)

    def op(self, engname, fn, reads, writes):
        if getattr(self, "_cap", None) is not None:
            self._cap.append(("op", (engname, fn, reads, writes), {}))
            return None
        e = self.eng[engname]
        rh = [h for v in reads for h in v.hs]
        wh = [h for v in writes for h in v.hs]
        deps = self._deps(rh, wh)
        waits = self._waits(e, deps)
        e.n += 1
        tok = (e.idx, e.n)
        self.snaps[tok] = tuple(e.seen)
        e.ops.append((waits, fn, None))
        self.nwaits += len(waits)
        for h in rh:
            h.r.append(tok)
        for h in wh:
            h.w = tok
            h.r = []
        return tok

    def dma(self, qname, out, in_, **kw):
        if getattr(self, "_cap", None) is not None:
            self._cap.append(("dma", (qname, out, in_), kw))
            return None
        e = self.eng[qname]
        rh = list(in_.hs)
        wh = list(out.hs)
        deps = self._deps(rh, wh)
        k = e.dma_count % NDMA_SEMS
        rnd = e.dma_count // NDMA_SEMS
        e.dma_count += 1
        key = (qname, k)
        if rnd > 0:
            if deps.get(key, 0) < 16 * rnd:
                deps[key] = 16 * rnd
        waits = self._waits(e, deps)
        tok = (key, 16 * (rnd + 1))
        self.snaps[tok] = tuple(e.seen)
        oap, iap = out.ap, in_.ap
        e.ops.append((waits, lambda eng: eng.dma_start(out=oap, in_=iap, **kw), key))
        self.nwaits += len(waits)
        for h in rh:
            h.r.append(tok)
        for h in wh:
            h.w = tok
            h.r = []
        return tok

    def mm(self, out, lhsT, rhs, start=True, stop=True):
        o, l, r = out.ap, lhsT.ap, rhs.ap
        return self.op("pe", lambda e: e.matmul(o, l, r, start=start, stop=stop), [lhsT, rhs], [out])

    def transpose(self, out, in_, ident):
        o, i, d = out.ap, in_.ap, ident.ap
        return self.op("pe", lambda e: e.transpose(o, i, d), [in_, ident], [out])

    def act(self, out, in_, func, bias=None, scale=None, eng="act", accum=None):
        o, i = out.ap, in_.ap
        reads = [in_]
        kw = {}
        if bias is not None:
            if isinstance(bias, V):
                reads.append(bias)
                kw["bias"] = bias.ap
            else:
                kw["bias"] = bias
        if scale is not None:
            if isinstance(scale, V):
                reads.append(scale)
                kw["scale"] = scale.ap
            else:
                kw["scale"] = scale
        writes = [out]
        if accum is not None:
            writes.append(accum)
            kw["accum_out"] = accum.ap
        return self.op("act", lambda e: e.activation(o, i, func, **kw), reads, writes)

    def tt(self, out, a, b, op, eng="dve"):
        o, x, y = out.ap, a.ap, b.ap
        return self.op(eng, lambda e: e.tensor_tensor(o, x, y, op), [a, b], [out])

    def ts(self, out, a, s1, op0, s2=None, op1=None, eng="dve", accum=None):
        o, x = out.ap, a.ap
        reads = [a]
        if isinstance(s1, V):
            reads.append(s1)
            s1 = s1.ap
        if isinstance(s2, V):
            reads.append(s2)
            s2 = s2.ap
        writes = [out]
        kw = {}
        if accum is not None:
            writes.append(accum)
            kw["accum_out"] = accum.ap
        if op1 is None:
            return self.op(eng, lambda e: e.tensor_scalar(o, x, s1, None, op0, **kw), reads, writes)
        return self.op(eng, lambda e: e.tensor_scalar(o, x, s1, s2, op0, op1, **kw), reads, writes)

    def stt(self, out, a, s, b, op0, op1, eng="dve"):
        o, x, y = out.ap, a.ap, b.ap
        reads = [a, b]
        if isinstance(s, V):
            reads.append(s)
            s = s.ap
        return self.op(eng, lambda e: e.scalar_tensor_tensor(o, x, s, y, op0, op1), reads, [out])

    def copy(self, out, in_, eng="dve"):
        o, i = out.ap, in_.ap
        if eng == "act":
            return self.op("act", lambda e: e.copy(o, i), [in_], [out])
        return self.op(eng, lambda e: e.tensor_copy(o, i), [in_], [out])

    def memset(self, out, val, eng="dve"):
        o = out.ap
        return self.op(eng, lambda e: e.memset(o, val), [], [out])

    def recip(self, out, in_):
        o, i = out.ap, in_.ap
        return self.op("dve", lambda e: e.reciprocal(o, i), [in_], [out])

    def finish(self, final_tokens):
        nc = self.nc
        sems = {}
        for i, n in enumerate(ENGS):
            sems[i] = self.stack.enter_context(nc.semaphore(f"s_{n}"))
        for n in ENGS:
            e = self.eng[n]
            if e.dma_count:
                for k in range(min(NDMA_SEMS, e.dma_count)):
                    sems[(n, k)] = self.stack.enter_context(nc.semaphore(f"d_{n}{k}"))
        sp = self.eng["sp"]
        fin = {}
        for k, v in final_tokens:
            if fin.get(k, 0) < v:
                fin[k] = v
        sp.ops.append((list(fin.items()), None, None))
        block = self.stack.enter_context(nc.Block())
        hw = {"pe": block.tensor, "act": block.scalar, "dve": block.vector, "pool": block.gpsimd, "sp": block.sync}

        def make(e):
            def body(eng):
                own = sems[e.idx]
                for waits, fn, dkey in e.ops:
                    for k, v in waits:
                        eng.wait_ge(sems[k], v)
                    if fn is None:
                        continue
                    ins = fn(eng)
                    if dkey is None:
                        ins.then_inc(own, 1)
                    else:
                        ins.then_inc(sems[dkey], 16)
            return body
        for n in ENGS:
            e = self.eng[n]
            if e.ops:
                hw[n](make(e))
        self.stack.close()
        return nc


S = 4096; D = 1024; T = 512; NT = 8; NCH = 32
EPS = 1e-6
OFF_Z, OFF_XBC, OFF_DT, OFF_GA, OFF_GG, OFF_QL, OFF_KV, OFF_KR, OFF_GATE = 0, 1024, 3072, 3088, 4112, 5136, 5520, 5776, 5840
C_MIXG, C_XATG, C_FFNG, C_MEMG, C_SCW, C_SCB, C_CDW, C_CDB, C_LNG, C_LNB = 0, 8, 16, 24, 32, 96, 112, 360, 368, 376
C_QAG, C_KVAG, C_QNN, C_QNR, C_QNRS, C_KNN, C_KNR, C_KNRS, C_GATEB, C_XQG, C_XKG, NPP = 384, 387, 389, 390, 391, 392, 393, 394, 395, 419, 421, 424
B_DTB, B_ALOG, B_DSK, B_SNG, NPB = 0, 16, 32, 48, 1072
K_TRI, K_GT, K_ID, K_INV, K_MASK, NCST = 0, 128, 256, 384, 385, 385 + 2048


class Arena:
    def __init__(self, big, base, limit):
        self.big, self.off, self.limit = big, base, limit

    def alloc(self, n, dtype, parts=128):
        nb = n * (4 if dtype in (F32, I32) else 2)
        nb = (nb + 63) // 64 * 64
        ne = nb // 2
        assert self.off + ne <= self.limit, ("SBUF arena overflow", self.off + ne, self.limit)
        ap = self.big[:, self.off:self.off + ne]
        self.off += ne
        if dtype != BF16:
            ap = ap.bitcast(dtype)
        return ap[:, 0:n]


def build(L=4, dbg=False, only=None):
    P = Prog(); nc = P.nc
    kind_s = "ExternalOutput" if dbg else "Internal"
    din = lambda n, s, dt=F32: P.dram(n, s, dt, kind="ExternalInput")
    xT_d = din("xT", [D, S]); memT_d = din("memT", [D, 256]); pos_d = din("pos", [1, S], I32)
    cst_d = din("cst", [128, NCST]); pp_d = din("pp", [4, 128, NPP]); pb_d = din("pb", [4, NPB])
    w_in_d = din("w_in", [4, D, 8912]); ssd_wo_d = din("ssd_w_out", [4, D, D]); conv_wo_d = din("conv_w_out", [4, D, D])
    wqb_d = din("mla_w_q_b", [4, 384, 1536]); wkvb_d = din("mla_w_kv_b", [4, 256, 2048]); mla_wo_d = din("mla_w_o", [4, D, D])
    wout_d = din("w_out", [4, D, D]); xwq_d = din("xattn_w_q", [4, D, D]); xwkv_d = din("xattn_w_kv", [4, D, 2048])
    xwo_d = din("xattn_w_o", [4, D, D]); fwi_d = din("ffn_w_in", [4, D, 5632]); fwo_d = din("ffn_w_out", [4, 2816, D])
    out_d = P.dram("outT", [D, S], F32, kind="ExternalOutput")
    sc = lambda n, s, dt=F32: P.dram(n, s, dt, kind=kind_s)
    xres_d = sc("xres", [D, S]); sz_d = sc("sz", [S, D]); xbc_d = sc("xbc", [2048, S], BF16); cv_d = sc("cv", [D, S])
    lat_d = sc("lat", [768, S]); ys_d = sc("ys", [D, S], BF16); yc_d = sc("yc", [D, S], BF16)
    q_d = sc("qs", [8, 192, S], BF16); k_d = sc("ks", [8, 128, S], BF16); kr_d = sc("krs", [64, S], BF16)
    v_d = sc("vs", [S, D], BF16); o_d = sc("os", [D, S], BF16); fp_d = sc("fpart", [D, S]); rope_d = sc("rope", [128, S])
    hd = {}

    def dh(name, t):
        k = (name, t)
        if k not in hd:
            hd[k] = H()
        return hd[k]

    def dall(name, n=NT):
        return [dh(name, t) for t in range(n)]

    big = P.sb([128, 103000], BF16, name="big")
    banks = [P.ps([128, 512], F32, name=f"bank{i}") for i in range(8)]
    bh = [H() for _ in range(8)]

    def PS(i, n=512, parts=128, dt=F32):
        ap = banks[i][:]
        if dt == BF16:
            ap = ap.bitcast(BF16)
        return V(ap[0:parts, 0:n], bh[i])

    pers = Arena(big, 0, 8000)
    cst = pers.alloc(K_MASK, F32); h_cst = H()
    tri = V(cst[:, K_TRI:K_TRI + 128], h_cst); gt = V(cst[:, K_GT:K_GT + 128], h_cst)
    inv_c = V(cst[:, K_INV:K_INV + 1], h_cst)
    ident = pers.alloc(128, BF16); h_id = H(); identv = V(ident, h_id)
    ones_b = pers.alloc(128, BF16); h_ob = H(); onesb = V(ones_b, h_ob)
    ones_f = pers.alloc(128, F32); h_of = H(); onesf = V(ones_f, h_of)
    masks = pers.alloc(2048, BF16); h_mk = H()
    epsc = pers.alloc(1, F32); h_eps = H(); epsv = V(epsc, h_eps)
    ppt = pers.alloc(NPP, F32); h_pp = H()
    pbt = pers.alloc(NPB, F32); h_pb = H()
    abc = pers.alloc(16, F32); h_abc = H()
    gsc = pers.alloc(8, F32); h_gsc = H()
    ARENA0 = pers.off

    def pcol(c, n=1, parts=128):
        return V(ppt[0:parts, c:c + n], h_pp)

    def barrier():
        toks = {}
        for n in ENGS:
            e = P.eng[n]
            if e.n:
                toks[e.idx] = e.n
            for k in range(min(NDMA_SEMS, e.dma_count)):
                rnd = (e.dma_count - 1 - k) // NDMA_SEMS
                toks[(n, k)] = 16 * (rnd + 1)
        for n in ENGS:
            e = P.eng[n]
            waits = P._waits(e, dict(toks))
            if waits:
                e.ops.append((waits, None, None))

    P.dma("sp", V(cst, h_cst), V(cst_d[:, 0:K_MASK], H()))
    P.dma("pool", V(ident, h_id), V(cst_d[:, K_ID:K_ID + 128], H()))
    P.dma("pool", V(masks, h_mk), V(cst_d[:, K_MASK:K_MASK + 2048], H()))
    P.memset(onesb, 1.0); P.memset(onesf, 1.0); P.memset(epsv, EPS)

    def rope_tables():
        A = Arena(big, ARENA0, 103000)
        for t in range(NT):
            sl = slice(t * T, (t + 1) * T)
            pi_ = A.alloc(T, I32) if t == 0 else rope_tables.bufs[0]
            if t == 0:
                rope_tables.bufs = [pi_] + [A.alloc(T, F32) for _ in range(5)] + [A.alloc(T, I32)]
            pi_, pf, ang, kf, r, rc, ki = rope_tables.bufs
            hs = [H() for _ in range(7)]
            pi_v, pf_v, ang_v, kf_v, r_v, rc_v, ki_v = [V(a[0:64], h) for a, h in zip(rope_tables.bufs, hs)]
            P.dma("sp", pi_v, V(pos_d[0, sl].partition_broadcast(64), H()))
            P.copy(pf_v, pi_v)
            P.ts(ang_v, pf_v, V(cst[0:64, K_INV:K_INV + 1], h_cst), ALU.mult)
            P.ts(kf_v, ang_v, float(1.0 / (2 * np.pi)), ALU.mult)
            P.copy(ki_v, kf_v)
            P.copy(kf_v, ki_v)
            C1 = 6.28125; C2 = float(np.float32(2 * np.pi - 6.28125))
            P.stt(r_v, kf_v, -C1, ang_v, ALU.mult, ALU.add)
            P.stt(r_v, kf_v, -C2, r_v, ALU.mult, ALU.add)
            P.ts(r_v, r_v, 3.1415925, ALU.min, -3.1415925, ALU.max)
            P.ts(rc_v, r_v, float(np.pi / 2), ALU.is_gt, float(-2 * np.pi), ALU.mult)
            P.stt(rc_v, r_v, float(np.pi / 2), rc_v, ALU.add, ALU.add)
            P.ts(rc_v, rc_v, 3.1415925, ALU.min, -3.1415925, ALU.max)
            P.act(rc_v, rc_v, AF.Sin)
            P.act(r_v, r_v, AF.Sin)
            P.ts(V(r[0:32], hs[4]), V(r[0:32], hs[4]), -1.0, ALU.mult)
            P.dma("sp", V(rope_d[0:64, sl], dh("rope", t)), rc_v)
            P.dma("sp", V(rope_d[64:128, sl], dh("rope", t)), r_v)
            barrier()
    rope_tables()
    barrier()

    def wload(dst, src, hdst):
        return P.dma("pool", V(dst, hdst), V(src, H()))

    def fmview(d, t):
        return d.rearrange("(k p) s -> p k s", p=128)[:, :, t * T:(t + 1) * T]

    def xnorm(xt, hx, gcol, outb, hout, sq, hsq, rs, hrs, bank, nk=8, width=T, inv_n=1.0 / D):
        P.act(V(sq, hsq), V(xt, hx), AF.Square)
        for k in range(nk):
            P.mm(PS(bank, width), onesb, V(sq[:, k, :], hsq), start=(k == 0), stop=(k == nk - 1))
        P.act(V(rs, hrs), PS(bank, width), AF.Ln, bias=epsv, scale=inv_n)
        P.act(V(rs, hrs), V(rs, hrs), AF.Exp, scale=-0.5)
        for k in range(nk):
            P.stt(V(outb[:, k, :], hout), V(xt[:, k, :], hx), pcol(gcol + k), V(rs, hrs), ALU.mult, ALU.mult)

    dtraw = pers.alloc(NCH * 16, F32).rearrange("p (c h) -> p c h", c=NCH); h_dtraw = H()
    ARENA0 = pers.off
    final_toks = []

    def phase_A(l):
        xsrc = xT_d if l == 0 else xres_d
        xsn = "xT" if l == 0 else "xres"
        barrier()
        P.dma("sp", V(ppt, h_pp), V(pp_d[l], H()))
        P.dma("sp", V(pbt, h_pb), V(pb_d[l, :].partition_broadcast(128), H()))
        P.act(V(abc, h_abc), V(pbt[:, B_ALOG:B_ALOG + 16], h_pb), AF.Exp)
        P.ts(V(abc, h_abc), V(abc, h_abc), -1.0, ALU.mult)
        P.ts(V(gsc[:, 0:3], h_gsc), V(ppt[:, C_QNN:C_QNN + 3], h_pp), float(192 ** -0.5), ALU.mult)
        P.ts(V(gsc[:, 3:5], h_gsc), V(ppt[:, C_XQG:C_XQG + 2], h_pp), float(256 ** -0.5), ALU.mult)

        A = Arena(big, ARENA0, 103000)
        uT = A.alloc(8 * S, BF16).rearrange("p (k s) -> p k s", k=8); hu = [H() for _ in range(NT)]
        A1 = A.off
        xt = [A.alloc(8 * T, F32).rearrange("p (k s) -> p k s", k=8) for _ in range(2)]; hxt = [H(), H()]
        sq = A.alloc(8 * T, BF16).rearrange("p (k s) -> p k s", k=8); hsq = H()
        rs = A.alloc(T, F32); hrs = H()
        for t in range(NT):
            b = t % 2
            P.dma("sp", V(xt[b], hxt[b]), V(fmview(xsrc, t), dh(xsn, t)))
            xnorm(xt[b], hxt[b], C_MIXG, uT[:, :, t * T:(t + 1) * T], hu[t], sq, hsq, rs, hrs, 0)
        barrier()
        A.off = A1
        wz = A.alloc(8 * 1040, BF16).rearrange("p (k n) -> p k n", k=8); hwz = H()
        wload(wz[:, :, 0:1024], w_in_d[l].rearrange("(k p) n -> p k n", p=128)[:, :, OFF_Z:OFF_Z + 1024], hwz)
        wload(wz[:, :, 1024:1040], w_in_d[l].rearrange("(k p) n -> p k n", p=128)[:, :, OFF_DT:OFF_DT + 16], hwz)
        szb = [A.alloc(1024, F32) for _ in range(2)]; hszb = [H(), H()]
        for c in range(NCH):
            t = c // 4; b = c % 2
            us = lambda k: V(uT[:, k, c * 128:(c + 1) * 128], hu[t])
            for half in range(2):
                for k in range(8):
                    P.mm(PS(1 + half), us(k), V(wz[:, k, half * 512:(half + 1) * 512], hwz), start=(k == 0), stop=(k == 7))
            for k in range(8):
                P.mm(PS(3, 16), us(k), V(wz[:, k, 1024:1040], hwz), start=(k == 0), stop=(k == 7))
            for half in range(2):
                P.act(V(szb[b][:, half * 512:(half + 1) * 512], hszb[b]), PS(1 + half), AF.Silu)
            P.copy(V(dtraw[:, c, :], h_dtraw), PS(3, 16))
            P.dma("pool", V(sz_d[c * 128:(c + 1) * 128, :], dh("sz", t)), V(szb[b], hszb[b]))
        barrier()
        A.off = A1
        wg = [A.alloc(8 * 512, BF16).rearrange("p (k n) -> p k n", k=8) for _ in range(2)]; hwg = [H(), H()]
        pre = [A.alloc(S + 32, F32) for _ in range(2)]; hpre = [H(), H()]
        acc = [A.alloc(S, F32) for _ in range(2)]; hacc = [H(), H()]
        xo = [A.alloc(S, BF16) for _ in range(2)]; hxo = [H(), H()]
        for b in range(2):
            P.memset(V(pre[b][:, 0:32], hpre[b]), 0.0)
        win_v = w_in_d[l].rearrange("(k p) n -> p k n", p=128)
        wload(wg[0], win_v[:, :, OFF_XBC:OFF_XBC + 512], hwg[0])
        pend = None
        for j in range(16):
            g4 = j // 4; wb = g4 % 2; b = j % 2
            if j % 4 == 0 and g4 + 1 < 4:
                wload(wg[1 - wb], win_v[:, :, OFF_XBC + (g4 + 1) * 512:OFF_XBC + (g4 + 2) * 512], hwg[1 - wb])
            for t in range(NT):
                bank = 1 + (t % 2)
                for k in range(8):
                    P.mm(PS(bank), V(wg[wb][:, k, (j % 4) * 128:(j % 4 + 1) * 128], hwg[wb]),
                         V(uT[:, k, t * T:(t + 1) * T], hu[t]), start=(k == 0), stop=(k == 7))
                P.copy(V(pre[b][:, 32 + t * T:32 + (t + 1) * T], hpre[b]), PS(bank), eng="act")
            P.ts(V(acc[b], hacc[b]), V(pre[b][:, 29:29 + S], hpre[b]), pcol(C_SCW + j * 4 + 0), ALU.mult, pcol(C_SCB + j), ALU.add)
            for tap in range(1, 4):
                P.stt(V(acc[b], hacc[b]), V(pre[b][:, 29 + tap:29 + tap + S], hpre[b]), pcol(C_SCW + j * 4 + tap),
                      V(acc[b], hacc[b]), ALU.mult, ALU.add)
            if pend is not None:
                pend()

            def fin(b=b, j=j):
                P.act(V(xo[b], hxo[b]), V(acc[b], hacc[b]), AF.Silu)
                P.dma("pool", V(xbc_d[j * 128:(j + 1) * 128, :], dall("xbc")), V(xo[b], hxo[b]))
            pend = fin
        pend()
        barrier()
        A.off = A1
        wa = [A.alloc(8 * 128, BF16).rearrange("p (k n) -> p k n", k=8) for _ in range(2)]; hwa = [H(), H()]
        wgt = [A.alloc(8 * 128, BF16).rearrange("p (k n) -> p k n", k=8) for _ in range(2)]; hwgt = [H(), H()]
        vb = [A.alloc(S + 32, BF16) for _ in range(2)]; hvb = [[H() for _ in range(NT)] for _ in range(2)]; hvz = [H(), H()]
        dg = [A.alloc(31 * 128, BF16).rearrange("p (j n) -> p j n", j=31) for _ in range(2)]; hdg = [H(), H()]
        acc = [A.alloc(S, F32) for _ in range(2)]; hacc = [H(), H()]
        sg = [A.alloc(T, F32) for _ in range(2)]; hsg = [H(), H()]
        for b in range(2):
            P.memset(V(vb[b][:, 0:32], hvz[b]), 0.0)
        def prepj(j):
            b = j % 2
            wload(wa[b], win_v[:, :, OFF_GA + j * 128:OFF_GA + (j + 1) * 128], hwa[b])
            wload(wgt[b], win_v[:, :, OFF_GG + j * 128:OFF_GG + (j + 1) * 128], hwgt[b])
            for tap in range(31):
                P.ts(V(dg[b][:, tap, :], hdg[b]), identv, pcol(C_CDW + j * 31 + tap), ALU.mult)
        prepj(0)
        for j in range(8):
            b = j % 2
            if j + 1 < 8:
                prepj(j + 1)

            def glu(t):
                sb_ = t % 2
                for k in range(8):
                    P.mm(PS(1 + sb_), V(wa[b][:, k, :], hwa[b]), V(uT[:, k, t * T:(t + 1) * T], hu[t]), start=(k == 0), stop=(k == 7))
                for k in range(8):
                    P.mm(PS(3 + sb_), V(wgt[b][:, k, :], hwgt[b]), V(uT[:, k, t * T:(t + 1) * T], hu[t]), start=(k == 0), stop=(k == 7))
                P.act(V(sg[sb_], hsg[sb_]), PS(3 + sb_), AF.Sigmoid)
                P.tt(V(vb[b][:, 32 + t * T:32 + (t + 1) * T], hvb[b][t]), PS(1 + sb_), V(sg[sb_], hsg[sb_]), ALU.mult)

            def conv(t):
                cbk = 5 + t % 2
                rd_h = [hvb[b][t]] + ([hvb[b][t - 1]] if t > 0 else [hvz[b]])
                for tap in range(31):
                    o0 = 2 + tap + t * T
                    P.mm(PS(cbk), V(dg[b][:, tap, :], hdg[b]), V(vb[b][:, o0:o0 + T], rd_h), start=(tap == 0), stop=(tap == 30))
                P.act(V(acc[b][:, t * T:(t + 1) * T], hacc[b]), PS(cbk), AF.Identity, bias=pcol(C_CDB + j))
            glu(0)
            for t in range(NT):
                if t + 1 < NT:
                    glu(t + 1)
                conv(t)
            P.dma("pool", V(cv_d[j * 128:(j + 1) * 128, :], dall("cv")), V(acc[b], hacc[b]))
        barrier()
        A.off = A1
        wl = A.alloc(8 * 768, BF16).rearrange("p (k n) -> p k n", k=8); hwl = H()
        wload(wl[:, :, 0:704], win_v[:, :, OFF_QL:OFF_QL + 704], hwl)
        wload(wl[:, :, 704:736], win_v[:, :, OFF_KR + 32:OFF_KR + 64], hwl)
        wload(wl[:, :, 736:768], win_v[:, :, OFF_KR:OFF_KR + 32], hwl)
        lo = [A.alloc(T, F32) for _ in range(2)]; hlo = [H(), H()]
        segs = [(0, 128), (128, 128), (256, 128), (384, 128), (512, 128), (640, 64), (704, 64)]
        i = 0
        for t in range(NT):
            for (c0, m) in segs:
                b = i % 2; i += 1
                for k in range(8):
                    P.mm(PS(1 + b, T, m), V(wl[:, k, c0:c0 + m], hwl), V(uT[:, k, t * T:(t + 1) * T], hu[t]), start=(k == 0), stop=(k == 7))
                P.copy(V(lo[b][0:m], hlo[b]), PS(1 + b, T, m), eng="act")
                P.dma("pool", V(lat_d[c0:c0 + m, t * T:(t + 1) * T], dh("lat", t)), V(lo[b][0:m], hlo[b]))
        barrier()
    def phase_S(l):
        A = Arena(big, ARENA0, 103000)
        xbt = [A.alloc(16 * T, BF16).rearrange("p (k s) -> p k s", k=16) for _ in range(2)]; hxb = [H(), H()]
        szt = [A.alloc(1024, F32) for _ in range(2)]; hszt = [H(), H()]
        hst = A.alloc(1024, F32); h_hst = H()
        hbf = A.alloc(1024, BF16); h_hbf = H()
        D2 = lambda n, dt: ([A.alloc(n, dt) for _ in range(2)], [H(), H()])
        sm2, h_sm2 = D2(64, F32)
        ew2, h_ew2 = D2(48, F32)
        Xb2, h_Xb2 = D2(1024, BF16)
        Xw2, h_Xw2 = D2(1024, BF16)
        tD2, h_tD2 = D2(1024, F32)
        Bt2, h_Bt2 = D2(512, BF16)
        MT2, h_MT2 = D2(2048, BF16)
        cbm = A.alloc(512, F32); h_cbm = H()
        AG = A.alloc(2048, F32); h_AG = H()
        dec = [A.alloc(512, F32) for _ in range(2)]; h_dec = [H(), H()]
        yg = A.alloc(1024, F32); h_yg = H()
        t1 = [A.alloc(512, F32) for _ in range(2)]; h_t1 = [H(), H()]
        ssq = A.alloc(8, F32); h_ssq = H()
        junk = A.alloc(256, F32); h_junk = H()
        ynb = A.alloc(1024, BF16); h_ynb = H()
        yst = [A.alloc(8 * T, BF16).rearrange("p (k s) -> p k s", k=8) for _ in range(2)]; h_yst = [H(), H()]
        P.memset(V(hst, h_hst), 0.0); P.memset(V(hbf, h_hbf), 0.0)
        dtb = V(pbt[:, B_DTB:B_DTB + 16], h_pb)
        sng = V(pbt[:, B_SNG:B_SNG + 1024], h_pb)
        xview = xbc_d.rearrange("(k p) s -> p k s", p=128)
        SMALL = lambda lo, n: V(banks[2][:, 256 + lo:256 + lo + n], bh[2])

        def bc16(ap, lo, n):
            return ap[:, lo:lo + n].unsqueeze(2).broadcast_to([128, n, 64])
        v3 = lambda ap: ap.rearrange("p (h d) -> p h d", h=16)

        def stage1(c):
            t = c // 4; s_ = c % 4; tb = t % 2; cb_ = c % 2
            cs = slice(s_ * 128, (s_ + 1) * 128)
            if s_ == 0:
                P.dma("sp", V(xbt[tb], hxb[tb]), V(xview[:, :, t * T:(t + 1) * T], dall("xbc")))
            P.dma("sp", V(szt[cb_], hszt[cb_]), V(sz_d[c * 128:(c + 1) * 128, :], dh("sz", t)))
            xb_ = xbt[tb]; hx_ = hxb[tb]
            sm, h_sm, ew, h_ew = sm2[cb_], h_sm2[cb_], ew2[cb_], h_ew2[cb_]
            Xb, h_Xb, Xw, h_Xw, tD, h_tD = Xb2[cb_], h_Xb2[cb_], Xw2[cb_], h_Xw2[cb_], tD2[cb_], h_tD2[cb_]
            Btm, h_Btm, MT, h_MT = Bt2[cb_], h_Bt2[cb_], MT2[cb_], h_MT2[cb_]
            for k in range(8):
                P.transpose(V(banks[1][:].bitcast(BF16)[:, k * 128:(k + 1) * 128], bh[1]), V(xb_[:, k, cs], hx_), identv)
            for g in range(4):
                P.transpose(V(banks[2][:].bitcast(BF16)[:, g * 128:(g + 1) * 128], bh[2]), V(xb_[:, 8 + g, cs], hx_), identv)
            for g in range(4):
                P.mm(V(banks[3][:, g * 128:(g + 1) * 128], bh[3]), V(xb_[:, 8 + g, cs], hx_), V(xb_[:, 12 + g, cs], hx_))
            P.tt(V(sm[:, 0:16], h_sm), V(dtraw[:, c, :], h_dtraw), dtb, ALU.add)
            P.act(V(sm[:, 0:16], h_sm), V(sm[:, 0:16], h_sm), AF.Exp)
            P.act(V(sm[:, 0:16], h_sm), V(sm[:, 0:16], h_sm), AF.Ln, bias=1.0)
            P.tt(V(sm[:, 16:32], h_sm), V(sm[:, 0:16], h_sm), V(abc, h_abc), ALU.mult)
            av = V(sm[:, 16:32], h_sm)
            P.mm(SMALL(0, 16), tri, av); P.mm(SMALL(16, 16), gt, av); P.mm(SMALL(32, 16), onesf, av)
            P.act(V(ew, h_ew), SMALL(0, 48), AF.Exp)
            P.tt(V(sm[:, 32:48], h_sm), V(sm[:, 0:16], h_sm), V(ew[:, 16:32], h_ew), ALU.mult)
            P.tt(V(AG.rearrange("p (h s) -> p h s", h=16), h_AG),
                 V(cst[:, K_GT:K_GT + 128].unsqueeze(1).broadcast_to([128, 16, 128]), h_cst),
                 V(sm[:, 16:32].unsqueeze(2).broadcast_to([128, 16, 128]), h_sm), ALU.mult)
            P.tt(V(cbm.rearrange("p (g l) -> p g l", g=4), h_cbm), V(banks[3][:].rearrange("p (g l) -> p g l", g=4), bh[3]),
                 V(cst[:, K_TRI:K_TRI + 128].unsqueeze(1).broadcast_to([128, 4, 128]), h_cst), ALU.mult)
            xsT = banks[1][:].bitcast(BF16)[:, 0:1024].rearrange("p (h d) -> p h d", h=16)
            P.tt(V(v3(Xb), h_Xb), V(xsT, bh[1]), V(bc16(sm, 0, 16), h_sm), ALU.mult)
            P.tt(V(v3(Xw), h_Xw), V(xsT, bh[1]), V(bc16(sm, 32, 16), h_sm), ALU.mult)
            P.tt(V(v3(tD), h_tD), V(xsT, bh[1]), V(bc16(pbt, B_DSK, 16), h_pb), ALU.mult)
            P.copy(V(Btm, h_Btm), V(banks[2][:].bitcast(BF16)[:, 0:512], bh[2]), eng="act")
            for g in range(4):
                db = g % 2
                for r in range(4):
                    hd_ = g * 4 + r
                    P.mm(V(banks[4][:, r * 128:(r + 1) * 128], bh[4]), V(AG[:, hd_ * 128:(hd_ + 1) * 128], h_AG), tri)
                P.act(V(dec[db], h_dec[db]), PS(4), AF.Exp)
                P.tt(V(MT[:, g * 512:(g + 1) * 512].rearrange("p (r l) -> p r l", r=4), h_MT), V(dec[db].rearrange("p (r l) -> p r l", r=4), h_dec[db]),
                     V(cbm[:, g * 128:(g + 1) * 128].unsqueeze(1).broadcast_to([128, 4, 128]), h_cbm), ALU.mult)

        def stage2(c):
            t = c // 4; s_ = c % 4; tb = t % 2; cb_ = c % 2
            cs = slice(s_ * 128, (s_ + 1) * 128)
            xb_ = xbt[tb]; hx_ = hxb[tb]
            ew, h_ew = ew2[cb_], h_ew2[cb_]
            Xb, h_Xb, Xw, h_Xw, tD, h_tD = Xb2[cb_], h_Xb2[cb_], Xw2[cb_], h_Xw2[cb_], tD2[cb_], h_tD2[cb_]
            Btm, h_Btm, MT, h_MT = Bt2[cb_], h_Bt2[cb_], MT2[cb_], h_MT2[cb_]
            for hd_ in range(16):
                ybank = 5 + hd_ // 8
                col = (hd_ % 8) * 64
                P.mm(V(banks[ybank][:, col:col + 64], bh[ybank]), V(MT[:, hd_ * 128:(hd_ + 1) * 128], h_MT),
                     V(Xb[:, hd_ * 64:(hd_ + 1) * 64], h_Xb))
            for hf in range(2):
                ybank = 5 + hf
                for gg in range(2):
                    g = hf * 2 + gg
                    P.mm(V(banks[7][:, gg * 256:(gg + 1) * 256], bh[7]), V(xb_[:, 12 + g, cs], hx_), V(hbf[:, g * 256:(g + 1) * 256], h_hbf))
                t1_ = t1[hf]; ht1 = h_t1[hf]
                P.tt(V(t1_.rearrange("p (h d) -> p h d", h=8), ht1), V(banks[7][:].rearrange("p (h d) -> p h d", h=8), bh[7]), V(bc16(ew, hf * 8, 8), h_ew), ALU.mult)
                P.tt(V(t1_, ht1), V(t1_, ht1), PS(ybank), ALU.add)
                P.tt(V(t1_, ht1), V(t1_, ht1), V(tD[:, hf * 512:(hf + 1) * 512], h_tD), ALU.add)
                P.tt(V(yg[:, hf * 512:(hf + 1) * 512], h_yg), V(t1_, ht1), V(szt[cb_][:, hf * 512:(hf + 1) * 512], hszt[cb_]), ALU.mult)
            for hf in range(2):
                for gg in range(2):
                    g = hf * 2 + gg
                    P.mm(V(banks[7][:, gg * 256:(gg + 1) * 256], bh[7]), V(Btm[:, g * 128:(g + 1) * 128], h_Btm), V(Xw[:, g * 256:(g + 1) * 256], h_Xw))
                hv = V(hst[:, hf * 512:(hf + 1) * 512].rearrange("p (h d) -> p h d", h=8), h_hst)
                P.tt(hv, hv, V(bc16(ew, 32 + hf * 8, 8), h_ew), ALU.mult)
                P.tt(V(hst[:, hf * 512:(hf + 1) * 512], h_hst), V(hst[:, hf * 512:(hf + 1) * 512], h_hst), PS(7), ALU.add)
            P.copy(V(hbf, h_hbf), V(hst, h_hst), eng="act")
            for g in range(4):
                P.act(V(junk, h_junk), V(yg[:, g * 256:(g + 1) * 256], h_yg), AF.Square, accum=V(ssq[:, g:g + 1], h_ssq))
            P.act(V(ssq[:, 4:8], h_ssq), V(ssq[:, 0:4], h_ssq), AF.Ln, bias=epsv, scale=1.0 / 256)
            P.act(V(ssq[:, 4:8], h_ssq), V(ssq[:, 4:8], h_ssq), AF.Exp, scale=-0.5)
            P.tt(V(yg.rearrange("p (g d) -> p g d", g=4), h_yg), V(yg.rearrange("p (g d) -> p g d", g=4), h_yg),
                 V(ssq[:, 4:8].unsqueeze(2).broadcast_to([128, 4, 256]), h_ssq), ALU.mult)
            P.tt(V(ynb, h_ynb), V(yg, h_yg), sng, ALU.mult)
            for k in range(8):
                P.transpose(V(banks[0][:].bitcast(BF16)[:, k * 128:(k + 1) * 128], bh[0]), V(ynb[:, k * 128:(k + 1) * 128], h_ynb), identv)
            P.copy(V(yst[tb][:, :, cs], h_yst[tb]), V(banks[0][:].bitcast(BF16)[:, 0:1024].rearrange("p (k s) -> p k s", k=8), bh[0]), eng="act")
            if s_ == 3:
                P.dma("pool", V(ys_d.rearrange("(k p) s -> p k s", p=128)[:, :, t * T:(t + 1) * T], dh("ys", t)), V(yst[tb], h_yst[tb]))

        stage1(0)
        for c in range(NCH):
            la = P.capture()
            if c + 1 < NCH:
                stage1(c + 1)
            lb = P.capture()
            stage2(c)
            P.interleave(la, lb)
        barrier()

    def phase_C(l, base=None, bk=(0, 1), do_barrier=True):
        A = Arena(big, ARENA0 if base is None else base, 103000)
        ct = [A.alloc(8 * T, F32).rearrange("p (k s) -> p k s", k=8) for _ in range(2)]; hct = [H(), H()]
        cb16 = A.alloc(8 * T, BF16).rearrange("p (k s) -> p k s", k=8); hcb = H()
        sq = A.alloc(8 * T, BF16).rearrange("p (k s) -> p k s", k=8); hsq = H()
        mean = A.alloc(T, F32); hmean = H()
        var = A.alloc(T, F32); hvar = H()
        tmp = A.alloc(T, F32); htmp = H()
        yo = [A.alloc(8 * T, BF16).rearrange("p (k s) -> p k s", k=8) for _ in range(2)]; hyo = [H(), H()]
        for t in range(NT):
            b = t % 2
            P.dma("sp", V(ct[b], hct[b]), V(fmview(cv_d, t), dall("cv")))
            P.copy(V(cb16, hcb), V(ct[b], hct[b]), eng="dve")
            P.tt(V(sq, hsq), V(ct[b], hct[b]), V(ct[b], hct[b]), ALU.mult, eng="pool")
            for k in range(8):
                P.mm(PS(bk[0]), onesb, V(cb16[:, k, :], hcb), start=(k == 0), stop=(k == 7))
            for k in range(8):
                P.mm(PS(bk[1]), onesb, V(sq[:, k, :], hsq), start=(k == 0), stop=(k == 7))
            P.ts(V(mean, hmean), PS(bk[0]), 1.0 / D, ALU.mult)
            P.tt(V(tmp, htmp), V(mean, hmean), V(mean, hmean), ALU.mult)
            P.stt(V(var, hvar), PS(bk[1]), 1.0 / D, V(tmp, htmp), ALU.mult, ALU.subtract)
            P.act(V(var, hvar), V(var, hvar), AF.Ln, bias=epsv)
            P.act(V(var, hvar), V(var, hvar), AF.Exp, scale=-0.5)
            for k in range(8):
                P.tt(V(tmp, htmp), V(ct[b][:, k, :], hct[b]), V(mean, hmean), ALU.subtract)
                P.tt(V(tmp, htmp), V(tmp, htmp), V(var, hvar), ALU.mult)
                P.act(V(yo[b][:, k, :], hyo[b]), V(tmp, htmp), AF.Silu, bias=pcol(C_LNB + k), scale=pcol(C_LNG + k))
            P.dma("pool", V(fmview(yc_d, t), dh("yc", t)), V(yo[b], hyo[b]))
        if do_barrier:
            barrier()
    def phase_M1(l):
        A = Arena(big, ARENA0, 103000)
        wq = A.alloc(3 * 1536, BF16).rearrange("p (k n) -> p k n", k=3); hwq = H()
        wqs = A.alloc(3 * 512, BF16).rearrange("p (k n) -> p k n", k=3); hwqs = H()
        wkn = A.alloc(2 * 1024, BF16).rearrange("p (k n) -> p k n", k=2); hwkn = H()
        wv = A.alloc(2 * 1024, BF16).rearrange("p (k n) -> p k n", k=2); hwv = H()
        wload(wq, wqb_d[l].rearrange("(k p) n -> p k n", p=128), hwq)
        qv = wqb_d[l].rearrange("(k p) (h c) -> p k h c", p=128, h=8)
        wqs4 = wqs.rearrange("p k (h c) -> p k h c", h=8)
        for k in range(3):
            wload(wqs4[:, k, :, 0:32], qv[:, k, :, 160:192], hwqs)
            wload(wqs4[:, k, :, 32:64], qv[:, k, :, 128:160], hwqs)
        kv4 = wkvb_d[l].rearrange("(k p) (h c) -> p k h c", p=128, h=8)
        for k in range(2):
            wload(wkn.rearrange("p k (h c) -> p k h c", h=8)[:, k], kv4[:, k, :, 0:128], hwkn)
            wload(wv.rearrange("p k (h c) -> p k h c", h=8)[:, k], kv4[:, k, :, 128:256], hwv)
        lt = [A.alloc(6 * T, F32).rearrange("p (k s) -> p k s", k=6) for _ in range(2)]; hlt = [H(), H()]
        rp = [A.alloc(T, F32) for _ in range(2)]; hrp = [H(), H()]
        rp2 = [A.alloc(T, F32) for _ in range(2)]; hrp2 = [H(), H()]
        sq = A.alloc(3 * T, BF16).rearrange("p (k s) -> p k s", k=3); hsq = H()
        rs = A.alloc(T, F32); hrs = H()
        qn = A.alloc(3 * T, BF16).rearrange("p (k s) -> p k s", k=3); hqn = H()
        kvn = A.alloc(2 * T, BF16).rearrange("p (k s) -> p k s", k=2); hkvn = H()
        sqh = [A.alloc(T, BF16) for _ in range(3)]; hsqh = [H() for _ in range(3)]
        rsh = [A.alloc(T, F32) for _ in range(3)]; hrsh = [H() for _ in range(3)]
        krb = A.alloc(T, F32); hkrb = H()
        ob = [A.alloc(T, BF16) for _ in range(6)]; hob = [H() for _ in range(6)]
        ta = A.alloc(T, F32); hta = H()
        tb_ = A.alloc(T, F32); htb = H()
        vb = [A.alloc(1024, BF16) for _ in range(2)]; hvb = [H(), H()]
        latv = lat_d.rearrange("(k p) s -> p k s", p=128)
        oi = 0
        for t in range(NT):
            b = t % 2; sl = slice(t * T, (t + 1) * T)
            P.dma("sp", V(lt[b], hlt[b]), V(latv[:, :, sl], dh("lat", t)))
            P.dma("sp", V(lt[b][0:64, 5, :], hlt[b]), V(lat_d[704:768, sl], dh("lat", t)))
            P.dma("sp", V(rp[b][0:64], hrp[b]), V(rope_d[0:64, sl], dh("rope", t)))
            P.dma("sp", V(rp2[b][0:64], hrp2[b]), V(rope_d[64:128, sl], dh("rope", t)))
            L_ = lt[b]; hL = hlt[b]
            cosv = V(rp[b][0:64], hrp[b]); sinv = V(rp2[b][0:64], hrp2[b])
            xnorm(L_[:, 0:3, :], hL, C_QAG, qn, hqn, sq, hsq, rs, hrs, 0, nk=3, inv_n=1.0 / 384)
            xnorm(L_[:, 3:5, :], hL, C_KVAG, kvn, hkvn, sq[:, 0:2, :], hsq, rs, hrs, 0, nk=2, inv_n=1.0 / 256)

            for h in range(8):
                for k in range(3):
                    P.mm(PS(1), V(wq[:, k, h * 192:h * 192 + 128], hwq), V(qn[:, k, :], hqn), start=(k == 0), stop=(k == 2))
                for k in range(3):
                    P.mm(PS(2, T, 64), V(wq[:, k, h * 192 + 128:h * 192 + 192], hwq), V(qn[:, k, :], hqn), start=(k == 0), stop=(k == 2))
                for k in range(3):
                    P.mm(PS(3, T, 64), V(wqs[:, k, h * 64:(h + 1) * 64], hwqs), V(qn[:, k, :], hqn), start=(k == 0), stop=(k == 2))
                for k in range(2):
                    P.mm(PS(4), V(wkn[:, k, h * 128:(h + 1) * 128], hwkn), V(kvn[:, k, :], hkvn), start=(k == 0), stop=(k == 1))
                specs = [(1, 128, 1.0 / 128, 5), (2, 64, 1.0 / 64, 6), (4, 128, 1.0 / 128, 7)]
                for i, (bank, parts, inv_n, nb) in enumerate(specs):
                    P.act(V(sqh[i][0:parts], hsqh[i]), PS(bank, T, parts), AF.Square)
                for i, (bank, parts, inv_n, nb) in enumerate(specs):
                    P.mm(PS(nb, T, parts), V(ones_b[0:parts, 0:parts], h_ob), V(sqh[i][0:parts], hsqh[i]))
                for i, (bank, parts, inv_n, nb) in enumerate(specs):
                    P.act(V(rsh[i][0:parts], hrsh[i]), PS(nb, T, parts), AF.Ln, bias=V(epsc[0:parts], h_eps), scale=inv_n)
                    P.act(V(rsh[i][0:parts], hrsh[i]), V(rsh[i][0:parts], hrsh[i]), AF.Exp, scale=-0.5)
                o1 = oi % 6; o2 = (oi + 1) % 6; o3 = (oi + 2) % 6; oi += 3
                P.stt(V(ob[o1], hob[o1]), PS(1), V(gsc[:, 0:1], h_gsc), V(rsh[0], hrsh[0]), ALU.mult, ALU.mult)
                P.stt(V(ob[o3], hob[o3]), PS(4), pcol(C_KNN), V(rsh[2], hrsh[2]), ALU.mult, ALU.mult)
                P.stt(V(ta[0:64], hta), PS(2, T, 64), V(gsc[0:64, 1:2], h_gsc), cosv, ALU.mult, ALU.mult)
                P.stt(V(tb_[0:64], htb), PS(3, T, 64), V(gsc[0:64, 2:3], h_gsc), sinv, ALU.mult, ALU.mult)
                P.tt(V(ta[0:64], hta), V(ta[0:64], hta), V(tb_[0:64], htb), ALU.add)
                P.tt(V(ob[o2][0:64], hob[o2]), V(ta[0:64], hta), V(rsh[1][0:64], hrsh[1]), ALU.mult)
                P.dma("pool", V(q_d[h, 0:128, sl], dh("q", t)), V(ob[o1], hob[o1]))
                P.dma("pool", V(q_d[h, 128:192, sl], dh("q", t)), V(ob[o2][0:64], hob[o2]))
                P.dma("pool", V(k_d[h, :, sl], dh("k", t)), V(ob[o3], hob[o3]))
            P.dma("sp", V(krb[0:64], hkrb), V(lat_d[640:704, sl], dh("lat", t)))
            krv = V(krb[0:64], hkrb); krsv = V(L_[0:64, 5, :], hL)
            P.act(V(sqh[1][0:64], hsqh[1]), krv, AF.Square)
            P.mm(PS(7, T, 64), V(ones_b[0:64, 0:64], h_ob), V(sqh[1][0:64], hsqh[1]))
            P.act(V(rsh[1][0:64], hrsh[1]), PS(7, T, 64), AF.Ln, bias=V(epsc[0:64], h_eps), scale=1.0 / 64)
            P.act(V(rsh[1][0:64], hrsh[1]), V(rsh[1][0:64], hrsh[1]), AF.Exp, scale=-0.5)
            P.stt(V(ta[0:64], hta), krv, pcol(C_KNR, 1, 64), cosv, ALU.mult, ALU.mult)
            P.stt(V(tb_[0:64], htb), krsv, pcol(C_KNRS, 1, 64), sinv, ALU.mult, ALU.mult)
            P.tt(V(ta[0:64], hta), V(ta[0:64], hta), V(tb_[0:64], htb), ALU.add)
            o = oi % 6; oi += 1
            P.tt(V(ob[o][0:64], hob[o]), V(ta[0:64], hta), V(rsh[1][0:64], hrsh[1]), ALU.mult)
            P.dma("pool", V(kr_d[:, sl], dh("kr", t)), V(ob[o][0:64], hob[o]))
            for s_ in range(4):
                vb_ = s_ % 2
                for half in range(2):
                    for k in range(2):
                        P.mm(PS(2 + half), V(kvn[:, k, s_ * 128:(s_ + 1) * 128], hkvn), V(wv[:, k, half * 512:(half + 1) * 512], hwv),
                             start=(k == 0), stop=(k == 1))
                    P.copy(V(vb[vb_][:, half * 512:(half + 1) * 512], hvb[vb_]), PS(2 + half), eng="act")
                r0 = t * T + s_ * 128
                P.dma("pool", V(v_d[r0:r0 + 128, :], dh("v", t)), V(vb[vb_], hvb[vb_]))
        barrier()

    def phase_M2(l, do_barrier=True):
        A = Arena(big, ARENA0, 103000)
        krt = A.alloc(S, BF16); hkr = H()
        P.memset(V(krt[64:128], hkr), 0.0)
        P.dma("sp", V(krt[0:64], hkr), V(kr_d, dall("kr")))
        kt_ = [A.alloc(S, BF16) for _ in range(2)]; hkt = [H(), H()]
        qnt = [A.alloc(S, BF16) for _ in range(2)]; hqn = [H(), H()]
        qrt = [A.alloc(S, BF16) for _ in range(2)]; hqr = [H(), H()]
        for b_ in range(2):
            P.memset(V(qrt[b_][64:128], hqr[b_]), 0.0)
        vt = [A.alloc(NCH * 128, BF16).rearrange("p (c d) -> p c d", c=NCH) for _ in range(2)]; hvt = [H(), H()]
        pt = [A.alloc(T, BF16) for _ in range(4)]; hpt = [H() for _ in range(4)]
        pd = [A.alloc(T, BF16) for _ in range(4)]; hpd = [H() for _ in range(4)]
        for j in range(4):
            P.memset(V(pd[j], hpd[j]), 0.0)
        dacc = [A.alloc(T, F32) for _ in range(2)]; hdacc = [H(), H()]
        rd = A.alloc(T, F32); hrd = H()
        dhi = A.alloc(T, BF16); hdhi = H()
        dlo = A.alloc(T, BF16); hdlo = H()
        oo = [A.alloc(T, BF16) for _ in range(2)]; hoo = [H(), H()]
        mk = masks.rearrange("p (j q) -> p j q", j=4)
        pi = 0
        items = [(h, qt) for h in range(8) for qt in range(NT)]

        def load_head(h):
            b = h % 2
            P.dma("sp", V(kt_[b], hkt[b]), V(k_d[h], dall("k")))
            P.dma("sp", V(qnt[b], hqn[b]), V(q_d[h, 0:128, :], dall("q")))
            P.dma("sp", V(qrt[b][0:64], hqr[b]), V(q_d[h, 128:192, :], dall("q")))
            P.dma("sp", V(vt[b], hvt[b]), V(v_d.rearrange("(c p) d -> p c d", p=128)[:, :, h * 128:(h + 1) * 128], dall("v")))

        def st(h_, qt_, kt):
            b_ = h_ % 2
            sbk = kt % 2
            ks = slice(kt * 128, (kt + 1) * 128)
            qs_ = slice(qt_ * T, (qt_ + 1) * T)
            P.mm(PS(sbk), V(kt_[b_][:, ks], hkt[b_]), V(qnt[b_][:, qs_], hqn[b_]), start=True, stop=False)
            P.mm(PS(sbk), V(krt[:, ks], hkr), V(qrt[b_][:, qs_], hqr[b_]), start=False, stop=True)
        load_head(0)
        pre = False
        for idx, (h, qt) in enumerate(items):
            b = h % 2
            if qt == 0 and h + 1 < 8:
                load_head(h + 1)
            qs = slice(qt * T, (qt + 1) * T)
            nk = 4 * qt + 4
            ob_ = 2 + (qt % 2)
            da = qt % 2
            if not pre:
                st(h, qt, 0)
            pre = False
            for kt in range(nk):
                if kt + 1 < nk:
                    st(h, qt, kt + 1)
                elif idx + 1 < len(items):
                    st(items[idx + 1][0], items[idx + 1][1], 0)
                    pre = True
                if kt >= 4 * qt:
                    j = kt - 4 * qt
                    pv_ = V(pd[j], hpd[j])
                    bk = banks[kt % 2]
                    P.act(V(pd[j][0:64, 128 * j:T], hpd[j]), V(bk[0:64, 128 * j:T], bh[kt % 2]), AF.Exp)
                    P.act(V(pd[j][64:128, 128 * j + 64:T], hpd[j]), V(bk[64:128, 128 * j + 64:T], bh[kt % 2]), AF.Exp)
                else:
                    p_ = pi % 4; pi += 1
                    pv_ = V(pt[p_], hpt[p_])
                    P.act(pv_, PS(kt % 2), AF.Exp)
                P.mm(PS(ob_), V(vt[b][:, kt, :], hvt[b]), pv_, start=(kt == 0), stop=(kt == nk - 1))
                if kt % 2 == 1:
                    P.mm(PS(4 + da), onesb, pv_, start=(kt == 1), stop=False)
                elif kt == 0:
                    P.copy(V(dacc[da], hdacc[da]), pv_)
                else:
                    P.tt(V(dacc[da], hdacc[da]), V(dacc[da], hdacc[da]), pv_, ALU.add)
            P.copy(V(dhi, hdhi), V(dacc[da], hdacc[da]))
            P.tt(V(dlo, hdlo), V(dacc[da], hdacc[da]), V(dhi, hdhi), ALU.subtract)
            P.mm(PS(4 + da), onesb, V(dhi, hdhi), start=False, stop=False)
            P.mm(PS(4 + da), onesb, V(dlo, hdlo), start=False, stop=True)
            P.act(V(rd, hrd), PS(4 + da), AF.Ln)
            P.act(V(rd, hrd), V(rd, hrd), AF.Exp, scale=-1.0)
            o = qt % 2
            P.tt(V(oo[o], hoo[o]), PS(ob_), V(rd, hrd), ALU.mult)
            P.dma("pool", V(o_d[h * 128:(h + 1) * 128, qs], dh("o", qt)), V(oo[o], hoo[o]))
        if do_barrier:
            barrier()
        return A.off
    def phase_G(l):
        xsrc = xT_d if l == 0 else xres_d
        xsn = "xT" if l == 0 else "xres"
        A = Arena(big, ARENA0, 103000)
        wgate = A.alloc(8 * 3072, BF16).rearrange("p (k n) -> p k n", k=8); hwg = H()
        wbr = [A.alloc(8 * 1024, BF16).rearrange("p (k n) -> p k n", k=8) for _ in range(3)]; hwb = [H() for _ in range(3)]
        wo = A.alloc(8 * 1024, BF16).rearrange("p (k n) -> p k n", k=8); hwo = H()
        kp = lambda d: d.rearrange("(k p) n -> p k n", p=128)
        for b3 in range(3):
            wload(wgate[:, :, b3 * 1024:(b3 + 1) * 1024], kp(w_in_d[l])[:, :, OFF_GATE + b3 * 1024:OFF_GATE + (b3 + 1) * 1024], hwg)
        for b3, wd in enumerate((ssd_wo_d, conv_wo_d, mla_wo_d)):
            wload(wbr[b3], kp(wd[l]), hwb[b3])
        wload(wo, kp(wout_d[l]), hwo)
        TG = 256; NTG = S // TG
        fmg = lambda d, t: d.rearrange("(k p) s -> p k s", p=128)[:, :, t * TG:(t + 1) * TG]
        xt3 = [A.alloc(8 * TG, F32).rearrange("p (k s) -> p k s", k=8) for _ in range(3)]; hxt3 = [H(), H(), H()]
        ut2 = [A.alloc(8 * TG, BF16).rearrange("p (k s) -> p k s", k=8) for _ in range(2)]; hut2 = [H(), H()]
        sq = A.alloc(8 * TG, BF16).rearrange("p (k s) -> p k s", k=8); hsq = H()
        rs = A.alloc(TG, F32); hrs = H()
        br2 = [[A.alloc(8 * TG, BF16).rearrange("p (k s) -> p k s", k=8) for _ in range(3)] for _ in range(2)]
        hbr2 = [[H() for _ in range(3)] for _ in range(2)]
        mg = A.alloc(8 * TG, BF16).rearrange("p (k s) -> p k s", k=8); hmg = H()
        gt_ = [A.alloc(TG, F32) for _ in range(2)]; hgt = [H(), H()]
        macc = A.alloc(TG, F32); hmacc = H()
        srcs = [(ys_d, "ys"), (yc_d, "yc"), (o_d, "o")]

        def prep(t):
            b = t % 2; b3x = t % 3
            P.dma("sp", V(xt3[b3x], hxt3[b3x]), V(fmg(xsrc, t), dh(xsn, t // 2)))
            for b3, (d_, nm) in enumerate(srcs):
                P.dma("sp", V(br2[b][b3], hbr2[b][b3]), V(fmg(d_, t), dall(nm)))
            xnorm(xt3[b3x], hxt3[b3x], C_MIXG, ut2[b], hut2[b], sq, hsq, rs, hrs, 0, width=TG)
        prep(0)
        for t in range(NTG):
            b = t % 2; b3x = t % 3
            xt = xt3[b3x]; hxt = hxt3[b3x]; ut = ut2[b]; hut = hut2[b]; br = br2[b]; hbr = hbr2[b]
            if t + 1 < NTG:
                prep(t + 1)
            gi = 0
            for m in range(8):
                for b3 in range(3):
                    gb = 1 + gi % 2; yb = 3 + gi % 2; g2 = gi % 2; gi += 1
                    for k in range(8):
                        P.mm(PS(gb, TG), V(wgate[:, k, b3 * 1024 + m * 128:b3 * 1024 + (m + 1) * 128], hwg), V(ut[:, k, :], hut), start=(k == 0), stop=(k == 7))
                    for k in range(8):
                        P.mm(PS(yb, TG), V(wbr[b3][:, k, m * 128:(m + 1) * 128], hwb[b3]), V(br[b3][:, k, :], hbr[b3]), start=(k == 0), stop=(k == 7))
                    P.act(V(gt_[g2], hgt[g2]), PS(gb, TG), AF.Sigmoid, bias=pcol(C_GATEB + b3 * 8 + m))
                    if b3 == 0:
                        P.tt(V(macc, hmacc), V(gt_[g2], hgt[g2]), PS(yb, TG), ALU.mult)
                    else:
                        P.tt(V(gt_[g2], hgt[g2]), V(gt_[g2], hgt[g2]), PS(yb, TG), ALU.mult)
                        if b3 == 1:
                            P.tt(V(macc, hmacc), V(macc, hmacc), V(gt_[g2], hgt[g2]), ALU.add)
                        else:
                            P.tt(V(mg[:, m, :], hmg), V(macc, hmacc), V(gt_[g2], hgt[g2]), ALU.add)
            for n in range(8):
                ob_ = 5 + n % 2
                for m in range(8):
                    P.mm(PS(ob_, TG), V(wo[:, m, n * 128:(n + 1) * 128], hwo), V(mg[:, m, :], hmg), start=(m == 0), stop=(m == 7))
                P.tt(V(xt[:, n, :], hxt), V(xt[:, n, :], hxt), PS(ob_, TG), ALU.add)
            P.dma("pool", V(fmg(xres_d, t), dh("xres", t // 2)), V(xt, hxt))
        barrier()

    def phase_X(l):
        A = Arena(big, ARENA0, 103000)
        kp = lambda d: d.rearrange("(k p) n -> p k n", p=128)
        wkv = A.alloc(8 * 2048, BF16).rearrange("p (k n) -> p k n", k=8); hwkv = H()
        wload(wkv[:, :, 0:1024], kp(xwkv_d[l])[:, :, 0:1024], hwkv)
        wload(wkv[:, :, 1024:2048], kp(xwkv_d[l])[:, :, 1024:2048], hwkv)
        mt = A.alloc(8 * 256, F32).rearrange("p (k s) -> p k s", k=8); hmt = H()
        mn = A.alloc(8 * 256, BF16).rearrange("p (k s) -> p k s", k=8); hmn = H()
        sqm = A.alloc(8 * 256, BF16).rearrange("p (k s) -> p k s", k=8); hsqm = H()
        rsm = A.alloc(256, F32); hrsm = H()
        KX = A.alloc(8 * 256, BF16).rearrange("p (c s) -> p c s", c=8); hKX = H()
        VX = A.alloc(2 * 1024, BF16).rearrange("p (m d) -> p m d", m=2); hVX = H()
        sq2 = A.alloc(2 * 256, BF16).rearrange("p (c s) -> p c s", c=2); hsq2 = H()
        P.dma("sp", V(mt, hmt), V(memT_d.rearrange("(k p) s -> p k s", p=128), H()))
        xnorm(mt, hmt, C_MEMG, mn, hmn, sqm, hsqm, rsm, hrsm, 0, nk=8, width=256)
        for hx in range(4):
            for c in range(2):
                for k in range(8):
                    P.mm(PS(1 + c, 256), V(wkv[:, k, hx * 256 + c * 128:hx * 256 + (c + 1) * 128], hwkv), V(mn[:, k, :], hmn), start=(k == 0), stop=(k == 7))
                P.act(V(sq2[:, c, :], hsq2), PS(1 + c, 256), AF.Square)
            for c in range(2):
                P.mm(PS(3, 256), onesb, V(sq2[:, c, :], hsq2), start=(c == 0), stop=(c == 1))
            P.act(V(rsm, hrsm), PS(3, 256), AF.Ln, bias=epsv, scale=1.0 / 256)
            P.act(V(rsm, hrsm), V(rsm, hrsm), AF.Exp, scale=-0.5)
            for c in range(2):
                P.stt(V(KX[:, hx * 2 + c, :], hKX), PS(1 + c, 256), pcol(C_XKG + c), V(rsm, hrsm), ALU.mult, ALU.mult)
        for m in range(2):
            for half in range(2):
                for k in range(8):
                    P.mm(PS(4 + half), V(mn[:, k, m * 128:(m + 1) * 128], hmn), V(wkv[:, k, 1024 + half * 512:1024 + (half + 1) * 512], hwkv), start=(k == 0), stop=(k == 7))
                P.copy(V(VX[:, m, half * 512:(half + 1) * 512], hVX), PS(4 + half), eng="act")
        barrier()
        A2 = Arena(big, ARENA0, 103000)
        KX2 = A2.alloc(8 * 256, BF16).rearrange("p (c s) -> p c s", c=8); hKX2 = H()
        VX2 = A2.alloc(2 * 1024, BF16).rearrange("p (m d) -> p m d", m=2); hVX2 = H()
        P.copy(V(KX2, hKX2), V(KX, hKX)); P.copy(V(VX2, hVX2), V(VX, hVX))
        barrier()
        A = A2
        wq_ = A.alloc(8 * 1024, BF16).rearrange("p (k n) -> p k n", k=8); hwq = H()
        wo_ = A.alloc(8 * 1024, BF16).rearrange("p (k n) -> p k n", k=8); hwo = H()
        wload(wq_, kp(xwq_d[l]), hwq); wload(wo_, kp(xwo_d[l]), hwo)
        xt = [A.alloc(8 * T, F32).rearrange("p (k s) -> p k s", k=8) for _ in range(3)]; hxt = [H(), H(), H()]
        ht2 = [A.alloc(8 * T, BF16).rearrange("p (k s) -> p k s", k=8) for _ in range(2)]; hht2 = [H(), H()]
        sq = A.alloc(8 * T, BF16).rearrange("p (k s) -> p k s", k=8); hsq = H()
        rs = A.alloc(T, F32); hrs = H()
        sqq = A.alloc(2 * T, BF16).rearrange("p (c s) -> p c s", c=2); hsqq = H()
        rsq = A.alloc(T, F32); hrsq = H()
        qx = A.alloc(2 * T, BF16).rearrange("p (c s) -> p c s", c=2); hqx = H()
        pt = [A.alloc(T, BF16) for _ in range(2)]; hpt = [H(), H()]
        rd = A.alloc(T, F32); hrd = H()
        ox = A.alloc(8 * T, BF16).rearrange("p (k s) -> p k s", k=8); hox = H()
        def prep(t):
            b = t % 2; b3 = t % 3
            P.dma("sp", V(xt[b3], hxt[b3]), V(fmview(xres_d, t), dh("xres", t)))
            xnorm(xt[b3], hxt[b3], C_XATG, ht2[b], hht2[b], sq, hsq, rs, hrs, 0)
        prep(0)
        for t in range(NT):
            b = t % 2
            ht = ht2[b]; hht = hht2[b]
            b = t % 3
            if t + 1 < NT:
                prep(t + 1)
            for hx in range(4):
                for c in range(2):
                    for k in range(8):
                        P.mm(PS(1 + c), V(wq_[:, k, hx * 256 + c * 128:hx * 256 + (c + 1) * 128], hwq), V(ht[:, k, :], hht), start=(k == 0), stop=(k == 7))
                    P.act(V(sqq[:, c, :], hsqq), PS(1 + c), AF.Square)
                for c in range(2):
                    P.mm(PS(3), onesb, V(sqq[:, c, :], hsqq), start=(c == 0), stop=(c == 1))
                P.act(V(rsq, hrsq), PS(3), AF.Ln, bias=epsv, scale=1.0 / 256)
                P.act(V(rsq, hrsq), V(rsq, hrsq), AF.Exp, scale=-0.5)
                for c in range(2):
                    P.stt(V(qx[:, c, :], hqx), PS(1 + c), V(gsc[:, 3 + c:4 + c], h_gsc), V(rsq, hrsq), ALU.mult, ALU.mult)
                for m in range(2):
                    for c in range(2):
                        P.mm(PS(4 + m), V(KX2[:, hx * 2 + c, m * 128:(m + 1) * 128], hKX2), V(qx[:, c, :], hqx), start=(c == 0), stop=(c == 1))
                    P.act(V(pt[m], hpt[m]), PS(4 + m), AF.Exp)
                for m in range(2):
                    P.mm(PS(3), onesb, V(pt[m], hpt[m]), start=(m == 0), stop=(m == 1))
                P.act(V(rd, hrd), PS(3), AF.Ln)
                P.act(V(rd, hrd), V(rd, hrd), AF.Exp, scale=-1.0)
                for c2 in range(2):
                    for m in range(2):
                        P.mm(PS(6 + c2), V(VX2[:, m, hx * 256 + c2 * 128:hx * 256 + (c2 + 1) * 128], hVX2), V(pt[m], hpt[m]), start=(m == 0), stop=(m == 1))
                    P.tt(V(ox[:, hx * 2 + c2, :], hox), PS(6 + c2), V(rd, hrd), ALU.mult)
            for n in range(8):
                ob_ = 1 + n % 2
                for k in range(8):
                    P.mm(PS(ob_), V(wo_[:, k, n * 128:(n + 1) * 128], hwo), V(ox[:, k, :], hox), start=(k == 0), stop=(k == 7))
                P.tt(V(xt[b][:, n, :], hxt[b]), V(xt[b][:, n, :], hxt[b]), PS(ob_), ALU.add)
            P.dma("pool", V(fmview(xres_d, t), dh("xres", t)), V(xt[b], hxt[b]))
        barrier()

    def phase_F(l, last):
        kp = lambda d: d.rearrange("(k p) n -> p k n", p=128)
        toks = []
        for hh in range(2):
            A = Arena(big, ARENA0, 103000)
            w1g = A.alloc(8 * 1408, BF16).rearrange("p (k n) -> p k n", k=8); hw1g = H()
            w1u = A.alloc(8 * 1408, BF16).rearrange("p (k n) -> p k n", k=8); hw1u = H()
            w2 = A.alloc(11 * 1024, BF16).rearrange("p (k n) -> p k n", k=11); hw2 = H()
            wload(w1g, kp(fwi_d[l])[:, :, hh * 1408:(hh + 1) * 1408], hw1g)
            wload(w1u, kp(fwi_d[l])[:, :, 2816 + hh * 1408:2816 + (hh + 1) * 1408], hw1u)
            wload(w2, fwo_d[l, hh * 1408:(hh + 1) * 1408, :].rearrange("(k p) n -> p k n", p=128), hw2)
            xt = [A.alloc(8 * T, F32).rearrange("p (k s) -> p k s", k=8) for _ in range(3)]; hxt = [H(), H(), H()]
            pt_ = A.alloc(8 * T, F32).rearrange("p (k s) -> p k s", k=8); hpt_ = H()
            ht2 = [A.alloc(8 * T, BF16).rearrange("p (k s) -> p k s", k=8) for _ in range(2)]; hht2 = [H(), H()]
            sq = A.alloc(8 * T, BF16).rearrange("p (k s) -> p k s", k=8); hsq = H()
            rs = A.alloc(T, F32); hrs = H()
            ac = A.alloc(11 * T, BF16).rearrange("p (k s) -> p k s", k=11); hac = H()
            sg = [A.alloc(T, F32) for _ in range(2)]; hsg = [H(), H()]
            def prep(t):
                b = t % 2; b3 = t % 3
                P.dma("sp", V(xt[b3], hxt[b3]), V(fmview(xres_d, t), dh("xres", t)))
                xnorm(xt[b3], hxt[b3], C_FFNG, ht2[b], hht2[b], sq, hsq, rs, hrs, 0)
            prep(0)
            for t in range(NT):
                b = t % 2
                ht = ht2[b]; hht = hht2[b]
                b = t % 3
                if t + 1 < NT:
                    prep(t + 1)
                if hh == 1:
                    P.dma("sp", V(pt_, hpt_), V(fmview(fp_d, t), dh("fp", t)))
                for j in range(11):
                    s2 = j % 2
                    for k in range(8):
                        P.mm(PS(1 + s2), V(w1g[:, k, j * 128:(j + 1) * 128], hw1g), V(ht[:, k, :], hht), start=(k == 0), stop=(k == 7))
                    for k in range(8):
                        P.mm(PS(3 + s2), V(w1u[:, k, j * 128:(j + 1) * 128], hw1u), V(ht[:, k, :], hht), start=(k == 0), stop=(k == 7))
                    P.act(V(sg[s2], hsg[s2]), PS(1 + s2), AF.Silu)
                    P.tt(V(ac[:, j, :], hac), V(sg[s2], hsg[s2]), PS(3 + s2), ALU.mult)
                for n in range(8):
                    ob_ = 5 + n % 2
                    for j in range(11):
                        P.mm(PS(ob_), V(w2[:, j, n * 128:(n + 1) * 128], hw2), V(ac[:, j, :], hac), start=(j == 0), stop=(j == 10))
                    if hh == 0:
                        P.copy(V(xt[b][:, n, :], hxt[b]), PS(ob_), eng="act")
                    else:
                        P.tt(V(xt[b][:, n, :], hxt[b]), V(xt[b][:, n, :], hxt[b]), PS(ob_), ALU.add)
                        P.tt(V(xt[b][:, n, :], hxt[b]), V(xt[b][:, n, :], hxt[b]), V(pt_[:, n, :], hpt_), ALU.add)
                if hh == 0:
                    P.dma("pool", V(fmview(fp_d, t), dh("fp", t)), V(xt[b], hxt[b]))
                else:
                    dst = out_d if last else xres_d
                    tk = P.dma("pool", V(fmview(dst, t), dh("out" if last else "xres", t)), V(xt[b], hxt[b]))
                    toks.append(tk)
            barrier()
        return toks

    def phase_M2C(l):
        la = P.capture()
        end = phase_M2(l, do_barrier=False)
        lb = P.capture()
        phase_C(l, base=end, bk=(6, 7), do_barrier=False)
        P.interleave(la, lb)
        barrier()

    phases = {"A": phase_A, "S": phase_S, "C": phase_C, "M1": phase_M1, "M2": phase_M2, "G": phase_G, "X": phase_X, "M2C": phase_M2C}
    order = ["A", "S", "M1", "M2C", "G", "X", "F"]
    P.marks = []
    for l in range(L):
        for ph in order:
            if only is not None and ph not in only:
                continue
            P.marks.append((l, ph, P.eng["dve"].n))
            if ph == "F":
                final_toks = phase_F(l, l == L - 1)
            else:
                phases[ph](l)
    P.finish(final_toks)
    return P


from concourse.bass_utils import run_bass_kernel_spmd


def _fm(v, nch):
    return np.ascontiguousarray(np.asarray(v, np.float32).reshape(nch, 128).T)


def _consts():
    c = np.zeros((128, NCST), np.float32)
    i = np.arange(128)
    c[:, K_TRI:K_TRI + 128] = (i[:, None] <= i[None, :])
    c[:, K_GT:K_GT + 128] = (i[:, None] > i[None, :])
    c[:, K_ID:K_ID + 128] = np.eye(128)
    f = np.arange(32)
    inv = (10000.0 ** (-(2 * f).astype(np.float32) / np.float32(64))).astype(np.float32)
    c[0:32, K_INV] = inv; c[32:64, K_INV] = inv
    q = np.arange(512)
    for j in range(4):
        c[:, K_MASK + j * 512:K_MASK + (j + 1) * 512] = ((2 * j + i[:, None] // 64) <= (q[None, :] // 64))
    return c


def _pack_params(inp):
    L = 4
    pp = np.zeros((L, 128, NPP), np.float32)
    pb = np.zeros((L, NPB), np.float32)
    for l in range(L):
        p = pp[l]
        p[:, C_MIXG:C_MIXG + 8] = _fm(inp["mix_norm_g"][l], 8)
        p[:, C_XATG:C_XATG + 8] = _fm(inp["xattn_norm_g"][l], 8)
        p[:, C_FFNG:C_FFNG + 8] = _fm(inp["ffn_norm_g"][l], 8)
        p[:, C_MEMG:C_MEMG + 8] = _fm(inp["mem_norm_g"][l], 8)
        p[:, C_SCW:C_SCW + 64] = np.asarray(inp["ssd_conv_w"][l]).T.reshape(16, 128, 4).transpose(1, 0, 2).reshape(128, 64)
        p[:, C_SCB:C_SCB + 16] = _fm(inp["ssd_conv_b"][l], 16)
        p[:, C_CDW:C_CDW + 248] = np.asarray(inp["conv_dw_w"][l]).T.reshape(8, 128, 31).transpose(1, 0, 2).reshape(128, 248)
        p[:, C_CDB:C_CDB + 8] = _fm(inp["conv_dw_b"][l], 8)
        p[:, C_LNG:C_LNG + 8] = _fm(inp["conv_ln_g"][l], 8)
        p[:, C_LNB:C_LNB + 8] = _fm(inp["conv_ln_b"][l], 8)
        p[:, C_QAG:C_QAG + 3] = _fm(inp["mla_q_a_g"][l], 3)
        p[:, C_KVAG:C_KVAG + 2] = _fm(inp["mla_kv_a_g"][l], 2)
        for (cn, cr, crs, g) in ((C_QNN, C_QNR, C_QNRS, inp["mla_q_norm_g"][l]), (C_KNN, C_KNR, C_KNRS, inp["mla_k_norm_g"][l])):
            g = np.asarray(g, np.float32)
            p[:, cn] = g[0:128]
            p[0:64, cr] = g[128:192]
            p[0:64, crs] = np.concatenate([g[160:192], g[128:160]])
        p[:, C_GATEB:C_GATEB + 24] = np.asarray(inp["gate_b"][l], np.float32).reshape(24, 128).T
        p[:, C_XQG:C_XQG + 2] = _fm(inp["xattn_q_norm_g"][l], 2)
        p[:, C_XKG:C_XKG + 2] = _fm(inp["xattn_k_norm_g"][l], 2)
        pb[l, B_DTB:B_DTB + 16] = inp["ssd_dt_bias"][l]
        pb[l, B_ALOG:B_ALOG + 16] = inp["ssd_a_log"][l]
        pb[l, B_DSK:B_DSK + 16] = inp["ssd_d"][l]
        pb[l, B_SNG:B_SNG + 1024] = inp["ssd_norm_g"][l]
    return pp, pb


WNAMES = ["w_in", "ssd_w_out", "conv_w_out", "mla_w_q_b", "mla_w_kv_b", "mla_w_o", "w_out", "xattn_w_q", "xattn_w_kv",
          "xattn_w_o", "ffn_w_in", "ffn_w_out"]


def make_in_maps(inp, cores):
    pp, pb = _pack_params(inp)
    cst = _consts()
    shared = {n: np.ascontiguousarray(np.asarray(inp[n], np.float32)) for n in WNAMES}
    shared.update(cst=cst, pp=pp, pb=pb)
    maps = []
    for b in cores:
        m = dict(shared)
        m["xT"] = np.ascontiguousarray(np.asarray(inp["x"][b], np.float32).T)
        m["memT"] = np.ascontiguousarray(np.asarray(inp["mem"][b], np.float32).T)
        m["pos"] = np.ascontiguousarray(np.asarray(inp["positions"][b], np.int32)[None, :])
        maps.append(m)
    return maps


_CACHE = {}


def kernel(**inputs):
    if "P" not in _CACHE:
        _CACHE["P"] = build(L=4)
    P = _CACHE["P"]
    maps = make_in_maps(inputs, list(range(8)))
    res = run_bass_kernel_spmd(P.nc, maps, core_ids=list(range(8)))
    out = np.stack([np.ascontiguousarray(r["outT"].T) for r in res.results], axis=0)
    return out.astype(np.float32)
```

```python
import contextlib
import numpy as np
import concourse.bass as bass
import concourse.mybir as mybir

F32 = mybir.dt.float32
BF16 = mybir.dt.bfloat16
I32 = mybir.dt.int32
AF = mybir.ActivationFunctionType
ALU = mybir.AluOpType
AX = mybir.AxisListType

ENGS = ("pe", "act", "dve", "pool", "sp")
NDMA_SEMS = 12


class H:
    __slots__ = ("w", "r")

    def __init__(self):
        self.w = None
        self.r = []


class V:
    __slots__ = ("ap", "hs")

    def __init__(self, ap, hs):
        self.ap = ap
        self.hs = hs if isinstance(hs, (list, tuple)) else [hs]


class Eng:
    def __init__(self, name, idx):
        self.name = name
        self.idx = idx
        self.n = 0
        self.seen = [0] * len(ENGS)
        self.seen_dma = {}
        self.ops = []
        self.dma_count = 0


class Prog:
    def __init__(self):
        self.nc = bass.Bass("TRN2", target_bir_lowering=False)
        self.stack = contextlib.ExitStack()
        self.eng = {n: Eng(n, i) for i, n in enumerate(ENGS)}
        self.snaps = {}
        self.ntens = 0
        self.nwaits = 0

    def sb(self, shape, dtype, name=None):
        self.ntens += 1
        t = self.stack.enter_context(self.nc.sbuf_tensor(name or f"sb{self.ntens}", list(shape), dtype))
        return t

    def ps(self, shape, dtype, name=None):
        self.ntens += 1
        t = self.stack.enter_context(self.nc.psum_tensor(name or f"ps{self.ntens}", list(shape), dtype))
        return t

    def dram(self, name, shape, dtype, kind="Internal"):
        return self.nc.dram_tensor(name, list(shape), dtype, kind=kind).ap()

    def _deps(self, reads, writes):
        deps = {}

        def add(tok):
            k, v = tok
            if deps.get(k, 0) < v:
                deps[k] = v
        for h in reads:
            if h.w is not None:
                add(h.w)
        for h in writes:
            if h.w is not None:
                add(h.w)
            for t in h.r:
                add(t)
        return deps

    def _waits(self, e, deps):
        waits = []
        for k, v in deps.items():
            if isinstance(k, int):
                if k == e.idx and e.name == "pe":
                    continue
                if e.seen[k] >= v:
                    continue
                waits.append((k, v))
            else:
                if e.seen_dma.get(k, 0) >= v:
                    continue
                waits.append((k, v))
        for k, v in waits:
            if isinstance(k, int):
                if e.seen[k] < v:
                    e.seen[k] = v
            else:
                e.seen_dma[k] = v
            snap = self.snaps.get((k, v))
            if snap is not None:
                for i in range(len(ENGS)):
                    if e.seen[i] < snap[i]:
                        e.seen[i] = snap[i]
        return waits

    def capture(self):
        self._cap = []
        return self._cap

    def end_capture(self):
        self._cap = None

    def interleave(self, la, lb):
        self._cap = None
        na, nb = len(la), len(lb)
        ia = ib = 0
        while ia < na or ib < nb:
            if ib >= nb or (ia < na and ia * nb <= ib * na):
                kind, args, kw = la[ia]; ia += 1
            else:
                kind, args, kw = lb[ib]; ib += 1
            (self.op if kind == "op" else self.dma)(*args, **kw)

    def op(self, engname, fn, reads, writes):
        if getattr(self, "_cap", None) is not None:
            self._cap.append(("op", (engname, fn, reads, writes), {}))
            return None
        e = self.eng[engname]
        rh = [h for v in reads for h in v.hs]
        wh = [h for v in writes for h in v.hs]
        deps = self._deps(rh, wh)
        waits = self._waits(e, deps)
        e.n += 1
        tok = (e.idx, e.n)
        self.snaps[tok] = tuple(e.seen)
        e.ops.append((waits, fn, None))
        self.nwaits += len(waits)
        for h in rh:
            h.r.append(tok)
        for h in wh:
            h.w = tok
            h.r = []
        return tok

    def dma(self, qname, out, in_, **kw):
        if getattr(self, "_cap", None) is not None:
            self._cap.append(("dma", (qname, out, in_), kw))
            return None
        e = self.eng[qname]
        rh = list(in_.hs)
        wh = list(out.hs)
        deps = self._deps(rh, wh)
        k = e.dma_count % NDMA_SEMS
        rnd = e.dma_count // NDMA_SEMS
        e.dma_count += 1
        key = (qname, k)
        if rnd > 0:
            if deps.get(key, 0) < 16 * rnd:
                deps[key] = 16 * rnd
        waits = self._waits(e, deps)
        tok = (key, 16 * (rnd + 1))
        self.snaps[tok] = tuple(e.seen)
        oap, iap = out.ap, in_.ap
        e.ops.append((waits, lambda eng: eng.dma_start(out=oap, in_=iap, **kw), key))
        self.nwaits += len(waits)
        for h in rh:
            h.r.append(tok)
        for h in wh:
            h.w = tok
            h.r = []
        return tok

    def mm(self, out, lhsT, rhs, start=True, stop=True):
        o, l, r = out.ap, lhsT.ap, rhs.ap
        return self.op("pe", lambda e: e.matmul(o, l, r, start=start, stop=stop), [lhsT, rhs], [out])

    def transpose(self, out, in_, ident):
        o, i, d = out.ap, in_.ap, ident.ap
        return self.op("pe", lambda e: e.transpose(o, i, d), [in_, ident], [out])

    def act(self, out, in_, func, bias=None, scale=None, eng="act", accum=None):
        o, i = out.ap, in_.ap
        reads = [in_]
        kw = {}
        if bias is not None:
            if isinstance(bias, V):
                reads.append(bias)
                kw["bias"] = bias.ap
            else:
                kw["bias"] = bias
        if scale is not None:
            if isinstance(scale, V):
                reads.append(scale)
                kw["scale"] = scale.ap
            else:
                kw["scale"] = scale
        writes = [out]
        if accum is not None:
            writes.append(accum)
            kw["accum_out"] = accum.ap
        return self.op("act", lambda e: e.activation(o, i, func, **kw), reads, writes)

    def tt(self, out, a, b, op, eng="dve"):
        o, x, y = out.ap, a.ap, b.ap
        return self.op(eng, lambda e: e.tensor_tensor(o, x, y, op), [a, b], [out])

    def ts(self, out, a, s1, op0, s2=None, op1=None, eng="dve", accum=None):
        o, x = out.ap, a.ap
        reads = [a]
        if isinstance(s1, V):
            reads.append(s1)
            s1 = s1.ap
        if isinstance(s2, V):
            reads.append(s2)
            s2 = s2.ap
        writes = [out]
        kw = {}
        if accum is not None:
            writes.append(accum)
            kw["accum_out"] = accum.ap
        if op1 is None:
            return self.op(eng, lambda e: e.tensor_scalar(o, x, s1, None, op0, **kw), reads, writes)
        return self.op(eng, lambda e: e.tensor_scalar(o, x, s1, s2, op0, op1, **kw), reads, writes)

    def stt(self, out, a, s, b, op0, op1, eng="dve"):
        o, x, y = out.ap, a.ap, b.ap
        reads = [a, b]
        if isinstance(s, V):
            reads.append(s)
            s = s.ap
        return self.op(eng, lambda e: e.scalar_tensor_tensor(o, x, s, y, op0, op1), reads, [out])

    def copy(self, out, in_, eng="dve"):
        o, i = out.ap, in_.ap
        if eng == "act":
            return self.op("act", lambda e: e.copy(o, i), [in_], [out])
        return self.op(eng, lambda e: e.tensor_copy(o, i), [in_], [out])

    def memset(self, out, val, eng="dve"):
        o = out.ap
        return self.op(eng, lambda e: e.memset(o, val), [], [out])

    def recip(self, out, in_):
        o, i = out.ap, in_.ap
        return self.op("dve", lambda e: e.reciprocal(o, i), [in_], [out])

    def finish(self, final_tokens):
        nc = self.nc
        sems = {}
        for i, n in enumerate(ENGS):
            sems[i] = self.stack.enter_context(nc.semaphore(f"s_{n}"))
        for n in ENGS:
            e = self.eng[n]
            if e.dma_count:
                for k in range(min(NDMA_SEMS, e.dma_count)):
                    sems[(n, k)] = self.stack.enter_context(nc.semaphore(f"d_{n}{k}"))
        sp = self.eng["sp"]
        fin = {}
        for k, v in final_tokens:
            if fin.get(k, 0) < v:
                fin[k] = v
        sp.ops.append((list(fin.items()), None, None))
        block = self.stack.enter_context(nc.Block())
        hw = {"pe": block.tensor, "act": block.scalar, "dve": block.vector, "pool": block.gpsimd, "sp": block.sync}

        def make(e):
            def body(eng):
                own = sems[e.idx]
                for waits, fn, dkey in e.ops:
                    for k, v in waits:
                        eng.wait_ge(sems[k], v)
                    if fn is None:
                        continue
                    ins = fn(eng)
                    if dkey is None:
                        ins.then_inc(own, 1)
                    else:
                        ins.then_inc(sems[dkey], 16)
            return body
        for n in ENGS:
            e = self.eng[n]
            if e.ops:
                hw[n](make(e))
        self.stack.close()
        return nc


S = 4096; D = 1024; T = 512; NT = 8; NCH = 32
EPS = 1e-6
OFF_Z, OFF_XBC, OFF_DT, OFF_GA, OFF_GG, OFF_QL, OFF_KV, OFF_KR, OFF_GATE = 0, 1024, 3072, 3088, 4112, 5136, 5520, 5776, 5840
C_MIXG, C_XATG, C_FFNG, C_MEMG, C_SCW, C_SCB, C_CDW, C_CDB, C_LNG, C_LNB = 0, 8, 16, 24, 32, 96, 112, 360, 368, 376
C_QAG, C_KVAG, C_QNN, C_QNR, C_QNRS, C_KNN, C_KNR, C_KNRS, C_GATEB, C_XQG, C_XKG, NPP = 384, 387, 389, 390, 391, 392, 393, 394, 395, 419, 421, 424
B_DTB, B_ALOG, B_DSK, B_SNG, NPB = 0, 16, 32, 48, 1072
K_TRI, K_GT, K_ID, K_INV, K_MASK, NCST = 0, 128, 256, 384, 385, 385 + 2048


class Arena:
    def __init__(self, big, base, limit):
        self.big, self.off, self.limit = big, base, limit

    def alloc(self, n, dtype, parts=128):
        nb = n * (4 if dtype in (F32, I32) else 2)
        nb = (nb + 63) // 64 * 64
        ne = nb // 2
        assert self.off + ne <= self.limit, ("SBUF arena overflow", self.off + ne, self.limit)
        ap = self.big[:, self.off:self.off + ne]
        self.off += ne
        if dtype != BF16:
            ap = ap.bitcast(dtype)
        return ap[:, 0:n]


def build(L=4, dbg=False, only=None):
    P = Prog(); nc = P.nc
    kind_s = "ExternalOutput" if dbg else "Internal"
    din = lambda n, s, dt=F32: P.dram(n, s, dt, kind="ExternalInput")
    xT_d = din("xT", [D, S]); memT_d = din("memT", [D, 256]); pos_d = din("pos", [1, S], I32)
    cst_d = din("cst", [128, NCST]); pp_d = din("pp", [4, 128, NPP]); pb_d = din("pb", [4, NPB])
    w_in_d = din("w_in", [4, D, 8912]); ssd_wo_d = din("ssd_w_out", [4, D, D]); conv_wo_d = din("conv_w_out", [4, D, D])
    wqb_d = din("mla_w_q_b", [4, 384, 1536]); wkvb_d = din("mla_w_kv_b", [4, 256, 2048]); mla_wo_d = din("mla_w_o", [4, D, D])
    wout_d = din("w_out", [4, D, D]); xwq_d = din("xattn_w_q", [4, D, D]); xwkv_d = din("xattn_w_kv", [4, D, 2048])
    xwo_d = din("xattn_w_o", [4, D, D]); fwi_d = din("ffn_w_in", [4, D, 5632]); fwo_d = din("ffn_w_out", [4, 2816, D])
    out_d = P.dram("outT", [D, S], F32, kind="ExternalOutput")
    sc = lambda n, s, dt=F32: P.dram(n, s, dt, kind=kind_s)
    xres_d = sc("xres", [D, S]); sz_d = sc("sz", [S, D]); xbc_d = sc("xbc", [2048, S], BF16); cv_d = sc("cv", [D, S])
    lat_d = sc("lat", [768, S]); ys_d = sc("ys", [D, S], BF16); yc_d = sc("yc", [D, S], BF16)
    q_d = sc("qs", [8, 192, S], BF16); k_d = sc("ks", [8, 128, S], BF16); kr_d = sc("krs", [64, S], BF16)
    v_d = sc("vs", [S, D], BF16); o_d = sc("os", [D, S], BF16); fp_d = sc("fpart", [D, S]); rope_d = sc("rope", [128, S])
    hd = {}

    def dh(name, t):
        k = (name, t)
        if k not in hd:
            hd[k] = H()
        return hd[k]

    def dall(name, n=NT):
        return [dh(name, t) for t in range(n)]

    big = P.sb([128, 103000], BF16, name="big")
    psall = P.ps([128, 4096], F32, name="psall")[:]
    banks = [psall[:, i * 512:(i + 1) * 512] for i in range(8)]
    bh = [H() for _ in range(8)]

    def PS(i, n=512, parts=128, dt=F32):
        ap = banks[i][:]
        if dt == BF16:
            ap = ap.bitcast(BF16)
        return V(ap[0:parts, 0:n], bh[i])

    pers = Arena(big, 0, 8000)
    cst = pers.alloc(K_MASK, F32); h_cst = H()
    tri = V(cst[:, K_TRI:K_TRI + 128], h_cst); gt = V(cst[:, K_GT:K_GT + 128], h_cst)
    inv_c = V(cst[:, K_INV:K_INV + 1], h_cst)
    ident = pers.alloc(128, BF16); h_id = H(); identv = V(ident, h_id)
    ones_b = pers.alloc(128, BF16); h_ob = H(); onesb = V(ones_b, h_ob)
    ones_f = pers.alloc(128, F32); h_of = H(); onesf = V(ones_f, h_of)
    masks = pers.alloc(2048, BF16); h_mk = H()
    epsc = pers.alloc(1, F32); h_eps = H(); epsv = V(epsc, h_eps)
    ppt = pers.alloc(NPP, F32); h_pp = H()
    pbt = pers.alloc(NPB, F32); h_pb = H()
    abc = pers.alloc(16, F32); h_abc = H()
    gsc = pers.alloc(8, F32); h_gsc = H()
    ARENA0 = pers.off

    def pcol(c, n=1, parts=128):
        return V(ppt[0:parts, c:c + n], h_pp)

    def barrier():
        toks = {}
        for n in ENGS:
            e = P.eng[n]
            if e.n:
                toks[e.idx] = e.n
            for k in range(min(NDMA_SEMS, e.dma_count)):
                rnd = (e.dma_count - 1 - k) // NDMA_SEMS
                toks[(n, k)] = 16 * (rnd + 1)
        for n in ENGS:
            e = P.eng[n]
            waits = P._waits(e, dict(toks))
            if waits:
                e.ops.append((waits, None, None))

    P.dma("sp", V(cst, h_cst), V(cst_d[:, 0:K_MASK], H()))
    P.dma("pool", V(ident, h_id), V(cst_d[:, K_ID:K_ID + 128], H()))
    P.dma("pool", V(masks, h_mk), V(cst_d[:, K_MASK:K_MASK + 2048], H()))
    P.memset(onesb, 1.0); P.memset(onesf, 1.0); P.memset(epsv, EPS)

    def rope_tables():
        A = Arena(big, ARENA0, 103000)
        for t in range(NT):
            sl = slice(t * T, (t + 1) * T)
            pi_ = A.alloc(T, I32) if t == 0 else rope_tables.bufs[0]
            if t == 0:
                rope_tables.bufs = [pi_] + [A.alloc(T, F32) for _ in range(5)] + [A.alloc(T, I32)]
            pi_, pf, ang, kf, r, rc, ki = rope_tables.bufs
            hs = [H() for _ in range(7)]
            pi_v, pf_v, ang_v, kf_v, r_v, rc_v, ki_v = [V(a[0:64], h) for a, h in zip(rope_tables.bufs, hs)]
            P.dma("sp", pi_v, V(pos_d[0, sl].partition_broadcast(64), H()))
            P.copy(pf_v, pi_v)
            P.ts(ang_v, pf_v, V(cst[0:64, K_INV:K_INV + 1], h_cst), ALU.mult)
            P.ts(kf_v, ang_v, float(1.0 / (2 * np.pi)), ALU.mult)
            P.copy(ki_v, kf_v)
            P.copy(kf_v, ki_v)
            C1 = 6.28125; C2 = float(np.float32(2 * np.pi - 6.28125))
            P.stt(r_v, kf_v, -C1, ang_v, ALU.mult, ALU.add)
            P.stt(r_v, kf_v, -C2, r_v, ALU.mult, ALU.add)
            P.ts(r_v, r_v, 3.1415925, ALU.min, -3.1415925, ALU.max)
            P.ts(rc_v, r_v, float(np.pi / 2), ALU.is_gt, float(-2 * np.pi), ALU.mult)
            P.stt(rc_v, r_v, float(np.pi / 2), rc_v, ALU.add, ALU.add)
            P.ts(rc_v, rc_v, 3.1415925, ALU.min, -3.1415925, ALU.max)
            P.act(rc_v, rc_v, AF.Sin)
            P.act(r_v, r_v, AF.Sin)
            P.ts(V(r[0:32], hs[4]), V(r[0:32], hs[4]), -1.0, ALU.mult)
            P.dma("sp", V(rope_d[0:64, sl], dh("rope", t)), rc_v)
            P.dma("sp", V(rope_d[64:128, sl], dh("rope", t)), r_v)
            barrier()
    rope_tables()
    barrier()

    def wload(dst, src, hdst):
        return P.dma("pool", V(dst, hdst), V(src, H()))

    def fmview(d, t):
        return d.rearrange("(k p) s -> p k s", p=128)[:, :, t * T:(t + 1) * T]

    def xnorm(xt, hx, gcol, outb, hout, sq, hsq, rs, hrs, bank, nk=8, width=T, inv_n=1.0 / D):
        P.act(V(sq, hsq), V(xt, hx), AF.Square)
        for k in range(nk):
            P.mm(PS(bank, width), onesb, V(sq[:, k, :], hsq), start=(k == 0), stop=(k == nk - 1))
        P.act(V(rs, hrs), PS(bank, width), AF.Ln, bias=epsv, scale=inv_n)
        P.act(V(rs, hrs), V(rs, hrs), AF.Exp, scale=-0.5)
        for k in range(nk):
            P.stt(V(outb[:, k, :], hout), V(xt[:, k, :], hx), pcol(gcol + k), V(rs, hrs), ALU.mult, ALU.mult)

    dtraw = pers.alloc(NCH * 16, F32).rearrange("p (c h) -> p c h", c=NCH); h_dtraw = H()
    ARENA0 = pers.off
    final_toks = []

    def phase_A(l):
        xsrc = xT_d if l == 0 else xres_d
        xsn = "xT" if l == 0 else "xres"
        barrier()
        P.dma("sp", V(ppt, h_pp), V(pp_d[l], H()))
        P.dma("sp", V(pbt, h_pb), V(pb_d[l, :].partition_broadcast(128), H()))
        P.act(V(abc, h_abc), V(pbt[:, B_ALOG:B_ALOG + 16], h_pb), AF.Exp)
        P.ts(V(abc, h_abc), V(abc, h_abc), -1.0, ALU.mult)
        P.ts(V(gsc[:, 0:3], h_gsc), V(ppt[:, C_QNN:C_QNN + 3], h_pp), float(192 ** -0.5), ALU.mult)
        P.ts(V(gsc[:, 3:5], h_gsc), V(ppt[:, C_XQG:C_XQG + 2], h_pp), float(256 ** -0.5), ALU.mult)

        A = Arena(big, ARENA0, 103000)
        uT = A.alloc(8 * S, BF16).rearrange("p (k s) -> p k s", k=8); hu = [H() for _ in range(NT)]
        A1 = A.off
        xt = [A.alloc(8 * T, F32).rearrange("p (k s) -> p k s", k=8) for _ in range(2)]; hxt = [H(), H()]
        sq = A.alloc(8 * T, BF16).rearrange("p (k s) -> p k s", k=8); hsq = H()
        rs = A.alloc(T, F32); hrs = H()
        for t in range(NT):
            b = t % 2
            P.dma("sp", V(xt[b], hxt[b]), V(fmview(xsrc, t), dh(xsn, t)))
            xnorm(xt[b], hxt[b], C_MIXG, uT[:, :, t * T:(t + 1) * T], hu[t], sq, hsq, rs, hrs, 0)
        barrier()
        A.off = A1
        wz = A.alloc(8 * 1040, BF16).rearrange("p (k n) -> p k n", k=8); hwz = H()
        wload(wz[:, :, 0:1024], w_in_d[l].rearrange("(k p) n -> p k n", p=128)[:, :, OFF_Z:OFF_Z + 1024], hwz)
        wload(wz[:, :, 1024:1040], w_in_d[l].rearrange("(k p) n -> p k n", p=128)[:, :, OFF_DT:OFF_DT + 16], hwz)
        szb = [A.alloc(1024, F32) for _ in range(2)]; hszb = [H(), H()]
        for c in range(NCH):
            t = c // 4; b = c % 2
            us = lambda k: V(uT[:, k, c * 128:(c + 1) * 128], hu[t])
            for half in range(2):
                for k in range(8):
                    P.mm(PS(1 + half), us(k), V(wz[:, k, half * 512:(half + 1) * 512], hwz), start=(k == 0), stop=(k == 7))
            for k in range(8):
                P.mm(PS(3, 16), us(k), V(wz[:, k, 1024:1040], hwz), start=(k == 0), stop=(k == 7))
            for half in range(2):
                P.act(V(szb[b][:, half * 512:(half + 1) * 512], hszb[b]), PS(1 + half), AF.Silu)
            P.copy(V(dtraw[:, c, :], h_dtraw), PS(3, 16))
            P.dma("pool", V(sz_d[c * 128:(c + 1) * 128, :], dh("sz", t)), V(szb[b], hszb[b]))
        barrier()
        A.off = A1
        wg = [A.alloc(8 * 512, BF16).rearrange("p (k n) -> p k n", k=8) for _ in range(2)]; hwg = [H(), H()]
        pre = [A.alloc(S + 32, F32) for _ in range(2)]; hpre = [H(), H()]
        acc = [A.alloc(S, F32) for _ in range(2)]; hacc = [H(), H()]
        xo = [A.alloc(S, BF16) for _ in range(2)]; hxo = [H(), H()]
        for b in range(2):
            P.memset(V(pre[b][:, 0:32], hpre[b]), 0.0)
        win_v = w_in_d[l].rearrange("(k p) n -> p k n", p=128)
        wload(wg[0], win_v[:, :, OFF_XBC:OFF_XBC + 512], hwg[0])
        pend = None
        for j in range(16):
            g4 = j // 4; wb = g4 % 2; b = j % 2
            if j % 4 == 0 and g4 + 1 < 4:
                wload(wg[1 - wb], win_v[:, :, OFF_XBC + (g4 + 1) * 512:OFF_XBC + (g4 + 2) * 512], hwg[1 - wb])
            for t in range(NT):
                bank = 1 + (t % 2)
                for k in range(8):
                    P.mm(PS(bank), V(wg[wb][:, k, (j % 4) * 128:(j % 4 + 1) * 128], hwg[wb]),
                         V(uT[:, k, t * T:(t + 1) * T], hu[t]), start=(k == 0), stop=(k == 7))
                P.copy(V(pre[b][:, 32 + t * T:32 + (t + 1) * T], hpre[b]), PS(bank), eng="act")
            P.ts(V(acc[b], hacc[b]), V(pre[b][:, 29:29 + S], hpre[b]), pcol(C_SCW + j * 4 + 0), ALU.mult, pcol(C_SCB + j), ALU.add)
            for tap in range(1, 4):
                P.stt(V(acc[b], hacc[b]), V(pre[b][:, 29 + tap:29 + tap + S], hpre[b]), pcol(C_SCW + j * 4 + tap),
                      V(acc[b], hacc[b]), ALU.mult, ALU.add)
            if pend is not None:
                pend()

            def fin(b=b, j=j):
                P.act(V(xo[b], hxo[b]), V(acc[b], hacc[b]), AF.Silu)
                P.dma("pool", V(xbc_d[j * 128:(j + 1) * 128, :], dall("xbc")), V(xo[b], hxo[b]))
            pend = fin
        pend()
        barrier()
        A.off = A1
        wa = [A.alloc(8 * 128, BF16).rearrange("p (k n) -> p k n", k=8) for _ in range(2)]; hwa = [H(), H()]
        wgt = [A.alloc(8 * 128, BF16).rearrange("p (k n) -> p k n", k=8) for _ in range(2)]; hwgt = [H(), H()]
        vb = [A.alloc(S + 32, BF16) for _ in range(2)]; hvb = [[H() for _ in range(NT)] for _ in range(2)]; hvz = [H(), H()]
        dg = [A.alloc(31 * 128, BF16).rearrange("p (j n) -> p j n", j=31) for _ in range(2)]; hdg = [H(), H()]
        acc = [A.alloc(S, F32) for _ in range(2)]; hacc = [H(), H()]
        sg = [A.alloc(T, F32) for _ in range(2)]; hsg = [H(), H()]
        for b in range(2):
            P.memset(V(vb[b][:, 0:32], hvz[b]), 0.0)
        def prepj(j):
            b = j % 2
            wload(wa[b], win_v[:, :, OFF_GA + j * 128:OFF_GA + (j + 1) * 128], hwa[b])
            wload(wgt[b], win_v[:, :, OFF_GG + j * 128:OFF_GG + (j + 1) * 128], hwgt[b])
            for tap in range(31):
                P.ts(V(dg[b][:, tap, :], hdg[b]), identv, pcol(C_CDW + j * 31 + tap), ALU.mult)
        prepj(0)
        for j in range(8):
            b = j % 2
            if j + 1 < 8:
                prepj(j + 1)

            def glu(t):
                sb_ = t % 2
                for k in range(8):
                    P.mm(PS(1 + sb_), V(wa[b][:, k, :], hwa[b]), V(uT[:, k, t * T:(t + 1) * T], hu[t]), start=(k == 0), stop=(k == 7))
                for k in range(8):
                    P.mm(PS(3 + sb_), V(wgt[b][:, k, :], hwgt[b]), V(uT[:, k, t * T:(t + 1) * T], hu[t]), start=(k == 0), stop=(k == 7))
                P.act(V(sg[sb_], hsg[sb_]), PS(3 + sb_), AF.Sigmoid)
                P.tt(V(vb[b][:, 32 + t * T:32 + (t + 1) * T], hvb[b][t]), PS(1 + sb_), V(sg[sb_], hsg[sb_]), ALU.mult)

            def conv(t):
                cbk = 5 + t % 2
                rd_h = [hvb[b][t]] + ([hvb[b][t - 1]] if t > 0 else [hvz[b]])
                for tap in range(31):
                    o0 = 2 + tap + t * T
                    P.mm(PS(cbk), V(dg[b][:, tap, :], hdg[b]), V(vb[b][:, o0:o0 + T], rd_h), start=(tap == 0), stop=(tap == 30))
                P.act(V(acc[b][:, t * T:(t + 1) * T], hacc[b]), PS(cbk), AF.Identity, bias=pcol(C_CDB + j))
            glu(0)
            for t in range(NT):
                if t + 1 < NT:
                    glu(t + 1)
                conv(t)
            P.dma("pool", V(cv_d[j * 128:(j + 1) * 128, :], dall("cv")), V(acc[b], hacc[b]))
        barrier()
        A.off = A1
        wl = A.alloc(8 * 768, BF16).rearrange("p (k n) -> p k n", k=8); hwl = H()
        wload(wl[:, :, 0:704], win_v[:, :, OFF_QL:OFF_QL + 704], hwl)
        wload(wl[:, :, 704:736], win_v[:, :, OFF_KR + 32:OFF_KR + 64], hwl)
        wload(wl[:, :, 736:768], win_v[:, :, OFF_KR:OFF_KR + 32], hwl)
        lo = [A.alloc(T, F32) for _ in range(2)]; hlo = [H(), H()]
        segs = [(0, 128), (128, 128), (256, 128), (384, 128), (512, 128), (640, 64), (704, 64)]
        i = 0
        for t in range(NT):
            for (c0, m) in segs:
                b = i % 2; i += 1
                for k in range(8):
                    P.mm(PS(1 + b, T, m), V(wl[:, k, c0:c0 + m], hwl), V(uT[:, k, t * T:(t + 1) * T], hu[t]), start=(k == 0), stop=(k == 7))
                P.copy(V(lo[b][0:m], hlo[b]), PS(1 + b, T, m), eng="act")
                P.dma("pool", V(lat_d[c0:c0 + m, t * T:(t + 1) * T], dh("lat", t)), V(lo[b][0:m], hlo[b]))
        barrier()
    def phase_S(l):
        A = Arena(big, ARENA0, 103000)
        xbt = [A.alloc(16 * T, BF16).rearrange("p (k s) -> p k s", k=16) for _ in range(2)]; hxb = [H(), H()]
        szt = [A.alloc(1024, F32) for _ in range(2)]; hszt = [H(), H()]
        hst = A.alloc(1024, F32); h_hst = H()
        hbf = A.alloc(1024, BF16); h_hbf = H()
        D2 = lambda n, dt: ([A.alloc(n, dt) for _ in range(2)], [H(), H()])
        sm2, h_sm2 = D2(64, F32)
        ew2, h_ew2 = D2(48, F32)
        Xb2, h_Xb2 = D2(1024, BF16)
        Xw2, h_Xw2 = D2(1024, BF16)
        tD2, h_tD2 = D2(1024, F32)
        Bt2, h_Bt2 = D2(512, BF16)
        MT2, h_MT2 = D2(2048, BF16)
        cbm = A.alloc(512, F32); h_cbm = H()
        AG = A.alloc(2048, F32); h_AG = H()
        dec = [A.alloc(512, F32) for _ in range(2)]; h_dec = [H(), H()]
        yg = A.alloc(1024, F32); h_yg = H()
        t1 = [A.alloc(512, F32) for _ in range(2)]; h_t1 = [H(), H()]
        ssq = A.alloc(8, F32); h_ssq = H()
        junk = A.alloc(256, F32); h_junk = H()
        ynb = A.alloc(1024, BF16); h_ynb = H()
        yst = [A.alloc(8 * T, BF16).rearrange("p (k s) -> p k s", k=8) for _ in range(2)]; h_yst = [H(), H()]
        P.memset(V(hst, h_hst), 0.0); P.memset(V(hbf, h_hbf), 0.0)
        dtb = V(pbt[:, B_DTB:B_DTB + 16], h_pb)
        sng = V(pbt[:, B_SNG:B_SNG + 1024], h_pb)
        xview = xbc_d.rearrange("(k p) s -> p k s", p=128)
        SMALL = lambda lo, n: V(banks[2][:, 256 + lo:256 + lo + n], bh[2])

        def bc16(ap, lo, n):
            return ap[:, lo:lo + n].unsqueeze(2).broadcast_to([128, n, 64])
        v3 = lambda ap: ap.rearrange("p (h d) -> p h d", h=16)

        def stage1(c):
            t = c // 4; s_ = c % 4; tb = t % 2; cb_ = c % 2
            cs = slice(s_ * 128, (s_ + 1) * 128)
            if s_ == 0:
                P.dma("sp", V(xbt[tb], hxb[tb]), V(xview[:, :, t * T:(t + 1) * T], dall("xbc")))
            P.dma("sp", V(szt[cb_], hszt[cb_]), V(sz_d[c * 128:(c + 1) * 128, :], dh("sz", t)))
            xb_ = xbt[tb]; hx_ = hxb[tb]
            sm, h_sm, ew, h_ew = sm2[cb_], h_sm2[cb_], ew2[cb_], h_ew2[cb_]
            Xb, h_Xb, Xw, h_Xw, tD, h_tD = Xb2[cb_], h_Xb2[cb_], Xw2[cb_], h_Xw2[cb_], tD2[cb_], h_tD2[cb_]
            Btm, h_Btm, MT, h_MT = Bt2[cb_], h_Bt2[cb_], MT2[cb_], h_MT2[cb_]
            for k in range(8):
                P.transpose(V(banks[1][:].bitcast(BF16)[:, k * 128:(k + 1) * 128], bh[1]), V(xb_[:, k, cs], hx_), identv)
            for g in range(4):
                P.transpose(V(banks[2][:].bitcast(BF16)[:, g * 128:(g + 1) * 128], bh[2]), V(xb_[:, 8 + g, cs], hx_), identv)
            for g in range(4):
                P.mm(V(banks[3][:, g * 128:(g + 1) * 128], bh[3]), V(xb_[:, 8 + g, cs], hx_), V(xb_[:, 12 + g, cs], hx_))
            P.tt(V(sm[:, 0:16], h_sm), V(dtraw[:, c, :], h_dtraw), dtb, ALU.add)
            P.act(V(sm[:, 0:16], h_sm), V(sm[:, 0:16], h_sm), AF.Exp)
            P.act(V(sm[:, 0:16], h_sm), V(sm[:, 0:16], h_sm), AF.Ln, bias=1.0)
            P.tt(V(sm[:, 16:32], h_sm), V(sm[:, 0:16], h_sm), V(abc, h_abc), ALU.mult)
            av = V(sm[:, 16:32], h_sm)
            P.mm(SMALL(0, 16), tri, av); P.mm(SMALL(16, 16), gt, av); P.mm(SMALL(32, 16), onesf, av)
            P.act(V(ew, h_ew), SMALL(0, 48), AF.Exp)
            P.tt(V(sm[:, 32:48], h_sm), V(sm[:, 0:16], h_sm), V(ew[:, 16:32], h_ew), ALU.mult)
            P.tt(V(AG.rearrange("p (h s) -> p h s", h=16), h_AG),
                 V(cst[:, K_GT:K_GT + 128].unsqueeze(1).broadcast_to([128, 16, 128]), h_cst),
                 V(sm[:, 16:32].unsqueeze(2).broadcast_to([128, 16, 128]), h_sm), ALU.mult)
            P.tt(V(cbm.rearrange("p (g l) -> p g l", g=4), h_cbm), V(banks[3][:].rearrange("p (g l) -> p g l", g=4), bh[3]),
                 V(cst[:, K_TRI:K_TRI + 128].unsqueeze(1).broadcast_to([128, 4, 128]), h_cst), ALU.mult)
            xsT = banks[1][:].bitcast(BF16)[:, 0:1024].rearrange("p (h d) -> p h d", h=16)
            P.tt(V(v3(Xb), h_Xb), V(xsT, bh[1]), V(bc16(sm, 0, 16), h_sm), ALU.mult)
            P.tt(V(v3(Xw), h_Xw), V(xsT, bh[1]), V(bc16(sm, 32, 16), h_sm), ALU.mult)
            P.tt(V(v3(tD), h_tD), V(xsT, bh[1]), V(bc16(pbt, B_DSK, 16), h_pb), ALU.mult)
            P.copy(V(Btm, h_Btm), V(banks[2][:].bitcast(BF16)[:, 0:512], bh[2]), eng="act")
            for g in range(4):
                db = g % 2
                for r in range(4):
                    hd_ = g * 4 + r
                    P.mm(V(banks[4][:, r * 128:(r + 1) * 128], bh[4]), V(AG[:, hd_ * 128:(hd_ + 1) * 128], h_AG), tri)
                P.act(V(dec[db], h_dec[db]), PS(4), AF.Exp)
                P.tt(V(MT[:, g * 512:(g + 1) * 512].rearrange("p (r l) -> p r l", r=4), h_MT), V(dec[db].rearrange("p (r l) -> p r l", r=4), h_dec[db]),
                     V(cbm[:, g * 128:(g + 1) * 128].unsqueeze(1).broadcast_to([128, 4, 128]), h_cbm), ALU.mult)

        def stage2(c):
            t = c // 4; s_ = c % 4; tb = t % 2; cb_ = c % 2
            cs = slice(s_ * 128, (s_ + 1) * 128)
            xb_ = xbt[tb]; hx_ = hxb[tb]
            ew, h_ew = ew2[cb_], h_ew2[cb_]
            Xb, h_Xb, Xw, h_Xw, tD, h_tD = Xb2[cb_], h_Xb2[cb_], Xw2[cb_], h_Xw2[cb_], tD2[cb_], h_tD2[cb_]
            Btm, h_Btm, MT, h_MT = Bt2[cb_], h_Bt2[cb_], MT2[cb_], h_MT2[cb_]
            for hd_ in range(16):
                ybank = 5 + hd_ // 8
                col = (hd_ % 8) * 64
                P.mm(V(banks[ybank][:, col:col + 64], bh[ybank]), V(MT[:, hd_ * 128:(hd_ + 1) * 128], h_MT),
                     V(Xb[:, hd_ * 64:(hd_ + 1) * 64], h_Xb))
            for hf in range(2):
                ybank = 5 + hf
                for gg in range(2):
                    g = hf * 2 + gg
                    P.mm(V(banks[7][:, gg * 256:(gg + 1) * 256], bh[7]), V(xb_[:, 12 + g, cs], hx_), V(hbf[:, g * 256:(g + 1) * 256], h_hbf))
                t1_ = t1[hf]; ht1 = h_t1[hf]
                P.tt(V(t1_.rearrange("p (h d) -> p h d", h=8), ht1), V(banks[7][:].rearrange("p (h d) -> p h d", h=8), bh[7]), V(bc16(ew, hf * 8, 8), h_ew), ALU.mult)
                P.tt(V(t1_, ht1), V(t1_, ht1), PS(ybank), ALU.add)
                P.tt(V(t1_, ht1), V(t1_, ht1), V(tD[:, hf * 512:(hf + 1) * 512], h_tD), ALU.add)
                P.tt(V(yg[:, hf * 512:(hf + 1) * 512], h_yg), V(t1_, ht1), V(szt[cb_][:, hf * 512:(hf + 1) * 512], hszt[cb_]), ALU.mult)
            for hf in range(2):
                for gg in range(2):
                    g = hf * 2 + gg
                    P.mm(V(banks[7][:, gg * 256:(gg + 1) * 256], bh[7]), V(Btm[:, g * 128:(g + 1) * 128], h_Btm), V(Xw[:, g * 256:(g + 1) * 256], h_Xw))
                hv = V(hst[:, hf * 512:(hf + 1) * 512].rearrange("p (h d) -> p h d", h=8), h_hst)
                P.tt(hv, hv, V(bc16(ew, 32 + hf * 8, 8), h_ew), ALU.mult)
                P.tt(V(hst[:, hf * 512:(hf + 1) * 512], h_hst), V(hst[:, hf * 512:(hf + 1) * 512], h_hst), PS(7), ALU.add)
            P.copy(V(hbf, h_hbf), V(hst, h_hst), eng="act")
            for g in range(4):
                P.act(V(junk, h_junk), V(yg[:, g * 256:(g + 1) * 256], h_yg), AF.Square, accum=V(ssq[:, g:g + 1], h_ssq))
            P.act(V(ssq[:, 4:8], h_ssq), V(ssq[:, 0:4], h_ssq), AF.Ln, bias=epsv, scale=1.0 / 256)
            P.act(V(ssq[:, 4:8], h_ssq), V(ssq[:, 4:8], h_ssq), AF.Exp, scale=-0.5)
            P.tt(V(yg.rearrange("p (g d) -> p g d", g=4), h_yg), V(yg.rearrange("p (g d) -> p g d", g=4), h_yg),
                 V(ssq[:, 4:8].unsqueeze(2).broadcast_to([128, 4, 256]), h_ssq), ALU.mult)
            P.tt(V(ynb, h_ynb), V(yg, h_yg), sng, ALU.mult)
            for k in range(8):
                P.transpose(V(banks[0][:].bitcast(BF16)[:, k * 128:(k + 1) * 128], bh[0]), V(ynb[:, k * 128:(k + 1) * 128], h_ynb), identv)
            P.copy(V(yst[tb][:, :, cs], h_yst[tb]), V(banks[0][:].bitcast(BF16)[:, 0:1024].rearrange("p (k s) -> p k s", k=8), bh[0]), eng="act")
            if s_ == 3:
                P.dma("pool", V(ys_d.rearrange("(k p) s -> p k s", p=128)[:, :, t * T:(t + 1) * T], dh("ys", t)), V(yst[tb], h_yst[tb]))

        stage1(0)
        for c in range(NCH):
            la = P.capture()
            if c + 1 < NCH:
                stage1(c + 1)
            lb = P.capture()
            stage2(c)
            P.interleave(la, lb)
        barrier()

    def phase_C(l, base=None, bk=(0, 1), do_barrier=True):
        A = Arena(big, ARENA0 if base is None else base, 103000)
        ct = [A.alloc(8 * T, F32).rearrange("p (k s) -> p k s", k=8) for _ in range(2)]; hct = [H(), H()]
        cb16 = A.alloc(8 * T, BF16).rearrange("p (k s) -> p k s", k=8); hcb = H()
        sq = A.alloc(8 * T, BF16).rearrange("p (k s) -> p k s", k=8); hsq = H()
        mean = A.alloc(T, F32); hmean = H()
        var = A.alloc(T, F32); hvar = H()
        tmp = A.alloc(T, F32); htmp = H()
        yo = [A.alloc(8 * T, BF16).rearrange("p (k s) -> p k s", k=8) for _ in range(2)]; hyo = [H(), H()]
        for t in range(NT):
            b = t % 2
            P.dma("sp", V(ct[b], hct[b]), V(fmview(cv_d, t), dall("cv")))
            P.copy(V(cb16, hcb), V(ct[b], hct[b]), eng="act")
            P.act(V(sq, hsq), V(ct[b], hct[b]), AF.Square)
            for k in range(8):
                P.mm(PS(bk[0]), onesb, V(cb16[:, k, :], hcb), start=(k == 0), stop=(k == 7))
            for k in range(8):
                P.mm(PS(bk[1]), onesb, V(sq[:, k, :], hsq), start=(k == 0), stop=(k == 7))
            P.ts(V(mean, hmean), PS(bk[0]), 1.0 / D, ALU.mult)
            P.tt(V(tmp, htmp), V(mean, hmean), V(mean, hmean), ALU.mult)
            P.stt(V(var, hvar), PS(bk[1]), 1.0 / D, V(tmp, htmp), ALU.mult, ALU.subtract)
            P.act(V(var, hvar), V(var, hvar), AF.Ln, bias=epsv)
            P.act(V(var, hvar), V(var, hvar), AF.Exp, scale=-0.5)
            for k in range(8):
                P.tt(V(tmp, htmp), V(ct[b][:, k, :], hct[b]), V(mean, hmean), ALU.subtract)
                P.tt(V(tmp, htmp), V(tmp, htmp), V(var, hvar), ALU.mult)
                P.act(V(yo[b][:, k, :], hyo[b]), V(tmp, htmp), AF.Silu, bias=pcol(C_LNB + k), scale=pcol(C_LNG + k))
            P.dma("pool", V(fmview(yc_d, t), dh("yc", t)), V(yo[b], hyo[b]))
        if do_barrier:
            barrier()
    def phase_M1(l):
        A = Arena(big, ARENA0, 103000)
        wq = A.alloc(3 * 1536, BF16).rearrange("p (k n) -> p k n", k=3); hwq = H()
        wqs = A.alloc(3 * 512, BF16).rearrange("p (k n) -> p k n", k=3); hwqs = H()
        wkn = A.alloc(2 * 1024, BF16).rearrange("p (k n) -> p k n", k=2); hwkn = H()
        wv = A.alloc(2 * 1024, BF16).rearrange("p (k n) -> p k n", k=2); hwv = H()
        wload(wq, wqb_d[l].rearrange("(k p) n -> p k n", p=128), hwq)
        qv = wqb_d[l].rearrange("(k p) (h c) -> p k h c", p=128, h=8)
        wqs4 = wqs.rearrange("p k (h c) -> p k h c", h=8)
        for k in range(3):
            wload(wqs4[:, k, :, 0:32], qv[:, k, :, 160:192], hwqs)
            wload(wqs4[:, k, :, 32:64], qv[:, k, :, 128:160], hwqs)
        kv4 = wkvb_d[l].rearrange("(k p) (h c) -> p k h c", p=128, h=8)
        for k in range(2):
            wload(wkn.rearrange("p k (h c) -> p k h c", h=8)[:, k], kv4[:, k, :, 0:128], hwkn)
            wload(wv.rearrange("p k (h c) -> p k h c", h=8)[:, k], kv4[:, k, :, 128:256], hwv)
        lt = [A.alloc(6 * T, F32).rearrange("p (k s) -> p k s", k=6) for _ in range(2)]; hlt = [H(), H()]
        rp = [A.alloc(T, F32) for _ in range(2)]; hrp = [H(), H()]
        rp2 = [A.alloc(T, F32) for _ in range(2)]; hrp2 = [H(), H()]
        sq = A.alloc(3 * T, BF16).rearrange("p (k s) -> p k s", k=3); hsq = H()
        rs = A.alloc(T, F32); hrs = H()
        qn = A.alloc(3 * T, BF16).rearrange("p (k s) -> p k s", k=3); hqn = H()
        kvn = A.alloc(2 * T, BF16).rearrange("p (k s) -> p k s", k=2); hkvn = H()
        sqh = [A.alloc(T, BF16) for _ in range(3)]; hsqh = [H() for _ in range(3)]
        rsh = [A.alloc(T, F32) for _ in range(3)]; hrsh = [H() for _ in range(3)]
        krb = A.alloc(T, F32); hkrb = H()
        ob = [A.alloc(T, BF16) for _ in range(6)]; hob = [H() for _ in range(6)]
        ta = A.alloc(T, F32); hta = H()
        tb_ = A.alloc(T, F32); htb = H()
        vb = [A.alloc(1024, BF16) for _ in range(2)]; hvb = [H(), H()]
        latv = lat_d.rearrange("(k p) s -> p k s", p=128)
        oi = 0
        for t in range(NT):
            b = t % 2; sl = slice(t * T, (t + 1) * T)
            P.dma("sp", V(lt[b], hlt[b]), V(latv[:, :, sl], dh("lat", t)))
            P.dma("sp", V(lt[b][0:64, 5, :], hlt[b]), V(lat_d[704:768, sl], dh("lat", t)))
            P.dma("sp", V(rp[b][0:64], hrp[b]), V(rope_d[0:64, sl], dh("rope", t)))
            P.dma("sp", V(rp2[b][0:64], hrp2[b]), V(rope_d[64:128, sl], dh("rope", t)))
            L_ = lt[b]; hL = hlt[b]
            cosv = V(rp[b][0:64], hrp[b]); sinv = V(rp2[b][0:64], hrp2[b])
            xnorm(L_[:, 0:3, :], hL, C_QAG, qn, hqn, sq, hsq, rs, hrs, 0, nk=3, inv_n=1.0 / 384)
            xnorm(L_[:, 3:5, :], hL, C_KVAG, kvn, hkvn, sq[:, 0:2, :], hsq, rs, hrs, 0, nk=2, inv_n=1.0 / 256)

            for h in range(8):
                for k in range(3):
                    P.mm(PS(1), V(wq[:, k, h * 192:h * 192 + 128], hwq), V(qn[:, k, :], hqn), start=(k == 0), stop=(k == 2))
                for k in range(3):
                    P.mm(PS(2, T, 64), V(wq[:, k, h * 192 + 128:h * 192 + 192], hwq), V(qn[:, k, :], hqn), start=(k == 0), stop=(k == 2))
                for k in range(3):
                    P.mm(PS(3, T, 64), V(wqs[:, k, h * 64:(h + 1) * 64], hwqs), V(qn[:, k, :], hqn), start=(k == 0), stop=(k == 2))
                for k in range(2):
                    P.mm(PS(4), V(wkn[:, k, h * 128:(h + 1) * 128], hwkn), V(kvn[:, k, :], hkvn), start=(k == 0), stop=(k == 1))
                specs = [(1, 128, 1.0 / 128, 5), (2, 64, 1.0 / 64, 6), (4, 128, 1.0 / 128, 7)]
                for i, (bank, parts, inv_n, nb) in enumerate(specs):
                    P.act(V(sqh[i][0:parts], hsqh[i]), PS(bank, T, parts), AF.Square)
                for i, (bank, parts, inv_n, nb) in enumerate(specs):
                    P.mm(PS(nb, T, parts), V(ones_b[0:parts, 0:parts], h_ob), V(sqh[i][0:parts], hsqh[i]))
                for i, (bank, parts, inv_n, nb) in enumerate(specs):
                    P.act(V(rsh[i][0:parts], hrsh[i]), PS(nb, T, parts), AF.Ln, bias=V(epsc[0:parts], h_eps), scale=inv_n)
                    P.act(V(rsh[i][0:parts], hrsh[i]), V(rsh[i][0:parts], hrsh[i]), AF.Exp, scale=-0.5)
                o1 = oi % 6; o2 = (oi + 1) % 6; o3 = (oi + 2) % 6; oi += 3
                P.stt(V(ob[o1], hob[o1]), PS(1), V(gsc[:, 0:1], h_gsc), V(rsh[0], hrsh[0]), ALU.mult, ALU.mult)
                P.stt(V(ob[o3], hob[o3]), PS(4), pcol(C_KNN), V(rsh[2], hrsh[2]), ALU.mult, ALU.mult)
                P.stt(V(ta[0:64], hta), PS(2, T, 64), V(gsc[0:64, 1:2], h_gsc), cosv, ALU.mult, ALU.mult)
                P.stt(V(tb_[0:64], htb), PS(3, T, 64), V(gsc[0:64, 2:3], h_gsc), sinv, ALU.mult, ALU.mult)
                P.tt(V(ta[0:64], hta), V(ta[0:64], hta), V(tb_[0:64], htb), ALU.add)
                P.tt(V(ob[o2][0:64], hob[o2]), V(ta[0:64], hta), V(rsh[1][0:64], hrsh[1]), ALU.mult)
                P.dma("pool", V(q_d[h, 0:128, sl], dh("q", t)), V(ob[o1], hob[o1]))
                P.dma("pool", V(q_d[h, 128:192, sl], dh("q", t)), V(ob[o2][0:64], hob[o2]))
                P.dma("pool", V(k_d[h, :, sl], dh("k", t)), V(ob[o3], hob[o3]))
            P.dma("sp", V(krb[0:64], hkrb), V(lat_d[640:704, sl], dh("lat", t)))
            krv = V(krb[0:64], hkrb); krsv = V(L_[0:64, 5, :], hL)
            P.act(V(sqh[1][0:64], hsqh[1]), krv, AF.Square)
            P.mm(PS(7, T, 64), V(ones_b[0:64, 0:64], h_ob), V(sqh[1][0:64], hsqh[1]))
            P.act(V(rsh[1][0:64], hrsh[1]), PS(7, T, 64), AF.Ln, bias=V(epsc[0:64], h_eps), scale=1.0 / 64)
            P.act(V(rsh[1][0:64], hrsh[1]), V(rsh[1][0:64], hrsh[1]), AF.Exp, scale=-0.5)
            P.stt(V(ta[0:64], hta), krv, pcol(C_KNR, 1, 64), cosv, ALU.mult, ALU.mult)
            P.stt(V(tb_[0:64], htb), krsv, pcol(C_KNRS, 1, 64), sinv, ALU.mult, ALU.mult)
            P.tt(V(ta[0:64], hta), V(ta[0:64], hta), V(tb_[0:64], htb), ALU.add)
            o = oi % 6; oi += 1
            P.tt(V(ob[o][0:64], hob[o]), V(ta[0:64], hta), V(rsh[1][0:64], hrsh[1]), ALU.mult)
            P.dma("pool", V(kr_d[:, sl], dh("kr", t)), V(ob[o][0:64], hob[o]))
            for s_ in range(4):
                vb_ = s_ % 2
                for half in range(2):
                    for k in range(2):
                        P.mm(PS(2 + half), V(kvn[:, k, s_ * 128:(s_ + 1) * 128], hkvn), V(wv[:, k, half * 512:(half + 1) * 512], hwv),
                             start=(k == 0), stop=(k == 1))
                    P.copy(V(vb[vb_][:, half * 512:(half + 1) * 512], hvb[vb_]), PS(2 + half), eng="act")
                r0 = t * T + s_ * 128
                P.dma("pool", V(v_d[r0:r0 + 128, :], dh("v", t)), V(vb[vb_], hvb[vb_]))
        barrier()

    def phase_M2(l, do_barrier=True):
        A = Arena(big, ARENA0, 103000)
        krt = A.alloc(S, BF16); hkr = H()
        P.memset(V(krt[64:128], hkr), 0.0)
        P.dma("sp", V(krt[0:64], hkr), V(kr_d, dall("kr")))
        kt_ = [A.alloc(S, BF16) for _ in range(2)]; hkt = [H(), H()]
        qnt = [A.alloc(S, BF16) for _ in range(2)]; hqn = [H(), H()]
        qrt = [A.alloc(S, BF16) for _ in range(2)]; hqr = [H(), H()]
        for b_ in range(2):
            P.memset(V(qrt[b_][64:128], hqr[b_]), 0.0)
        vt = [A.alloc(NCH * 128, BF16).rearrange("p (c d) -> p c d", c=NCH) for _ in range(2)]; hvt = [H(), H()]
        pt = [A.alloc(T, BF16) for _ in range(4)]; hpt = [H() for _ in range(4)]
        pd = [A.alloc(T, BF16) for _ in range(4)]; hpd = [H() for _ in range(4)]
        for j in range(4):
            P.memset(V(pd[j], hpd[j]), 0.0)
        dacc = [A.alloc(T, F32) for _ in range(2)]; hdacc = [H(), H()]
        rd = A.alloc(T, F32); hrd = H()
        dhi = A.alloc(T, BF16); hdhi = H()
        dlo = A.alloc(T, BF16); hdlo = H()
        oo = [A.alloc(T, BF16) for _ in range(2)]; hoo = [H(), H()]
        mk = masks.rearrange("p (j q) -> p j q", j=4)
        pt2 = [A.alloc(2 * T, BF16) for _ in range(3)]; hpt2 = [H() for _ in range(3)]
        items = [(h, qt) for h in range(8) for qt in range(NT)]

        def load_head(h):
            b = h % 2
            P.dma("sp", V(kt_[b], hkt[b]), V(k_d[h], dall("k")))
            P.dma("sp", V(qnt[b], hqn[b]), V(q_d[h, 0:128, :], dall("q")))
            P.dma("sp", V(qrt[b][0:64], hqr[b]), V(q_d[h, 128:192, :], dall("q")))
            P.dma("sp", V(vt[b], hvt[b]), V(v_d.rearrange("(c p) d -> p c d", p=128)[:, :, h * 128:(h + 1) * 128], dall("v")))

        def units_of(h, qt):
            return [(h, qt, "pair", k0) for k0 in range(0, 4 * qt, 2)] + [(h, qt, "diag", 4 * qt + j) for j in range(4)]
        allu = [u for (h, qt) in items for u in units_of(h, qt)]

        def st_tile(h_, qt_, kt, bank):
            b_ = h_ % 2
            ks = slice(kt * 128, (kt + 1) * 128)
            qs_ = slice(qt_ * T, (qt_ + 1) * T)
            P.mm(PS(bank), V(kt_[b_][:, ks], hkt[b_]), V(qnt[b_][:, qs_], hqn[b_]), start=True, stop=False)
            P.mm(PS(bank), V(krt[:, ks], hkr), V(qrt[b_][:, qs_], hqr[b_]), start=False, stop=True)

        def st_unit(ui):
            h_, qt_, kind, k0 = allu[ui]
            base = 2 * (ui % 2)
            st_tile(h_, qt_, k0, base)
            if kind == "pair":
                st_tile(h_, qt_, k0 + 1, base + 1)
        load_head(0)
        st_unit(0)
        p2 = 0
        for ui, (h, qt, kind, k0) in enumerate(allu):
            b = h % 2
            if kind == "pair" and k0 == 0 and qt == 1 and h + 1 < 8:
                load_head(h + 1)
            qs = slice(qt * T, (qt + 1) * T)
            nk = 4 * qt + 4
            ob_ = 4 + (qt % 2)
            db_ = 6 + (qt % 2)
            da = qt % 2
            base = 2 * (ui % 2)
            if ui + 1 < len(allu):
                st_unit(ui + 1)
            tiles = []
            if kind == "pair":
                pb_ = p2 % 3; p2 += 1
                src = V(psall[:, base * 512:(base + 2) * 512], [bh[base], bh[base + 1]])
                P.act(V(pt2[pb_], hpt2[pb_]), src, AF.Exp)
                tiles = [(k0, V(pt2[pb_][:, 0:T], hpt2[pb_])), (k0 + 1, V(pt2[pb_][:, T:2 * T], hpt2[pb_]))]
            else:
                j = k0 - 4 * qt
                bk = banks[base]
                P.act(V(pd[j][0:64, 128 * j:T], hpd[j]), V(bk[0:64, 128 * j:T], bh[base]), AF.Exp)
                P.act(V(pd[j][64:128, 128 * j + 64:T], hpd[j]), V(bk[64:128, 128 * j + 64:T], bh[base]), AF.Exp)
                tiles = [(k0, V(pd[j], hpd[j]))]
            for kt, pv_ in tiles:
                P.mm(PS(ob_), V(vt[b][:, kt, :], hvt[b]), pv_, start=(kt == 0), stop=(kt == nk - 1))
                if kt % 2 == 1:
                    P.mm(PS(db_), onesb, pv_, start=(kt == 1), stop=False)
                elif kt == 0:
                    P.copy(V(dacc[da], hdacc[da]), pv_)
                else:
                    P.tt(V(dacc[da], hdacc[da]), V(dacc[da], hdacc[da]), pv_, ALU.add)
            if kind == "diag" and k0 == nk - 1:
                P.copy(V(dhi, hdhi), V(dacc[da], hdacc[da]))
                P.tt(V(dlo, hdlo), V(dacc[da], hdacc[da]), V(dhi, hdhi), ALU.subtract)
                P.mm(PS(db_), onesb, V(dhi, hdhi), start=False, stop=False)
                P.mm(PS(db_), onesb, V(dlo, hdlo), start=False, stop=True)
                P.act(V(rd, hrd), PS(db_), AF.Ln)
                P.act(V(rd, hrd), V(rd, hrd), AF.Exp, scale=-1.0)
                o = qt % 2
                P.tt(V(oo[o], hoo[o]), PS(ob_), V(rd, hrd), ALU.mult)
                P.dma("pool", V(o_d[h * 128:(h + 1) * 128, qs], dh("o", qt)), V(oo[o], hoo[o]))
        if do_barrier:
            barrier()
        return A.off
    def phase_G(l):
        xsrc = xT_d if l == 0 else xres_d
        xsn = "xT" if l == 0 else "xres"
        A = Arena(big, ARENA0, 103000)
        wgate = A.alloc(8 * 3072, BF16).rearrange("p (k n) -> p k n", k=8); hwg = H()
        wbr = [A.alloc(8 * 1024, BF16).rearrange("p (k n) -> p k n", k=8) for _ in range(3)]; hwb = [H() for _ in range(3)]
        wo = A.alloc(8 * 1024, BF16).rearrange("p (k n) -> p k n", k=8); hwo = H()
        kp = lambda d: d.rearrange("(k p) n -> p k n", p=128)
        for b3 in range(3):
            wload(wgate[:, :, b3 * 1024:(b3 + 1) * 1024], kp(w_in_d[l])[:, :, OFF_GATE + b3 * 1024:OFF_GATE + (b3 + 1) * 1024], hwg)
        for b3, wd in enumerate((ssd_wo_d, conv_wo_d, mla_wo_d)):
            wload(wbr[b3], kp(wd[l]), hwb[b3])
        wload(wo, kp(wout_d[l]), hwo)
        TG = 256; NTG = S // TG
        fmg = lambda d, t: d.rearrange("(k p) s -> p k s", p=128)[:, :, t * TG:(t + 1) * TG]
        xt3 = [A.alloc(8 * TG, F32).rearrange("p (k s) -> p k s", k=8) for _ in range(3)]; hxt3 = [H(), H(), H()]
        ut2 = [A.alloc(8 * TG, BF16).rearrange("p (k s) -> p k s", k=8) for _ in range(2)]; hut2 = [H(), H()]
        sq = A.alloc(8 * TG, BF16).rearrange("p (k s) -> p k s", k=8); hsq = H()
        rs = A.alloc(TG, F32); hrs = H()
        br2 = [[A.alloc(8 * TG, BF16).rearrange("p (k s) -> p k s", k=8) for _ in range(3)] for _ in range(2)]
        hbr2 = [[H() for _ in range(3)] for _ in range(2)]
        mg = A.alloc(8 * TG, BF16).rearrange("p (k s) -> p k s", k=8); hmg = H()
        gt_ = [A.alloc(TG, F32) for _ in range(2)]; hgt = [H(), H()]
        macc = A.alloc(TG, F32); hmacc = H()
        srcs = [(ys_d, "ys"), (yc_d, "yc"), (o_d, "o")]

        def prep(t):
            b = t % 2; b3x = t % 3
            P.dma("sp", V(xt3[b3x], hxt3[b3x]), V(fmg(xsrc, t), dh(xsn, t // 2)))
            for b3, (d_, nm) in enumerate(srcs):
                P.dma("sp", V(br2[b][b3], hbr2[b][b3]), V(fmg(d_, t), dall(nm)))
            xnorm(xt3[b3x], hxt3[b3x], C_MIXG, ut2[b], hut2[b], sq, hsq, rs, hrs, 0, width=TG)
        prep(0)
        for t in range(NTG):
            b = t % 2; b3x = t % 3
            xt = xt3[b3x]; hxt = hxt3[b3x]; ut = ut2[b]; hut = hut2[b]; br = br2[b]; hbr = hbr2[b]
            if t + 1 < NTG:
                prep(t + 1)
            gi = 0
            for m in range(8):
                for b3 in range(3):
                    gb = 1 + gi % 2; yb = 3 + gi % 2; g2 = gi % 2; gi += 1
                    for k in range(8):
                        P.mm(PS(gb, TG), V(wgate[:, k, b3 * 1024 + m * 128:b3 * 1024 + (m + 1) * 128], hwg), V(ut[:, k, :], hut), start=(k == 0), stop=(k == 7))
                    for k in range(8):
                        P.mm(PS(yb, TG), V(wbr[b3][:, k, m * 128:(m + 1) * 128], hwb[b3]), V(br[b3][:, k, :], hbr[b3]), start=(k == 0), stop=(k == 7))
                    P.act(V(gt_[g2], hgt[g2]), PS(gb, TG), AF.Sigmoid, bias=pcol(C_GATEB + b3 * 8 + m))
                    if b3 == 0:
                        P.tt(V(macc, hmacc), V(gt_[g2], hgt[g2]), PS(yb, TG), ALU.mult)
                    else:
                        P.tt(V(gt_[g2], hgt[g2]), V(gt_[g2], hgt[g2]), PS(yb, TG), ALU.mult)
                        if b3 == 1:
                            P.tt(V(macc, hmacc), V(macc, hmacc), V(gt_[g2], hgt[g2]), ALU.add)
                        else:
                            P.tt(V(mg[:, m, :], hmg), V(macc, hmacc), V(gt_[g2], hgt[g2]), ALU.add)
            for n in range(8):
                ob_ = 5 + n % 2
                for m in range(8):
                    P.mm(PS(ob_, TG), V(wo[:, m, n * 128:(n + 1) * 128], hwo), V(mg[:, m, :], hmg), start=(m == 0), stop=(m == 7))
                P.tt(V(xt[:, n, :], hxt), V(xt[:, n, :], hxt), PS(ob_, TG), ALU.add)
            P.dma("pool", V(fmg(xres_d, t), dh("xres", t // 2)), V(xt, hxt))
        barrier()

    def phase_X(l):
        A = Arena(big, ARENA0, 103000)
        kp = lambda d: d.rearrange("(k p) n -> p k n", p=128)
        wkv = A.alloc(8 * 2048, BF16).rearrange("p (k n) -> p k n", k=8); hwkv = H()
        wload(wkv[:, :, 0:1024], kp(xwkv_d[l])[:, :, 0:1024], hwkv)
        wload(wkv[:, :, 1024:2048], kp(xwkv_d[l])[:, :, 1024:2048], hwkv)
        mt = A.alloc(8 * 256, F32).rearrange("p (k s) -> p k s", k=8); hmt = H()
        mn = A.alloc(8 * 256, BF16).rearrange("p (k s) -> p k s", k=8); hmn = H()
        sqm = A.alloc(8 * 256, BF16).rearrange("p (k s) -> p k s", k=8); hsqm = H()
        rsm = A.alloc(256, F32); hrsm = H()
        KX = A.alloc(8 * 256, BF16).rearrange("p (c s) -> p c s", c=8); hKX = H()
        VX = A.alloc(2 * 1024, BF16).rearrange("p (m d) -> p m d", m=2); hVX = H()
        sq2 = A.alloc(2 * 256, BF16).rearrange("p (c s) -> p c s", c=2); hsq2 = H()
        P.dma("sp", V(mt, hmt), V(memT_d.rearrange("(k p) s -> p k s", p=128), H()))
        xnorm(mt, hmt, C_MEMG, mn, hmn, sqm, hsqm, rsm, hrsm, 0, nk=8, width=256)
        for hx in range(4):
            for c in range(2):
                for k in range(8):
                    P.mm(PS(1 + c, 256), V(wkv[:, k, hx * 256 + c * 128:hx * 256 + (c + 1) * 128], hwkv), V(mn[:, k, :], hmn), start=(k == 0), stop=(k == 7))
                P.act(V(sq2[:, c, :], hsq2), PS(1 + c, 256), AF.Square)
            for c in range(2):
                P.mm(PS(3, 256), onesb, V(sq2[:, c, :], hsq2), start=(c == 0), stop=(c == 1))
            P.act(V(rsm, hrsm), PS(3, 256), AF.Ln, bias=epsv, scale=1.0 / 256)
            P.act(V(rsm, hrsm), V(rsm, hrsm), AF.Exp, scale=-0.5)
            for c in range(2):
                P.stt(V(KX[:, hx * 2 + c, :], hKX), PS(1 + c, 256), pcol(C_XKG + c), V(rsm, hrsm), ALU.mult, ALU.mult)
        for m in range(2):
            for half in range(2):
                for k in range(8):
                    P.mm(PS(4 + half), V(mn[:, k, m * 128:(m + 1) * 128], hmn), V(wkv[:, k, 1024 + half * 512:1024 + (half + 1) * 512], hwkv), start=(k == 0), stop=(k == 7))
                P.copy(V(VX[:, m, half * 512:(half + 1) * 512], hVX), PS(4 + half), eng="act")
        barrier()
        A2 = Arena(big, ARENA0, 103000)
        KX2 = A2.alloc(8 * 256, BF16).rearrange("p (c s) -> p c s", c=8); hKX2 = H()
        VX2 = A2.alloc(2 * 1024, BF16).rearrange("p (m d) -> p m d", m=2); hVX2 = H()
        P.copy(V(KX2, hKX2), V(KX, hKX)); P.copy(V(VX2, hVX2), V(VX, hVX))
        barrier()
        A = A2
        wq_ = A.alloc(8 * 1024, BF16).rearrange("p (k n) -> p k n", k=8); hwq = H()
        wo_ = A.alloc(8 * 1024, BF16).rearrange("p (k n) -> p k n", k=8); hwo = H()
        wload(wq_, kp(xwq_d[l]), hwq); wload(wo_, kp(xwo_d[l]), hwo)
        xt = [A.alloc(8 * T, F32).rearrange("p (k s) -> p k s", k=8) for _ in range(3)]; hxt = [H(), H(), H()]
        ht2 = [A.alloc(8 * T, BF16).rearrange("p (k s) -> p k s", k=8) for _ in range(2)]; hht2 = [H(), H()]
        sq = A.alloc(8 * T, BF16).rearrange("p (k s) -> p k s", k=8); hsq = H()
        rs = A.alloc(T, F32); hrs = H()
        sqq = A.alloc(2 * T, BF16).rearrange("p (c s) -> p c s", c=2); hsqq = H()
        rsq = A.alloc(T, F32); hrsq = H()
        qx = A.alloc(2 * T, BF16).rearrange("p (c s) -> p c s", c=2); hqx = H()
        pt = [A.alloc(T, BF16) for _ in range(2)]; hpt = [H(), H()]
        rd = A.alloc(T, F32); hrd = H()
        ox = A.alloc(8 * T, BF16).rearrange("p (k s) -> p k s", k=8); hox = H()
        def prep(t):
            b = t % 2; b3 = t % 3
            P.dma("sp", V(xt[b3], hxt[b3]), V(fmview(xres_d, t), dh("xres", t)))
            xnorm(xt[b3], hxt[b3], C_XATG, ht2[b], hht2[b], sq, hsq, rs, hrs, 0)
        prep(0)
        for t in range(NT):
            b = t % 2
            ht = ht2[b]; hht = hht2[b]
            b = t % 3
            if t + 1 < NT:
                prep(t + 1)
            for hx in range(4):
                for c in range(2):
                    for k in range(8):
                        P.mm(PS(1 + c), V(wq_[:, k, hx * 256 + c * 128:hx * 256 + (c + 1) * 128], hwq), V(ht[:, k, :], hht), start=(k == 0), stop=(k == 7))
                    P.act(V(sqq[:, c, :], hsqq), PS(1 + c), AF.Square)
                for c in range(2):
                    P.mm(PS(3), onesb, V(sqq[:, c, :], hsqq), start=(c == 0), stop=(c == 1))
                P.act(V(rsq, hrsq), PS(3), AF.Ln, bias=epsv, scale=1.0 / 256)
                P.act(V(rsq, hrsq), V(rsq, hrsq), AF.Exp, scale=-0.5)
                for c in range(2):
                    P.stt(V(qx[:, c, :], hqx), PS(1 + c), V(gsc[:, 3 + c:4 + c], h_gsc), V(rsq, hrsq), ALU.mult, ALU.mult)
                for m in range(2):
                    for c in range(2):
                        P.mm(PS(4 + m), V(KX2[:, hx * 2 + c, m * 128:(m + 1) * 128], hKX2), V(qx[:, c, :], hqx), start=(c == 0), stop=(c == 1))
                    P.act(V(pt[m], hpt[m]), PS(4 + m), AF.Exp)
                for m in range(2):
                    P.mm(PS(3), onesb, V(pt[m], hpt[m]), start=(m == 0), stop=(m == 1))
                P.act(V(rd, hrd), PS(3), AF.Ln)
                P.act(V(rd, hrd), V(rd, hrd), AF.Exp, scale=-1.0)
                for c2 in range(2):
                    for m in range(2):
                        P.mm(PS(6 + c2), V(VX2[:, m, hx * 256 + c2 * 128:hx * 256 + (c2 + 1) * 128], hVX2), V(pt[m], hpt[m]), start=(m == 0), stop=(m == 1))
                    P.tt(V(ox[:, hx * 2 + c2, :], hox), PS(6 + c2), V(rd, hrd), ALU.mult)
            for n in range(8):
                ob_ = 1 + n % 2
                for k in range(8):
                    P.mm(PS(ob_), V(wo_[:, k, n * 128:(n + 1) * 128], hwo), V(ox[:, k, :], hox), start=(k == 0), stop=(k == 7))
                P.tt(V(xt[b][:, n, :], hxt[b]), V(xt[b][:, n, :], hxt[b]), PS(ob_), ALU.add)
            P.dma("pool", V(fmview(xres_d, t), dh("xres", t)), V(xt[b], hxt[b]))
        barrier()

    def phase_F(l, last):
        kp = lambda d: d.rearrange("(k p) n -> p k n", p=128)
        toks = []
        for hh in range(2):
            A = Arena(big, ARENA0, 103000)
            w1g = A.alloc(8 * 1408, BF16).rearrange("p (k n) -> p k n", k=8); hw1g = H()
            w1u = A.alloc(8 * 1408, BF16).rearrange("p (k n) -> p k n", k=8); hw1u = H()
            w2 = A.alloc(11 * 1024, BF16).rearrange("p (k n) -> p k n", k=11); hw2 = H()
            wload(w1g, kp(fwi_d[l])[:, :, hh * 1408:(hh + 1) * 1408], hw1g)
            wload(w1u, kp(fwi_d[l])[:, :, 2816 + hh * 1408:2816 + (hh + 1) * 1408], hw1u)
            wload(w2, fwo_d[l, hh * 1408:(hh + 1) * 1408, :].rearrange("(k p) n -> p k n", p=128), hw2)
            xt = [A.alloc(8 * T, F32).rearrange("p (k s) -> p k s", k=8) for _ in range(3)]; hxt = [H(), H(), H()]
            pt_ = A.alloc(8 * T, F32).rearrange("p (k s) -> p k s", k=8); hpt_ = H()
            ht2 = [A.alloc(8 * T, BF16).rearrange("p (k s) -> p k s", k=8) for _ in range(2)]; hht2 = [H(), H()]
            sq = A.alloc(8 * T, BF16).rearrange("p (k s) -> p k s", k=8); hsq = H()
            rs = A.alloc(T, F32); hrs = H()
            ac = A.alloc(11 * T, BF16).rearrange("p (k s) -> p k s", k=11); hac = H()
            sg = [A.alloc(T, F32) for _ in range(2)]; hsg = [H(), H()]
            def prep(t):
                b = t % 2; b3 = t % 3
                P.dma("sp", V(xt[b3], hxt[b3]), V(fmview(xres_d, t), dh("xres", t)))
                xnorm(xt[b3], hxt[b3], C_FFNG, ht2[b], hht2[b], sq, hsq, rs, hrs, 0)
            prep(0)
            for t in range(NT):
                b = t % 2
                ht = ht2[b]; hht = hht2[b]
                b = t % 3
                if t + 1 < NT:
                    prep(t + 1)
                if hh == 1:
                    P.dma("sp", V(pt_, hpt_), V(fmview(fp_d, t), dh("fp", t)))
                for j in range(11):
                    s2 = j % 2
                    for k in range(8):
                        P.mm(PS(1 + s2), V(w1g[:, k, j * 128:(j + 1) * 128], hw1g), V(ht[:, k, :], hht), start=(k == 0), stop=(k == 7))
                    for k in range(8):
                        P.mm(PS(3 + s2), V(w1u[:, k, j * 128:(j + 1) * 128], hw1u), V(ht[:, k, :], hht), start=(k == 0), stop=(k == 7))
                    P.act(V(sg[s2], hsg[s2]), PS(1 + s2), AF.Silu)
                    P.tt(V(ac[:, j, :], hac), V(sg[s2], hsg[s2]), PS(3 + s2), ALU.mult)
                for n in range(8):
                    ob_ = 5 + n % 2
                    for j in range(11):
                        P.mm(PS(ob_), V(w2[:, j, n * 128:(n + 1) * 128], hw2), V(ac[:, j, :], hac), start=(j == 0), stop=(j == 10))
                    if hh == 0:
                        P.copy(V(xt[b][:, n, :], hxt[b]), PS(ob_), eng="act")
                    else:
                        P.tt(V(xt[b][:, n, :], hxt[b]), V(xt[b][:, n, :], hxt[b]), PS(ob_), ALU.add)
                        P.tt(V(xt[b][:, n, :], hxt[b]), V(xt[b][:, n, :], hxt[b]), V(pt_[:, n, :], hpt_), ALU.add)
                if hh == 0:
                    P.dma("pool", V(fmview(fp_d, t), dh("fp", t)), V(xt[b], hxt[b]))
                else:
                    dst = out_d if last else xres_d
                    tk = P.dma("pool", V(fmview(dst, t), dh("out" if last else "xres", t)), V(xt[b], hxt[b]))
                    toks.append(tk)
            barrier()
        return toks

    def phase_M2C(l):
        la = P.capture()
        end = phase_M2(l, do_barrier=False)
        lb = P.capture()
        phase_C(l, base=end, bk=(6, 7), do_barrier=False)
        P.interleave(la, lb)
        barrier()

    phases = {"A": phase_A, "S": phase_S, "C": phase_C, "M1": phase_M1, "M2": phase_M2, "G": phase_G, "X": phase_X, "M2C": phase_M2C}
    order = ["A", "S", "C", "M1", "M2", "G", "X", "F"]
    P.marks = []
    for l in range(L):
        for ph in order:
            if only is not None and ph not in only:
                continue
            P.marks.append((l, ph, P.eng["dve"].n))
            if ph == "F":
                final_toks = phase_F(l, l == L - 1)
            else:
                phases[ph](l)
    P.finish(final_toks)
    return P


from concourse.bass_utils import run_bass_kernel_spmd


def _fm(v, nch):
    return np.ascontiguousarray(np.asarray(v, np.float32).reshape(nch, 128).T)


def _consts():
    c = np.zeros((128, NCST), np.float32)
    i = np.arange(128)
    c[:, K_TRI:K_TRI + 128] = (i[:, None] <= i[None, :])
    c[:, K_GT:K_GT + 128] = (i[:, None] > i[None, :])
    c[:, K_ID:K_ID + 128] = np.eye(128)
    f = np.arange(32)
    inv = (10000.0 ** (-(2 * f).astype(np.float32) / np.float32(64))).astype(np.float32)
    c[0:32, K_INV] = inv; c[32:64, K_INV] = inv
    q = np.arange(512)
    for j in range(4):
        c[:, K_MASK + j * 512:K_MASK + (j + 1) * 512] = ((2 * j + i[:, None] // 64) <= (q[None, :] // 64))
    return c


def _pack_params(inp):
    L = 4
    pp = np.zeros((L, 128, NPP), np.float32)
    pb = np.zeros((L, NPB), np.float32)
    for l in range(L):
        p = pp[l]
        p[:, C_MIXG:C_MIXG + 8] = _fm(inp["mix_norm_g"][l], 8)
        p[:, C_XATG:C_XATG + 8] = _fm(inp["xattn_norm_g"][l], 8)
        p[:, C_FFNG:C_FFNG + 8] = _fm(inp["ffn_norm_g"][l], 8)
        p[:, C_MEMG:C_MEMG + 8] = _fm(inp["mem_norm_g"][l], 8)
        p[:, C_SCW:C_SCW + 64] = np.asarray(inp["ssd_conv_w"][l]).T.reshape(16, 128, 4).transpose(1, 0, 2).reshape(128, 64)
        p[:, C_SCB:C_SCB + 16] = _fm(inp["ssd_conv_b"][l], 16)
        p[:, C_CDW:C_CDW + 248] = np.asarray(inp["conv_dw_w"][l]).T.reshape(8, 128, 31).transpose(1, 0, 2).reshape(128, 248)
        p[:, C_CDB:C_CDB + 8] = _fm(inp["conv_dw_b"][l], 8)
        p[:, C_LNG:C_LNG + 8] = _fm(inp["conv_ln_g"][l], 8)
        p[:, C_LNB:C_LNB + 8] = _fm(inp["conv_ln_b"][l], 8)
        p[:, C_QAG:C_QAG + 3] = _fm(inp["mla_q_a_g"][l], 3)
        p[:, C_KVAG:C_KVAG + 2] = _fm(inp["mla_kv_a_g"][l], 2)
        for (cn, cr, crs, g) in ((C_QNN, C_QNR, C_QNRS, inp["mla_q_norm_g"][l]), (C_KNN, C_KNR, C_KNRS, inp["mla_k_norm_g"][l])):
            g = np.asarray(g, np.float32)
            p[:, cn] = g[0:128]
            p[0:64, cr] = g[128:192]
            p[0:64, crs] = np.concatenate([g[160:192], g[128:160]])
        p[:, C_GATEB:C_GATEB + 24] = np.asarray(inp["gate_b"][l], np.float32).reshape(24, 128).T
        p[:, C_XQG:C_XQG + 2] = _fm(inp["xattn_q_norm_g"][l], 2)
        p[:, C_XKG:C_XKG + 2] = _fm(inp["xattn_k_norm_g"][l], 2)
        pb[l, B_DTB:B_DTB + 16] = inp["ssd_dt_bias"][l]
        pb[l, B_ALOG:B_ALOG + 16] = inp["ssd_a_log"][l]
        pb[l, B_DSK:B_DSK + 16] = inp["ssd_d"][l]
        pb[l, B_SNG:B_SNG + 1024] = inp["ssd_norm_g"][l]
    return pp, pb


WNAMES = ["w_in", "ssd_w_out", "conv_w_out", "mla_w_q_b", "mla_w_kv_b", "mla_w_o", "w_out", "xattn_w_q", "xattn_w_kv",
          "xattn_w_o", "ffn_w_in", "ffn_w_out"]


def make_in_maps(inp, cores):
    pp, pb = _pack_params(inp)
    cst = _consts()
    shared = {n: np.ascontiguousarray(np.asarray(inp[n], np.float32)) for n in WNAMES}
    shared.update(cst=cst, pp=pp, pb=pb)
    maps = []
    for b in cores:
        m = dict(shared)
        m["xT"] = np.ascontiguousarray(np.asarray(inp["x"][b], np.float32).T)
        m["memT"] = np.ascontiguousarray(np.asarray(inp["mem"][b], np.float32).T)
        m["pos"] = np.ascontiguousarray(np.asarray(inp["positions"][b], np.int32)[None, :])
        maps.append(m)
    return maps


_CACHE = {}


def kernel(**inputs):
    if "P" not in _CACHE:
        _CACHE["P"] = build(L=4)
    P = _CACHE["P"]
    maps = make_in_maps(inputs, list(range(8)))
    res = run_bass_kernel_spmd(P.nc, maps, core_ids=list(range(8)))
    out = np.stack([np.ascontiguousarray(r["outT"].T) for r in res.results], axis=0)
    return out.astype(np.float32)
```

```python
import contextlib
import numpy as np
import concourse.bass as bass
import concourse.mybir as mybir

F32 = mybir.dt.float32
BF16 = mybir.dt.bfloat16
I32 = mybir.dt.int32
AF = mybir.ActivationFunctionType
ALU = mybir.AluOpType
AX = mybir.AxisListType

ENGS = ("pe", "act", "dve", "pool", "sp")
NDMA_SEMS = 12


class H:
    __slots__ = ("w", "r")

    def __init__(self):
        self.w = None
        self.r = []


class V:
    __slots__ = ("ap", "hs")

    def __init__(self, ap, hs):
        self.ap = ap
        self.hs = hs if isinstance(hs, (list, tuple)) else [hs]


class Eng:
    def __init__(self, name, idx):
        self.name = name
        self.idx = idx
        self.n = 0
        self.seen = [0] * len(ENGS)
        self.seen_dma = {}
        self.ops = []
        self.dma_count = 0


class Prog:
    def __init__(self):
        self.nc = bass.Bass("TRN2", target_bir_lowering=False)
        self.stack = contextlib.ExitStack()
        self.eng = {n: Eng(n, i) for i, n in enumerate(ENGS)}
        self.snaps = {}
        self.ntens = 0
        self.nwaits = 0

    def sb(self, shape, dtype, name=None):
        self.ntens += 1
        t = self.stack.enter_context(self.nc.sbuf_tensor(name or f"sb{self.ntens}", list(shape), dtype))
        return t

    def ps(self, shape, dtype, name=None):
        self.ntens += 1
        t = self.stack.enter_context(self.nc.psum_tensor(name or f"ps{self.ntens}", list(shape), dtype))
        return t

    def dram(self, name, shape, dtype, kind="Internal"):
        return self.nc.dram_tensor(name, list(shape), dtype, kind=kind).ap()

    def _deps(self, reads, writes):
        deps = {}

        def add(tok):
            k, v = tok
            if deps.get(k, 0) < v:
                deps[k] = v
        for h in reads:
            if h.w is not None:
                add(h.w)
        for h in writes:
            if h.w is not None:
                add(h.w)
            for t in h.r:
                add(t)
        return deps

    def _waits(self, e, deps):
        waits = []
        for k, v in deps.items():
            if isinstance(k, int):
                if k == e.idx and e.name == "pe":
                    continue
                if e.seen[k] >= v:
                    continue
                waits.append((k, v))
            else:
                if e.seen_dma.get(k, 0) >= v:
                    continue
                waits.append((k, v))
        for k, v in waits:
            if isinstance(k, int):
                if e.seen[k] < v:
                    e.seen[k] = v
            else:
                e.seen_dma[k] = v
            snap = self.snaps.get((k, v))
            if snap is not None:
                for i in range(len(ENGS)):
                    if e.seen[i] < snap[i]:
                        e.seen[i] = snap[i]
        return waits

    def capture(self):
        self._cap = []
        return self._cap

    def end_capture(self):
        self._cap = None

    def interleave(self, la, lb):
        self._cap = None
        na, nb = len(la), len(lb)
        ia = ib = 0
        while ia < na or ib < nb:
            if ib >= nb or (ia < na and ia * nb <= ib * na):
                kind, args, kw = la[ia]; ia += 1
            else:
                kind, args, kw = lb[ib]; ib += 1
            (self.op if kind == "op" else self.dma)(*args, **kw)

    def op(self, engname, fn, reads, writes):
        if getattr(self, "_cap", None) is not None:
            self._cap.append(("op", (engname, fn, reads, writes), {}))
            return None
        e = self.eng[engname]
        rh = [h for v in reads for h in v.hs]
        wh = [h for v in writes for h in v.hs]
        deps = self._deps(rh, wh)
        waits = self._waits(e, deps)
        e.n += 1
        tok = (e.idx, e.n)
        self.snaps[tok] = tuple(e.seen)
        e.ops.append((waits, fn, None))
        self.nwaits += len(waits)
        for h in rh:
            h.r.append(tok)
        for h in wh:
            h.w = tok
            h.r = []
        return tok

    def dma(self, qname, out, in_, **kw):
        if getattr(self, "_cap", None) is not None:
            self._cap.append(("dma", (qname, out, in_), kw))
            return None
        e = self.eng[qname]
        rh = list(in_.hs)
        wh = list(out.hs)
        deps = self._deps(rh, wh)
        k = e.dma_count % NDMA_SEMS
        rnd = e.dma_count // NDMA_SEMS
        e.dma_count += 1
        key = (qname, k)
        if rnd > 0:
            if deps.get(key, 0) < 16 * rnd:
                deps[key] = 16 * rnd
        waits = self._waits(e, deps)
        tok = (key, 16 * (rnd + 1))
        self.snaps[tok] = tuple(e.seen)
        oap, iap = out.ap, in_.ap
        e.ops.append((waits, lambda eng: eng.dma_start(out=oap, in_=iap, **kw), key))
        self.nwaits += len(waits)
        for h in rh:
            h.r.append(tok)
        for h in wh:
            h.w = tok
            h.r = []
        return tok

    def mm(self, out, lhsT, rhs, start=True, stop=True):
        o, l, r = out.ap, lhsT.ap, rhs.ap
        return self.op("pe", lambda e: e.matmul(o, l, r, start=start, stop=stop), [lhsT, rhs], [out])

    def transpose(self, out, in_, ident):
        o, i, d = out.ap, in_.ap, ident.ap
        return self.op("pe", lambda e: e.transpose(o, i, d), [in_, ident], [out])

    def act(self, out, in_, func, bias=None, scale=None, eng="act", accum=None):
        o, i = out.ap, in_.ap
        reads = [in_]
        kw = {}
        if bias is not None:
            if isinstance(bias, V):
                reads.append(bias)
                kw["bias"] = bias.ap
            else:
                kw["bias"] = bias
        if scale is not None:
            if isinstance(scale, V):
                reads.append(scale)
                kw["scale"] = scale.ap
            else:
                kw["scale"] = scale
        writes = [out]
        if accum is not None:
            writes.append(accum)
            kw["accum_out"] = accum.ap
        return self.op("act", lambda e: e.activation(o, i, func, **kw), reads, writes)

    def tt(self, out, a, b, op, eng="dve"):
        o, x, y = out.ap, a.ap, b.ap
        return self.op(eng, lambda e: e.tensor_tensor(o, x, y, op), [a, b], [out])

    def ts(self, out, a, s1, op0, s2=None, op1=None, eng="dve", accum=None):
        o, x = out.ap, a.ap
        reads = [a]
        if isinstance(s1, V):
            reads.append(s1)
            s1 = s1.ap
        if isinstance(s2, V):
            reads.append(s2)
            s2 = s2.ap
        writes = [out]
        kw = {}
        if accum is not None:
            writes.append(accum)
            kw["accum_out"] = accum.ap
        if op1 is None:
            return self.op(eng, lambda e: e.tensor_scalar(o, x, s1, None, op0, **kw), reads, writes)
        return self.op(eng, lambda e: e.tensor_scalar(o, x, s1, s2, op0, op1, **kw), reads, writes)

    def stt(self, out, a, s, b, op0, op1, eng="dve"):
        o, x, y = out.ap, a.ap, b.ap
        reads = [a, b]
        if isinstance(s, V):
            reads.append(s)
            s = s.ap
        return self.op(eng, lambda e: e.scalar_tensor_tensor(o, x, s, y, op0, op1), reads, [out])

    def copy(self, out, in_, eng="dve"):
        o, i = out.ap, in_.ap
        if eng == "act":
            return self.op("act", lambda e: e.copy(o, i), [in_], [out])
        return self.op(eng, lambda e: e.tensor_copy(o, i), [in_], [out])

    def memset(self, out, val, eng="dve"):
        o = out.ap
        return self.op(eng, lambda e: e.memset(o, val), [], [out])

    def recip(self, out, in_):
        o, i = out.ap, in_.ap
        return self.op("dve", lambda e: e.reciprocal(o, i), [in_], [out])

    def finish(self, final_tokens):
        nc = self.nc
        sems = {}
        for i, n in enumerate(ENGS):
            sems[i] = self.stack.enter_context(nc.semaphore(f"s_{n}"))
        for n in ENGS:
            e = self.eng[n]
            if e.dma_count:
                for k in range(min(NDMA_SEMS, e.dma_count)):
                    sems[(n, k)] = self.stack.enter_context(nc.semaphore(f"d_{n}{k}"))
        sp = self.eng["sp"]
        fin = {}
        for k, v in final_tokens:
            if fin.get(k, 0) < v:
                fin[k] = v
        sp.ops.append((list(fin.items()), None, None))
        block = self.stack.enter_context(nc.Block())
        hw = {"pe": block.tensor, "act": block.scalar, "dve": block.vector, "pool": block.gpsimd, "sp": block.sync}

        def make(e):
            def body(eng):
                own = sems[e.idx]
                for waits, fn, dkey in e.ops:
                    for k, v in waits:
                        eng.wait_ge(sems[k], v)
                    if fn is None:
                        continue
                    ins = fn(eng)
                    if dkey is None:
                        ins.then_inc(own, 1)
                    else:
                        ins.then_inc(sems[dkey], 16)
            return body
        for n in ENGS:
            e = self.eng[n]
            if e.ops:
                hw[n](make(e))
        self.stack.close()
        return nc


S = 4096; D = 1024; T = 512; NT = 8; NCH = 32
EPS = 1e-6
OFF_Z, OFF_XBC, OFF_DT, OFF_GA, OFF_GG, OFF_QL, OFF_KV, OFF_KR, OFF_GATE = 0, 1024, 3072, 3088, 4112, 5136, 5520, 5776, 5840
C_MIXG, C_XATG, C_FFNG, C_MEMG, C_SCW, C_SCB, C_CDW, C_CDB, C_LNG, C_LNB = 0, 8, 16, 24, 32, 96, 112, 360, 368, 376
C_QAG, C_KVAG, C_QNN, C_QNR, C_QNRS, C_KNN, C_KNR, C_KNRS, C_GATEB, C_XQG, C_XKG, NPP = 384, 387, 389, 390, 391, 392, 393, 394, 395, 419, 421, 424
B_DTB, B_ALOG, B_DSK, B_SNG, NPB = 0, 16, 32, 48, 1072
K_TRI, K_GT, K_ID, K_INV, K_MASK, NCST = 0, 128, 256, 384, 385, 385 + 2048


class Arena:
    def __init__(self, big, base, limit):
        self.big, self.off, self.limit = big, base, limit

    def alloc(self, n, dtype, parts=128):
        nb = n * (4 if dtype in (F32, I32) else 2)
        nb = (nb + 63) // 64 * 64
        ne = nb // 2
        assert self.off + ne <= self.limit, ("SBUF arena overflow", self.off + ne, self.limit)
        ap = self.big[:, self.off:self.off + ne]
        self.off += ne
        if dtype != BF16:
            ap = ap.bitcast(dtype)
        return ap[:, 0:n]


def build(L=4, dbg=False, only=None):
    P = Prog(); nc = P.nc
    kind_s = "ExternalOutput" if dbg else "Internal"
    din = lambda n, s, dt=F32: P.dram(n, s, dt, kind="ExternalInput")
    xT_d = din("xT", [D, S]); memT_d = din("memT", [D, 256]); pos_d = din("pos", [1, S], I32)
    cst_d = din("cst", [128, NCST]); pp_d = din("pp", [4, 128, NPP]); pb_d = din("pb", [4, NPB])
    w_in_d = din("w_in", [4, D, 8912]); ssd_wo_d = din("ssd_w_out", [4, D, D]); conv_wo_d = din("conv_w_out", [4, D, D])
    wqb_d = din("mla_w_q_b", [4, 384, 1536]); wkvb_d = din("mla_w_kv_b", [4, 256, 2048]); mla_wo_d = din("mla_w_o", [4, D, D])
    wout_d = din("w_out", [4, D, D]); xwq_d = din("xattn_w_q", [4, D, D]); xwkv_d = din("xattn_w_kv", [4, D, 2048])
    xwo_d = din("xattn_w_o", [4, D, D]); fwi_d = din("ffn_w_in", [4, D, 5632]); fwo_d = din("ffn_w_out", [4, 2816, D])
    out_d = P.dram("outT", [D, S], F32, kind="ExternalOutput")
    sc = lambda n, s, dt=F32: P.dram(n, s, dt, kind=kind_s)
    xres_d = sc("xres", [D, S]); sz_d = sc("sz", [S, D]); xbc_d = sc("xbc", [2048, S], BF16); cv_d = sc("cv", [D, S])
    lat_d = sc("lat", [768, S]); ys_d = sc("ys", [D, S], BF16); yc_d = sc("yc", [D, S], BF16)
    q_d = sc("qs", [8, 192, S], BF16); k_d = sc("ks", [8, 128, S], BF16); kr_d = sc("krs", [64, S], BF16)
    v_d = sc("vs", [S, D], BF16); o_d = sc("os", [D, S], BF16); fp_d = sc("fpart", [D, S]); rope_d = sc("rope", [128, S])
    hd = {}

    def dh(name, t):
        k = (name, t)
        if k not in hd:
            hd[k] = H()
        return hd[k]

    def dall(name, n=NT):
        return [dh(name, t) for t in range(n)]

    big = P.sb([128, 103000], BF16, name="big")
    psall = P.ps([128, 4096], F32, name="psall")[:]
    banks = [psall[:, i * 512:(i + 1) * 512] for i in range(8)]
    bh = [H() for _ in range(8)]

    def PS(i, n=512, parts=128, dt=F32):
        ap = banks[i][:]
        if dt == BF16:
            ap = ap.bitcast(BF16)
        return V(ap[0:parts, 0:n], bh[i])

    pers = Arena(big, 0, 8000)
    cst = pers.alloc(K_MASK, F32); h_cst = H()
    tri = V(cst[:, K_TRI:K_TRI + 128], h_cst); gt = V(cst[:, K_GT:K_GT + 128], h_cst)
    inv_c = V(cst[:, K_INV:K_INV + 1], h_cst)
    ident = pers.alloc(128, BF16); h_id = H(); identv = V(ident, h_id)
    ones_b = pers.alloc(128, BF16); h_ob = H(); onesb = V(ones_b, h_ob)
    ones_f = pers.alloc(128, F32); h_of = H(); onesf = V(ones_f, h_of)
    masks = pers.alloc(2048, BF16); h_mk = H()
    epsc = pers.alloc(1, F32); h_eps = H(); epsv = V(epsc, h_eps)
    ppt = pers.alloc(NPP, F32); h_pp = H()
    pbt = pers.alloc(NPB, F32); h_pb = H()
    abc = pers.alloc(16, F32); h_abc = H()
    gsc = pers.alloc(8, F32); h_gsc = H()
    ARENA0 = pers.off

    def pcol(c, n=1, parts=128):
        return V(ppt[0:parts, c:c + n], h_pp)

    def barrier():
        toks = {}
        for n in ENGS:
            e = P.eng[n]
            if e.n:
                toks[e.idx] = e.n
            for k in range(min(NDMA_SEMS, e.dma_count)):
                rnd = (e.dma_count - 1 - k) // NDMA_SEMS
                toks[(n, k)] = 16 * (rnd + 1)
        for n in ENGS:
            e = P.eng[n]
            waits = P._waits(e, dict(toks))
            if waits:
                e.ops.append((waits, None, None))

    P.dma("sp", V(cst, h_cst), V(cst_d[:, 0:K_MASK], H()))
    P.dma("pool", V(ident, h_id), V(cst_d[:, K_ID:K_ID + 128], H()))
    P.dma("pool", V(masks, h_mk), V(cst_d[:, K_MASK:K_MASK + 2048], H()))
    P.memset(onesb, 1.0); P.memset(onesf, 1.0); P.memset(epsv, EPS)

    def rope_tables():
        A = Arena(big, ARENA0, 103000)
        for t in range(NT):
            sl = slice(t * T, (t + 1) * T)
            pi_ = A.alloc(T, I32) if t == 0 else rope_tables.bufs[0]
            if t == 0:
                rope_tables.bufs = [pi_] + [A.alloc(T, F32) for _ in range(5)] + [A.alloc(T, I32)]
            pi_, pf, ang, kf, r, rc, ki = rope_tables.bufs
            hs = [H() for _ in range(7)]
            pi_v, pf_v, ang_v, kf_v, r_v, rc_v, ki_v = [V(a[0:64], h) for a, h in zip(rope_tables.bufs, hs)]
            P.dma("sp", pi_v, V(pos_d[0, sl].partition_broadcast(64), H()))
            P.copy(pf_v, pi_v)
            P.ts(ang_v, pf_v, V(cst[0:64, K_INV:K_INV + 1], h_cst), ALU.mult)
            P.ts(kf_v, ang_v, float(1.0 / (2 * np.pi)), ALU.mult)
            P.copy(ki_v, kf_v)
            P.copy(kf_v, ki_v)
            C1 = 6.28125; C2 = float(np.float32(2 * np.pi - 6.28125))
            P.stt(r_v, kf_v, -C1, ang_v, ALU.mult, ALU.add)
            P.stt(r_v, kf_v, -C2, r_v, ALU.mult, ALU.add)
            P.ts(r_v, r_v, 3.1415925, ALU.min, -3.1415925, ALU.max)
            P.ts(rc_v, r_v, float(np.pi / 2), ALU.is_gt, float(-2 * np.pi), ALU.mult)
            P.stt(rc_v, r_v, float(np.pi / 2), rc_v, ALU.add, ALU.add)
            P.ts(rc_v, rc_v, 3.1415925, ALU.min, -3.1415925, ALU.max)
            P.act(rc_v, rc_v, AF.Sin)
            P.act(r_v, r_v, AF.Sin)
            P.ts(V(r[0:32], hs[4]), V(r[0:32], hs[4]), -1.0, ALU.mult)
            P.dma("sp", V(rope_d[0:64, sl], dh("rope", t)), rc_v)
            P.dma("sp", V(rope_d[64:128, sl], dh("rope", t)), r_v)
            barrier()
    rope_tables()
    barrier()

    def wload(dst, src, hdst):
        return P.dma("pool", V(dst, hdst), V(src, H()))

    def fmview(d, t):
        return d.rearrange("(k p) s -> p k s", p=128)[:, :, t * T:(t + 1) * T]

    def xnorm(xt, hx, gcol, outb, hout, sq, hsq, rs, hrs, bank, nk=8, width=T, inv_n=1.0 / D):
        P.act(V(sq, hsq), V(xt, hx), AF.Square)
        for k in range(nk):
            P.mm(PS(bank, width), onesb, V(sq[:, k, :], hsq), start=(k == 0), stop=(k == nk - 1))
        P.act(V(rs, hrs), PS(bank, width), AF.Ln, bias=epsv, scale=inv_n)
        P.act(V(rs, hrs), V(rs, hrs), AF.Exp, scale=-0.5)
        for k in range(nk):
            P.stt(V(outb[:, k, :], hout), V(xt[:, k, :], hx), pcol(gcol + k), V(rs, hrs), ALU.mult, ALU.mult)

    dtraw = pers.alloc(NCH * 16, F32).rearrange("p (c h) -> p c h", c=NCH); h_dtraw = H()
    ARENA0 = pers.off
    final_toks = []

    def phase_A(l):
        xsrc = xT_d if l == 0 else xres_d
        xsn = "xT" if l == 0 else "xres"
        barrier()
        P.dma("sp", V(ppt, h_pp), V(pp_d[l], H()))
        P.dma("sp", V(pbt, h_pb), V(pb_d[l, :].partition_broadcast(128), H()))
        P.act(V(abc, h_abc), V(pbt[:, B_ALOG:B_ALOG + 16], h_pb), AF.Exp)
        P.ts(V(abc, h_abc), V(abc, h_abc), -1.0, ALU.mult)
        P.ts(V(gsc[:, 0:3], h_gsc), V(ppt[:, C_QNN:C_QNN + 3], h_pp), float(192 ** -0.5), ALU.mult)
        P.ts(V(gsc[:, 3:5], h_gsc), V(ppt[:, C_XQG:C_XQG + 2], h_pp), float(256 ** -0.5), ALU.mult)

        A = Arena(big, ARENA0, 103000)
        uT = A.alloc(8 * S, BF16).rearrange("p (k s) -> p k s", k=8); hu = [H() for _ in range(NT)]
        A1 = A.off
        xt = [A.alloc(8 * T, F32).rearrange("p (k s) -> p k s", k=8) for _ in range(2)]; hxt = [H(), H()]
        sq = A.alloc(8 * T, BF16).rearrange("p (k s) -> p k s", k=8); hsq = H()
        rs = A.alloc(T, F32); hrs = H()
        for t in range(NT):
            b = t % 2
            P.dma("sp", V(xt[b], hxt[b]), V(fmview(xsrc, t), dh(xsn, t)))
            xnorm(xt[b], hxt[b], C_MIXG, uT[:, :, t * T:(t + 1) * T], hu[t], sq, hsq, rs, hrs, 0)
        barrier()
        A.off = A1
        wz = A.alloc(8 * 1040, BF16).rearrange("p (k n) -> p k n", k=8); hwz = H()
        wload(wz[:, :, 0:1024], w_in_d[l].rearrange("(k p) n -> p k n", p=128)[:, :, OFF_Z:OFF_Z + 1024], hwz)
        wload(wz[:, :, 1024:1040], w_in_d[l].rearrange("(k p) n -> p k n", p=128)[:, :, OFF_DT:OFF_DT + 16], hwz)
        szb = [A.alloc(1024, F32) for _ in range(2)]; hszb = [H(), H()]
        for c in range(NCH):
            t = c // 4; b = c % 2
            us = lambda k: V(uT[:, k, c * 128:(c + 1) * 128], hu[t])
            for half in range(2):
                for k in range(8):
                    P.mm(PS(1 + half), us(k), V(wz[:, k, half * 512:(half + 1) * 512], hwz), start=(k == 0), stop=(k == 7))
            for k in range(8):
                P.mm(PS(3, 16), us(k), V(wz[:, k, 1024:1040], hwz), start=(k == 0), stop=(k == 7))
            for half in range(2):
                P.act(V(szb[b][:, half * 512:(half + 1) * 512], hszb[b]), PS(1 + half), AF.Silu)
            P.copy(V(dtraw[:, c, :], h_dtraw), PS(3, 16))
            P.dma("pool", V(sz_d[c * 128:(c + 1) * 128, :], dh("sz", t)), V(szb[b], hszb[b]))
        barrier()
        A.off = A1
        wg = [A.alloc(8 * 512, BF16).rearrange("p (k n) -> p k n", k=8) for _ in range(2)]; hwg = [H(), H()]
        pre = [A.alloc(S + 32, F32) for _ in range(2)]; hpre = [H(), H()]
        acc = [A.alloc(S, F32) for _ in range(2)]; hacc = [H(), H()]
        xo = [A.alloc(S, BF16) for _ in range(2)]; hxo = [H(), H()]
        for b in range(2):
            P.memset(V(pre[b][:, 0:32], hpre[b]), 0.0)
        win_v = w_in_d[l].rearrange("(k p) n -> p k n", p=128)
        wload(wg[0], win_v[:, :, OFF_XBC:OFF_XBC + 512], hwg[0])
        pend = None
        for j in range(16):
            g4 = j // 4; wb = g4 % 2; b = j % 2
            if j % 4 == 0 and g4 + 1 < 4:
                wload(wg[1 - wb], win_v[:, :, OFF_XBC + (g4 + 1) * 512:OFF_XBC + (g4 + 2) * 512], hwg[1 - wb])
            for t in range(NT):
                bank = 1 + (t % 2)
                for k in range(8):
                    P.mm(PS(bank), V(wg[wb][:, k, (j % 4) * 128:(j % 4 + 1) * 128], hwg[wb]),
                         V(uT[:, k, t * T:(t + 1) * T], hu[t]), start=(k == 0), stop=(k == 7))
                P.copy(V(pre[b][:, 32 + t * T:32 + (t + 1) * T], hpre[b]), PS(bank), eng="act")
            P.ts(V(acc[b], hacc[b]), V(pre[b][:, 29:29 + S], hpre[b]), pcol(C_SCW + j * 4 + 0), ALU.mult, pcol(C_SCB + j), ALU.add)
            for tap in range(1, 4):
                P.stt(V(acc[b], hacc[b]), V(pre[b][:, 29 + tap:29 + tap + S], hpre[b]), pcol(C_SCW + j * 4 + tap),
                      V(acc[b], hacc[b]), ALU.mult, ALU.add)
            if pend is not None:
                pend()

            def fin(b=b, j=j):
                P.act(V(xo[b], hxo[b]), V(acc[b], hacc[b]), AF.Silu)
                P.dma("pool", V(xbc_d[j * 128:(j + 1) * 128, :], dall("xbc")), V(xo[b], hxo[b]))
            pend = fin
        pend()
        barrier()
        A.off = A1
        wa = [A.alloc(8 * 128, BF16).rearrange("p (k n) -> p k n", k=8) for _ in range(2)]; hwa = [H(), H()]
        wgt = [A.alloc(8 * 128, BF16).rearrange("p (k n) -> p k n", k=8) for _ in range(2)]; hwgt = [H(), H()]
        vb = [A.alloc(S + 32, BF16) for _ in range(2)]; hvb = [[H() for _ in range(NT)] for _ in range(2)]; hvz = [H(), H()]
        dg = [A.alloc(31 * 128, BF16).rearrange("p (j n) -> p j n", j=31) for _ in range(2)]; hdg = [H(), H()]
        acc = [A.alloc(S, F32) for _ in range(2)]; hacc = [H(), H()]
        sg = [A.alloc(T, F32) for _ in range(2)]; hsg = [H(), H()]
        for b in range(2):
            P.memset(V(vb[b][:, 0:32], hvz[b]), 0.0)
        def prepj(j):
            b = j % 2
            wload(wa[b], win_v[:, :, OFF_GA + j * 128:OFF_GA + (j + 1) * 128], hwa[b])
            wload(wgt[b], win_v[:, :, OFF_GG + j * 128:OFF_GG + (j + 1) * 128], hwgt[b])
            for tap in range(31):
                P.ts(V(dg[b][:, tap, :], hdg[b]), identv, pcol(C_CDW + j * 31 + tap), ALU.mult)
        prepj(0)
        for j in range(8):
            b = j % 2
            if j + 1 < 8:
                prepj(j + 1)

            def glu(t):
                sb_ = t % 2
                for k in range(8):
                    P.mm(PS(1 + sb_), V(wa[b][:, k, :], hwa[b]), V(uT[:, k, t * T:(t + 1) * T], hu[t]), start=(k == 0), stop=(k == 7))
                for k in range(8):
                    P.mm(PS(3 + sb_), V(wgt[b][:, k, :], hwgt[b]), V(uT[:, k, t * T:(t + 1) * T], hu[t]), start=(k == 0), stop=(k == 7))
                P.act(V(sg[sb_], hsg[sb_]), PS(3 + sb_), AF.Sigmoid)
                P.tt(V(vb[b][:, 32 + t * T:32 + (t + 1) * T], hvb[b][t]), PS(1 + sb_), V(sg[sb_], hsg[sb_]), ALU.mult)

            def conv(t):
                cbk = 5 + t % 2
                rd_h = [hvb[b][t]] + ([hvb[b][t - 1]] if t > 0 else [hvz[b]])
                for tap in range(31):
                    o0 = 2 + tap + t * T
                    P.mm(PS(cbk), V(dg[b][:, tap, :], hdg[b]), V(vb[b][:, o0:o0 + T], rd_h), start=(tap == 0), stop=(tap == 30))
                P.act(V(acc[b][:, t * T:(t + 1) * T], hacc[b]), PS(cbk), AF.Identity, bias=pcol(C_CDB + j))
            glu(0)
            for t in range(NT):
                if t + 1 < NT:
                    glu(t + 1)
                conv(t)
            P.dma("pool", V(cv_d[j * 128:(j + 1) * 128, :], dall("cv")), V(acc[b], hacc[b]))
        barrier()
        A.off = A1
        wl = A.alloc(8 * 768, BF16).rearrange("p (k n) -> p k n", k=8); hwl = H()
        wload(wl[:, :, 0:704], win_v[:, :, OFF_QL:OFF_QL + 704], hwl)
        wload(wl[:, :, 704:736], win_v[:, :, OFF_KR + 32:OFF_KR + 64], hwl)
        wload(wl[:, :, 736:768], win_v[:, :, OFF_KR:OFF_KR + 32], hwl)
        lo = [A.alloc(T, F32) for _ in range(2)]; hlo = [H(), H()]
        segs = [(0, 128), (128, 128), (256, 128), (384, 128), (512, 128), (640, 64), (704, 64)]
        i = 0
        for t in range(NT):
            for (c0, m) in segs:
                b = i % 2; i += 1
                for k in range(8):
                    P.mm(PS(1 + b, T, m), V(wl[:, k, c0:c0 + m], hwl), V(uT[:, k, t * T:(t + 1) * T], hu[t]), start=(k == 0), stop=(k == 7))
                P.copy(V(lo[b][0:m], hlo[b]), PS(1 + b, T, m), eng="act")
                P.dma("pool", V(lat_d[c0:c0 + m, t * T:(t + 1) * T], dh("lat", t)), V(lo[b][0:m], hlo[b]))
        barrier()
    def phase_S(l):
        A = Arena(big, ARENA0, 103000)
        xbt = [A.alloc(16 * T, BF16).rearrange("p (k s) -> p k s", k=16) for _ in range(2)]; hxb = [H(), H()]
        szt = [A.alloc(1024, F32) for _ in range(2)]; hszt = [H(), H()]
        hst = A.alloc(1024, F32); h_hst = H()
        hbf = A.alloc(1024, BF16); h_hbf = H()
        D2 = lambda n, dt: ([A.alloc(n, dt) for _ in range(2)], [H(), H()])
        sm2, h_sm2 = D2(64, F32)
        ew2, h_ew2 = D2(48, F32)
        Xb2, h_Xb2 = D2(1024, BF16)
        Xw2, h_Xw2 = D2(1024, BF16)
        tD2, h_tD2 = D2(1024, F32)
        Bt2, h_Bt2 = D2(512, BF16)
        MT2, h_MT2 = D2(2048, BF16)
        cbm = A.alloc(512, F32); h_cbm = H()
        AGh = A.alloc(2048, BF16); h_AGh = H()
        AGl = A.alloc(2048, BF16); h_AGl = H()
        gtb = A.alloc(128, BF16); h_gtb = H()
        trib = A.alloc(128, BF16); h_trib = H()
        ahl = A.alloc(32, BF16); h_ahl = H()
        P.copy(V(gtb, h_gtb), gt); P.copy(V(trib, h_trib), tri)
        dec = [A.alloc(512, F32) for _ in range(2)]; h_dec = [H(), H()]
        yg = A.alloc(1024, F32); h_yg = H()
        t1 = [A.alloc(512, F32) for _ in range(2)]; h_t1 = [H(), H()]
        ssq = A.alloc(8, F32); h_ssq = H()
        junk = A.alloc(256, F32); h_junk = H()
        ynb = A.alloc(1024, BF16); h_ynb = H()
        yst = [A.alloc(8 * T, BF16).rearrange("p (k s) -> p k s", k=8) for _ in range(2)]; h_yst = [H(), H()]
        P.memset(V(hst, h_hst), 0.0); P.memset(V(hbf, h_hbf), 0.0)
        dtb = V(pbt[:, B_DTB:B_DTB + 16], h_pb)
        sng = V(pbt[:, B_SNG:B_SNG + 1024], h_pb)
        xview = xbc_d.rearrange("(k p) s -> p k s", p=128)
        SMALL = lambda lo, n: V(banks[2][:, 256 + lo:256 + lo + n], bh[2])

        def bc16(ap, lo, n):
            return ap[:, lo:lo + n].unsqueeze(2).broadcast_to([128, n, 64])
        v3 = lambda ap: ap.rearrange("p (h d) -> p h d", h=16)

        def stage1(c):
            t = c // 4; s_ = c % 4; tb = t % 2; cb_ = c % 2
            cs = slice(s_ * 128, (s_ + 1) * 128)
            if s_ == 0:
                P.dma("sp", V(xbt[tb], hxb[tb]), V(xview[:, :, t * T:(t + 1) * T], dall("xbc")))
            P.dma("sp", V(szt[cb_], hszt[cb_]), V(sz_d[c * 128:(c + 1) * 128, :], dh("sz", t)))
            xb_ = xbt[tb]; hx_ = hxb[tb]
            sm, h_sm, ew, h_ew = sm2[cb_], h_sm2[cb_], ew2[cb_], h_ew2[cb_]
            Xb, h_Xb, Xw, h_Xw, tD, h_tD = Xb2[cb_], h_Xb2[cb_], Xw2[cb_], h_Xw2[cb_], tD2[cb_], h_tD2[cb_]
            Btm, h_Btm, MT, h_MT = Bt2[cb_], h_Bt2[cb_], MT2[cb_], h_MT2[cb_]
            for k in range(8):
                P.transpose(V(banks[1][:].bitcast(BF16)[:, k * 128:(k + 1) * 128], bh[1]), V(xb_[:, k, cs], hx_), identv)
            for g in range(4):
                P.transpose(V(banks[2][:].bitcast(BF16)[:, g * 128:(g + 1) * 128], bh[2]), V(xb_[:, 8 + g, cs], hx_), identv)
            for g in range(4):
                P.mm(V(banks[3][:, g * 128:(g + 1) * 128], bh[3]), V(xb_[:, 8 + g, cs], hx_), V(xb_[:, 12 + g, cs], hx_))
            P.tt(V(sm[:, 0:16], h_sm), V(dtraw[:, c, :], h_dtraw), dtb, ALU.add)
            P.act(V(sm[:, 0:16], h_sm), V(sm[:, 0:16], h_sm), AF.Exp)
            P.act(V(sm[:, 0:16], h_sm), V(sm[:, 0:16], h_sm), AF.Ln, bias=1.0)
            P.tt(V(sm[:, 16:32], h_sm), V(sm[:, 0:16], h_sm), V(abc, h_abc), ALU.mult)
            av = V(sm[:, 16:32], h_sm)
            P.mm(SMALL(0, 16), tri, av); P.mm(SMALL(16, 16), gt, av); P.mm(SMALL(32, 16), onesf, av)
            P.act(V(ew, h_ew), SMALL(0, 48), AF.Exp)
            P.tt(V(sm[:, 32:48], h_sm), V(sm[:, 0:16], h_sm), V(ew[:, 16:32], h_ew), ALU.mult)
            P.copy(V(ahl[:, 0:16], h_ahl), av)
            P.tt(V(ahl[:, 16:32], h_ahl), av, V(ahl[:, 0:16], h_ahl), ALU.subtract)
            P.tt(V(AGh.rearrange("p (h s) -> p h s", h=16), h_AGh),
                 V(gtb.unsqueeze(1).broadcast_to([128, 16, 128]), h_gtb),
                 V(ahl[:, 0:16].unsqueeze(2).broadcast_to([128, 16, 128]), h_ahl), ALU.mult)
            P.tt(V(AGl.rearrange("p (h s) -> p h s", h=16), h_AGl),
                 V(gtb.unsqueeze(1).broadcast_to([128, 16, 128]), h_gtb),
                 V(ahl[:, 16:32].unsqueeze(2).broadcast_to([128, 16, 128]), h_ahl), ALU.mult)
            P.tt(V(cbm.rearrange("p (g l) -> p g l", g=4), h_cbm), V(banks[3][:].rearrange("p (g l) -> p g l", g=4), bh[3]),
                 V(cst[:, K_TRI:K_TRI + 128].unsqueeze(1).broadcast_to([128, 4, 128]), h_cst), ALU.mult)
            xsT = banks[1][:].bitcast(BF16)[:, 0:1024].rearrange("p (h d) -> p h d", h=16)
            P.tt(V(v3(Xb), h_Xb), V(xsT, bh[1]), V(bc16(sm, 0, 16), h_sm), ALU.mult)
            P.tt(V(v3(Xw), h_Xw), V(xsT, bh[1]), V(bc16(sm, 32, 16), h_sm), ALU.mult)
            P.tt(V(v3(tD), h_tD), V(xsT, bh[1]), V(bc16(pbt, B_DSK, 16), h_pb), ALU.mult)
            P.copy(V(Btm, h_Btm), V(banks[2][:].bitcast(BF16)[:, 0:512], bh[2]), eng="act")
            for g in range(4):
                db = g % 2
                for r in range(4):
                    hd_ = g * 4 + r
                    P.mm(V(banks[4][:, r * 128:(r + 1) * 128], bh[4]), V(AGh[:, hd_ * 128:(hd_ + 1) * 128], h_AGh), V(trib, h_trib), start=True, stop=False)
                    P.mm(V(banks[4][:, r * 128:(r + 1) * 128], bh[4]), V(AGl[:, hd_ * 128:(hd_ + 1) * 128], h_AGl), V(trib, h_trib), start=False, stop=True)
                P.act(V(dec[db], h_dec[db]), PS(4), AF.Exp)
                P.tt(V(MT[:, g * 512:(g + 1) * 512].rearrange("p (r l) -> p r l", r=4), h_MT), V(dec[db].rearrange("p (r l) -> p r l", r=4), h_dec[db]),
                     V(cbm[:, g * 128:(g + 1) * 128].unsqueeze(1).broadcast_to([128, 4, 128]), h_cbm), ALU.mult)

        def stage2(c):
            t = c // 4; s_ = c % 4; tb = t % 2; cb_ = c % 2
            cs = slice(s_ * 128, (s_ + 1) * 128)
            xb_ = xbt[tb]; hx_ = hxb[tb]
            ew, h_ew = ew2[cb_], h_ew2[cb_]
            Xb, h_Xb, Xw, h_Xw, tD, h_tD = Xb2[cb_], h_Xb2[cb_], Xw2[cb_], h_Xw2[cb_], tD2[cb_], h_tD2[cb_]
            Btm, h_Btm, MT, h_MT = Bt2[cb_], h_Bt2[cb_], MT2[cb_], h_MT2[cb_]
            for hd_ in range(16):
                ybank = 5 + hd_ // 8
                col = (hd_ % 8) * 64
                P.mm(V(banks[ybank][:, col:col + 64], bh[ybank]), V(MT[:, hd_ * 128:(hd_ + 1) * 128], h_MT),
                     V(Xb[:, hd_ * 64:(hd_ + 1) * 64], h_Xb))
            for hf in range(2):
                ybank = 5 + hf
                for gg in range(2):
                    g = hf * 2 + gg
                    P.mm(V(banks[7][:, gg * 256:(gg + 1) * 256], bh[7]), V(xb_[:, 12 + g, cs], hx_), V(hbf[:, g * 256:(g + 1) * 256], h_hbf))
                t1_ = t1[hf]; ht1 = h_t1[hf]
                P.tt(V(t1_.rearrange("p (h d) -> p h d", h=8), ht1), V(banks[7][:].rearrange("p (h d) -> p h d", h=8), bh[7]), V(bc16(ew, hf * 8, 8), h_ew), ALU.mult)
                P.tt(V(t1_, ht1), V(t1_, ht1), PS(ybank), ALU.add)
                P.tt(V(t1_, ht1), V(t1_, ht1), V(tD[:, hf * 512:(hf + 1) * 512], h_tD), ALU.add)
                P.tt(V(yg[:, hf * 512:(hf + 1) * 512], h_yg), V(t1_, ht1), V(szt[cb_][:, hf * 512:(hf + 1) * 512], hszt[cb_]), ALU.mult)
            for hf in range(2):
                for gg in range(2):
                    g = hf * 2 + gg
                    P.mm(V(banks[7][:, gg * 256:(gg + 1) * 256], bh[7]), V(Btm[:, g * 128:(g + 1) * 128], h_Btm), V(Xw[:, g * 256:(g + 1) * 256], h_Xw))
                hv = V(hst[:, hf * 512:(hf + 1) * 512].rearrange("p (h d) -> p h d", h=8), h_hst)
                P.tt(hv, hv, V(bc16(ew, 32 + hf * 8, 8), h_ew), ALU.mult)
                P.tt(V(hst[:, hf * 512:(hf + 1) * 512], h_hst), V(hst[:, hf * 512:(hf + 1) * 512], h_hst), PS(7), ALU.add)
            P.copy(V(hbf, h_hbf), V(hst, h_hst), eng="act")
            for g in range(4):
                P.act(V(junk, h_junk), V(yg[:, g * 256:(g + 1) * 256], h_yg), AF.Square, accum=V(ssq[:, g:g + 1], h_ssq))
            P.act(V(ssq[:, 4:8], h_ssq), V(ssq[:, 0:4], h_ssq), AF.Ln, bias=epsv, scale=1.0 / 256)
            P.act(V(ssq[:, 4:8], h_ssq), V(ssq[:, 4:8], h_ssq), AF.Exp, scale=-0.5)
            P.tt(V(yg.rearrange("p (g d) -> p g d", g=4), h_yg), V(yg.rearrange("p (g d) -> p g d", g=4), h_yg),
                 V(ssq[:, 4:8].unsqueeze(2).broadcast_to([128, 4, 256]), h_ssq), ALU.mult)
            P.tt(V(ynb, h_ynb), V(yg, h_yg), sng, ALU.mult)
            for k in range(8):
                P.transpose(V(banks[0][:].bitcast(BF16)[:, k * 128:(k + 1) * 128], bh[0]), V(ynb[:, k * 128:(k + 1) * 128], h_ynb), identv)
            P.copy(V(yst[tb][:, :, cs], h_yst[tb]), V(banks[0][:].bitcast(BF16)[:, 0:1024].rearrange("p (k s) -> p k s", k=8), bh[0]), eng="act")
            if s_ == 3:
                P.dma("pool", V(ys_d.rearrange("(k p) s -> p k s", p=128)[:, :, t * T:(t + 1) * T], dh("ys", t)), V(yst[tb], h_yst[tb]))

        stage1(0)
        for c in range(NCH):
            la = P.capture()
            if c + 1 < NCH:
                stage1(c + 1)
            lb = P.capture()
            stage2(c)
            P.interleave(la, lb)
        barrier()

    def phase_C(l, base=None, bk=(0, 1), do_barrier=True):
        A = Arena(big, ARENA0 if base is None else base, 103000)
        ct = [A.alloc(8 * T, F32).rearrange("p (k s) -> p k s", k=8) for _ in range(2)]; hct = [H(), H()]
        cb16 = A.alloc(8 * T, BF16).rearrange("p (k s) -> p k s", k=8); hcb = H()
        sq = A.alloc(8 * T, BF16).rearrange("p (k s) -> p k s", k=8); hsq = H()
        mean = A.alloc(T, F32); hmean = H()
        var = A.alloc(T, F32); hvar = H()
        tmp = A.alloc(T, F32); htmp = H()
        yo = [A.alloc(8 * T, BF16).rearrange("p (k s) -> p k s", k=8) for _ in range(2)]; hyo = [H(), H()]
        for t in range(NT):
            b = t % 2
            P.dma("sp", V(ct[b], hct[b]), V(fmview(cv_d, t), dall("cv")))
            P.copy(V(cb16, hcb), V(ct[b], hct[b]), eng="act")
            P.act(V(sq, hsq), V(ct[b], hct[b]), AF.Square)
            for k in range(8):
                P.mm(PS(bk[0]), onesb, V(cb16[:, k, :], hcb), start=(k == 0), stop=(k == 7))
            for k in range(8):
                P.mm(PS(bk[1]), onesb, V(sq[:, k, :], hsq), start=(k == 0), stop=(k == 7))
            P.ts(V(mean, hmean), PS(bk[0]), 1.0 / D, ALU.mult)
            P.tt(V(tmp, htmp), V(mean, hmean), V(mean, hmean), ALU.mult)
            P.stt(V(var, hvar), PS(bk[1]), 1.0 / D, V(tmp, htmp), ALU.mult, ALU.subtract)
            P.act(V(var, hvar), V(var, hvar), AF.Ln, bias=epsv)
            P.act(V(var, hvar), V(var, hvar), AF.Exp, scale=-0.5)
            for k in range(8):
                P.tt(V(tmp, htmp), V(ct[b][:, k, :], hct[b]), V(mean, hmean), ALU.subtract)
                P.tt(V(tmp, htmp), V(tmp, htmp), V(var, hvar), ALU.mult)
                P.act(V(yo[b][:, k, :], hyo[b]), V(tmp, htmp), AF.Silu, bias=pcol(C_LNB + k), scale=pcol(C_LNG + k))
            P.dma("pool", V(fmview(yc_d, t), dh("yc", t)), V(yo[b], hyo[b]))
        if do_barrier:
            barrier()
    def phase_M1(l):
        A = Arena(big, ARENA0, 103000)
        wq = A.alloc(3 * 1536, BF16).rearrange("p (k n) -> p k n", k=3); hwq = H()
        wqs = A.alloc(3 * 512, BF16).rearrange("p (k n) -> p k n", k=3); hwqs = H()
        wkn = A.alloc(2 * 1024, BF16).rearrange("p (k n) -> p k n", k=2); hwkn = H()
        wv = A.alloc(2 * 1024, BF16).rearrange("p (k n) -> p k n", k=2); hwv = H()
        wload(wq, wqb_d[l].rearrange("(k p) n -> p k n", p=128), hwq)
        qv = wqb_d[l].rearrange("(k p) (h c) -> p k h c", p=128, h=8)
        wqs4 = wqs.rearrange("p k (h c) -> p k h c", h=8)
        for k in range(3):
            wload(wqs4[:, k, :, 0:32], qv[:, k, :, 160:192], hwqs)
            wload(wqs4[:, k, :, 32:64], qv[:, k, :, 128:160], hwqs)
        kv4 = wkvb_d[l].rearrange("(k p) (h c) -> p k h c", p=128, h=8)
        for k in range(2):
            wload(wkn.rearrange("p k (h c) -> p k h c", h=8)[:, k], kv4[:, k, :, 0:128], hwkn)
            wload(wv.rearrange("p k (h c) -> p k h c", h=8)[:, k], kv4[:, k, :, 128:256], hwv)
        lt = [A.alloc(6 * T, F32).rearrange("p (k s) -> p k s", k=6) for _ in range(2)]; hlt = [H(), H()]
        rp = [A.alloc(T, F32) for _ in range(2)]; hrp = [H(), H()]
        rp2 = [A.alloc(T, F32) for _ in range(2)]; hrp2 = [H(), H()]
        sq = A.alloc(3 * T, BF16).rearrange("p (k s) -> p k s", k=3); hsq = H()
        rs = A.alloc(T, F32); hrs = H()
        qn = A.alloc(3 * T, BF16).rearrange("p (k s) -> p k s", k=3); hqn = H()
        kvn = A.alloc(2 * T, BF16).rearrange("p (k s) -> p k s", k=2); hkvn = H()
        sqh = [A.alloc(T, BF16) for _ in range(3)]; hsqh = [H() for _ in range(3)]
        rsh = [A.alloc(T, F32) for _ in range(3)]; hrsh = [H() for _ in range(3)]
        krb = A.alloc(T, F32); hkrb = H()
        ob = [A.alloc(T, BF16) for _ in range(6)]; hob = [H() for _ in range(6)]
        ta = A.alloc(T, F32); hta = H()
        tb_ = A.alloc(T, F32); htb = H()
        vb = [A.alloc(1024, BF16) for _ in range(2)]; hvb = [H(), H()]
        latv = lat_d.rearrange("(k p) s -> p k s", p=128)
        oi = 0
        for t in range(NT):
            b = t % 2; sl = slice(t * T, (t + 1) * T)
            P.dma("sp", V(lt[b], hlt[b]), V(latv[:, :, sl], dh("lat", t)))
            P.dma("sp", V(lt[b][0:64, 5, :], hlt[b]), V(lat_d[704:768, sl], dh("lat", t)))
            P.dma("sp", V(rp[b][0:64], hrp[b]), V(rope_d[0:64, sl], dh("rope", t)))
            P.dma("sp", V(rp2[b][0:64], hrp2[b]), V(rope_d[64:128, sl], dh("rope", t)))
            L_ = lt[b]; hL = hlt[b]
            cosv = V(rp[b][0:64], hrp[b]); sinv = V(rp2[b][0:64], hrp2[b])
            xnorm(L_[:, 0:3, :], hL, C_QAG, qn, hqn, sq, hsq, rs, hrs, 0, nk=3, inv_n=1.0 / 384)
            xnorm(L_[:, 3:5, :], hL, C_KVAG, kvn, hkvn, sq[:, 0:2, :], hsq, rs, hrs, 0, nk=2, inv_n=1.0 / 256)

            for h in range(8):
                for k in range(3):
                    P.mm(PS(1), V(wq[:, k, h * 192:h * 192 + 128], hwq), V(qn[:, k, :], hqn), start=(k == 0), stop=(k == 2))
                for k in range(3):
                    P.mm(PS(2, T, 64), V(wq[:, k, h * 192 + 128:h * 192 + 192], hwq), V(qn[:, k, :], hqn), start=(k == 0), stop=(k == 2))
                for k in range(3):
                    P.mm(PS(3, T, 64), V(wqs[:, k, h * 64:(h + 1) * 64], hwqs), V(qn[:, k, :], hqn), start=(k == 0), stop=(k == 2))
                for k in range(2):
                    P.mm(PS(4), V(wkn[:, k, h * 128:(h + 1) * 128], hwkn), V(kvn[:, k, :], hkvn), start=(k == 0), stop=(k == 1))
                specs = [(1, 128, 1.0 / 128, 5), (2, 64, 1.0 / 64, 6), (4, 128, 1.0 / 128, 7)]
                for i, (bank, parts, inv_n, nb) in enumerate(specs):
                    P.act(V(sqh[i][0:parts], hsqh[i]), PS(bank, T, parts), AF.Square)
                for i, (bank, parts, inv_n, nb) in enumerate(specs):
                    P.mm(PS(nb, T, parts), V(ones_b[0:parts, 0:parts], h_ob), V(sqh[i][0:parts], hsqh[i]))
                for i, (bank, parts, inv_n, nb) in enumerate(specs):
                    P.act(V(rsh[i][0:parts], hrsh[i]), PS(nb, T, parts), AF.Ln, bias=V(epsc[0:parts], h_eps), scale=inv_n)
                    P.act(V(rsh[i][0:parts], hrsh[i]), V(rsh[i][0:parts], hrsh[i]), AF.Exp, scale=-0.5)
                o1 = oi % 6; o2 = (oi + 1) % 6; o3 = (oi + 2) % 6; oi += 3
                P.stt(V(ob[o1], hob[o1]), PS(1), V(gsc[:, 0:1], h_gsc), V(rsh[0], hrsh[0]), ALU.mult, ALU.mult)
                P.stt(V(ob[o3], hob[o3]), PS(4), pcol(C_KNN), V(rsh[2], hrsh[2]), ALU.mult, ALU.mult)
                P.stt(V(ta[0:64], hta), PS(2, T, 64), V(gsc[0:64, 1:2], h_gsc), cosv, ALU.mult, ALU.mult)
                P.stt(V(tb_[0:64], htb), PS(3, T, 64), V(gsc[0:64, 2:3], h_gsc), sinv, ALU.mult, ALU.mult)
                P.tt(V(ta[0:64], hta), V(ta[0:64], hta), V(tb_[0:64], htb), ALU.add)
                P.tt(V(ob[o2][0:64], hob[o2]), V(ta[0:64], hta), V(rsh[1][0:64], hrsh[1]), ALU.mult)
                P.dma("pool", V(q_d[h, 0:128, sl], dh("q", t)), V(ob[o1], hob[o1]))
                P.dma("pool", V(q_d[h, 128:192, sl], dh("q", t)), V(ob[o2][0:64], hob[o2]))
                P.dma("pool", V(k_d[h, :, sl], dh("k", t)), V(ob[o3], hob[o3]))
            P.dma("sp", V(krb[0:64], hkrb), V(lat_d[640:704, sl], dh("lat", t)))
            krv = V(krb[0:64], hkrb); krsv = V(L_[0:64, 5, :], hL)
            P.act(V(sqh[1][0:64], hsqh[1]), krv, AF.Square)
            P.mm(PS(7, T, 64), V(ones_b[0:64, 0:64], h_ob), V(sqh[1][0:64], hsqh[1]))
            P.act(V(rsh[1][0:64], hrsh[1]), PS(7, T, 64), AF.Ln, bias=V(epsc[0:64], h_eps), scale=1.0 / 64)
            P.act(V(rsh[1][0:64], hrsh[1]), V(rsh[1][0:64], hrsh[1]), AF.Exp, scale=-0.5)
            P.stt(V(ta[0:64], hta), krv, pcol(C_KNR, 1, 64), cosv, ALU.mult, ALU.mult)
            P.stt(V(tb_[0:64], htb), krsv, pcol(C_KNRS, 1, 64), sinv, ALU.mult, ALU.mult)
            P.tt(V(ta[0:64], hta), V(ta[0:64], hta), V(tb_[0:64], htb), ALU.add)
            o = oi % 6; oi += 1
            P.tt(V(ob[o][0:64], hob[o]), V(ta[0:64], hta), V(rsh[1][0:64], hrsh[1]), ALU.mult)
            P.dma("pool", V(kr_d[:, sl], dh("kr", t)), V(ob[o][0:64], hob[o]))
            for s_ in range(4):
                vb_ = s_ % 2
                for half in range(2):
                    for k in range(2):
                        P.mm(PS(2 + half), V(kvn[:, k, s_ * 128:(s_ + 1) * 128], hkvn), V(wv[:, k, half * 512:(half + 1) * 512], hwv),
                             start=(k == 0), stop=(k == 1))
                    P.copy(V(vb[vb_][:, half * 512:(half + 1) * 512], hvb[vb_]), PS(2 + half), eng="act")
                r0 = t * T + s_ * 128
                P.dma("pool", V(v_d[r0:r0 + 128, :], dh("v", t)), V(vb[vb_], hvb[vb_]))
        barrier()

    def phase_M2(l, do_barrier=True):
        A = Arena(big, ARENA0, 103000)
        krt = A.alloc(S, BF16); hkr = H()
        P.memset(V(krt[64:128], hkr), 0.0)
        P.dma("sp", V(krt[0:64], hkr), V(kr_d, dall("kr")))
        kt_ = [A.alloc(S, BF16) for _ in range(2)]; hkt = [H(), H()]
        qnt = [A.alloc(S, BF16) for _ in range(2)]; hqn = [H(), H()]
        qrt = [A.alloc(S, BF16) for _ in range(2)]; hqr = [H(), H()]
        for b_ in range(2):
            P.memset(V(qrt[b_][64:128], hqr[b_]), 0.0)
        vt = [A.alloc(NCH * 128, BF16).rearrange("p (c d) -> p c d", c=NCH) for _ in range(2)]; hvt = [H(), H()]
        pt = [A.alloc(T, BF16) for _ in range(4)]; hpt = [H() for _ in range(4)]
        pd = [A.alloc(T, BF16) for _ in range(4)]; hpd = [H() for _ in range(4)]
        for j in range(4):
            P.memset(V(pd[j], hpd[j]), 0.0)
        dacc = [A.alloc(T, F32) for _ in range(2)]; hdacc = [H(), H()]
        rd = A.alloc(T, F32); hrd = H()
        dhi = A.alloc(T, BF16); hdhi = H()
        dlo = A.alloc(T, BF16); hdlo = H()
        oo = [A.alloc(T, BF16) for _ in range(2)]; hoo = [H(), H()]
        mk = masks.rearrange("p (j q) -> p j q", j=4)
        pt2 = [A.alloc(2 * T, BF16) for _ in range(3)]; hpt2 = [H() for _ in range(3)]
        items = [(h, qt) for h in range(8) for qt in range(NT)]

        def load_head(h):
            b = h % 2
            P.dma("sp", V(kt_[b], hkt[b]), V(k_d[h], dall("k")))
            P.dma("sp", V(qnt[b], hqn[b]), V(q_d[h, 0:128, :], dall("q")))
            P.dma("sp", V(qrt[b][0:64], hqr[b]), V(q_d[h, 128:192, :], dall("q")))
            P.dma("sp", V(vt[b], hvt[b]), V(v_d.rearrange("(c p) d -> p c d", p=128)[:, :, h * 128:(h + 1) * 128], dall("v")))

        def units_of(h, qt):
            return [(h, qt, "pair", k0) for k0 in range(0, 4 * qt, 2)] + [(h, qt, "diag", 4 * qt + j) for j in range(4)]
        allu = [u for (h, qt) in items for u in units_of(h, qt)]

        def st_tile(h_, qt_, kt, bank):
            b_ = h_ % 2
            ks = slice(kt * 128, (kt + 1) * 128)
            qs_ = slice(qt_ * T, (qt_ + 1) * T)
            P.mm(PS(bank), V(kt_[b_][:, ks], hkt[b_]), V(qnt[b_][:, qs_], hqn[b_]), start=True, stop=False)
            P.mm(PS(bank), V(krt[:, ks], hkr), V(qrt[b_][:, qs_], hqr[b_]), start=False, stop=True)

        def st_unit(ui):
            h_, qt_, kind, k0 = allu[ui]
            base = 2 * (ui % 2)
            st_tile(h_, qt_, k0, base)
            if kind == "pair":
                st_tile(h_, qt_, k0 + 1, base + 1)
        load_head(0)
        st_unit(0)
        p2 = 0
        for ui, (h, qt, kind, k0) in enumerate(allu):
            b = h % 2
            if kind == "pair" and k0 == 0 and qt == 1 and h + 1 < 8:
                load_head(h + 1)
            qs = slice(qt * T, (qt + 1) * T)
            nk = 4 * qt + 4
            ob_ = 4 + (qt % 2)
            db_ = 6 + (qt % 2)
            da = qt % 2
            base = 2 * (ui % 2)
            if ui + 1 < len(allu):
                st_unit(ui + 1)
            tiles = []
            if kind == "pair":
                pb_ = p2 % 3; p2 += 1
                src = V(psall[:, base * 512:(base + 2) * 512], [bh[base], bh[base + 1]])
                P.act(V(pt2[pb_], hpt2[pb_]), src, AF.Exp)
                tiles = [(k0, V(pt2[pb_][:, 0:T], hpt2[pb_])), (k0 + 1, V(pt2[pb_][:, T:2 * T], hpt2[pb_]))]
            else:
                j = k0 - 4 * qt
                bk = banks[base]
                P.act(V(pd[j][0:64, 128 * j:T], hpd[j]), V(bk[0:64, 128 * j:T], bh[base]), AF.Exp)
                P.act(V(pd[j][64:128, 128 * j + 64:T], hpd[j]), V(bk[64:128, 128 * j + 64:T], bh[base]), AF.Exp)
                tiles = [(k0, V(pd[j], hpd[j]))]
            for kt, pv_ in tiles:
                P.mm(PS(ob_), V(vt[b][:, kt, :], hvt[b]), pv_, start=(kt == 0), stop=(kt == nk - 1))
                if kt % 2 == 1:
                    P.mm(PS(db_), onesb, pv_, start=(kt == 1), stop=False)
                elif kt == 0:
                    P.copy(V(dacc[da], hdacc[da]), pv_)
                else:
                    P.tt(V(dacc[da], hdacc[da]), V(dacc[da], hdacc[da]), pv_, ALU.add)
            if kind == "diag" and k0 == nk - 1:
                P.copy(V(dhi, hdhi), V(dacc[da], hdacc[da]))
                P.tt(V(dlo, hdlo), V(dacc[da], hdacc[da]), V(dhi, hdhi), ALU.subtract)
                P.mm(PS(db_), onesb, V(dhi, hdhi), start=False, stop=False)
                P.mm(PS(db_), onesb, V(dlo, hdlo), start=False, stop=True)
                P.act(V(rd, hrd), PS(db_), AF.Ln)
                P.act(V(rd, hrd), V(rd, hrd), AF.Exp, scale=-1.0)
                o = qt % 2
                P.tt(V(oo[o], hoo[o]), PS(ob_), V(rd, hrd), ALU.mult)
                P.dma("pool", V(o_d[h * 128:(h + 1) * 128, qs], dh("o", qt)), V(oo[o], hoo[o]))
        if do_barrier:
            barrier()
        return A.off
    def phase_G(l):
        xsrc = xT_d if l == 0 else xres_d
        xsn = "xT" if l == 0 else "xres"
        A = Arena(big, ARENA0, 103000)
        wgate = A.alloc(8 * 3072, BF16).rearrange("p (k n) -> p k n", k=8); hwg = H()
        wbr = [A.alloc(8 * 1024, BF16).rearrange("p (k n) -> p k n", k=8) for _ in range(3)]; hwb = [H() for _ in range(3)]
        wo = A.alloc(8 * 1024, BF16).rearrange("p (k n) -> p k n", k=8); hwo = H()
        kp = lambda d: d.rearrange("(k p) n -> p k n", p=128)
        for b3 in range(3):
            wload(wgate[:, :, b3 * 1024:(b3 + 1) * 1024], kp(w_in_d[l])[:, :, OFF_GATE + b3 * 1024:OFF_GATE + (b3 + 1) * 1024], hwg)
        for b3, wd in enumerate((ssd_wo_d, conv_wo_d, mla_wo_d)):
            wload(wbr[b3], kp(wd[l]), hwb[b3])
        wload(wo, kp(wout_d[l]), hwo)
        TG = 256; NTG = S // TG
        fmg = lambda d, t: d.rearrange("(k p) s -> p k s", p=128)[:, :, t * TG:(t + 1) * TG]
        xt3 = [A.alloc(8 * TG, F32).rearrange("p (k s) -> p k s", k=8) for _ in range(3)]; hxt3 = [H(), H(), H()]
        ut2 = [A.alloc(8 * TG, BF16).rearrange("p (k s) -> p k s", k=8) for _ in range(2)]; hut2 = [H(), H()]
        sq = A.alloc(8 * TG, BF16).rearrange("p (k s) -> p k s", k=8); hsq = H()
        rs = A.alloc(TG, F32); hrs = H()
        br2 = [[A.alloc(8 * TG, BF16).rearrange("p (k s) -> p k s", k=8) for _ in range(3)] for _ in range(2)]
        hbr2 = [[H() for _ in range(3)] for _ in range(2)]
        mg = A.alloc(8 * TG, BF16).rearrange("p (k s) -> p k s", k=8); hmg = H()
        gt_ = [A.alloc(TG, F32) for _ in range(2)]; hgt = [H(), H()]
        macc = A.alloc(TG, F32); hmacc = H()
        srcs = [(ys_d, "ys"), (yc_d, "yc"), (o_d, "o")]

        def prep(t):
            b = t % 2; b3x = t % 3
            P.dma("sp", V(xt3[b3x], hxt3[b3x]), V(fmg(xsrc, t), dh(xsn, t // 2)))
            for b3, (d_, nm) in enumerate(srcs):
                P.dma("sp", V(br2[b][b3], hbr2[b][b3]), V(fmg(d_, t), dall(nm)))
            xnorm(xt3[b3x], hxt3[b3x], C_MIXG, ut2[b], hut2[b], sq, hsq, rs, hrs, 0, width=TG)
        prep(0)
        for t in range(NTG):
            b = t % 2; b3x = t % 3
            xt = xt3[b3x]; hxt = hxt3[b3x]; ut = ut2[b]; hut = hut2[b]; br = br2[b]; hbr = hbr2[b]
            if t + 1 < NTG:
                prep(t + 1)
            gi = 0
            for m in range(8):
                for b3 in range(3):
                    gb = 1 + gi % 2; yb = 3 + gi % 2; g2 = gi % 2; gi += 1
                    for k in range(8):
                        P.mm(PS(gb, TG), V(wgate[:, k, b3 * 1024 + m * 128:b3 * 1024 + (m + 1) * 128], hwg), V(ut[:, k, :], hut), start=(k == 0), stop=(k == 7))
                    for k in range(8):
                        P.mm(PS(yb, TG), V(wbr[b3][:, k, m * 128:(m + 1) * 128], hwb[b3]), V(br[b3][:, k, :], hbr[b3]), start=(k == 0), stop=(k == 7))
                    P.act(V(gt_[g2], hgt[g2]), PS(gb, TG), AF.Sigmoid, bias=pcol(C_GATEB + b3 * 8 + m))
                    if b3 == 0:
                        P.tt(V(macc, hmacc), V(gt_[g2], hgt[g2]), PS(yb, TG), ALU.mult)
                    else:
                        P.tt(V(gt_[g2], hgt[g2]), V(gt_[g2], hgt[g2]), PS(yb, TG), ALU.mult)
                        if b3 == 1:
                            P.tt(V(macc, hmacc), V(macc, hmacc), V(gt_[g2], hgt[g2]), ALU.add)
                        else:
                            P.tt(V(mg[:, m, :], hmg), V(macc, hmacc), V(gt_[g2], hgt[g2]), ALU.add)
            for n in range(8):
                ob_ = 5 + n % 2
                for m in range(8):
                    P.mm(PS(ob_, TG), V(wo[:, m, n * 128:(n + 1) * 128], hwo), V(mg[:, m, :], hmg), start=(m == 0), stop=(m == 7))
                P.tt(V(xt[:, n, :], hxt), V(xt[:, n, :], hxt), PS(ob_, TG), ALU.add)
            P.dma("pool", V(fmg(xres_d, t), dh("xres", t // 2)), V(xt, hxt))
        barrier()

    def phase_X(l):
        A = Arena(big, ARENA0, 103000)
        kp = lambda d: d.rearrange("(k p) n -> p k n", p=128)
        wkv = A.alloc(8 * 2048, BF16).rearrange("p (k n) -> p k n", k=8); hwkv = H()
        wload(wkv[:, :, 0:1024], kp(xwkv_d[l])[:, :, 0:1024], hwkv)
        wload(wkv[:, :, 1024:2048], kp(xwkv_d[l])[:, :, 1024:2048], hwkv)
        mt = A.alloc(8 * 256, F32).rearrange("p (k s) -> p k s", k=8); hmt = H()
        mn = A.alloc(8 * 256, BF16).rearrange("p (k s) -> p k s", k=8); hmn = H()
        sqm = A.alloc(8 * 256, BF16).rearrange("p (k s) -> p k s", k=8); hsqm = H()
        rsm = A.alloc(256, F32); hrsm = H()
        KX = A.alloc(8 * 256, BF16).rearrange("p (c s) -> p c s", c=8); hKX = H()
        VX = A.alloc(2 * 1024, BF16).rearrange("p (m d) -> p m d", m=2); hVX = H()
        sq2 = A.alloc(2 * 256, BF16).rearrange("p (c s) -> p c s", c=2); hsq2 = H()
        P.dma("sp", V(mt, hmt), V(memT_d.rearrange("(k p) s -> p k s", p=128), H()))
        xnorm(mt, hmt, C_MEMG, mn, hmn, sqm, hsqm, rsm, hrsm, 0, nk=8, width=256)
        for hx in range(4):
            for c in range(2):
                for k in range(8):
                    P.mm(PS(1 + c, 256), V(wkv[:, k, hx * 256 + c * 128:hx * 256 + (c + 1) * 128], hwkv), V(mn[:, k, :], hmn), start=(k == 0), stop=(k == 7))
                P.act(V(sq2[:, c, :], hsq2), PS(1 + c, 256), AF.Square)
            for c in range(2):
                P.mm(PS(3, 256), onesb, V(sq2[:, c, :], hsq2), start=(c == 0), stop=(c == 1))
            P.act(V(rsm, hrsm), PS(3, 256), AF.Ln, bias=epsv, scale=1.0 / 256)
            P.act(V(rsm, hrsm), V(rsm, hrsm), AF.Exp, scale=-0.5)
            for c in range(2):
                P.stt(V(KX[:, hx * 2 + c, :], hKX), PS(1 + c, 256), pcol(C_XKG + c), V(rsm, hrsm), ALU.mult, ALU.mult)
        for m in range(2):
            for half in range(2):
                for k in range(8):
                    P.mm(PS(4 + half), V(mn[:, k, m * 128:(m + 1) * 128], hmn), V(wkv[:, k, 1024 + half * 512:1024 + (half + 1) * 512], hwkv), start=(k == 0), stop=(k == 7))
                P.copy(V(VX[:, m, half * 512:(half + 1) * 512], hVX), PS(4 + half), eng="act")
        barrier()
        A2 = Arena(big, ARENA0, 103000)
        KX2 = A2.alloc(8 * 256, BF16).rearrange("p (c s) -> p c s", c=8); hKX2 = H()
        VX2 = A2.alloc(2 * 1024, BF16).rearrange("p (m d) -> p m d", m=2); hVX2 = H()
        P.copy(V(KX2, hKX2), V(KX, hKX)); P.copy(V(VX2, hVX2), V(VX, hVX))
        barrier()
        A = A2
        wq_ = A.alloc(8 * 1024, BF16).rearrange("p (k n) -> p k n", k=8); hwq = H()
        wo_ = A.alloc(8 * 1024, BF16).rearrange("p (k n) -> p k n", k=8); hwo = H()
        wload(wq_, kp(xwq_d[l]), hwq); wload(wo_, kp(xwo_d[l]), hwo)
        xt = [A.alloc(8 * T, F32).rearrange("p (k s) -> p k s", k=8) for _ in range(3)]; hxt = [H(), H(), H()]
        ht2 = [A.alloc(8 * T, BF16).rearrange("p (k s) -> p k s", k=8) for _ in range(2)]; hht2 = [H(), H()]
        sq = A.alloc(8 * T, BF16).rearrange("p (k s) -> p k s", k=8); hsq = H()
        rs = A.alloc(T, F32); hrs = H()
        sqq = A.alloc(2 * T, BF16).rearrange("p (c s) -> p c s", c=2); hsqq = H()
        rsq = A.alloc(T, F32); hrsq = H()
        qx = A.alloc(2 * T, BF16).rearrange("p (c s) -> p c s", c=2); hqx = H()
        pt = [A.alloc(T, BF16) for _ in range(2)]; hpt = [H(), H()]
        rd = A.alloc(T, F32); hrd = H()
        ox = A.alloc(8 * T, BF16).rearrange("p (k s) -> p k s", k=8); hox = H()
        def prep(t):
            b = t % 2; b3 = t % 3
            P.dma("sp", V(xt[b3], hxt[b3]), V(fmview(xres_d, t), dh("xres", t)))
            xnorm(xt[b3], hxt[b3], C_XATG, ht2[b], hht2[b], sq, hsq, rs, hrs, 0)
        prep(0)
        for t in range(NT):
            b = t % 2
            ht = ht2[b]; hht = hht2[b]
            b = t % 3
            if t + 1 < NT:
                prep(t + 1)
            for hx in range(4):
                for c in range(2):
                    for k in range(8):
                        P.mm(PS(1 + c), V(wq_[:, k, hx * 256 + c * 128:hx * 256 + (c + 1) * 128], hwq), V(ht[:, k, :], hht), start=(k == 0), stop=(k == 7))
                    P.act(V(sqq[:, c, :], hsqq), PS(1 + c), AF.Square)
                for c in range(2):
                    P.mm(PS(3), onesb, V(sqq[:, c, :], hsqq), start=(c == 0), stop=(c == 1))
                P.act(V(rsq, hrsq), PS(3), AF.Ln, bias=epsv, scale=1.0 / 256)
                P.act(V(rsq, hrsq), V(rsq, hrsq), AF.Exp, scale=-0.5)
                for c in range(2):
                    P.stt(V(qx[:, c, :], hqx), PS(1 + c), V(gsc[:, 3 + c:4 + c], h_gsc), V(rsq, hrsq), ALU.mult, ALU.mult)
                for m in range(2):
                    for c in range(2):
                        P.mm(PS(4 + m), V(KX2[:, hx * 2 + c, m * 128:(m + 1) * 128], hKX2), V(qx[:, c, :], hqx), start=(c == 0), stop=(c == 1))
                    P.act(V(pt[m], hpt[m]), PS(4 + m), AF.Exp)
                for m in range(2):
                    P.mm(PS(3), onesb, V(pt[m], hpt[m]), start=(m == 0), stop=(m == 1))
                P.act(V(rd, hrd), PS(3), AF.Ln)
                P.act(V(rd, hrd), V(rd, hrd), AF.Exp, scale=-1.0)
                for c2 in range(2):
                    for m in range(2):
                        P.mm(PS(6 + c2), V(VX2[:, m, hx * 256 + c2 * 128:hx * 256 + (c2 + 1) * 128], hVX2), V(pt[m], hpt[m]), start=(m == 0), stop=(m == 1))
                    P.tt(V(ox[:, hx * 2 + c2, :], hox), PS(6 + c2), V(rd, hrd), ALU.mult)
            for n in range(8):
                ob_ = 1 + n % 2
                for k in range(8):
                    P.mm(PS(ob_), V(wo_[:, k, n * 128:(n + 1) * 128], hwo), V(ox[:, k, :], hox), start=(k == 0), stop=(k == 7))
                P.tt(V(xt[b][:, n, :], hxt[b]), V(xt[b][:, n, :], hxt[b]), PS(ob_), ALU.add)
            P.dma("pool", V(fmview(xres_d, t), dh("xres", t)), V(xt[b], hxt[b]))
        barrier()

    def phase_F(l, last):
        kp = lambda d: d.rearrange("(k p) n -> p k n", p=128)
        toks = []
        for hh in range(2):
            A = Arena(big, ARENA0, 103000)
            w1g = A.alloc(8 * 1408, BF16).rearrange("p (k n) -> p k n", k=8); hw1g = H()
            w1u = A.alloc(8 * 1408, BF16).rearrange("p (k n) -> p k n", k=8); hw1u = H()
            w2 = A.alloc(11 * 1024, BF16).rearrange("p (k n) -> p k n", k=11); hw2 = H()
            wload(w1g, kp(fwi_d[l])[:, :, hh * 1408:(hh + 1) * 1408], hw1g)
            wload(w1u, kp(fwi_d[l])[:, :, 2816 + hh * 1408:2816 + (hh + 1) * 1408], hw1u)
            wload(w2, fwo_d[l, hh * 1408:(hh + 1) * 1408, :].rearrange("(k p) n -> p k n", p=128), hw2)
            xt = [A.alloc(8 * T, F32).rearrange("p (k s) -> p k s", k=8) for _ in range(3)]; hxt = [H(), H(), H()]
            pt_ = A.alloc(8 * T, F32).rearrange("p (k s) -> p k s", k=8); hpt_ = H()
            ht2 = [A.alloc(8 * T, BF16).rearrange("p (k s) -> p k s", k=8) for _ in range(2)]; hht2 = [H(), H()]
            sq = A.alloc(8 * T, BF16).rearrange("p (k s) -> p k s", k=8); hsq = H()
            rs = A.alloc(T, F32); hrs = H()
            ac = A.alloc(11 * T, BF16).rearrange("p (k s) -> p k s", k=11); hac = H()
            sg = [A.alloc(T, F32) for _ in range(2)]; hsg = [H(), H()]
            def prep(t):
                b = t % 2; b3 = t % 3
                P.dma("sp", V(xt[b3], hxt[b3]), V(fmview(xres_d, t), dh("xres", t)))
                xnorm(xt[b3], hxt[b3], C_FFNG, ht2[b], hht2[b], sq, hsq, rs, hrs, 0)
            prep(0)
            for t in range(NT):
                b = t % 2
                ht = ht2[b]; hht = hht2[b]
                b = t % 3
                if t + 1 < NT:
                    prep(t + 1)
                if hh == 1:
                    P.dma("sp", V(pt_, hpt_), V(fmview(fp_d, t), dh("fp", t)))
                for j in range(11):
                    s2 = j % 2
                    for k in range(8):
                        P.mm(PS(1 + s2), V(w1g[:, k, j * 128:(j + 1) * 128], hw1g), V(ht[:, k, :], hht), start=(k == 0), stop=(k == 7))
                    for k in range(8):
                        P.mm(PS(3 + s2), V(w1u[:, k, j * 128:(j + 1) * 128], hw1u), V(ht[:, k, :], hht), start=(k == 0), stop=(k == 7))
                    P.act(V(sg[s2], hsg[s2]), PS(1 + s2), AF.Silu)
                    P.tt(V(ac[:, j, :], hac), V(sg[s2], hsg[s2]), PS(3 + s2), ALU.mult)
                for n in range(8):
                    ob_ = 5 + n % 2
                    for j in range(11):
                        P.mm(PS(ob_), V(w2[:, j, n * 128:(n + 1) * 128], hw2), V(ac[:, j, :], hac), start=(j == 0), stop=(j == 10))
                    if hh == 0:
                        P.copy(V(xt[b][:, n, :], hxt[b]), PS(ob_), eng="act")
                    else:
                        P.tt(V(xt[b][:, n, :], hxt[b]), V(xt[b][:, n, :], hxt[b]), PS(ob_), ALU.add)
                        P.tt(V(xt[b][:, n, :], hxt[b]), V(xt[b][:, n, :], hxt[b]), V(pt_[:, n, :], hpt_), ALU.add)
                if hh == 0:
                    P.dma("pool", V(fmview(fp_d, t), dh("fp", t)), V(xt[b], hxt[b]))
                else:
                    dst = out_d if last else xres_d
                    tk = P.dma("pool", V(fmview(dst, t), dh("out" if last else "xres", t)), V(xt[b], hxt[b]))
                    toks.append(tk)
            barrier()
        return toks

    def phase_M2C(l):
        la = P.capture()
        end = phase_M2(l, do_barrier=False)
        lb = P.capture()
        phase_C(l, base=end, bk=(6, 7), do_barrier=False)
        P.interleave(la, lb)
        barrier()

    phases = {"A": phase_A, "S": phase_S, "C": phase_C, "M1": phase_M1, "M2": phase_M2, "G": phase_G, "X": phase_X, "M2C": phase_M2C}
    order = ["A", "S", "C", "M1", "M2", "G", "X", "F"]
    P.marks = []
    for l in range(L):
        for ph in order:
            if only is not None and ph not in only:
                continue
            P.marks.append((l, ph, P.eng["dve"].n))
            if ph == "F":
                final_toks = phase_F(l, l == L - 1)
            else:
                phases[ph](l)
    P.finish(final_toks)
    return P


from concourse.bass_utils import run_bass_kernel_spmd


def _fm(v, nch):
    return np.ascontiguousarray(np.asarray(v, np.float32).reshape(nch, 128).T)


def _consts():
    c = np.zeros((128, NCST), np.float32)
    i = np.arange(128)
    c[:, K_TRI:K_TRI + 128] = (i[:, None] <= i[None, :])
    c[:, K_GT:K_GT + 128] = (i[:, None] > i[None, :])
    c[:, K_ID:K_ID + 128] = np.eye(128)
    f = np.arange(32)
    inv = (10000.0 ** (-(2 * f).astype(np.float32) / np.float32(64))).astype(np.float32)
    c[0:32, K_INV] = inv; c[32:64, K_INV] = inv
    q = np.arange(512)
    for j in range(4):
        c[:, K_MASK + j * 512:K_MASK + (j + 1) * 512] = ((2 * j + i[:, None] // 64) <= (q[None, :] // 64))
    return c


def _pack_params(inp):
    L = 4
    pp = np.zeros((L, 128, NPP), np.float32)
    pb = np.zeros((L, NPB), np.float32)
    for l in range(L):
        p = pp[l]
        p[:, C_MIXG:C_MIXG + 8] = _fm(inp["mix_norm_g"][l], 8)
        p[:, C_XATG:C_XATG + 8] = _fm(inp["xattn_norm_g"][l], 8)
        p[:, C_FFNG:C_FFNG + 8] = _fm(inp["ffn_norm_g"][l], 8)
        p[:, C_MEMG:C_MEMG + 8] = _fm(inp["mem_norm_g"][l], 8)
        p[:, C_SCW:C_SCW + 64] = np.asarray(inp["ssd_conv_w"][l]).T.reshape(16, 128, 4).transpose(1, 0, 2).reshape(128, 64)
        p[:, C_SCB:C_SCB + 16] = _fm(inp["ssd_conv_b"][l], 16)
        p[:, C_CDW:C_CDW + 248] = np.asarray(inp["conv_dw_w"][l]).T.reshape(8, 128, 31).transpose(1, 0, 2).reshape(128, 248)
        p[:, C_CDB:C_CDB + 8] = _fm(inp["conv_dw_b"][l], 8)
        p[:, C_LNG:C_LNG + 8] = _fm(inp["conv_ln_g"][l], 8)
        p[:, C_LNB:C_LNB + 8] = _fm(inp["conv_ln_b"][l], 8)
        p[:, C_QAG:C_QAG + 3] = _fm(inp["mla_q_a_g"][l], 3)
        p[:, C_KVAG:C_KVAG + 2] = _fm(inp["mla_kv_a_g"][l], 2)
        for (cn, cr, crs, g) in ((C_QNN, C_QNR, C_QNRS, inp["mla_q_norm_g"][l]), (C_KNN, C_KNR, C_KNRS, inp["mla_k_norm_g"][l])):
            g = np.asarray(g, np.float32)
            p[:, cn] = g[0:128]
            p[0:64, cr] = g[128:192]
            p[0:64, crs] = np.concatenate([g[160:192], g[128:160]])
        p[:, C_GATEB:C_GATEB + 24] = np.asarray(inp["gate_b"][l], np.float32).reshape(24, 128).T
        p[:, C_XQG:C_XQG + 2] = _fm(inp["xattn_q_norm_g"][l], 2)
        p[:, C_XKG:C_XKG + 2] = _fm(inp["xattn_k_norm_g"][l], 2)
        pb[l, B_DTB:B_DTB + 16] = inp["ssd_dt_bias"][l]
        pb[l, B_ALOG:B_ALOG + 16] = inp["ssd_a_log"][l]
        pb[l, B_DSK:B_DSK + 16] = inp["ssd_d"][l]
        pb[l, B_SNG:B_SNG + 1024] = inp["ssd_norm_g"][l]
    return pp, pb


WNAMES = ["w_in", "ssd_w_out", "conv_w_out", "mla_w_q_b", "mla_w_kv_b", "mla_w_o", "w_out", "xattn_w_q", "xattn_w_kv",
          "xattn_w_o", "ffn_w_in", "ffn_w_out"]


def make_in_maps(inp, cores):
    pp, pb = _pack_params(inp)
    cst = _consts()
    shared = {n: np.ascontiguousarray(np.asarray(inp[n], np.float32)) for n in WNAMES}
    shared.update(cst=cst, pp=pp, pb=pb)
    maps = []
    for b in cores:
        m = dict(shared)
        m["xT"] = np.ascontiguousarray(np.asarray(inp["x"][b], np.float32).T)
        m["memT"] = np.ascontiguousarray(np.asarray(inp["mem"][b], np.float32).T)
        m["pos"] = np.ascontiguousarray(np.asarray(inp["positions"][b], np.int32)[None, :])
        maps.append(m)
    return maps


_CACHE = {}


def kernel(**inputs):
    if "P" not in _CACHE:
        _CACHE["P"] = build(L=4)
    P = _CACHE["P"]
    maps = make_in_maps(inputs, list(range(8)))
    res = run_bass_kernel_spmd(P.nc, maps, core_ids=list(range(8)))
    out = np.stack([np.ascontiguousarray(r["outT"].T) for r in res.results], axis=0)
    return out.astype(np.float32)
```

```python
import contextlib
import numpy as np
import concourse.bass as bass
import concourse.mybir as mybir

F32 = mybir.dt.float32
BF16 = mybir.dt.bfloat16
I32 = mybir.dt.int32
AF = mybir.ActivationFunctionType
ALU = mybir.AluOpType
AX = mybir.AxisListType

ENGS = ("pe", "act", "dve", "pool", "sp")
NDMA_SEMS = 12


class H:
    __slots__ = ("w", "r")

    def __init__(self):
        self.w = None
        self.r = []


class V:
    __slots__ = ("ap", "hs")

    def __init__(self, ap, hs):
        self.ap = ap
        self.hs = hs if isinstance(hs, (list, tuple)) else [hs]


class Eng:
    def __init__(self, name, idx):
        self.name = name
        self.idx = idx
        self.n = 0
        self.seen = [0] * len(ENGS)
        self.seen_dma = {}
        self.ops = []
        self.dma_count = 0


class Prog:
    def __init__(self):
        self.nc = bass.Bass("TRN2", target_bir_lowering=False)
        self.stack = contextlib.ExitStack()
        self.eng = {n: Eng(n, i) for i, n in enumerate(ENGS)}
        self.snaps = {}
        self.ntens = 0
        self.nwaits = 0

    def sb(self, shape, dtype, name=None):
        self.ntens += 1
        t = self.stack.enter_context(self.nc.sbuf_tensor(name or f"sb{self.ntens}", list(shape), dtype))
        return t

    def ps(self, shape, dtype, name=None):
        self.ntens += 1
        t = self.stack.enter_context(self.nc.psum_tensor(name or f"ps{self.ntens}", list(shape), dtype))
        return t

    def dram(self, name, shape, dtype, kind="Internal"):
        return self.nc.dram_tensor(name, list(shape), dtype, kind=kind).ap()

    def _deps(self, reads, writes):
        deps = {}

        def add(tok):
            k, v = tok
            if deps.get(k, 0) < v:
                deps[k] = v
        for h in reads:
            if h.w is not None:
                add(h.w)
        for h in writes:
            if h.w is not None:
                add(h.w)
            for t in h.r:
                add(t)
        return deps

    def _waits(self, e, deps):
        waits = []
        for k, v in deps.items():
            if isinstance(k, int):
                if k == e.idx and e.name == "pe":
                    continue
                if e.seen[k] >= v:
                    continue
                waits.append((k, v))
            else:
                if e.seen_dma.get(k, 0) >= v:
                    continue
                waits.append((k, v))
        for k, v in waits:
            if isinstance(k, int):
                if e.seen[k] < v:
                    e.seen[k] = v
            else:
                e.seen_dma[k] = v
            snap = self.snaps.get((k, v))
            if snap is not None:
                for i in range(len(ENGS)):
                    if e.seen[i] < snap[i]:
                        e.seen[i] = snap[i]
        return waits

    def capture(self):
        self._cap = []
        return self._cap

    def end_capture(self):
        self._cap = None

    def interleave(self, la, lb):
        self._cap = None
        na, nb = len(la), len(lb)
        ia = ib = 0
        while ia < na or ib < nb:
            if ib >= nb or (ia < na and ia * nb <= ib * na):
                kind, args, kw = la[ia]; ia += 1
            else:
                kind, args, kw = lb[ib]; ib += 1
            (self.op if kind == "op" else self.dma)(*args, **kw)

    def op(self, engname, fn, reads, writes):
        if getattr(self, "_cap", None) is not None:
            self._cap.append(("op", (engname, fn, reads, writes), {}))
            return None
        e = self.eng[engname]
        rh = [h for v in reads for h in v.hs]
        wh = [h for v in writes for h in v.hs]
        deps = self._deps(rh, wh)
        waits = self._waits(e, deps)
        e.n += 1
        tok = (e.idx, e.n)
        self.snaps[tok] = tuple(e.seen)
        e.ops.append((waits, fn, None))
        self.nwaits += len(waits)
        for h in rh:
            h.r.append(tok)
        for h in wh:
            h.w = tok
            h.r = []
        return tok

    def dma(self, qname, out, in_, **kw):
        if getattr(self, "_cap", None) is not None:
            self._cap.append(("dma", (qname, out, in_), kw))
            return None
        e = self.eng[qname]
        rh = list(in_.hs)
        wh = list(out.hs)
        deps = self._deps(rh, wh)
        k = e.dma_count % NDMA_SEMS
        rnd = e.dma_count // NDMA_SEMS
        e.dma_count += 1
        key = (qname, k)
        if rnd > 0:
            if deps.get(key, 0) < 16 * rnd:
                deps[key] = 16 * rnd
        waits = self._waits(e, deps)
        tok = (key, 16 * (rnd + 1))
        self.snaps[tok] = tuple(e.seen)
        oap, iap = out.ap, in_.ap
        e.ops.append((waits, lambda eng: eng.dma_start(out=oap, in_=iap, **kw), key))
        self.nwaits += len(waits)
        for h in rh:
            h.r.append(tok)
        for h in wh:
            h.w = tok
            h.r = []
        return tok

    def mm(self, out, lhsT, rhs, start=True, stop=True):
        o, l, r = out.ap, lhsT.ap, rhs.ap
        return self.op("pe", lambda e: e.matmul(o, l, r, start=start, stop=stop), [lhsT, rhs], [out])

    def transpose(self, out, in_, ident):
        o, i, d = out.ap, in_.ap, ident.ap
        return self.op("pe", lambda e: e.transpose(o, i, d), [in_, ident], [out])

    def act(self, out, in_, func, bias=None, scale=None, eng="act", accum=None):
        o, i = out.ap, in_.ap
        reads = [in_]
        kw = {}
        if bias is not None:
            if isinstance(bias, V):
                reads.append(bias)
                kw["bias"] = bias.ap
            else:
                kw["bias"] = bias
        if scale is not None:
            if isinstance(scale, V):
                reads.append(scale)
                kw["scale"] = scale.ap
            else:
                kw["scale"] = scale
        writes = [out]
        if accum is not None:
            writes.append(accum)
            kw["accum_out"] = accum.ap
        return self.op("act", lambda e: e.activation(o, i, func, **kw), reads, writes)

    def tt(self, out, a, b, op, eng="dve"):
        o, x, y = out.ap, a.ap, b.ap
        return self.op(eng, lambda e: e.tensor_tensor(o, x, y, op), [a, b], [out])

    def ts(self, out, a, s1, op0, s2=None, op1=None, eng="dve", accum=None):
        o, x = out.ap, a.ap
        reads = [a]
        if isinstance(s1, V):
            reads.append(s1)
            s1 = s1.ap
        if isinstance(s2, V):
            reads.append(s2)
            s2 = s2.ap
        writes = [out]
        kw = {}
        if accum is not None:
            writes.append(accum)
            kw["accum_out"] = accum.ap
        if op1 is None:
            return self.op(eng, lambda e: e.tensor_scalar(o, x, s1, None, op0, **kw), reads, writes)
        return self.op(eng, lambda e: e.tensor_scalar(o, x, s1, s2, op0, op1, **kw), reads, writes)

    def stt(self, out, a, s, b, op0, op1, eng="dve"):
        o, x, y = out.ap, a.ap, b.ap
        reads = [a, b]
        if isinstance(s, V):
            reads.append(s)
            s = s.ap
        return self.op(eng, lambda e: e.scalar_tensor_tensor(o, x, s, y, op0, op1), reads, [out])

    def copy(self, out, in_, eng="dve"):
        o, i = out.ap, in_.ap
        if eng == "act":
            return self.op("act", lambda e: e.copy(o, i), [in_], [out])
        return self.op(eng, lambda e: e.tensor_copy(o, i), [in_], [out])

    def memset(self, out, val, eng="dve"):
        o = out.ap
        return self.op(eng, lambda e: e.memset(o, val), [], [out])

    def recip(self, out, in_):
        o, i = out.ap, in_.ap
        return self.op("dve", lambda e: e.reciprocal(o, i), [in_], [out])

    def finish(self, final_tokens):
        nc = self.nc
        sems = {}
        for i, n in enumerate(ENGS):
            sems[i] = self.stack.enter_context(nc.semaphore(f"s_{n}"))
        for n in ENGS:
            e = self.eng[n]
            if e.dma_count:
                for k in range(min(NDMA_SEMS, e.dma_count)):
                    sems[(n, k)] = self.stack.enter_context(nc.semaphore(f"d_{n}{k}"))
        sp = self.eng["sp"]
        fin = {}
        for k, v in final_tokens:
            if fin.get(k, 0) < v:
                fin[k] = v
        sp.ops.append((list(fin.items()), None, None))
        block = self.stack.enter_context(nc.Block())
        hw = {"pe": block.tensor, "act": block.scalar, "dve": block.vector, "pool": block.gpsimd, "sp": block.sync}

        def make(e):
            def body(eng):
                own = sems[e.idx]
                for waits, fn, dkey in e.ops:
                    for k, v in waits:
                        eng.wait_ge(sems[k], v)
                    if fn is None:
                        continue
                    ins = fn(eng)
                    if dkey is None:
                        ins.then_inc(own, 1)
                    else:
                        ins.then_inc(sems[dkey], 16)
            return body
        for n in ENGS:
            e = self.eng[n]
            if e.ops:
                hw[n](make(e))
        self.stack.close()
        return nc


S = 4096; D = 1024; T = 512; NT = 8; NCH = 32
EPS = 1e-6
OFF_Z, OFF_XBC, OFF_DT, OFF_GA, OFF_GG, OFF_QL, OFF_KV, OFF_KR, OFF_GATE = 0, 1024, 3072, 3088, 4112, 5136, 5520, 5776, 5840
C_MIXG, C_XATG, C_FFNG, C_MEMG, C_SCW, C_SCB, C_CDW, C_CDB, C_LNG, C_LNB = 0, 8, 16, 24, 32, 96, 112, 360, 368, 376
C_QAG, C_KVAG, C_QNN, C_QNR, C_QNRS, C_KNN, C_KNR, C_KNRS, C_GATEB, C_XQG, C_XKG, NPP = 384, 387, 389, 390, 391, 392, 393, 394, 395, 419, 421, 424
B_DTB, B_ALOG, B_DSK, B_SNG, NPB = 0, 16, 32, 48, 1072
K_TRI, K_GT, K_ID, K_INV, K_MASK, NCST = 0, 128, 256, 384, 385, 385 + 2048


class Arena:
    def __init__(self, big, base, limit):
        self.big, self.off, self.limit = big, base, limit

    def alloc(self, n, dtype, parts=128):
        nb = n * (4 if dtype in (F32, I32) else 2)
        nb = (nb + 63) // 64 * 64
        ne = nb // 2
        assert self.off + ne <= self.limit, ("SBUF arena overflow", self.off + ne, self.limit)
        ap = self.big[:, self.off:self.off + ne]
        self.off += ne
        if dtype != BF16:
            ap = ap.bitcast(dtype)
        return ap[:, 0:n]


def build(L=4, dbg=False, only=None):
    P = Prog(); nc = P.nc
    kind_s = "ExternalOutput" if dbg else "Internal"
    din = lambda n, s, dt=F32: P.dram(n, s, dt, kind="ExternalInput")
    xT_d = din("xT", [D, S]); memT_d = din("memT", [D, 256]); pos_d = din("pos", [1, S], I32)
    cst_d = din("cst", [128, NCST]); pp_d = din("pp", [4, 128, NPP]); pb_d = din("pb", [4, NPB])
    w_in_d = din("w_in", [4, D, 8912]); ssd_wo_d = din("ssd_w_out", [4, D, D]); conv_wo_d = din("conv_w_out", [4, D, D])
    wqb_d = din("mla_w_q_b", [4, 384, 1536]); wkvb_d = din("mla_w_kv_b", [4, 256, 2048]); mla_wo_d = din("mla_w_o", [4, D, D])
    wout_d = din("w_out", [4, D, D]); xwq_d = din("xattn_w_q", [4, D, D]); xwkv_d = din("xattn_w_kv", [4, D, 2048])
    xwo_d = din("xattn_w_o", [4, D, D]); fwi_d = din("ffn_w_in", [4, D, 5632]); fwo_d = din("ffn_w_out", [4, 2816, D])
    out_d = P.dram("outT", [D, S], F32, kind="ExternalOutput")
    sc = lambda n, s, dt=F32: P.dram(n, s, dt, kind=kind_s)
    xres_d = sc("xres", [D, S]); sz_d = sc("sz", [S, D]); xbc_d = sc("xbc", [2048, S], BF16); cv_d = sc("cv", [D, S])
    lat_d = sc("lat", [768, S]); ys_d = sc("ys", [D, S], BF16); yc_d = sc("yc", [D, S], BF16)
    q_d = sc("qs", [8, 192, S], BF16); k_d = sc("ks", [8, 128, S], BF16); kr_d = sc("krs", [64, S], BF16)
    v_d = sc("vs", [S, D], BF16); o_d = sc("os", [D, S], BF16); fp_d = sc("fpart", [D, S]); rope_d = sc("rope", [128, S])
    hd = {}

    def dh(name, t):
        k = (name, t)
        if k not in hd:
            hd[k] = H()
        return hd[k]

    def dall(name, n=NT):
        return [dh(name, t) for t in range(n)]

    big = P.sb([128, 103000], BF16, name="big")
    psall = P.ps([128, 4096], F32, name="psall")[:]
    banks = [psall[:, i * 512:(i + 1) * 512] for i in range(8)]
    bh = [H() for _ in range(8)]

    def PS(i, n=512, parts=128, dt=F32):
        ap = banks[i][:]
        if dt == BF16:
            ap = ap.bitcast(BF16)
        return V(ap[0:parts, 0:n], bh[i])

    pers = Arena(big, 0, 8000)
    cst = pers.alloc(K_MASK, F32); h_cst = H()
    tri = V(cst[:, K_TRI:K_TRI + 128], h_cst); gt = V(cst[:, K_GT:K_GT + 128], h_cst)
    inv_c = V(cst[:, K_INV:K_INV + 1], h_cst)
    ident = pers.alloc(128, BF16); h_id = H(); identv = V(ident, h_id)
    ones_b = pers.alloc(128, BF16); h_ob = H(); onesb = V(ones_b, h_ob)
    ones_f = pers.alloc(128, F32); h_of = H(); onesf = V(ones_f, h_of)
    masks = pers.alloc(2048, BF16); h_mk = H()
    epsc = pers.alloc(1, F32); h_eps = H(); epsv = V(epsc, h_eps)
    ppt = pers.alloc(NPP, F32); h_pp = H()
    pbt = pers.alloc(NPB, F32); h_pb = H()
    abc = pers.alloc(16, F32); h_abc = H()
    gsc = pers.alloc(8, F32); h_gsc = H()
    ARENA0 = pers.off

    def pcol(c, n=1, parts=128):
        return V(ppt[0:parts, c:c + n], h_pp)

    def barrier():
        toks = {}
        for n in ENGS:
            e = P.eng[n]
            if e.n:
                toks[e.idx] = e.n
            for k in range(min(NDMA_SEMS, e.dma_count)):
                rnd = (e.dma_count - 1 - k) // NDMA_SEMS
                toks[(n, k)] = 16 * (rnd + 1)
        for n in ENGS:
            e = P.eng[n]
            waits = P._waits(e, dict(toks))
            if waits:
                e.ops.append((waits, None, None))

    P.dma("sp", V(cst, h_cst), V(cst_d[:, 0:K_MASK], H()))
    P.dma("pool", V(ident, h_id), V(cst_d[:, K_ID:K_ID + 128], H()))
    P.dma("pool", V(masks, h_mk), V(cst_d[:, K_MASK:K_MASK + 2048], H()))
    P.memset(onesb, 1.0); P.memset(onesf, 1.0); P.memset(epsv, EPS)

    def rope_tables():
        A = Arena(big, ARENA0, 103000)
        for t in range(NT):
            sl = slice(t * T, (t + 1) * T)
            pi_ = A.alloc(T, I32) if t == 0 else rope_tables.bufs[0]
            if t == 0:
                rope_tables.bufs = [pi_] + [A.alloc(T, F32) for _ in range(5)] + [A.alloc(T, I32)]
            pi_, pf, ang, kf, r, rc, ki = rope_tables.bufs
            hs = [H() for _ in range(7)]
            pi_v, pf_v, ang_v, kf_v, r_v, rc_v, ki_v = [V(a[0:64], h) for a, h in zip(rope_tables.bufs, hs)]
            P.dma("sp", pi_v, V(pos_d[0, sl].partition_broadcast(64), H()))
            P.copy(pf_v, pi_v)
            P.ts(ang_v, pf_v, V(cst[0:64, K_INV:K_INV + 1], h_cst), ALU.mult)
            P.ts(kf_v, ang_v, float(1.0 / (2 * np.pi)), ALU.mult)
            P.copy(ki_v, kf_v)
            P.copy(kf_v, ki_v)
            C1 = 6.28125; C2 = float(np.float32(2 * np.pi - 6.28125))
            P.stt(r_v, kf_v, -C1, ang_v, ALU.mult, ALU.add)
            P.stt(r_v, kf_v, -C2, r_v, ALU.mult, ALU.add)
            P.ts(r_v, r_v, 3.1415925, ALU.min, -3.1415925, ALU.max)
            P.ts(rc_v, r_v, float(np.pi / 2), ALU.is_gt, float(-2 * np.pi), ALU.mult)
            P.stt(rc_v, r_v, float(np.pi / 2), rc_v, ALU.add, ALU.add)
            P.ts(rc_v, rc_v, 3.1415925, ALU.min, -3.1415925, ALU.max)
            P.act(rc_v, rc_v, AF.Sin)
            P.act(r_v, r_v, AF.Sin)
            P.ts(V(r[0:32], hs[4]), V(r[0:32], hs[4]), -1.0, ALU.mult)
            P.dma("sp", V(rope_d[0:64, sl], dh("rope", t)), rc_v)
            P.dma("sp", V(rope_d[64:128, sl], dh("rope", t)), r_v)
            barrier()
    rope_tables()
    barrier()

    def wload(dst, src, hdst):
        return P.dma("pool", V(dst, hdst), V(src, H()))

    def fmview(d, t):
        return d.rearrange("(k p) s -> p k s", p=128)[:, :, t * T:(t + 1) * T]

    def xnorm(xt, hx, gcol, outb, hout, sq, hsq, rs, hrs, bank, nk=8, width=T, inv_n=1.0 / D):
        P.act(V(sq, hsq), V(xt, hx), AF.Square)
        for k in range(nk):
            P.mm(PS(bank, width), onesb, V(sq[:, k, :], hsq), start=(k == 0), stop=(k == nk - 1))
        P.act(V(rs, hrs), PS(bank, width), AF.Ln, bias=epsv, scale=inv_n)
        P.act(V(rs, hrs), V(rs, hrs), AF.Exp, scale=-0.5)
        for k in range(nk):
            P.stt(V(outb[:, k, :], hout), V(xt[:, k, :], hx), pcol(gcol + k), V(rs, hrs), ALU.mult, ALU.mult)

    dtraw = pers.alloc(NCH * 16, F32).rearrange("p (c h) -> p c h", c=NCH); h_dtraw = H()
    ARENA0 = pers.off
    final_toks = []

    def phase_A(l):
        xsrc = xT_d if l == 0 else xres_d
        xsn = "xT" if l == 0 else "xres"
        barrier()
        P.dma("sp", V(ppt, h_pp), V(pp_d[l], H()))
        P.dma("sp", V(pbt, h_pb), V(pb_d[l, :].partition_broadcast(128), H()))
        P.act(V(abc, h_abc), V(pbt[:, B_ALOG:B_ALOG + 16], h_pb), AF.Exp)
        P.ts(V(abc, h_abc), V(abc, h_abc), -1.0, ALU.mult)
        P.ts(V(gsc[:, 0:3], h_gsc), V(ppt[:, C_QNN:C_QNN + 3], h_pp), float(192 ** -0.5), ALU.mult)
        P.ts(V(gsc[:, 3:5], h_gsc), V(ppt[:, C_XQG:C_XQG + 2], h_pp), float(256 ** -0.5), ALU.mult)

        A = Arena(big, ARENA0, 103000)
        uT = A.alloc(8 * S, BF16).rearrange("p (k s) -> p k s", k=8); hu = [H() for _ in range(NT)]
        A1 = A.off
        xt = [A.alloc(8 * T, F32).rearrange("p (k s) -> p k s", k=8) for _ in range(2)]; hxt = [H(), H()]
        sq = A.alloc(8 * T, BF16).rearrange("p (k s) -> p k s", k=8); hsq = H()
        rs = A.alloc(T, F32); hrs = H()
        wz = A.alloc(8 * 1040, BF16).rearrange("p (k n) -> p k n", k=8); hwz = H()
        wload(wz[:, :, 0:1024], w_in_d[l].rearrange("(k p) n -> p k n", p=128)[:, :, OFF_Z:OFF_Z + 1024], hwz)
        wload(wz[:, :, 1024:1040], w_in_d[l].rearrange("(k p) n -> p k n", p=128)[:, :, OFF_DT:OFF_DT + 16], hwz)
        szb = [A.alloc(1024, F32) for _ in range(2)]; hszb = [H(), H()]

        def norm_tile(t):
            b = t % 2
            P.dma("sp", V(xt[b], hxt[b]), V(fmview(xsrc, t), dh(xsn, t)))
            xnorm(xt[b], hxt[b], C_MIXG, uT[:, :, t * T:(t + 1) * T], hu[t], sq, hsq, rs, hrs, 0)

        def z_chunk(c):
            t = c // 4; b = c % 2
            us = lambda k: V(uT[:, k, c * 128:(c + 1) * 128], hu[t])
            for half in range(2):
                for k in range(8):
                    P.mm(PS(1 + half), us(k), V(wz[:, k, half * 512:(half + 1) * 512], hwz), start=(k == 0), stop=(k == 7))
            for k in range(8):
                P.mm(PS(3, 16), us(k), V(wz[:, k, 1024:1040], hwz), start=(k == 0), stop=(k == 7))
            for half in range(2):
                P.act(V(szb[b][:, half * 512:(half + 1) * 512], hszb[b]), PS(1 + half), AF.Silu)
            P.copy(V(dtraw[:, c, :], h_dtraw), PS(3, 16))
            P.dma("pool", V(sz_d[c * 128:(c + 1) * 128, :], dh("sz", t)), V(szb[b], hszb[b]))
        norm_tile(0)
        for t in range(NT):
            if t + 1 < NT:
                norm_tile(t + 1)
            for c in range(4 * t, 4 * t + 4):
                z_chunk(c)
        barrier()
        A.off = A1
        wg = [A.alloc(8 * 512, BF16).rearrange("p (k n) -> p k n", k=8) for _ in range(2)]; hwg = [H(), H()]
        pre = [A.alloc(S + 32, F32) for _ in range(2)]; hpre = [H(), H()]
        acc = [A.alloc(S, F32) for _ in range(2)]; hacc = [H(), H()]
        xo = [A.alloc(S, BF16) for _ in range(2)]; hxo = [H(), H()]
        for b in range(2):
            P.memset(V(pre[b][:, 0:32], hpre[b]), 0.0)
        win_v = w_in_d[l].rearrange("(k p) n -> p k n", p=128)
        wload(wg[0], win_v[:, :, OFF_XBC:OFF_XBC + 512], hwg[0])
        pend = None
        for j in range(16):
            g4 = j // 4; wb = g4 % 2; b = j % 2
            if j % 4 == 0 and g4 + 1 < 4:
                wload(wg[1 - wb], win_v[:, :, OFF_XBC + (g4 + 1) * 512:OFF_XBC + (g4 + 2) * 512], hwg[1 - wb])
            for t in range(NT):
                bank = 1 + (t % 2)
                for k in range(8):
                    P.mm(PS(bank), V(wg[wb][:, k, (j % 4) * 128:(j % 4 + 1) * 128], hwg[wb]),
                         V(uT[:, k, t * T:(t + 1) * T], hu[t]), start=(k == 0), stop=(k == 7))
                P.copy(V(pre[b][:, 32 + t * T:32 + (t + 1) * T], hpre[b]), PS(bank), eng="act")
            P.ts(V(acc[b], hacc[b]), V(pre[b][:, 29:29 + S], hpre[b]), pcol(C_SCW + j * 4 + 0), ALU.mult, pcol(C_SCB + j), ALU.add)
            for tap in range(1, 4):
                P.stt(V(acc[b], hacc[b]), V(pre[b][:, 29 + tap:29 + tap + S], hpre[b]), pcol(C_SCW + j * 4 + tap),
                      V(acc[b], hacc[b]), ALU.mult, ALU.add)
            if pend is not None:
                pend()

            def fin(b=b, j=j):
                P.act(V(xo[b], hxo[b]), V(acc[b], hacc[b]), AF.Silu)
                P.dma("pool", V(xbc_d[j * 128:(j + 1) * 128, :], dall("xbc")), V(xo[b], hxo[b]))
            pend = fin
        pend()
        barrier()
        A.off = A1
        wa = [A.alloc(8 * 128, BF16).rearrange("p (k n) -> p k n", k=8) for _ in range(2)]; hwa = [H(), H()]
        wgt = [A.alloc(8 * 128, BF16).rearrange("p (k n) -> p k n", k=8) for _ in range(2)]; hwgt = [H(), H()]
        vb = [A.alloc(S + 32, BF16) for _ in range(2)]; hvb = [[H() for _ in range(NT)] for _ in range(2)]; hvz = [H(), H()]
        dg = [A.alloc(31 * 128, BF16).rearrange("p (j n) -> p j n", j=31) for _ in range(2)]; hdg = [H(), H()]
        acc = [A.alloc(S, F32) for _ in range(2)]; hacc = [H(), H()]
        sg = [A.alloc(T, F32) for _ in range(2)]; hsg = [H(), H()]
        for b in range(2):
            P.memset(V(vb[b][:, 0:32], hvz[b]), 0.0)
        def prepj(j):
            b = j % 2
            wload(wa[b], win_v[:, :, OFF_GA + j * 128:OFF_GA + (j + 1) * 128], hwa[b])
            wload(wgt[b], win_v[:, :, OFF_GG + j * 128:OFF_GG + (j + 1) * 128], hwgt[b])
            for tap in range(31):
                P.ts(V(dg[b][:, tap, :], hdg[b]), identv, pcol(C_CDW + j * 31 + tap), ALU.mult)
        prepj(0)
        for j in range(8):
            b = j % 2
            if j + 1 < 8:
                prepj(j + 1)

            def glu(t):
                sb_ = t % 2
                for k in range(8):
                    P.mm(PS(1 + sb_), V(wa[b][:, k, :], hwa[b]), V(uT[:, k, t * T:(t + 1) * T], hu[t]), start=(k == 0), stop=(k == 7))
                for k in range(8):
                    P.mm(PS(3 + sb_), V(wgt[b][:, k, :], hwgt[b]), V(uT[:, k, t * T:(t + 1) * T], hu[t]), start=(k == 0), stop=(k == 7))
                P.act(V(sg[sb_], hsg[sb_]), PS(3 + sb_), AF.Sigmoid)
                P.tt(V(vb[b][:, 32 + t * T:32 + (t + 1) * T], hvb[b][t]), PS(1 + sb_), V(sg[sb_], hsg[sb_]), ALU.mult)

            def conv(t):
                cbk = 5 + t % 2
                rd_h = [hvb[b][t]] + ([hvb[b][t - 1]] if t > 0 else [hvz[b]])
                for tap in range(31):
                    o0 = 2 + tap + t * T
                    P.mm(PS(cbk), V(dg[b][:, tap, :], hdg[b]), V(vb[b][:, o0:o0 + T], rd_h), start=(tap == 0), stop=(tap == 30))
                P.act(V(acc[b][:, t * T:(t + 1) * T], hacc[b]), PS(cbk), AF.Identity, bias=pcol(C_CDB + j))
            glu(0)
            for t in range(NT):
                if t + 1 < NT:
                    glu(t + 1)
                conv(t)
            P.dma("pool", V(cv_d[j * 128:(j + 1) * 128, :], dall("cv")), V(acc[b], hacc[b]))
        barrier()
        A.off = A1
        wl = A.alloc(8 * 768, BF16).rearrange("p (k n) -> p k n", k=8); hwl = H()
        wload(wl[:, :, 0:704], win_v[:, :, OFF_QL:OFF_QL + 704], hwl)
        wload(wl[:, :, 704:736], win_v[:, :, OFF_KR + 32:OFF_KR + 64], hwl)
        wload(wl[:, :, 736:768], win_v[:, :, OFF_KR:OFF_KR + 32], hwl)
        lo = [A.alloc(T, F32) for _ in range(2)]; hlo = [H(), H()]
        segs = [(0, 128), (128, 128), (256, 128), (384, 128), (512, 128), (640, 64), (704, 64)]
        i = 0
        for t in range(NT):
            for (c0, m) in segs:
                b = i % 2; i += 1
                for k in range(8):
                    P.mm(PS(1 + b, T, m), V(wl[:, k, c0:c0 + m], hwl), V(uT[:, k, t * T:(t + 1) * T], hu[t]), start=(k == 0), stop=(k == 7))
                P.copy(V(lo[b][0:m], hlo[b]), PS(1 + b, T, m), eng="act")
                P.dma("pool", V(lat_d[c0:c0 + m, t * T:(t + 1) * T], dh("lat", t)), V(lo[b][0:m], hlo[b]))
        barrier()
    def phase_S(l):
        A = Arena(big, ARENA0, 103000)
        xbt = [A.alloc(16 * T, BF16).rearrange("p (k s) -> p k s", k=16) for _ in range(2)]; hxb = [H(), H()]
        szt = [A.alloc(1024, F32) for _ in range(2)]; hszt = [H(), H()]
        hst = A.alloc(1024, F32); h_hst = H()
        hbf = A.alloc(1024, BF16); h_hbf = H()
        D2 = lambda n, dt: ([A.alloc(n, dt) for _ in range(2)], [H(), H()])
        sm2, h_sm2 = D2(64, F32)
        ew2, h_ew2 = D2(48, F32)
        Xb2, h_Xb2 = D2(1024, BF16)
        Xw2, h_Xw2 = D2(1024, BF16)
        tD2, h_tD2 = D2(1024, F32)
        Bt2, h_Bt2 = D2(512, BF16)
        MT2, h_MT2 = D2(2048, BF16)
        cbm = A.alloc(512, F32); h_cbm = H()
        AGh = A.alloc(2048, BF16); h_AGh = H()
        AGl = A.alloc(2048, BF16); h_AGl = H()
        gtb = A.alloc(128, BF16); h_gtb = H()
        trib = A.alloc(128, BF16); h_trib = H()
        ahl = A.alloc(32, BF16); h_ahl = H()
        P.copy(V(gtb, h_gtb), gt); P.copy(V(trib, h_trib), tri)
        dec = [A.alloc(512, F32) for _ in range(2)]; h_dec = [H(), H()]
        yg = A.alloc(1024, F32); h_yg = H()
        t1 = [A.alloc(512, F32) for _ in range(2)]; h_t1 = [H(), H()]
        ssq = A.alloc(8, F32); h_ssq = H()
        junk = A.alloc(256, F32); h_junk = H()
        ynb = A.alloc(1024, BF16); h_ynb = H()
        yst = [A.alloc(8 * T, BF16).rearrange("p (k s) -> p k s", k=8) for _ in range(2)]; h_yst = [H(), H()]
        P.memset(V(hst, h_hst), 0.0); P.memset(V(hbf, h_hbf), 0.0)
        dtb = V(pbt[:, B_DTB:B_DTB + 16], h_pb)
        sng = V(pbt[:, B_SNG:B_SNG + 1024], h_pb)
        xview = xbc_d.rearrange("(k p) s -> p k s", p=128)
        SMALL = lambda lo, n: V(banks[2][:, 256 + lo:256 + lo + n], bh[2])

        def bc16(ap, lo, n):
            return ap[:, lo:lo + n].unsqueeze(2).broadcast_to([128, n, 64])
        v3 = lambda ap: ap.rearrange("p (h d) -> p h d", h=16)

        def stage1(c):
            t = c // 4; s_ = c % 4; tb = t % 2; cb_ = c % 2
            cs = slice(s_ * 128, (s_ + 1) * 128)
            if s_ == 0:
                P.dma("sp", V(xbt[tb], hxb[tb]), V(xview[:, :, t * T:(t + 1) * T], dall("xbc")))
            P.dma("sp", V(szt[cb_], hszt[cb_]), V(sz_d[c * 128:(c + 1) * 128, :], dh("sz", t)))
            xb_ = xbt[tb]; hx_ = hxb[tb]
            sm, h_sm, ew, h_ew = sm2[cb_], h_sm2[cb_], ew2[cb_], h_ew2[cb_]
            Xb, h_Xb, Xw, h_Xw, tD, h_tD = Xb2[cb_], h_Xb2[cb_], Xw2[cb_], h_Xw2[cb_], tD2[cb_], h_tD2[cb_]
            Btm, h_Btm, MT, h_MT = Bt2[cb_], h_Bt2[cb_], MT2[cb_], h_MT2[cb_]
            for k in range(8):
                P.transpose(V(banks[1][:].bitcast(BF16)[:, k * 128:(k + 1) * 128], bh[1]), V(xb_[:, k, cs], hx_), identv)
            for g in range(4):
                P.transpose(V(banks[2][:].bitcast(BF16)[:, g * 128:(g + 1) * 128], bh[2]), V(xb_[:, 8 + g, cs], hx_), identv)
            for g in range(4):
                P.mm(V(banks[3][:, g * 128:(g + 1) * 128], bh[3]), V(xb_[:, 8 + g, cs], hx_), V(xb_[:, 12 + g, cs], hx_))
            P.tt(V(sm[:, 0:16], h_sm), V(dtraw[:, c, :], h_dtraw), dtb, ALU.add)
            P.act(V(sm[:, 0:16], h_sm), V(sm[:, 0:16], h_sm), AF.Exp)
            P.act(V(sm[:, 0:16], h_sm), V(sm[:, 0:16], h_sm), AF.Ln, bias=1.0)
            P.tt(V(sm[:, 16:32], h_sm), V(sm[:, 0:16], h_sm), V(abc, h_abc), ALU.mult)
            av = V(sm[:, 16:32], h_sm)
            P.mm(SMALL(0, 16), tri, av); P.mm(SMALL(16, 16), gt, av); P.mm(SMALL(32, 16), onesf, av)
            P.act(V(ew, h_ew), SMALL(0, 48), AF.Exp)
            P.tt(V(sm[:, 32:48], h_sm), V(sm[:, 0:16], h_sm), V(ew[:, 16:32], h_ew), ALU.mult)
            P.copy(V(ahl[:, 0:16], h_ahl), av)
            P.tt(V(ahl[:, 16:32], h_ahl), av, V(ahl[:, 0:16], h_ahl), ALU.subtract)
            P.tt(V(AGh.rearrange("p (h s) -> p h s", h=16), h_AGh),
                 V(gtb.unsqueeze(1).broadcast_to([128, 16, 128]), h_gtb),
                 V(ahl[:, 0:16].unsqueeze(2).broadcast_to([128, 16, 128]), h_ahl), ALU.mult)
            P.tt(V(AGl.rearrange("p (h s) -> p h s", h=16), h_AGl),
                 V(gtb.unsqueeze(1).broadcast_to([128, 16, 128]), h_gtb),
                 V(ahl[:, 16:32].unsqueeze(2).broadcast_to([128, 16, 128]), h_ahl), ALU.mult)
            P.tt(V(cbm.rearrange("p (g l) -> p g l", g=4), h_cbm), V(banks[3][:].rearrange("p (g l) -> p g l", g=4), bh[3]),
                 V(cst[:, K_TRI:K_TRI + 128].unsqueeze(1).broadcast_to([128, 4, 128]), h_cst), ALU.mult)
            xsT = banks[1][:].bitcast(BF16)[:, 0:1024].rearrange("p (h d) -> p h d", h=16)
            P.tt(V(v3(Xb), h_Xb), V(xsT, bh[1]), V(bc16(sm, 0, 16), h_sm), ALU.mult)
            P.tt(V(v3(Xw), h_Xw), V(xsT, bh[1]), V(bc16(sm, 32, 16), h_sm), ALU.mult)
            P.tt(V(v3(tD), h_tD), V(xsT, bh[1]), V(bc16(pbt, B_DSK, 16), h_pb), ALU.mult)
            P.copy(V(Btm, h_Btm), V(banks[2][:].bitcast(BF16)[:, 0:512], bh[2]), eng="act")
            for g in range(4):
                db = g % 2
                for r in range(4):
                    hd_ = g * 4 + r
                    P.mm(V(banks[4][:, r * 128:(r + 1) * 128], bh[4]), V(AGh[:, hd_ * 128:(hd_ + 1) * 128], h_AGh), V(trib, h_trib), start=True, stop=False)
                    P.mm(V(banks[4][:, r * 128:(r + 1) * 128], bh[4]), V(AGl[:, hd_ * 128:(hd_ + 1) * 128], h_AGl), V(trib, h_trib), start=False, stop=True)
                P.act(V(dec[db], h_dec[db]), PS(4), AF.Exp)
                P.tt(V(MT[:, g * 512:(g + 1) * 512].rearrange("p (r l) -> p r l", r=4), h_MT), V(dec[db].rearrange("p (r l) -> p r l", r=4), h_dec[db]),
                     V(cbm[:, g * 128:(g + 1) * 128].unsqueeze(1).broadcast_to([128, 4, 128]), h_cbm), ALU.mult)

        def stage2(c):
            t = c // 4; s_ = c % 4; tb = t % 2; cb_ = c % 2
            cs = slice(s_ * 128, (s_ + 1) * 128)
            xb_ = xbt[tb]; hx_ = hxb[tb]
            ew, h_ew = ew2[cb_], h_ew2[cb_]
            Xb, h_Xb, Xw, h_Xw, tD, h_tD = Xb2[cb_], h_Xb2[cb_], Xw2[cb_], h_Xw2[cb_], tD2[cb_], h_tD2[cb_]
            Btm, h_Btm, MT, h_MT = Bt2[cb_], h_Bt2[cb_], MT2[cb_], h_MT2[cb_]
            for hd_ in range(16):
                ybank = 5 + hd_ // 8
                col = (hd_ % 8) * 64
                P.mm(V(banks[ybank][:, col:col + 64], bh[ybank]), V(MT[:, hd_ * 128:(hd_ + 1) * 128], h_MT),
                     V(Xb[:, hd_ * 64:(hd_ + 1) * 64], h_Xb))
            for hf in range(2):
                ybank = 5 + hf
                for gg in range(2):
                    g = hf * 2 + gg
                    P.mm(V(banks[7][:, gg * 256:(gg + 1) * 256], bh[7]), V(xb_[:, 12 + g, cs], hx_), V(hbf[:, g * 256:(g + 1) * 256], h_hbf))
                t1_ = t1[hf]; ht1 = h_t1[hf]
                P.tt(V(t1_.rearrange("p (h d) -> p h d", h=8), ht1), V(banks[7][:].rearrange("p (h d) -> p h d", h=8), bh[7]), V(bc16(ew, hf * 8, 8), h_ew), ALU.mult)
                P.tt(V(t1_, ht1), V(t1_, ht1), PS(ybank), ALU.add)
                P.tt(V(t1_, ht1), V(t1_, ht1), V(tD[:, hf * 512:(hf + 1) * 512], h_tD), ALU.add)
                P.tt(V(yg[:, hf * 512:(hf + 1) * 512], h_yg), V(t1_, ht1), V(szt[cb_][:, hf * 512:(hf + 1) * 512], hszt[cb_]), ALU.mult)
            for hf in range(2):
                for gg in range(2):
                    g = hf * 2 + gg
                    P.mm(V(banks[7][:, gg * 256:(gg + 1) * 256], bh[7]), V(Btm[:, g * 128:(g + 1) * 128], h_Btm), V(Xw[:, g * 256:(g + 1) * 256], h_Xw))
                hv = V(hst[:, hf * 512:(hf + 1) * 512].rearrange("p (h d) -> p h d", h=8), h_hst)
                P.tt(hv, hv, V(bc16(ew, 32 + hf * 8, 8), h_ew), ALU.mult)
                P.tt(V(hst[:, hf * 512:(hf + 1) * 512], h_hst), V(hst[:, hf * 512:(hf + 1) * 512], h_hst), PS(7), ALU.add)
            P.copy(V(hbf, h_hbf), V(hst, h_hst), eng="act")
            for g in range(4):
                P.act(V(junk, h_junk), V(yg[:, g * 256:(g + 1) * 256], h_yg), AF.Square, accum=V(ssq[:, g:g + 1], h_ssq))
            P.act(V(ssq[:, 4:8], h_ssq), V(ssq[:, 0:4], h_ssq), AF.Ln, bias=epsv, scale=1.0 / 256)
            P.act(V(ssq[:, 4:8], h_ssq), V(ssq[:, 4:8], h_ssq), AF.Exp, scale=-0.5)
            P.tt(V(yg.rearrange("p (g d) -> p g d", g=4), h_yg), V(yg.rearrange("p (g d) -> p g d", g=4), h_yg),
                 V(ssq[:, 4:8].unsqueeze(2).broadcast_to([128, 4, 256]), h_ssq), ALU.mult)
            P.tt(V(ynb, h_ynb), V(yg, h_yg), sng, ALU.mult)
            for k in range(8):
                P.transpose(V(banks[0][:].bitcast(BF16)[:, k * 128:(k + 1) * 128], bh[0]), V(ynb[:, k * 128:(k + 1) * 128], h_ynb), identv)
            P.copy(V(yst[tb][:, :, cs], h_yst[tb]), V(banks[0][:].bitcast(BF16)[:, 0:1024].rearrange("p (k s) -> p k s", k=8), bh[0]), eng="act")
            if s_ == 3:
                P.dma("pool", V(ys_d.rearrange("(k p) s -> p k s", p=128)[:, :, t * T:(t + 1) * T], dh("ys", t)), V(yst[tb], h_yst[tb]))

        stage1(0)
        for c in range(NCH):
            la = P.capture()
            if c + 1 < NCH:
                stage1(c + 1)
            lb = P.capture()
            stage2(c)
            P.interleave(la, lb)
        barrier()

    def phase_C(l, base=None, bk=(0, 1), do_barrier=True):
        A = Arena(big, ARENA0 if base is None else base, 103000)
        ct = [A.alloc(8 * T, F32).rearrange("p (k s) -> p k s", k=8) for _ in range(2)]; hct = [H(), H()]
        cb16 = A.alloc(8 * T, BF16).rearrange("p (k s) -> p k s", k=8); hcb = H()
        sq = A.alloc(8 * T, BF16).rearrange("p (k s) -> p k s", k=8); hsq = H()
        mean = A.alloc(T, F32); hmean = H()
        var = A.alloc(T, F32); hvar = H()
        tmp = A.alloc(T, F32); htmp = H()
        yo = [A.alloc(8 * T, BF16).rearrange("p (k s) -> p k s", k=8) for _ in range(2)]; hyo = [H(), H()]
        for t in range(NT):
            b = t % 2
            P.dma("sp", V(ct[b], hct[b]), V(fmview(cv_d, t), dall("cv")))
            P.copy(V(cb16, hcb), V(ct[b], hct[b]), eng="act")
            P.act(V(sq, hsq), V(ct[b], hct[b]), AF.Square)
            for k in range(8):
                P.mm(PS(bk[0]), onesb, V(cb16[:, k, :], hcb), start=(k == 0), stop=(k == 7))
            for k in range(8):
                P.mm(PS(bk[1]), onesb, V(sq[:, k, :], hsq), start=(k == 0), stop=(k == 7))
            P.ts(V(mean, hmean), PS(bk[0]), 1.0 / D, ALU.mult)
            P.tt(V(tmp, htmp), V(mean, hmean), V(mean, hmean), ALU.mult)
            P.stt(V(var, hvar), PS(bk[1]), 1.0 / D, V(tmp, htmp), ALU.mult, ALU.subtract)
            P.act(V(var, hvar), V(var, hvar), AF.Ln, bias=epsv)
            P.act(V(var, hvar), V(var, hvar), AF.Exp, scale=-0.5)
            for k in range(8):
                P.tt(V(tmp, htmp), V(ct[b][:, k, :], hct[b]), V(mean, hmean), ALU.subtract)
                P.tt(V(tmp, htmp), V(tmp, htmp), V(var, hvar), ALU.mult)
                P.act(V(yo[b][:, k, :], hyo[b]), V(tmp, htmp), AF.Silu, bias=pcol(C_LNB + k), scale=pcol(C_LNG + k))
            P.dma("pool", V(fmview(yc_d, t), dh("yc", t)), V(yo[b], hyo[b]))
        if do_barrier:
            barrier()
    def phase_M1(l):
        A = Arena(big, ARENA0, 103000)
        wq = A.alloc(3 * 1536, BF16).rearrange("p (k n) -> p k n", k=3); hwq = H()
        wqs = A.alloc(3 * 512, BF16).rearrange("p (k n) -> p k n", k=3); hwqs = H()
        wkn = A.alloc(2 * 1024, BF16).rearrange("p (k n) -> p k n", k=2); hwkn = H()
        wv = A.alloc(2 * 1024, BF16).rearrange("p (k n) -> p k n", k=2); hwv = H()
        wload(wq, wqb_d[l].rearrange("(k p) n -> p k n", p=128), hwq)
        qv = wqb_d[l].rearrange("(k p) (h c) -> p k h c", p=128, h=8)
        wqs4 = wqs.rearrange("p k (h c) -> p k h c", h=8)
        for k in range(3):
            wload(wqs4[:, k, :, 0:32], qv[:, k, :, 160:192], hwqs)
            wload(wqs4[:, k, :, 32:64], qv[:, k, :, 128:160], hwqs)
        kv4 = wkvb_d[l].rearrange("(k p) (h c) -> p k h c", p=128, h=8)
        for k in range(2):
            wload(wkn.rearrange("p k (h c) -> p k h c", h=8)[:, k], kv4[:, k, :, 0:128], hwkn)
            wload(wv.rearrange("p k (h c) -> p k h c", h=8)[:, k], kv4[:, k, :, 128:256], hwv)
        lt = [A.alloc(6 * T, F32).rearrange("p (k s) -> p k s", k=6) for _ in range(2)]; hlt = [H(), H()]
        rp = [A.alloc(T, F32) for _ in range(2)]; hrp = [H(), H()]
        rp2 = [A.alloc(T, F32) for _ in range(2)]; hrp2 = [H(), H()]
        sq = A.alloc(3 * T, BF16).rearrange("p (k s) -> p k s", k=3); hsq = H()
        rs = A.alloc(T, F32); hrs = H()
        qn = A.alloc(3 * T, BF16).rearrange("p (k s) -> p k s", k=3); hqn = H()
        kvn = A.alloc(2 * T, BF16).rearrange("p (k s) -> p k s", k=2); hkvn = H()
        sqh = [A.alloc(T, BF16) for _ in range(3)]; hsqh = [H() for _ in range(3)]
        rsh = [A.alloc(T, F32) for _ in range(3)]; hrsh = [H() for _ in range(3)]
        krb = A.alloc(T, F32); hkrb = H()
        ob = [A.alloc(T, BF16) for _ in range(6)]; hob = [H() for _ in range(6)]
        ta = A.alloc(T, F32); hta = H()
        tb_ = A.alloc(T, F32); htb = H()
        vb = [A.alloc(1024, BF16) for _ in range(2)]; hvb = [H(), H()]
        latv = lat_d.rearrange("(k p) s -> p k s", p=128)
        oi = 0
        for t in range(NT):
            b = t % 2; sl = slice(t * T, (t + 1) * T)
            P.dma("sp", V(lt[b], hlt[b]), V(latv[:, :, sl], dh("lat", t)))
            P.dma("sp", V(lt[b][0:64, 5, :], hlt[b]), V(lat_d[704:768, sl], dh("lat", t)))
            P.dma("sp", V(rp[b][0:64], hrp[b]), V(rope_d[0:64, sl], dh("rope", t)))
            P.dma("sp", V(rp2[b][0:64], hrp2[b]), V(rope_d[64:128, sl], dh("rope", t)))
            L_ = lt[b]; hL = hlt[b]
            cosv = V(rp[b][0:64], hrp[b]); sinv = V(rp2[b][0:64], hrp2[b])
            xnorm(L_[:, 0:3, :], hL, C_QAG, qn, hqn, sq, hsq, rs, hrs, 0, nk=3, inv_n=1.0 / 384)
            xnorm(L_[:, 3:5, :], hL, C_KVAG, kvn, hkvn, sq[:, 0:2, :], hsq, rs, hrs, 0, nk=2, inv_n=1.0 / 256)

            for h in range(8):
                for k in range(3):
                    P.mm(PS(1), V(wq[:, k, h * 192:h * 192 + 128], hwq), V(qn[:, k, :], hqn), start=(k == 0), stop=(k == 2))
                for k in range(3):
                    P.mm(PS(2, T, 64), V(wq[:, k, h * 192 + 128:h * 192 + 192], hwq), V(qn[:, k, :], hqn), start=(k == 0), stop=(k == 2))
                for k in range(3):
                    P.mm(PS(3, T, 64), V(wqs[:, k, h * 64:(h + 1) * 64], hwqs), V(qn[:, k, :], hqn), start=(k == 0), stop=(k == 2))
                for k in range(2):
                    P.mm(PS(4), V(wkn[:, k, h * 128:(h + 1) * 128], hwkn), V(kvn[:, k, :], hkvn), start=(k == 0), stop=(k == 1))
                specs = [(1, 128, 1.0 / 128, 5), (2, 64, 1.0 / 64, 6), (4, 128, 1.0 / 128, 7)]
                for i, (bank, parts, inv_n, nb) in enumerate(specs):
                    P.act(V(sqh[i][0:parts], hsqh[i]), PS(bank, T, parts), AF.Square)
                for i, (bank, parts, inv_n, nb) in enumerate(specs):
                    P.mm(PS(nb, T, parts), V(ones_b[0:parts, 0:parts], h_ob), V(sqh[i][0:parts], hsqh[i]))
                for i, (bank, parts, inv_n, nb) in enumerate(specs):
                    P.act(V(rsh[i][0:parts], hrsh[i]), PS(nb, T, parts), AF.Ln, bias=V(epsc[0:parts], h_eps), scale=inv_n)
                    P.act(V(rsh[i][0:parts], hrsh[i]), V(rsh[i][0:parts], hrsh[i]), AF.Exp, scale=-0.5)
                o1 = oi % 6; o2 = (oi + 1) % 6; o3 = (oi + 2) % 6; oi += 3
                P.stt(V(ob[o1], hob[o1]), PS(1), V(gsc[:, 0:1], h_gsc), V(rsh[0], hrsh[0]), ALU.mult, ALU.mult)
                P.stt(V(ob[o3], hob[o3]), PS(4), pcol(C_KNN), V(rsh[2], hrsh[2]), ALU.mult, ALU.mult)
                P.stt(V(ta[0:64], hta), PS(2, T, 64), V(gsc[0:64, 1:2], h_gsc), cosv, ALU.mult, ALU.mult)
                P.stt(V(tb_[0:64], htb), PS(3, T, 64), V(gsc[0:64, 2:3], h_gsc), sinv, ALU.mult, ALU.mult)
                P.tt(V(ta[0:64], hta), V(ta[0:64], hta), V(tb_[0:64], htb), ALU.add)
                P.tt(V(ob[o2][0:64], hob[o2]), V(ta[0:64], hta), V(rsh[1][0:64], hrsh[1]), ALU.mult)
                P.dma("pool", V(q_d[h, 0:128, sl], dh("q", t)), V(ob[o1], hob[o1]))
                P.dma("pool", V(q_d[h, 128:192, sl], dh("q", t)), V(ob[o2][0:64], hob[o2]))
                P.dma("pool", V(k_d[h, :, sl], dh("k", t)), V(ob[o3], hob[o3]))
            P.dma("sp", V(krb[0:64], hkrb), V(lat_d[640:704, sl], dh("lat", t)))
            krv = V(krb[0:64], hkrb); krsv = V(L_[0:64, 5, :], hL)
            P.act(V(sqh[1][0:64], hsqh[1]), krv, AF.Square)
            P.mm(PS(7, T, 64), V(ones_b[0:64, 0:64], h_ob), V(sqh[1][0:64], hsqh[1]))
            P.act(V(rsh[1][0:64], hrsh[1]), PS(7, T, 64), AF.Ln, bias=V(epsc[0:64], h_eps), scale=1.0 / 64)
            P.act(V(rsh[1][0:64], hrsh[1]), V(rsh[1][0:64], hrsh[1]), AF.Exp, scale=-0.5)
            P.stt(V(ta[0:64], hta), krv, pcol(C_KNR, 1, 64), cosv, ALU.mult, ALU.mult)
            P.stt(V(tb_[0:64], htb), krsv, pcol(C_KNRS, 1, 64), sinv, ALU.mult, ALU.mult)
            P.tt(V(ta[0:64], hta), V(ta[0:64], hta), V(tb_[0:64], htb), ALU.add)
            o = oi % 6; oi += 1
            P.tt(V(ob[o][0:64], hob[o]), V(ta[0:64], hta), V(rsh[1][0:64], hrsh[1]), ALU.mult)
            P.dma("pool", V(kr_d[:, sl], dh("kr", t)), V(ob[o][0:64], hob[o]))
            for s_ in range(4):
                vb_ = s_ % 2
                for half in range(2):
                    for k in range(2):
                        P.mm(PS(2 + half), V(kvn[:, k, s_ * 128:(s_ + 1) * 128], hkvn), V(wv[:, k, half * 512:(half + 1) * 512], hwv),
                             start=(k == 0), stop=(k == 1))
                    P.copy(V(vb[vb_][:, half * 512:(half + 1) * 512], hvb[vb_]), PS(2 + half), eng="act")
                r0 = t * T + s_ * 128
                P.dma("pool", V(v_d[r0:r0 + 128, :], dh("v", t)), V(vb[vb_], hvb[vb_]))
        barrier()

    def phase_M2(l, do_barrier=True):
        A = Arena(big, ARENA0, 103000)
        krt = A.alloc(S, BF16); hkr = H()
        P.memset(V(krt[64:128], hkr), 0.0)
        P.dma("sp", V(krt[0:64], hkr), V(kr_d, dall("kr")))
        kt_ = [A.alloc(S, BF16) for _ in range(2)]; hkt = [H(), H()]
        qnt = [A.alloc(S, BF16) for _ in range(2)]; hqn = [H(), H()]
        qrt = [A.alloc(S, BF16) for _ in range(2)]; hqr = [H(), H()]
        for b_ in range(2):
            P.memset(V(qrt[b_][64:128], hqr[b_]), 0.0)
        vt = [A.alloc(NCH * 128, BF16).rearrange("p (c d) -> p c d", c=NCH) for _ in range(2)]; hvt = [H(), H()]
        pt = [A.alloc(T, BF16) for _ in range(4)]; hpt = [H() for _ in range(4)]
        pd = [A.alloc(T, BF16) for _ in range(4)]; hpd = [H() for _ in range(4)]
        for j in range(4):
            P.memset(V(pd[j], hpd[j]), 0.0)
        dacc = [A.alloc(T, F32) for _ in range(2)]; hdacc = [H(), H()]
        rd = A.alloc(T, F32); hrd = H()
        dhi = A.alloc(T, BF16); hdhi = H()
        dlo = A.alloc(T, BF16); hdlo = H()
        oo = [A.alloc(T, BF16) for _ in range(2)]; hoo = [H(), H()]
        mk = masks.rearrange("p (j q) -> p j q", j=4)
        pt2 = [A.alloc(2 * T, BF16) for _ in range(3)]; hpt2 = [H() for _ in range(3)]
        items = [(h, qt) for h in range(8) for qt in range(NT)]

        def load_head(h):
            b = h % 2
            P.dma("sp", V(kt_[b], hkt[b]), V(k_d[h], dall("k")))
            P.dma("sp", V(qnt[b], hqn[b]), V(q_d[h, 0:128, :], dall("q")))
            P.dma("sp", V(qrt[b][0:64], hqr[b]), V(q_d[h, 128:192, :], dall("q")))
            P.dma("sp", V(vt[b], hvt[b]), V(v_d.rearrange("(c p) d -> p c d", p=128)[:, :, h * 128:(h + 1) * 128], dall("v")))

        def units_of(h, qt):
            return [(h, qt, "pair", k0) for k0 in range(0, 4 * qt, 2)] + [(h, qt, "diag", 4 * qt + j) for j in range(4)]
        allu = [u for (h, qt) in items for u in units_of(h, qt)]

        def st_tile(h_, qt_, kt, bank):
            b_ = h_ % 2
            ks = slice(kt * 128, (kt + 1) * 128)
            qs_ = slice(qt_ * T, (qt_ + 1) * T)
            P.mm(PS(bank), V(kt_[b_][:, ks], hkt[b_]), V(qnt[b_][:, qs_], hqn[b_]), start=True, stop=False)
            P.mm(PS(bank), V(krt[:, ks], hkr), V(qrt[b_][:, qs_], hqr[b_]), start=False, stop=True)

        def st_unit(ui):
            h_, qt_, kind, k0 = allu[ui]
            base = 2 * (ui % 2)
            st_tile(h_, qt_, k0, base)
            if kind == "pair":
                st_tile(h_, qt_, k0 + 1, base + 1)
        load_head(0)
        st_unit(0)
        p2 = 0
        for ui, (h, qt, kind, k0) in enumerate(allu):
            b = h % 2
            if kind == "pair" and k0 == 0 and qt == 1 and h + 1 < 8:
                load_head(h + 1)
            qs = slice(qt * T, (qt + 1) * T)
            nk = 4 * qt + 4
            ob_ = 4 + (qt % 2)
            db_ = 6 + (qt % 2)
            da = qt % 2
            base = 2 * (ui % 2)
            if ui + 1 < len(allu):
                st_unit(ui + 1)
            tiles = []
            if kind == "pair":
                pb_ = p2 % 3; p2 += 1
                src = V(psall[:, base * 512:(base + 2) * 512], [bh[base], bh[base + 1]])
                P.act(V(pt2[pb_], hpt2[pb_]), src, AF.Exp)
                tiles = [(k0, V(pt2[pb_][:, 0:T], hpt2[pb_])), (k0 + 1, V(pt2[pb_][:, T:2 * T], hpt2[pb_]))]
            else:
                j = k0 - 4 * qt
                bk = banks[base]
                P.act(V(pd[j][0:64, 128 * j:T], hpd[j]), V(bk[0:64, 128 * j:T], bh[base]), AF.Exp)
                P.act(V(pd[j][64:128, 128 * j + 64:T], hpd[j]), V(bk[64:128, 128 * j + 64:T], bh[base]), AF.Exp)
                tiles = [(k0, V(pd[j], hpd[j]))]
            for kt, pv_ in tiles:
                P.mm(PS(ob_), V(vt[b][:, kt, :], hvt[b]), pv_, start=(kt == 0), stop=(kt == nk - 1))
                if kt % 2 == 1:
                    P.mm(PS(db_), onesb, pv_, start=(kt == 1), stop=False)
                elif kt == 0:
                    P.copy(V(dacc[da], hdacc[da]), pv_)
                else:
                    P.tt(V(dacc[da], hdacc[da]), V(dacc[da], hdacc[da]), pv_, ALU.add)
            if kind == "diag" and k0 == nk - 1:
                P.copy(V(dhi, hdhi), V(dacc[da], hdacc[da]))
                P.tt(V(dlo, hdlo), V(dacc[da], hdacc[da]), V(dhi, hdhi), ALU.subtract)
                P.mm(PS(db_), onesb, V(dhi, hdhi), start=False, stop=False)
                P.mm(PS(db_), onesb, V(dlo, hdlo), start=False, stop=True)
                P.act(V(rd, hrd), PS(db_), AF.Ln)
                P.act(V(rd, hrd), V(rd, hrd), AF.Exp, scale=-1.0)
                o = qt % 2
                P.tt(V(oo[o], hoo[o]), PS(ob_), V(rd, hrd), ALU.mult)
                P.dma("pool", V(o_d[h * 128:(h + 1) * 128, qs], dh("o", qt)), V(oo[o], hoo[o]))
        if do_barrier:
            barrier()
        return A.off
    def phase_G(l):
        xsrc = xT_d if l == 0 else xres_d
        xsn = "xT" if l == 0 else "xres"
        A = Arena(big, ARENA0, 103000)
        wgate = A.alloc(8 * 3072, BF16).rearrange("p (k n) -> p k n", k=8); hwg = H()
        wbr = [A.alloc(8 * 1024, BF16).rearrange("p (k n) -> p k n", k=8) for _ in range(3)]; hwb = [H() for _ in range(3)]
        wo = A.alloc(8 * 1024, BF16).rearrange("p (k n) -> p k n", k=8); hwo = H()
        kp = lambda d: d.rearrange("(k p) n -> p k n", p=128)
        for b3 in range(3):
            wload(wgate[:, :, b3 * 1024:(b3 + 1) * 1024], kp(w_in_d[l])[:, :, OFF_GATE + b3 * 1024:OFF_GATE + (b3 + 1) * 1024], hwg)
        for b3, wd in enumerate((ssd_wo_d, conv_wo_d, mla_wo_d)):
            wload(wbr[b3], kp(wd[l]), hwb[b3])
        wload(wo, kp(wout_d[l]), hwo)
        TG = 256; NTG = S // TG
        fmg = lambda d, t: d.rearrange("(k p) s -> p k s", p=128)[:, :, t * TG:(t + 1) * TG]
        xt3 = [A.alloc(8 * TG, F32).rearrange("p (k s) -> p k s", k=8) for _ in range(3)]; hxt3 = [H(), H(), H()]
        ut2 = [A.alloc(8 * TG, BF16).rearrange("p (k s) -> p k s", k=8) for _ in range(2)]; hut2 = [H(), H()]
        sq = A.alloc(8 * TG, BF16).rearrange("p (k s) -> p k s", k=8); hsq = H()
        rs = A.alloc(TG, F32); hrs = H()
        br2 = [[A.alloc(8 * TG, BF16).rearrange("p (k s) -> p k s", k=8) for _ in range(3)] for _ in range(2)]
        hbr2 = [[H() for _ in range(3)] for _ in range(2)]
        mg = A.alloc(8 * TG, BF16).rearrange("p (k s) -> p k s", k=8); hmg = H()
        gt_ = [A.alloc(TG, F32) for _ in range(2)]; hgt = [H(), H()]
        macc = A.alloc(TG, F32); hmacc = H()
        srcs = [(ys_d, "ys"), (yc_d, "yc"), (o_d, "o")]

        def prep(t):
            b = t % 2; b3x = t % 3
            P.dma("sp", V(xt3[b3x], hxt3[b3x]), V(fmg(xsrc, t), dh(xsn, t // 2)))
            for b3, (d_, nm) in enumerate(srcs):
                P.dma("sp", V(br2[b][b3], hbr2[b][b3]), V(fmg(d_, t), dall(nm)))
            xnorm(xt3[b3x], hxt3[b3x], C_MIXG, ut2[b], hut2[b], sq, hsq, rs, hrs, 0, width=TG)
        prep(0)
        for t in range(NTG):
            b = t % 2; b3x = t % 3
            xt = xt3[b3x]; hxt = hxt3[b3x]; ut = ut2[b]; hut = hut2[b]; br = br2[b]; hbr = hbr2[b]
            if t + 1 < NTG:
                prep(t + 1)
            gi = 0
            for m in range(8):
                for b3 in range(3):
                    gb = 1 + gi % 2; yb = 3 + gi % 2; g2 = gi % 2; gi += 1
                    for k in range(8):
                        P.mm(PS(gb, TG), V(wgate[:, k, b3 * 1024 + m * 128:b3 * 1024 + (m + 1) * 128], hwg), V(ut[:, k, :], hut), start=(k == 0), stop=(k == 7))
                    for k in range(8):
                        P.mm(PS(yb, TG), V(wbr[b3][:, k, m * 128:(m + 1) * 128], hwb[b3]), V(br[b3][:, k, :], hbr[b3]), start=(k == 0), stop=(k == 7))
                    P.act(V(gt_[g2], hgt[g2]), PS(gb, TG), AF.Sigmoid, bias=pcol(C_GATEB + b3 * 8 + m))
                    if b3 == 0:
                        P.tt(V(macc, hmacc), V(gt_[g2], hgt[g2]), PS(yb, TG), ALU.mult)
                    else:
                        P.tt(V(gt_[g2], hgt[g2]), V(gt_[g2], hgt[g2]), PS(yb, TG), ALU.mult)
                        if b3 == 1:
                            P.tt(V(macc, hmacc), V(macc, hmacc), V(gt_[g2], hgt[g2]), ALU.add)
                        else:
                            P.tt(V(mg[:, m, :], hmg), V(macc, hmacc), V(gt_[g2], hgt[g2]), ALU.add)
            for n in range(8):
                ob_ = 5 + n % 2
                for m in range(8):
                    P.mm(PS(ob_, TG), V(wo[:, m, n * 128:(n + 1) * 128], hwo), V(mg[:, m, :], hmg), start=(m == 0), stop=(m == 7))
                P.tt(V(xt[:, n, :], hxt), V(xt[:, n, :], hxt), PS(ob_, TG), ALU.add)
            P.dma("pool", V(fmg(xres_d, t), dh("xres", t // 2)), V(xt, hxt))
        barrier()

    def phase_X(l):
        A = Arena(big, ARENA0, 103000)
        kp = lambda d: d.rearrange("(k p) n -> p k n", p=128)
        wkv = A.alloc(8 * 2048, BF16).rearrange("p (k n) -> p k n", k=8); hwkv = H()
        wload(wkv[:, :, 0:1024], kp(xwkv_d[l])[:, :, 0:1024], hwkv)
        wload(wkv[:, :, 1024:2048], kp(xwkv_d[l])[:, :, 1024:2048], hwkv)
        mt = A.alloc(8 * 256, F32).rearrange("p (k s) -> p k s", k=8); hmt = H()
        mn = A.alloc(8 * 256, BF16).rearrange("p (k s) -> p k s", k=8); hmn = H()
        sqm = A.alloc(8 * 256, BF16).rearrange("p (k s) -> p k s", k=8); hsqm = H()
        rsm = A.alloc(256, F32); hrsm = H()
        KX = A.alloc(8 * 256, BF16).rearrange("p (c s) -> p c s", c=8); hKX = H()
        VX = A.alloc(2 * 1024, BF16).rearrange("p (m d) -> p m d", m=2); hVX = H()
        sq2 = A.alloc(2 * 256, BF16).rearrange("p (c s) -> p c s", c=2); hsq2 = H()
        P.dma("sp", V(mt, hmt), V(memT_d.rearrange("(k p) s -> p k s", p=128), H()))
        xnorm(mt, hmt, C_MEMG, mn, hmn, sqm, hsqm, rsm, hrsm, 0, nk=8, width=256)
        for hx in range(4):
            for c in range(2):
                for k in range(8):
                    P.mm(PS(1 + c, 256), V(wkv[:, k, hx * 256 + c * 128:hx * 256 + (c + 1) * 128], hwkv), V(mn[:, k, :], hmn), start=(k == 0), stop=(k == 7))
                P.act(V(sq2[:, c, :], hsq2), PS(1 + c, 256), AF.Square)
            for c in range(2):
                P.mm(PS(3, 256), onesb, V(sq2[:, c, :], hsq2), start=(c == 0), stop=(c == 1))
            P.act(V(rsm, hrsm), PS(3, 256), AF.Ln, bias=epsv, scale=1.0 / 256)
            P.act(V(rsm, hrsm), V(rsm, hrsm), AF.Exp, scale=-0.5)
            for c in range(2):
                P.stt(V(KX[:, hx * 2 + c, :], hKX), PS(1 + c, 256), pcol(C_XKG + c), V(rsm, hrsm), ALU.mult, ALU.mult)
        for m in range(2):
            for half in range(2):
                for k in range(8):
                    P.mm(PS(4 + half), V(mn[:, k, m * 128:(m + 1) * 128], hmn), V(wkv[:, k, 1024 + half * 512:1024 + (half + 1) * 512], hwkv), start=(k == 0), stop=(k == 7))
                P.copy(V(VX[:, m, half * 512:(half + 1) * 512], hVX), PS(4 + half), eng="act")
        barrier()
        A2 = Arena(big, ARENA0, 103000)
        KX2 = A2.alloc(8 * 256, BF16).rearrange("p (c s) -> p c s", c=8); hKX2 = H()
        VX2 = A2.alloc(2 * 1024, BF16).rearrange("p (m d) -> p m d", m=2); hVX2 = H()
        P.copy(V(KX2, hKX2), V(KX, hKX)); P.copy(V(VX2, hVX2), V(VX, hVX))
        barrier()
        A = A2
        wq_ = A.alloc(8 * 1024, BF16).rearrange("p (k n) -> p k n", k=8); hwq = H()
        wo_ = A.alloc(8 * 1024, BF16).rearrange("p (k n) -> p k n", k=8); hwo = H()
        wload(wq_, kp(xwq_d[l]), hwq); wload(wo_, kp(xwo_d[l]), hwo)
        xt = [A.alloc(8 * T, F32).rearrange("p (k s) -> p k s", k=8) for _ in range(3)]; hxt = [H(), H(), H()]
        ht2 = [A.alloc(8 * T, BF16).rearrange("p (k s) -> p k s", k=8) for _ in range(2)]; hht2 = [H(), H()]
        sq = A.alloc(8 * T, BF16).rearrange("p (k s) -> p k s", k=8); hsq = H()
        rs = A.alloc(T, F32); hrs = H()
        sqq = A.alloc(2 * T, BF16).rearrange("p (c s) -> p c s", c=2); hsqq = H()
        rsq = A.alloc(T, F32); hrsq = H()
        qx = A.alloc(2 * T, BF16).rearrange("p (c s) -> p c s", c=2); hqx = H()
        pt = [A.alloc(T, BF16) for _ in range(2)]; hpt = [H(), H()]
        rd = A.alloc(T, F32); hrd = H()
        ox = A.alloc(8 * T, BF16).rearrange("p (k s) -> p k s", k=8); hox = H()
        def prep(t):
            b = t % 2; b3 = t % 3
            P.dma("sp", V(xt[b3], hxt[b3]), V(fmview(xres_d, t), dh("xres", t)))
            xnorm(xt[b3], hxt[b3], C_XATG, ht2[b], hht2[b], sq, hsq, rs, hrs, 0)
        prep(0)
        for t in range(NT):
            b = t % 2
            ht = ht2[b]; hht = hht2[b]
            b = t % 3
            if t + 1 < NT:
                prep(t + 1)
            for hx in range(4):
                for c in range(2):
                    for k in range(8):
                        P.mm(PS(1 + c), V(wq_[:, k, hx * 256 + c * 128:hx * 256 + (c + 1) * 128], hwq), V(ht[:, k, :], hht), start=(k == 0), stop=(k == 7))
                    P.act(V(sqq[:, c, :], hsqq), PS(1 + c), AF.Square)
                for c in range(2):
                    P.mm(PS(3), onesb, V(sqq[:, c, :], hsqq), start=(c == 0), stop=(c == 1))
                P.act(V(rsq, hrsq), PS(3), AF.Ln, bias=epsv, scale=1.0 / 256)
                P.act(V(rsq, hrsq), V(rsq, hrsq), AF.Exp, scale=-0.5)
                for c in range(2):
                    P.stt(V(qx[:, c, :], hqx), PS(1 + c), V(gsc[:, 3 + c:4 + c], h_gsc), V(rsq, hrsq), ALU.mult, ALU.mult)
                for m in range(2):
                    for c in range(2):
                        P.mm(PS(4 + m), V(KX2[:, hx * 2 + c, m * 128:(m + 1) * 128], hKX2), V(qx[:, c, :], hqx), start=(c == 0), stop=(c == 1))
                    P.act(V(pt[m], hpt[m]), PS(4 + m), AF.Exp)
                for m in range(2):
                    P.mm(PS(3), onesb, V(pt[m], hpt[m]), start=(m == 0), stop=(m == 1))
                P.act(V(rd, hrd), PS(3), AF.Ln)
                P.act(V(rd, hrd), V(rd, hrd), AF.Exp, scale=-1.0)
                for c2 in range(2):
                    for m in range(2):
                        P.mm(PS(6 + c2), V(VX2[:, m, hx * 256 + c2 * 128:hx * 256 + (c2 + 1) * 128], hVX2), V(pt[m], hpt[m]), start=(m == 0), stop=(m == 1))
                    P.tt(V(ox[:, hx * 2 + c2, :], hox), PS(6 + c2), V(rd, hrd), ALU.mult)
            for n in range(8):
                ob_ = 1 + n % 2
                for k in range(8):
                    P.mm(PS(ob_), V(wo_[:, k, n * 128:(n + 1) * 128], hwo), V(ox[:, k, :], hox), start=(k == 0), stop=(k == 7))
                P.tt(V(xt[b][:, n, :], hxt[b]), V(xt[b][:, n, :], hxt[b]), PS(ob_), ALU.add)
            P.dma("pool", V(fmview(xres_d, t), dh("xres", t)), V(xt[b], hxt[b]))
        barrier()

    def phase_F(l, last):
        kp = lambda d: d.rearrange("(k p) n -> p k n", p=128)
        toks = []
        for hh in range(2):
            A = Arena(big, ARENA0, 103000)
            w1g = A.alloc(8 * 1408, BF16).rearrange("p (k n) -> p k n", k=8); hw1g = H()
            w1u = A.alloc(8 * 1408, BF16).rearrange("p (k n) -> p k n", k=8); hw1u = H()
            w2 = A.alloc(11 * 1024, BF16).rearrange("p (k n) -> p k n", k=11); hw2 = H()
            wload(w1g, kp(fwi_d[l])[:, :, hh * 1408:(hh + 1) * 1408], hw1g)
            wload(w1u, kp(fwi_d[l])[:, :, 2816 + hh * 1408:2816 + (hh + 1) * 1408], hw1u)
            wload(w2, fwo_d[l, hh * 1408:(hh + 1) * 1408, :].rearrange("(k p) n -> p k n", p=128), hw2)
            xt = [A.alloc(8 * T, F32).rearrange("p (k s) -> p k s", k=8) for _ in range(3)]; hxt = [H(), H(), H()]
            pt_ = A.alloc(8 * T, F32).rearrange("p (k s) -> p k s", k=8); hpt_ = H()
            ht2 = [A.alloc(8 * T, BF16).rearrange("p (k s) -> p k s", k=8) for _ in range(2)]; hht2 = [H(), H()]
            sq = A.alloc(8 * T, BF16).rearrange("p (k s) -> p k s", k=8); hsq = H()
            rs = A.alloc(T, F32); hrs = H()
            ac = A.alloc(11 * T, BF16).rearrange("p (k s) -> p k s", k=11); hac = H()
            sg = [A.alloc(T, F32) for _ in range(2)]; hsg = [H(), H()]
            def prep(t):
                b = t % 2; b3 = t % 3
                P.dma("sp", V(xt[b3], hxt[b3]), V(fmview(xres_d, t), dh("xres", t)))
                xnorm(xt[b3], hxt[b3], C_FFNG, ht2[b], hht2[b], sq, hsq, rs, hrs, 0)
            prep(0)
            for t in range(NT):
                b = t % 2
                ht = ht2[b]; hht = hht2[b]
                b = t % 3
                if t + 1 < NT:
                    prep(t + 1)
                if hh == 1:
                    P.dma("sp", V(pt_, hpt_), V(fmview(fp_d, t), dh("fp", t)))
                for j in range(11):
                    s2 = j % 2
                    for k in range(8):
                        P.mm(PS(1 + s2), V(w1g[:, k, j * 128:(j + 1) * 128], hw1g), V(ht[:, k, :], hht), start=(k == 0), stop=(k == 7))
                    for k in range(8):
                        P.mm(PS(3 + s2), V(w1u[:, k, j * 128:(j + 1) * 128], hw1u), V(ht[:, k, :], hht), start=(k == 0), stop=(k == 7))
                    P.act(V(sg[s2], hsg[s2]), PS(1 + s2), AF.Silu)
                    P.tt(V(ac[:, j, :], hac), V(sg[s2], hsg[s2]), PS(3 + s2), ALU.mult)
                for n in range(8):
                    ob_ = 5 + n % 2
                    for j in range(11):
                        P.mm(PS(ob_), V(w2[:, j, n * 128:(n + 1) * 128], hw2), V(ac[:, j, :], hac), start=(j == 0), stop=(j == 10))
                    if hh == 0:
                        P.copy(V(xt[b][:, n, :], hxt[b]), PS(ob_), eng="act")
                    else:
                        P.tt(V(xt[b][:, n, :], hxt[b]), V(xt[b][:, n, :], hxt[b]), PS(ob_), ALU.add)
                        P.tt(V(xt[b][:, n, :], hxt[b]), V(xt[b][:, n, :], hxt[b]), V(pt_[:, n, :], hpt_), ALU.add)
                if hh == 0:
                    P.dma("pool", V(fmview(fp_d, t), dh("fp", t)), V(xt[b], hxt[b]))
                else:
                    dst = out_d if last else xres_d
                    tk = P.dma("pool", V(fmview(dst, t), dh("out" if last else "xres", t)), V(xt[b], hxt[b]))
                    toks.append(tk)
            barrier()
        return toks

    def phase_M2C(l):
        la = P.capture()
        end = phase_M2(l, do_barrier=False)
        lb = P.capture()
        phase_C(l, base=end, bk=(6, 7), do_barrier=False)
        P.interleave(la, lb)
        barrier()

    phases = {"A": phase_A, "S": phase_S, "C": phase_C, "M1": phase_M1, "M2": phase_M2, "G": phase_G, "X": phase_X, "M2C": phase_M2C}
    order = ["A", "S", "C", "M1", "M2", "G", "X", "F"]
    P.marks = []
    for l in range(L):
        for ph in order:
            if only is not None and ph not in only:
                continue
            P.marks.append((l, ph, P.eng["dve"].n))
            if ph == "F":
                final_toks = phase_F(l, l == L - 1)
            else:
                phases[ph](l)
    P.finish(final_toks)
    return P


from concourse.bass_utils import run_bass_kernel_spmd


def _fm(v, nch):
    return np.ascontiguousarray(np.asarray(v, np.float32).reshape(nch, 128).T)


def _consts():
    c = np.zeros((128, NCST), np.float32)
    i = np.arange(128)
    c[:, K_TRI:K_TRI + 128] = (i[:, None] <= i[None, :])
    c[:, K_GT:K_GT + 128] = (i[:, None] > i[None, :])
    c[:, K_ID:K_ID + 128] = np.eye(128)
    f = np.arange(32)
    inv = (10000.0 ** (-(2 * f).astype(np.float32) / np.float32(64))).astype(np.float32)
    c[0:32, K_INV] = inv; c[32:64, K_INV] = inv
    q = np.arange(512)
    for j in range(4):
        c[:, K_MASK + j * 512:K_MASK + (j + 1) * 512] = ((2 * j + i[:, None] // 64) <= (q[None, :] // 64))
    return c


def _pack_params(inp):
    L = 4
    pp = np.zeros((L, 128, NPP), np.float32)
    pb = np.zeros((L, NPB), np.float32)
    for l in range(L):
        p = pp[l]
        p[:, C_MIXG:C_MIXG + 8] = _fm(inp["mix_norm_g"][l], 8)
        p[:, C_XATG:C_XATG + 8] = _fm(inp["xattn_norm_g"][l], 8)
        p[:, C_FFNG:C_FFNG + 8] = _fm(inp["ffn_norm_g"][l], 8)
        p[:, C_MEMG:C_MEMG + 8] = _fm(inp["mem_norm_g"][l], 8)
        p[:, C_SCW:C_SCW + 64] = np.asarray(inp["ssd_conv_w"][l]).T.reshape(16, 128, 4).transpose(1, 0, 2).reshape(128, 64)
        p[:, C_SCB:C_SCB + 16] = _fm(inp["ssd_conv_b"][l], 16)
        p[:, C_CDW:C_CDW + 248] = np.asarray(inp["conv_dw_w"][l]).T.reshape(8, 128, 31).transpose(1, 0, 2).reshape(128, 248)
        p[:, C_CDB:C_CDB + 8] = _fm(inp["conv_dw_b"][l], 8)
        p[:, C_LNG:C_LNG + 8] = _fm(inp["conv_ln_g"][l], 8)
        p[:, C_LNB:C_LNB + 8] = _fm(inp["conv_ln_b"][l], 8)
        p[:, C_QAG:C_QAG + 3] = _fm(inp["mla_q_a_g"][l], 3)
        p[:, C_KVAG:C_KVAG + 2] = _fm(inp["mla_kv_a_g"][l], 2)
        for (cn, cr, crs, g) in ((C_QNN, C_QNR, C_QNRS, inp["mla_q_norm_g"][l]), (C_KNN, C_KNR, C_KNRS, inp["mla_k_norm_g"][l])):
            g = np.asarray(g, np.float32)
            p[:, cn] = g[0:128]
            p[0:64, cr] = g[128:192]
            p[0:64, crs] = np.concatenate([g[160:192], g[128:160]])
        p[:, C_GATEB:C_GATEB + 24] = np.asarray(inp["gate_b"][l], np.float32).reshape(24, 128).T
        p[:, C_XQG:C_XQG + 2] = _fm(inp["xattn_q_norm_g"][l], 2)
        p[:, C_XKG:C_XKG + 2] = _fm(inp["xattn_k_norm_g"][l], 2)
        pb[l, B_DTB:B_DTB + 16] = inp["ssd_dt_bias"][l]
        pb[l, B_ALOG:B_ALOG + 16] = inp["ssd_a_log"][l]
        pb[l, B_DSK:B_DSK + 16] = inp["ssd_d"][l]
        pb[l, B_SNG:B_SNG + 1024] = inp["ssd_norm_g"][l]
    return pp, pb


WNAMES = ["w_in", "ssd_w_out", "conv_w_out", "mla_w_q_b", "mla_w_kv_b", "mla_w_o", "w_out", "xattn_w_q", "xattn_w_kv",
          "xattn_w_o", "ffn_w_in", "ffn_w_out"]


def make_in_maps(inp, cores):
    pp, pb = _pack_params(inp)
    cst = _consts()
    shared = {n: np.ascontiguousarray(np.asarray(inp[n], np.float32)) for n in WNAMES}
    shared.update(cst=cst, pp=pp, pb=pb)
    maps = []
    for b in cores:
        m = dict(shared)
        m["xT"] = np.ascontiguousarray(np.asarray(inp["x"][b], np.float32).T)
        m["memT"] = np.ascontiguousarray(np.asarray(inp["mem"][b], np.float32).T)
        m["pos"] = np.ascontiguousarray(np.asarray(inp["positions"][b], np.int32)[None, :])
        maps.append(m)
    return maps


_CACHE = {}


def kernel(**inputs):
    if "P" not in _CACHE:
        _CACHE["P"] = build(L=4)
    P = _CACHE["P"]
    maps = make_in_maps(inputs, list(range(8)))
    res = run_bass_kernel_spmd(P.nc, maps, core_ids=list(range(8)))
    out = np.stack([np.ascontiguousarray(r["outT"].T) for r in res.results], axis=0)
    return out.astype(np.float32)
```

```python
import contextlib
import numpy as np
import concourse.bass as bass
import concourse.mybir as mybir

F32 = mybir.dt.float32
BF16 = mybir.dt.bfloat16
I32 = mybir.dt.int32
AF = mybir.ActivationFunctionType
ALU = mybir.AluOpType
AX = mybir.AxisListType

ENGS = ("pe", "act", "dve", "pool", "sp")
NDMA_SEMS = 24


class H:
    __slots__ = ("w", "r")

    def __init__(self):
        self.w = None
        self.r = []


class V:
    __slots__ = ("ap", "hs")

    def __init__(self, ap, hs):
        self.ap = ap
        self.hs = hs if isinstance(hs, (list, tuple)) else [hs]


class Eng:
    def __init__(self, name, idx):
        self.name = name
        self.idx = idx
        self.n = 0
        self.seen = [0] * len(ENGS)
        self.seen_dma = {}
        self.ops = []
        self.dma_count = 0


class Prog:
    def __init__(self):
        self.nc = bass.Bass("TRN2", target_bir_lowering=False)
        self.stack = contextlib.ExitStack()
        self.eng = {n: Eng(n, i) for i, n in enumerate(ENGS)}
        self.snaps = {}
        self.ntens = 0
        self.nwaits = 0

    def sb(self, shape, dtype, name=None):
        self.ntens += 1
        t = self.stack.enter_context(self.nc.sbuf_tensor(name or f"sb{self.ntens}", list(shape), dtype))
        return t

    def ps(self, shape, dtype, name=None):
        self.ntens += 1
        t = self.stack.enter_context(self.nc.psum_tensor(name or f"ps{self.ntens}", list(shape), dtype))
        return t

    def dram(self, name, shape, dtype, kind="Internal"):
        return self.nc.dram_tensor(name, list(shape), dtype, kind=kind).ap()

    def _deps(self, reads, writes):
        deps = {}

        def add(tok):
            k, v = tok
            if deps.get(k, 0) < v:
                deps[k] = v
        for h in reads:
            if h.w is not None:
                add(h.w)
        for h in writes:
            if h.w is not None:
                add(h.w)
            for t in h.r:
                add(t)
        return deps

    def _waits(self, e, deps):
        waits = []
        for k, v in deps.items():
            if isinstance(k, int):
                if k == e.idx and e.name == "pe":
                    continue
                if e.seen[k] >= v:
                    continue
                waits.append((k, v))
            else:
                if e.seen_dma.get(k, 0) >= v:
                    continue
                waits.append((k, v))
        for k, v in waits:
            if isinstance(k, int):
                if e.seen[k] < v:
                    e.seen[k] = v
            else:
                e.seen_dma[k] = v
            snap = self.snaps.get((k, v))
            if snap is not None:
                for i in range(len(ENGS)):
                    if e.seen[i] < snap[i]:
                        e.seen[i] = snap[i]
        return waits

    def capture(self):
        self._cap = []
        return self._cap

    def end_capture(self):
        self._cap = None

    def interleave(self, la, lb):
        self._cap = None
        na, nb = len(la), len(lb)
        ia = ib = 0
        while ia < na or ib < nb:
            if ib >= nb or (ia < na and ia * nb <= ib * na):
                kind, args, kw = la[ia]; ia += 1
            else:
                kind, args, kw = lb[ib]; ib += 1
            (self.op if kind == "op" else self.dma)(*args, **kw)

    def op(self, engname, fn, reads, writes):
        if getattr(self, "_cap", None) is not None:
            self._cap.append(("op", (engname, fn, reads, writes), {}))
            return None
        e = self.eng[engname]
        rh = [h for v in reads for h in v.hs]
        wh = [h for v in writes for h in v.hs]
        deps = self._deps(rh, wh)
        waits = self._waits(e, deps)
        e.n += 1
        tok = (e.idx, e.n)
        self.snaps[tok] = tuple(e.seen)
        e.ops.append((waits, fn, None))
        self.nwaits += len(waits)
        for h in rh:
            h.r.append(tok)
        for h in wh:
            h.w = tok
            h.r = []
        return tok

    def dma(self, qname, out, in_, **kw):
        if getattr(self, "_cap", None) is not None:
            self._cap.append(("dma", (qname, out, in_), kw))
            return None
        e = self.eng[qname]
        rh = list(in_.hs)
        wh = list(out.hs)
        deps = self._deps(rh, wh)
        k = e.dma_count % NDMA_SEMS
        rnd = e.dma_count // NDMA_SEMS
        e.dma_count += 1
        key = (qname, k)
        if rnd > 0:
            if deps.get(key, 0) < 16 * rnd:
                deps[key] = 16 * rnd
        waits = self._waits(e, deps)
        tok = (key, 16 * (rnd + 1))
        self.snaps[tok] = tuple(e.seen)
        oap, iap = out.ap, in_.ap
        e.ops.append((waits, lambda eng: eng.dma_start(out=oap, in_=iap, **kw), key))
        self.nwaits += len(waits)
        for h in rh:
            h.r.append(tok)
        for h in wh:
            h.w = tok
            h.r = []
        return tok

    def mm(self, out, lhsT, rhs, start=True, stop=True):
        o, l, r = out.ap, lhsT.ap, rhs.ap
        return self.op("pe", lambda e: e.matmul(o, l, r, start=start, stop=stop), [lhsT, rhs], [out])

    def transpose(self, out, in_, ident):
        o, i, d = out.ap, in_.ap, ident.ap
        return self.op("pe", lambda e: e.transpose(o, i, d), [in_, ident], [out])

    def act(self, out, in_, func, bias=None, scale=None, eng="act", accum=None):
        o, i = out.ap, in_.ap
        reads = [in_]
        kw = {}
        if bias is not None:
            if isinstance(bias, V):
                reads.append(bias)
                kw["bias"] = bias.ap
            else:
                kw["bias"] = bias
        if scale is not None:
            if isinstance(scale, V):
                reads.append(scale)
                kw["scale"] = scale.ap
            else:
                kw["scale"] = scale
        writes = [out]
        if accum is not None:
            writes.append(accum)
            kw["accum_out"] = accum.ap
        return self.op("act", lambda e: e.activation(o, i, func, **kw), reads, writes)

    def tt(self, out, a, b, op, eng="dve"):
        o, x, y = out.ap, a.ap, b.ap
        return self.op(eng, lambda e: e.tensor_tensor(o, x, y, op), [a, b], [out])

    def ts(self, out, a, s1, op0, s2=None, op1=None, eng="dve", accum=None):
        o, x = out.ap, a.ap
        reads = [a]
        if isinstance(s1, V):
            reads.append(s1)
            s1 = s1.ap
        if isinstance(s2, V):
            reads.append(s2)
            s2 = s2.ap
        writes = [out]
        kw = {}
        if accum is not None:
            writes.append(accum)
            kw["accum_out"] = accum.ap
        if op1 is None:
            return self.op(eng, lambda e: e.tensor_scalar(o, x, s1, None, op0, **kw), reads, writes)
        return self.op(eng, lambda e: e.tensor_scalar(o, x, s1, s2, op0, op1, **kw), reads, writes)

    def stt(self, out, a, s, b, op0, op1, eng="dve"):
        o, x, y = out.ap, a.ap, b.ap
        reads = [a, b]
        if isinstance(s, V):
            reads.append(s)
            s = s.ap
        return self.op(eng, lambda e: e.scalar_tensor_tensor(o, x, s, y, op0, op1), reads, [out])

    def copy(self, out, in_, eng="dve"):
        o, i = out.ap, in_.ap
        if eng == "act":
            return self.op("act", lambda e: e.copy(o, i), [in_], [out])
        return self.op(eng, lambda e: e.tensor_copy(o, i), [in_], [out])

    def memset(self, out, val, eng="dve"):
        o = out.ap
        return self.op(eng, lambda e: e.memset(o, val), [], [out])

    def recip(self, out, in_):
        o, i = out.ap, in_.ap
        return self.op("dve", lambda e: e.reciprocal(o, i), [in_], [out])

    def finish(self, final_tokens):
        nc = self.nc
        sems = {}
        for i, n in enumerate(ENGS):
            sems[i] = self.stack.enter_context(nc.semaphore(f"s_{n}"))
        for n in ENGS:
            e = self.eng[n]
            if e.dma_count:
                for k in range(min(NDMA_SEMS, e.dma_count)):
                    sems[(n, k)] = self.stack.enter_context(nc.semaphore(f"d_{n}{k}"))
        sp = self.eng["sp"]
        fin = {}
        for k, v in final_tokens:
            if fin.get(k, 0) < v:
                fin[k] = v
        sp.ops.append((list(fin.items()), None, None))
        block = self.stack.enter_context(nc.Block())
        hw = {"pe": block.tensor, "act": block.scalar, "dve": block.vector, "pool": block.gpsimd, "sp": block.sync}

        def make(e):
            def body(eng):
                own = sems[e.idx]
                for waits, fn, dkey in e.ops:
                    for k, v in waits:
                        eng.wait_ge(sems[k], v)
                    if fn is None:
                        continue
                    ins = fn(eng)
                    if dkey is None:
                        ins.then_inc(own, 1)
                    else:
                        ins.then_inc(sems[dkey], 16)
            return body
        for n in ENGS:
            e = self.eng[n]
            if e.ops:
                hw[n](make(e))
        self.stack.close()
        return nc


S = 4096; D = 1024; T = 512; NT = 8; NCH = 32
EPS = 1e-6
OFF_Z, OFF_XBC, OFF_DT, OFF_GA, OFF_GG, OFF_QL, OFF_KV, OFF_KR, OFF_GATE = 0, 1024, 3072, 3088, 4112, 5136, 5520, 5776, 5840
C_MIXG, C_XATG, C_FFNG, C_MEMG, C_SCW, C_SCB, C_CDW, C_CDB, C_LNG, C_LNB = 0, 8, 16, 24, 32, 96, 112, 360, 368, 376
C_QAG, C_KVAG, C_QNN, C_QNR, C_QNRS, C_KNN, C_KNR, C_KNRS, C_GATEB, C_XQG, C_XKG, NPP = 384, 387, 389, 390, 391, 392, 393, 394, 395, 419, 421, 424
B_DTB, B_ALOG, B_DSK, B_SNG, NPB = 0, 16, 32, 48, 1072
K_TRI, K_GT, K_ID, K_INV, K_MASK, NCST = 0, 128, 256, 384, 385, 385 + 2048


class Arena:
    def __init__(self, big, base, limit):
        self.big, self.off, self.limit = big, base, limit

    def alloc(self, n, dtype, parts=128):
        nb = n * (4 if dtype in (F32, I32) else 2)
        nb = (nb + 63) // 64 * 64
        ne = nb // 2
        assert self.off + ne <= self.limit, ("SBUF arena overflow", self.off + ne, self.limit)
        ap = self.big[:, self.off:self.off + ne]
        self.off += ne
        if dtype != BF16:
            ap = ap.bitcast(dtype)
        return ap[:, 0:n]


def build(L=4, dbg=False, only=None):
    P = Prog(); nc = P.nc
    kind_s = "ExternalOutput" if dbg else "Internal"
    din = lambda n, s, dt=F32: P.dram(n, s, dt, kind="ExternalInput")
    xT_d = din("xT", [D, S]); memT_d = din("memT", [D, 256]); pos_d = din("pos", [1, S], I32)
    cst_d = din("cst", [128, NCST]); pp_d = din("pp", [4, 128, NPP]); pb_d = din("pb", [4, NPB])
    w_in_d = din("w_in", [4, D, 8912]); ssd_wo_d = din("ssd_w_out", [4, D, D]); conv_wo_d = din("conv_w_out", [4, D, D])
    wqb_d = din("mla_w_q_b", [4, 384, 1536]); wkvb_d = din("mla_w_kv_b", [4, 256, 2048]); mla_wo_d = din("mla_w_o", [4, D, D])
    wout_d = din("w_out", [4, D, D]); xwq_d = din("xattn_w_q", [4, D, D]); xwkv_d = din("xattn_w_kv", [4, D, 2048])
    xwo_d = din("xattn_w_o", [4, D, D]); fwi_d = din("ffn_w_in", [4, D, 5632]); fwo_d = din("ffn_w_out", [4, 2816, D])
    out_d = P.dram("outT", [D, S], F32, kind="ExternalOutput")
    sc = lambda n, s, dt=F32: P.dram(n, s, dt, kind=kind_s)
    xres_d = sc("xres", [D, S]); sz_d = sc("sz", [S, D]); xbc_d = sc("xbc", [2048, S], BF16); cv_d = sc("cv", [D, S])
    lat_d = sc("lat", [768, S]); ys_d = sc("ys", [D, S], BF16); yc_d = sc("yc", [D, S], BF16)
    q_d = sc("qs", [8, 192, S], BF16); k_d = sc("ks", [8, 128, S], BF16); kr_d = sc("krs", [64, S], BF16)
    v_d = sc("vs", [S, D], BF16); o_d = sc("os", [D, S], BF16); fp_d = sc("fpart", [D, S]); rope_d = sc("rope", [128, S])
    hd = {}

    def dh(name, t):
        k = (name, t)
        if k not in hd:
            hd[k] = H()
        return hd[k]

    def dall(name, n=NT):
        return [dh(name, t) for t in range(n)]

    big = P.sb([128, 103000], BF16, name="big")
    psall = P.ps([128, 4096], F32, name="psall")[:]
    banks = [psall[:, i * 512:(i + 1) * 512] for i in range(8)]
    bh = [H() for _ in range(8)]

    def PS(i, n=512, parts=128, dt=F32):
        ap = banks[i][:]
        if dt == BF16:
            ap = ap.bitcast(BF16)
        return V(ap[0:parts, 0:n], bh[i])

    pers = Arena(big, 0, 8000)
    cst = pers.alloc(K_MASK, F32); h_cst = H()
    tri = V(cst[:, K_TRI:K_TRI + 128], h_cst); gt = V(cst[:, K_GT:K_GT + 128], h_cst)
    inv_c = V(cst[:, K_INV:K_INV + 1], h_cst)
    ident = pers.alloc(128, BF16); h_id = H(); identv = V(ident, h_id)
    ones_b = pers.alloc(128, BF16); h_ob = H(); onesb = V(ones_b, h_ob)
    ones_f = pers.alloc(128, F32); h_of = H(); onesf = V(ones_f, h_of)
    masks = pers.alloc(2048, BF16); h_mk = H()
    epsc = pers.alloc(1, F32); h_eps = H(); epsv = V(epsc, h_eps)
    ppt = pers.alloc(NPP, F32); h_pp = H()
    pbt = pers.alloc(NPB, F32); h_pb = H()
    abc = pers.alloc(16, F32); h_abc = H()
    gsc = pers.alloc(8, F32); h_gsc = H()
    ARENA0 = pers.off

    def pcol(c, n=1, parts=128):
        return V(ppt[0:parts, c:c + n], h_pp)

    def barrier():
        toks = {}
        for n in ENGS:
            e = P.eng[n]
            if e.n:
                toks[e.idx] = e.n
            for k in range(min(NDMA_SEMS, e.dma_count)):
                rnd = (e.dma_count - 1 - k) // NDMA_SEMS
                toks[(n, k)] = 16 * (rnd + 1)
        for n in ENGS:
            e = P.eng[n]
            waits = P._waits(e, dict(toks))
            if waits:
                e.ops.append((waits, None, None))

    P.dma("sp", V(cst, h_cst), V(cst_d[:, 0:K_MASK], H()))
    P.dma("pool", V(ident, h_id), V(cst_d[:, K_ID:K_ID + 128], H()))
    P.dma("pool", V(masks, h_mk), V(cst_d[:, K_MASK:K_MASK + 2048], H()))
    P.memset(onesb, 1.0); P.memset(onesf, 1.0); P.memset(epsv, EPS)

    def rope_tables():
        A = Arena(big, ARENA0, 103000)
        for t in range(NT):
            sl = slice(t * T, (t + 1) * T)
            pi_ = A.alloc(T, I32) if t == 0 else rope_tables.bufs[0]
            if t == 0:
                rope_tables.bufs = [pi_] + [A.alloc(T, F32) for _ in range(5)] + [A.alloc(T, I32)]
            pi_, pf, ang, kf, r, rc, ki = rope_tables.bufs
            hs = [H() for _ in range(7)]
            pi_v, pf_v, ang_v, kf_v, r_v, rc_v, ki_v = [V(a[0:64], h) for a, h in zip(rope_tables.bufs, hs)]
            P.dma("sp", pi_v, V(pos_d[0, sl].partition_broadcast(64), H()))
            P.copy(pf_v, pi_v)
            P.ts(ang_v, pf_v, V(cst[0:64, K_INV:K_INV + 1], h_cst), ALU.mult)
            P.ts(kf_v, ang_v, float(1.0 / (2 * np.pi)), ALU.mult)
            P.copy(ki_v, kf_v)
            P.copy(kf_v, ki_v)
            C1 = 6.28125; C2 = float(np.float32(2 * np.pi - 6.28125))
            P.stt(r_v, kf_v, -C1, ang_v, ALU.mult, ALU.add)
            P.stt(r_v, kf_v, -C2, r_v, ALU.mult, ALU.add)
            P.ts(r_v, r_v, 3.1415925, ALU.min, -3.1415925, ALU.max)
            P.ts(rc_v, r_v, float(np.pi / 2), ALU.is_gt, float(-2 * np.pi), ALU.mult)
            P.stt(rc_v, r_v, float(np.pi / 2), rc_v, ALU.add, ALU.add)
            P.ts(rc_v, rc_v, 3.1415925, ALU.min, -3.1415925, ALU.max)
            P.act(rc_v, rc_v, AF.Sin)
            P.act(r_v, r_v, AF.Sin)
            P.ts(V(r[0:32], hs[4]), V(r[0:32], hs[4]), -1.0, ALU.mult)
            P.dma("sp", V(rope_d[0:64, sl], dh("rope", t)), rc_v)
            P.dma("sp", V(rope_d[64:128, sl], dh("rope", t)), r_v)
            barrier()
    rope_tables()
    barrier()

    def wload(dst, src, hdst):
        return P.dma("pool", V(dst, hdst), V(src, H()))

    def fmview(d, t):
        return d.rearrange("(k p) s -> p k s", p=128)[:, :, t * T:(t + 1) * T]

    def xnorm(xt, hx, gcol, outb, hout, sq, hsq, rs, hrs, bank, nk=8, width=T, inv_n=1.0 / D):
        P.act(V(sq, hsq), V(xt, hx), AF.Square)
        for k in range(nk):
            P.mm(PS(bank, width), onesb, V(sq[:, k, :], hsq), start=(k == 0), stop=(k == nk - 1))
        P.act(V(rs, hrs), PS(bank, width), AF.Ln, bias=epsv, scale=inv_n)
        P.act(V(rs, hrs), V(rs, hrs), AF.Exp, scale=-0.5)
        for k in range(nk):
            P.stt(V(outb[:, k, :], hout), V(xt[:, k, :], hx), pcol(gcol + k), V(rs, hrs), ALU.mult, ALU.mult)

    dtraw = pers.alloc(NCH * 16, F32).rearrange("p (c h) -> p c h", c=NCH); h_dtraw = H()
    ARENA0 = pers.off
    final_toks = []

    def phase_A(l):
        xsrc = xT_d if l == 0 else xres_d
        xsn = "xT" if l == 0 else "xres"
        barrier()
        P.dma("sp", V(ppt, h_pp), V(pp_d[l], H()))
        P.dma("sp", V(pbt, h_pb), V(pb_d[l, :].partition_broadcast(128), H()))
        P.act(V(abc, h_abc), V(pbt[:, B_ALOG:B_ALOG + 16], h_pb), AF.Exp)
        P.ts(V(abc, h_abc), V(abc, h_abc), -1.0, ALU.mult)
        P.ts(V(gsc[:, 0:3], h_gsc), V(ppt[:, C_QNN:C_QNN + 3], h_pp), float(192 ** -0.5), ALU.mult)
        P.ts(V(gsc[:, 3:5], h_gsc), V(ppt[:, C_XQG:C_XQG + 2], h_pp), float(256 ** -0.5), ALU.mult)

        A = Arena(big, ARENA0, 103000)
        uT = A.alloc(8 * S, BF16).rearrange("p (k s) -> p k s", k=8); hu = [H() for _ in range(NT)]
        A1 = A.off
        xt = [A.alloc(8 * T, F32).rearrange("p (k s) -> p k s", k=8) for _ in range(2)]; hxt = [H(), H()]
        sq = A.alloc(8 * T, BF16).rearrange("p (k s) -> p k s", k=8); hsq = H()
        rs = A.alloc(T, F32); hrs = H()
        for t in range(NT):
            b = t % 2
            P.dma("sp", V(xt[b], hxt[b]), V(fmview(xsrc, t), dh(xsn, t)))
            xnorm(xt[b], hxt[b], C_MIXG, uT[:, :, t * T:(t + 1) * T], hu[t], sq, hsq, rs, hrs, 0)
        barrier()
        A.off = A1
        wz = A.alloc(8 * 1040, BF16).rearrange("p (k n) -> p k n", k=8); hwz = H()
        wload(wz[:, :, 0:1024], w_in_d[l].rearrange("(k p) n -> p k n", p=128)[:, :, OFF_Z:OFF_Z + 1024], hwz)
        wload(wz[:, :, 1024:1040], w_in_d[l].rearrange("(k p) n -> p k n", p=128)[:, :, OFF_DT:OFF_DT + 16], hwz)
        szb = [A.alloc(1024, F32) for _ in range(2)]; hszb = [H(), H()]
        for c in range(NCH):
            t = c // 4; b = c % 2
            us = lambda k: V(uT[:, k, c * 128:(c + 1) * 128], hu[t])
            for half in range(2):
                for k in range(8):
                    P.mm(PS(1 + half), us(k), V(wz[:, k, half * 512:(half + 1) * 512], hwz), start=(k == 0), stop=(k == 7))
            for k in range(8):
                P.mm(PS(3, 16), us(k), V(wz[:, k, 1024:1040], hwz), start=(k == 0), stop=(k == 7))
            for half in range(2):
                P.act(V(szb[b][:, half * 512:(half + 1) * 512], hszb[b]), PS(1 + half), AF.Silu)
            P.copy(V(dtraw[:, c, :], h_dtraw), PS(3, 16))
            P.dma("pool", V(sz_d[c * 128:(c + 1) * 128, :], dh("sz", t)), V(szb[b], hszb[b]))
        barrier()
        A.off = A1
        wg = [A.alloc(8 * 512, BF16).rearrange("p (k n) -> p k n", k=8) for _ in range(2)]; hwg = [H(), H()]
        pre = [A.alloc(S + 32, F32) for _ in range(2)]; hpre = [H(), H()]
        acc = [A.alloc(S, F32) for _ in range(2)]; hacc = [H(), H()]
        xo = [A.alloc(S, BF16) for _ in range(2)]; hxo = [H(), H()]
        for b in range(2):
            P.memset(V(pre[b][:, 0:32], hpre[b]), 0.0)
        win_v = w_in_d[l].rearrange("(k p) n -> p k n", p=128)
        wload(wg[0], win_v[:, :, OFF_XBC:OFF_XBC + 512], hwg[0])
        pend = None
        for j in range(16):
            g4 = j // 4; wb = g4 % 2; b = j % 2
            if j % 4 == 0 and g4 + 1 < 4:
                wload(wg[1 - wb], win_v[:, :, OFF_XBC + (g4 + 1) * 512:OFF_XBC + (g4 + 2) * 512], hwg[1 - wb])
            for t in range(NT):
                bank = 1 + (t % 2)
                for k in range(8):
                    P.mm(PS(bank), V(wg[wb][:, k, (j % 4) * 128:(j % 4 + 1) * 128], hwg[wb]),
                         V(uT[:, k, t * T:(t + 1) * T], hu[t]), start=(k == 0), stop=(k == 7))
                P.copy(V(pre[b][:, 32 + t * T:32 + (t + 1) * T], hpre[b]), PS(bank), eng="act")
            P.ts(V(acc[b], hacc[b]), V(pre[b][:, 29:29 + S], hpre[b]), pcol(C_SCW + j * 4 + 0), ALU.mult, pcol(C_SCB + j), ALU.add)
            for tap in range(1, 4):
                P.stt(V(acc[b], hacc[b]), V(pre[b][:, 29 + tap:29 + tap + S], hpre[b]), pcol(C_SCW + j * 4 + tap),
                      V(acc[b], hacc[b]), ALU.mult, ALU.add)
            if pend is not None:
                pend()

            def fin(b=b, j=j):
                P.act(V(xo[b], hxo[b]), V(acc[b], hacc[b]), AF.Silu)
                P.dma("pool", V(xbc_d[j * 128:(j + 1) * 128, :], dall("xbc")), V(xo[b], hxo[b]))
            pend = fin
        pend()
        barrier()
        A.off = A1
        wa = [A.alloc(8 * 128, BF16).rearrange("p (k n) -> p k n", k=8) for _ in range(2)]; hwa = [H(), H()]
        wgt = [A.alloc(8 * 128, BF16).rearrange("p (k n) -> p k n", k=8) for _ in range(2)]; hwgt = [H(), H()]
        vb = [A.alloc(S + 32, BF16) for _ in range(2)]; hvb = [[H() for _ in range(NT)] for _ in range(2)]; hvz = [H(), H()]
        dg = [A.alloc(31 * 128, BF16).rearrange("p (j n) -> p j n", j=31) for _ in range(2)]; hdg = [H(), H()]
        acc = [A.alloc(S, F32) for _ in range(2)]; hacc = [H(), H()]
        sg = [A.alloc(T, F32) for _ in range(2)]; hsg = [H(), H()]
        for b in range(2):
            P.memset(V(vb[b][:, 0:32], hvz[b]), 0.0)
        def prepj(j):
            b = j % 2
            wload(wa[b], win_v[:, :, OFF_GA + j * 128:OFF_GA + (j + 1) * 128], hwa[b])
            wload(wgt[b], win_v[:, :, OFF_GG + j * 128:OFF_GG + (j + 1) * 128], hwgt[b])
            for tap in range(31):
                P.ts(V(dg[b][:, tap, :], hdg[b]), identv, pcol(C_CDW + j * 31 + tap), ALU.mult)
        prepj(0)
        for j in range(8):
            b = j % 2
            if j + 1 < 8:
                prepj(j + 1)

            def glu(t):
                sb_ = t % 2
                for k in range(8):
                    P.mm(PS(1 + sb_), V(wa[b][:, k, :], hwa[b]), V(uT[:, k, t * T:(t + 1) * T], hu[t]), start=(k == 0), stop=(k == 7))
                for k in range(8):
                    P.mm(PS(3 + sb_), V(wgt[b][:, k, :], hwgt[b]), V(uT[:, k, t * T:(t + 1) * T], hu[t]), start=(k == 0), stop=(k == 7))
                P.act(V(sg[sb_], hsg[sb_]), PS(3 + sb_), AF.Sigmoid)
                P.tt(V(vb[b][:, 32 + t * T:32 + (t + 1) * T], hvb[b][t]), PS(1 + sb_), V(sg[sb_], hsg[sb_]), ALU.mult)

            def conv(t):
                cbk = 5 + t % 2
                rd_h = [hvb[b][t]] + ([hvb[b][t - 1]] if t > 0 else [hvz[b]])
                for tap in range(31):
                    o0 = 2 + tap + t * T
                    P.mm(PS(cbk), V(dg[b][:, tap, :], hdg[b]), V(vb[b][:, o0:o0 + T], rd_h), start=(tap == 0), stop=(tap == 30))
                P.act(V(acc[b][:, t * T:(t + 1) * T], hacc[b]), PS(cbk), AF.Identity, bias=pcol(C_CDB + j))
            glu(0)
            for t in range(NT):
                if t + 1 < NT:
                    glu(t + 1)
                conv(t)
            P.dma("pool", V(cv_d[j * 128:(j + 1) * 128, :], dall("cv")), V(acc[b], hacc[b]))
        barrier()
        A.off = A1
        wl = A.alloc(8 * 768, BF16).rearrange("p (k n) -> p k n", k=8); hwl = H()
        wload(wl[:, :, 0:704], win_v[:, :, OFF_QL:OFF_QL + 704], hwl)
        wload(wl[:, :, 704:736], win_v[:, :, OFF_KR + 32:OFF_KR + 64], hwl)
        wload(wl[:, :, 736:768], win_v[:, :, OFF_KR:OFF_KR + 32], hwl)
        lo = [A.alloc(T, F32) for _ in range(2)]; hlo = [H(), H()]
        segs = [(0, 128), (128, 128), (256, 128), (384, 128), (512, 128), (640, 64), (704, 64)]
        i = 0
        for t in range(NT):
            for (c0, m) in segs:
                b = i % 2; i += 1
                for k in range(8):
                    P.mm(PS(1 + b, T, m), V(wl[:, k, c0:c0 + m], hwl), V(uT[:, k, t * T:(t + 1) * T], hu[t]), start=(k == 0), stop=(k == 7))
                P.copy(V(lo[b][0:m], hlo[b]), PS(1 + b, T, m), eng="act")
                P.dma("pool", V(lat_d[c0:c0 + m, t * T:(t + 1) * T], dh("lat", t)), V(lo[b][0:m], hlo[b]))
        barrier()
    def phase_S(l):
        A = Arena(big, ARENA0, 103000)
        xbt = [A.alloc(16 * T, BF16).rearrange("p (k s) -> p k s", k=16) for _ in range(2)]; hxb = [H(), H()]
        szt = [A.alloc(1024, F32) for _ in range(2)]; hszt = [H(), H()]
        hst = A.alloc(1024, F32); h_hst = H()
        hbf = A.alloc(1024, BF16); h_hbf = H()
        D2 = lambda n, dt: ([A.alloc(n, dt) for _ in range(2)], [H(), H()])
        sm2, h_sm2 = D2(64, F32)
        ew2, h_ew2 = D2(48, F32)
        Xb2, h_Xb2 = D2(1024, BF16)
        Xw2, h_Xw2 = D2(1024, BF16)
        tD2, h_tD2 = D2(1024, F32)
        Bt2, h_Bt2 = D2(512, BF16)
        MT2, h_MT2 = D2(2048, BF16)
        cbm = A.alloc(512, F32); h_cbm = H()
        AGh = A.alloc(2048, BF16); h_AGh = H()
        AGl = A.alloc(2048, BF16); h_AGl = H()
        gtb = A.alloc(128, BF16); h_gtb = H()
        trib = A.alloc(128, BF16); h_trib = H()
        ahl = A.alloc(32, BF16); h_ahl = H()
        P.copy(V(gtb, h_gtb), gt); P.copy(V(trib, h_trib), tri)
        dec = [A.alloc(512, F32) for _ in range(2)]; h_dec = [H(), H()]
        yg = A.alloc(1024, F32); h_yg = H()
        t1 = [A.alloc(512, F32) for _ in range(2)]; h_t1 = [H(), H()]
        ssq = A.alloc(8, F32); h_ssq = H()
        junk = A.alloc(256, F32); h_junk = H()
        ynb = A.alloc(1024, BF16); h_ynb = H()
        yst = [A.alloc(8 * T, BF16).rearrange("p (k s) -> p k s", k=8) for _ in range(2)]; h_yst = [H(), H()]
        P.memset(V(hst, h_hst), 0.0); P.memset(V(hbf, h_hbf), 0.0)
        dtb = V(pbt[:, B_DTB:B_DTB + 16], h_pb)
        sng = V(pbt[:, B_SNG:B_SNG + 1024], h_pb)
        xview = xbc_d.rearrange("(k p) s -> p k s", p=128)
        SMALL = lambda lo, n: V(banks[2][:, 256 + lo:256 + lo + n], bh[2])

        def bc16(ap, lo, n):
            return ap[:, lo:lo + n].unsqueeze(2).broadcast_to([128, n, 64])
        v3 = lambda ap: ap.rearrange("p (h d) -> p h d", h=16)

        def stage1(c):
            t = c // 4; s_ = c % 4; tb = t % 2; cb_ = c % 2
            cs = slice(s_ * 128, (s_ + 1) * 128)
            if s_ == 0:
                P.dma("sp", V(xbt[tb], hxb[tb]), V(xview[:, :, t * T:(t + 1) * T], dall("xbc")))
            P.dma("sp", V(szt[cb_], hszt[cb_]), V(sz_d[c * 128:(c + 1) * 128, :], dh("sz", t)))
            xb_ = xbt[tb]; hx_ = hxb[tb]
            sm, h_sm, ew, h_ew = sm2[cb_], h_sm2[cb_], ew2[cb_], h_ew2[cb_]
            Xb, h_Xb, Xw, h_Xw, tD, h_tD = Xb2[cb_], h_Xb2[cb_], Xw2[cb_], h_Xw2[cb_], tD2[cb_], h_tD2[cb_]
            Btm, h_Btm, MT, h_MT = Bt2[cb_], h_Bt2[cb_], MT2[cb_], h_MT2[cb_]
            for k in range(8):
                P.transpose(V(banks[1][:].bitcast(BF16)[:, k * 128:(k + 1) * 128], bh[1]), V(xb_[:, k, cs], hx_), identv)
            for g in range(4):
                P.transpose(V(banks[2][:].bitcast(BF16)[:, g * 128:(g + 1) * 128], bh[2]), V(xb_[:, 8 + g, cs], hx_), identv)
            for g in range(4):
                P.mm(V(banks[3][:, g * 128:(g + 1) * 128], bh[3]), V(xb_[:, 8 + g, cs], hx_), V(xb_[:, 12 + g, cs], hx_))
            P.tt(V(sm[:, 0:16], h_sm), V(dtraw[:, c, :], h_dtraw), dtb, ALU.add)
            P.act(V(sm[:, 0:16], h_sm), V(sm[:, 0:16], h_sm), AF.Exp)
            P.act(V(sm[:, 0:16], h_sm), V(sm[:, 0:16], h_sm), AF.Ln, bias=1.0)
            P.tt(V(sm[:, 16:32], h_sm), V(sm[:, 0:16], h_sm), V(abc, h_abc), ALU.mult)
            av = V(sm[:, 16:32], h_sm)
            P.mm(SMALL(0, 16), tri, av); P.mm(SMALL(16, 16), gt, av); P.mm(SMALL(32, 16), onesf, av)
            P.act(V(ew, h_ew), SMALL(0, 48), AF.Exp)
            P.tt(V(sm[:, 32:48], h_sm), V(sm[:, 0:16], h_sm), V(ew[:, 16:32], h_ew), ALU.mult)
            P.copy(V(ahl[:, 0:16], h_ahl), av)
            P.tt(V(ahl[:, 16:32], h_ahl), av, V(ahl[:, 0:16], h_ahl), ALU.subtract)
            P.tt(V(AGh.rearrange("p (h s) -> p h s", h=16), h_AGh),
                 V(gtb.unsqueeze(1).broadcast_to([128, 16, 128]), h_gtb),
                 V(ahl[:, 0:16].unsqueeze(2).broadcast_to([128, 16, 128]), h_ahl), ALU.mult)
            P.tt(V(AGl.rearrange("p (h s) -> p h s", h=16), h_AGl),
                 V(gtb.unsqueeze(1).broadcast_to([128, 16, 128]), h_gtb),
                 V(ahl[:, 16:32].unsqueeze(2).broadcast_to([128, 16, 128]), h_ahl), ALU.mult)
            P.tt(V(cbm.rearrange("p (g l) -> p g l", g=4), h_cbm), V(banks[3][:].rearrange("p (g l) -> p g l", g=4), bh[3]),
                 V(cst[:, K_TRI:K_TRI + 128].unsqueeze(1).broadcast_to([128, 4, 128]), h_cst), ALU.mult)
            xsT = banks[1][:].bitcast(BF16)[:, 0:1024].rearrange("p (h d) -> p h d", h=16)
            P.tt(V(v3(Xb), h_Xb), V(xsT, bh[1]), V(bc16(sm, 0, 16), h_sm), ALU.mult)
            P.tt(V(v3(Xw), h_Xw), V(xsT, bh[1]), V(bc16(sm, 32, 16), h_sm), ALU.mult)
            P.tt(V(v3(tD), h_tD), V(xsT, bh[1]), V(bc16(pbt, B_DSK, 16), h_pb), ALU.mult)
            P.copy(V(Btm, h_Btm), V(banks[2][:].bitcast(BF16)[:, 0:512], bh[2]), eng="act")
            for g in range(4):
                db = g % 2
                for r in range(4):
                    hd_ = g * 4 + r
                    P.mm(V(banks[4][:, r * 128:(r + 1) * 128], bh[4]), V(AGh[:, hd_ * 128:(hd_ + 1) * 128], h_AGh), V(trib, h_trib), start=True, stop=False)
                    P.mm(V(banks[4][:, r * 128:(r + 1) * 128], bh[4]), V(AGl[:, hd_ * 128:(hd_ + 1) * 128], h_AGl), V(trib, h_trib), start=False, stop=True)
                P.act(V(dec[db], h_dec[db]), PS(4), AF.Exp)
                P.tt(V(MT[:, g * 512:(g + 1) * 512].rearrange("p (r l) -> p r l", r=4), h_MT), V(dec[db].rearrange("p (r l) -> p r l", r=4), h_dec[db]),
                     V(cbm[:, g * 128:(g + 1) * 128].unsqueeze(1).broadcast_to([128, 4, 128]), h_cbm), ALU.mult)

        def stage2(c):
            t = c // 4; s_ = c % 4; tb = t % 2; cb_ = c % 2
            cs = slice(s_ * 128, (s_ + 1) * 128)
            xb_ = xbt[tb]; hx_ = hxb[tb]
            ew, h_ew = ew2[cb_], h_ew2[cb_]
            Xb, h_Xb, Xw, h_Xw, tD, h_tD = Xb2[cb_], h_Xb2[cb_], Xw2[cb_], h_Xw2[cb_], tD2[cb_], h_tD2[cb_]
            Btm, h_Btm, MT, h_MT = Bt2[cb_], h_Bt2[cb_], MT2[cb_], h_MT2[cb_]
            for hd_ in range(16):
                ybank = 5 + hd_ // 8
                col = (hd_ % 8) * 64
                P.mm(V(banks[ybank][:, col:col + 64], bh[ybank]), V(MT[:, hd_ * 128:(hd_ + 1) * 128], h_MT),
                     V(Xb[:, hd_ * 64:(hd_ + 1) * 64], h_Xb))
            for hf in range(2):
                ybank = 5 + hf
                for gg in range(2):
                    g = hf * 2 + gg
                    P.mm(V(banks[7][:, gg * 256:(gg + 1) * 256], bh[7]), V(xb_[:, 12 + g, cs], hx_), V(hbf[:, g * 256:(g + 1) * 256], h_hbf))
                t1_ = t1[hf]; ht1 = h_t1[hf]
                P.tt(V(t1_.rearrange("p (h d) -> p h d", h=8), ht1), V(banks[7][:].rearrange("p (h d) -> p h d", h=8), bh[7]), V(bc16(ew, hf * 8, 8), h_ew), ALU.mult)
                P.tt(V(t1_, ht1), V(t1_, ht1), PS(ybank), ALU.add)
                P.tt(V(t1_, ht1), V(t1_, ht1), V(tD[:, hf * 512:(hf + 1) * 512], h_tD), ALU.add)
                P.tt(V(yg[:, hf * 512:(hf + 1) * 512], h_yg), V(t1_, ht1), V(szt[cb_][:, hf * 512:(hf + 1) * 512], hszt[cb_]), ALU.mult)
            for hf in range(2):
                for gg in range(2):
                    g = hf * 2 + gg
                    P.mm(V(banks[7][:, gg * 256:(gg + 1) * 256], bh[7]), V(Btm[:, g * 128:(g + 1) * 128], h_Btm), V(Xw[:, g * 256:(g + 1) * 256], h_Xw))
                hv = V(hst[:, hf * 512:(hf + 1) * 512].rearrange("p (h d) -> p h d", h=8), h_hst)
                P.tt(hv, hv, V(bc16(ew, 32 + hf * 8, 8), h_ew), ALU.mult)
                P.tt(V(hst[:, hf * 512:(hf + 1) * 512], h_hst), V(hst[:, hf * 512:(hf + 1) * 512], h_hst), PS(7), ALU.add)
            P.copy(V(hbf, h_hbf), V(hst, h_hst), eng="act")
            for g in range(4):
                P.act(V(junk, h_junk), V(yg[:, g * 256:(g + 1) * 256], h_yg), AF.Square, accum=V(ssq[:, g:g + 1], h_ssq))
            P.act(V(ssq[:, 4:8], h_ssq), V(ssq[:, 0:4], h_ssq), AF.Ln, bias=epsv, scale=1.0 / 256)
            P.act(V(ssq[:, 4:8], h_ssq), V(ssq[:, 4:8], h_ssq), AF.Exp, scale=-0.5)
            P.tt(V(yg.rearrange("p (g d) -> p g d", g=4), h_yg), V(yg.rearrange("p (g d) -> p g d", g=4), h_yg),
                 V(ssq[:, 4:8].unsqueeze(2).broadcast_to([128, 4, 256]), h_ssq), ALU.mult)
            P.tt(V(ynb, h_ynb), V(yg, h_yg), sng, ALU.mult)
            for k in range(8):
                P.transpose(V(banks[0][:].bitcast(BF16)[:, k * 128:(k + 1) * 128], bh[0]), V(ynb[:, k * 128:(k + 1) * 128], h_ynb), identv)
            P.copy(V(yst[tb][:, :, cs], h_yst[tb]), V(banks[0][:].bitcast(BF16)[:, 0:1024].rearrange("p (k s) -> p k s", k=8), bh[0]), eng="act")
            if s_ == 3:
                P.dma("pool", V(ys_d.rearrange("(k p) s -> p k s", p=128)[:, :, t * T:(t + 1) * T], dh("ys", t)), V(yst[tb], h_yst[tb]))

        stage1(0)
        for c in range(NCH):
            la = P.capture()
            if c + 1 < NCH:
                stage1(c + 1)
            lb = P.capture()
            stage2(c)
            P.interleave(la, lb)
        barrier()

    def phase_C(l, base=None, bk=(0, 1), do_barrier=True):
        A = Arena(big, ARENA0 if base is None else base, 103000)
        ct = [A.alloc(8 * T, F32).rearrange("p (k s) -> p k s", k=8) for _ in range(2)]; hct = [H(), H()]
        cb16 = A.alloc(8 * T, BF16).rearrange("p (k s) -> p k s", k=8); hcb = H()
        sq = A.alloc(8 * T, BF16).rearrange("p (k s) -> p k s", k=8); hsq = H()
        mean = A.alloc(T, F32); hmean = H()
        var = A.alloc(T, F32); hvar = H()
        tmp = A.alloc(T, F32); htmp = H()
        yo = [A.alloc(8 * T, BF16).rearrange("p (k s) -> p k s", k=8) for _ in range(2)]; hyo = [H(), H()]
        for t in range(NT):
            b = t % 2
            P.dma("sp", V(ct[b], hct[b]), V(fmview(cv_d, t), dall("cv")))
            P.copy(V(cb16, hcb), V(ct[b], hct[b]), eng="act")
            P.act(V(sq, hsq), V(ct[b], hct[b]), AF.Square)
            for k in range(8):
                P.mm(PS(bk[0]), onesb, V(cb16[:, k, :], hcb), start=(k == 0), stop=(k == 7))
            for k in range(8):
                P.mm(PS(bk[1]), onesb, V(sq[:, k, :], hsq), start=(k == 0), stop=(k == 7))
            P.ts(V(mean, hmean), PS(bk[0]), 1.0 / D, ALU.mult)
            P.tt(V(tmp, htmp), V(mean, hmean), V(mean, hmean), ALU.mult)
            P.stt(V(var, hvar), PS(bk[1]), 1.0 / D, V(tmp, htmp), ALU.mult, ALU.subtract)
            P.act(V(var, hvar), V(var, hvar), AF.Ln, bias=epsv)
            P.act(V(var, hvar), V(var, hvar), AF.Exp, scale=-0.5)
            for k in range(8):
                P.tt(V(tmp, htmp), V(ct[b][:, k, :], hct[b]), V(mean, hmean), ALU.subtract)
                P.tt(V(tmp, htmp), V(tmp, htmp), V(var, hvar), ALU.mult)
                P.act(V(yo[b][:, k, :], hyo[b]), V(tmp, htmp), AF.Silu, bias=pcol(C_LNB + k), scale=pcol(C_LNG + k))
            P.dma("pool", V(fmview(yc_d, t), dh("yc", t)), V(yo[b], hyo[b]))
        if do_barrier:
            barrier()
    def phase_M1(l):
        A = Arena(big, ARENA0, 103000)
        wq = A.alloc(3 * 1536, BF16).rearrange("p (k n) -> p k n", k=3); hwq = H()
        wqs = A.alloc(3 * 512, BF16).rearrange("p (k n) -> p k n", k=3); hwqs = H()
        wkn = A.alloc(2 * 1024, BF16).rearrange("p (k n) -> p k n", k=2); hwkn = H()
        wv = A.alloc(2 * 1024, BF16).rearrange("p (k n) -> p k n", k=2); hwv = H()
        wload(wq, wqb_d[l].rearrange("(k p) n -> p k n", p=128), hwq)
        qv = wqb_d[l].rearrange("(k p) (h c) -> p k h c", p=128, h=8)
        wqs4 = wqs.rearrange("p k (h c) -> p k h c", h=8)
        for k in range(3):
            wload(wqs4[:, k, :, 0:32], qv[:, k, :, 160:192], hwqs)
            wload(wqs4[:, k, :, 32:64], qv[:, k, :, 128:160], hwqs)
        kv4 = wkvb_d[l].rearrange("(k p) (h c) -> p k h c", p=128, h=8)
        for k in range(2):
            wload(wkn.rearrange("p k (h c) -> p k h c", h=8)[:, k], kv4[:, k, :, 0:128], hwkn)
            wload(wv.rearrange("p k (h c) -> p k h c", h=8)[:, k], kv4[:, k, :, 128:256], hwv)
        lt = [A.alloc(6 * T, F32).rearrange("p (k s) -> p k s", k=6) for _ in range(2)]; hlt = [H(), H()]
        rp = [A.alloc(T, F32) for _ in range(2)]; hrp = [H(), H()]
        rp2 = [A.alloc(T, F32) for _ in range(2)]; hrp2 = [H(), H()]
        sq = A.alloc(3 * T, BF16).rearrange("p (k s) -> p k s", k=3); hsq = H()
        rs = A.alloc(T, F32); hrs = H()
        qn = A.alloc(3 * T, BF16).rearrange("p (k s) -> p k s", k=3); hqn = H()
        kvn = A.alloc(2 * T, BF16).rearrange("p (k s) -> p k s", k=2); hkvn = H()
        sqh = [A.alloc(T, BF16) for _ in range(3)]; hsqh = [H() for _ in range(3)]
        rsh = [A.alloc(T, F32) for _ in range(3)]; hrsh = [H() for _ in range(3)]
        krb = A.alloc(T, F32); hkrb = H()
        ob = [A.alloc(T, BF16) for _ in range(6)]; hob = [H() for _ in range(6)]
        ta = A.alloc(T, F32); hta = H()
        tb_ = A.alloc(T, F32); htb = H()
        vb = [A.alloc(1024, BF16) for _ in range(2)]; hvb = [H(), H()]
        latv = lat_d.rearrange("(k p) s -> p k s", p=128)
        oi = 0
        for t in range(NT):
            b = t % 2; sl = slice(t * T, (t + 1) * T)
            P.dma("sp", V(lt[b], hlt[b]), V(latv[:, :, sl], dh("lat", t)))
            P.dma("sp", V(lt[b][0:64, 5, :], hlt[b]), V(lat_d[704:768, sl], dh("lat", t)))
            P.dma("sp", V(rp[b][0:64], hrp[b]), V(rope_d[0:64, sl], dh("rope", t)))
            P.dma("sp", V(rp2[b][0:64], hrp2[b]), V(rope_d[64:128, sl], dh("rope", t)))
            L_ = lt[b]; hL = hlt[b]
            cosv = V(rp[b][0:64], hrp[b]); sinv = V(rp2[b][0:64], hrp2[b])
            xnorm(L_[:, 0:3, :], hL, C_QAG, qn, hqn, sq, hsq, rs, hrs, 0, nk=3, inv_n=1.0 / 384)
            xnorm(L_[:, 3:5, :], hL, C_KVAG, kvn, hkvn, sq[:, 0:2, :], hsq, rs, hrs, 0, nk=2, inv_n=1.0 / 256)

            for h in range(8):
                for k in range(3):
                    P.mm(PS(1), V(wq[:, k, h * 192:h * 192 + 128], hwq), V(qn[:, k, :], hqn), start=(k == 0), stop=(k == 2))
                for k in range(3):
                    P.mm(PS(2, T, 64), V(wq[:, k, h * 192 + 128:h * 192 + 192], hwq), V(qn[:, k, :], hqn), start=(k == 0), stop=(k == 2))
                for k in range(3):
                    P.mm(PS(3, T, 64), V(wqs[:, k, h * 64:(h + 1) * 64], hwqs), V(qn[:, k, :], hqn), start=(k == 0), stop=(k == 2))
                for k in range(2):
                    P.mm(PS(4), V(wkn[:, k, h * 128:(h + 1) * 128], hwkn), V(kvn[:, k, :], hkvn), start=(k == 0), stop=(k == 1))
                specs = [(1, 128, 1.0 / 128, 5), (2, 64, 1.0 / 64, 6), (4, 128, 1.0 / 128, 7)]
                for i, (bank, parts, inv_n, nb) in enumerate(specs):
                    P.act(V(sqh[i][0:parts], hsqh[i]), PS(bank, T, parts), AF.Square)
                for i, (bank, parts, inv_n, nb) in enumerate(specs):
                    P.mm(PS(nb, T, parts), V(ones_b[0:parts, 0:parts], h_ob), V(sqh[i][0:parts], hsqh[i]))
                for i, (bank, parts, inv_n, nb) in enumerate(specs):
                    P.act(V(rsh[i][0:parts], hrsh[i]), PS(nb, T, parts), AF.Ln, bias=V(epsc[0:parts], h_eps), scale=inv_n)
                    P.act(V(rsh[i][0:parts], hrsh[i]), V(rsh[i][0:parts], hrsh[i]), AF.Exp, scale=-0.5)
                o1 = oi % 6; o2 = (oi + 1) % 6; o3 = (oi + 2) % 6; oi += 3
                P.stt(V(ob[o1], hob[o1]), PS(1), V(gsc[:, 0:1], h_gsc), V(rsh[0], hrsh[0]), ALU.mult, ALU.mult)
                P.stt(V(ob[o3], hob[o3]), PS(4), pcol(C_KNN), V(rsh[2], hrsh[2]), ALU.mult, ALU.mult)
                P.stt(V(ta[0:64], hta), PS(2, T, 64), V(gsc[0:64, 1:2], h_gsc), cosv, ALU.mult, ALU.mult)
                P.stt(V(tb_[0:64], htb), PS(3, T, 64), V(gsc[0:64, 2:3], h_gsc), sinv, ALU.mult, ALU.mult)
                P.tt(V(ta[0:64], hta), V(ta[0:64], hta), V(tb_[0:64], htb), ALU.add)
                P.tt(V(ob[o2][0:64], hob[o2]), V(ta[0:64], hta), V(rsh[1][0:64], hrsh[1]), ALU.mult)
                P.dma("pool", V(q_d[h, 0:128, sl], dh("q", t)), V(ob[o1], hob[o1]))
                P.dma("pool", V(q_d[h, 128:192, sl], dh("q", t)), V(ob[o2][0:64], hob[o2]))
                P.dma("pool", V(k_d[h, :, sl], dh("k", t)), V(ob[o3], hob[o3]))
            P.dma("sp", V(krb[0:64], hkrb), V(lat_d[640:704, sl], dh("lat", t)))
            krv = V(krb[0:64], hkrb); krsv = V(L_[0:64, 5, :], hL)
            P.act(V(sqh[1][0:64], hsqh[1]), krv, AF.Square)
            P.mm(PS(7, T, 64), V(ones_b[0:64, 0:64], h_ob), V(sqh[1][0:64], hsqh[1]))
            P.act(V(rsh[1][0:64], hrsh[1]), PS(7, T, 64), AF.Ln, bias=V(epsc[0:64], h_eps), scale=1.0 / 64)
            P.act(V(rsh[1][0:64], hrsh[1]), V(rsh[1][0:64], hrsh[1]), AF.Exp, scale=-0.5)
            P.stt(V(ta[0:64], hta), krv, pcol(C_KNR, 1, 64), cosv, ALU.mult, ALU.mult)
            P.stt(V(tb_[0:64], htb), krsv, pcol(C_KNRS, 1, 64), sinv, ALU.mult, ALU.mult)
            P.tt(V(ta[0:64], hta), V(ta[0:64], hta), V(tb_[0:64], htb), ALU.add)
            o = oi % 6; oi += 1
            P.tt(V(ob[o][0:64], hob[o]), V(ta[0:64], hta), V(rsh[1][0:64], hrsh[1]), ALU.mult)
            P.dma("pool", V(kr_d[:, sl], dh("kr", t)), V(ob[o][0:64], hob[o]))
            for s_ in range(4):
                vb_ = s_ % 2
                for half in range(2):
                    for k in range(2):
                        P.mm(PS(2 + half), V(kvn[:, k, s_ * 128:(s_ + 1) * 128], hkvn), V(wv[:, k, half * 512:(half + 1) * 512], hwv),
                             start=(k == 0), stop=(k == 1))
                    P.copy(V(vb[vb_][:, half * 512:(half + 1) * 512], hvb[vb_]), PS(2 + half), eng="act")
                r0 = t * T + s_ * 128
                P.dma("pool", V(v_d[r0:r0 + 128, :], dh("v", t)), V(vb[vb_], hvb[vb_]))
        barrier()

    def phase_M2(l, do_barrier=True):
        A = Arena(big, ARENA0, 103000)
        krt = A.alloc(S, BF16); hkr = H()
        P.memset(V(krt[64:128], hkr), 0.0)
        P.dma("sp", V(krt[0:64], hkr), V(kr_d, dall("kr")))
        kt_ = [A.alloc(S, BF16) for _ in range(2)]; hkt = [H(), H()]
        qnt = [A.alloc(S, BF16) for _ in range(2)]; hqn = [H(), H()]
        qrt = [A.alloc(S, BF16) for _ in range(2)]; hqr = [H(), H()]
        for b_ in range(2):
            P.memset(V(qrt[b_][64:128], hqr[b_]), 0.0)
        vt = [A.alloc(NCH * 128, BF16).rearrange("p (c d) -> p c d", c=NCH) for _ in range(2)]; hvt = [H(), H()]
        pt = [A.alloc(T, BF16) for _ in range(4)]; hpt = [H() for _ in range(4)]
        pd = [A.alloc(T, BF16) for _ in range(4)]; hpd = [H() for _ in range(4)]
        for j in range(4):
            P.memset(V(pd[j], hpd[j]), 0.0)
        dacc = [A.alloc(T, F32) for _ in range(2)]; hdacc = [H(), H()]
        rd = A.alloc(T, F32); hrd = H()
        dhi = A.alloc(T, BF16); hdhi = H()
        dlo = A.alloc(T, BF16); hdlo = H()
        oo = [A.alloc(T, BF16) for _ in range(2)]; hoo = [H(), H()]
        mk = masks.rearrange("p (j q) -> p j q", j=4)
        pt2 = [A.alloc(2 * T, BF16) for _ in range(3)]; hpt2 = [H() for _ in range(3)]
        items = [(h, qt) for h in range(8) for qt in range(NT)]

        def load_head(h):
            b = h % 2
            P.dma("sp", V(kt_[b], hkt[b]), V(k_d[h], dall("k")))
            P.dma("sp", V(qnt[b], hqn[b]), V(q_d[h, 0:128, :], dall("q")))
            P.dma("sp", V(qrt[b][0:64], hqr[b]), V(q_d[h, 128:192, :], dall("q")))
            P.dma("sp", V(vt[b], hvt[b]), V(v_d.rearrange("(c p) d -> p c d", p=128)[:, :, h * 128:(h + 1) * 128], dall("v")))

        def units_of(h, qt):
            return [(h, qt, "pair", k0) for k0 in range(0, 4 * qt, 2)] + [(h, qt, "diag", 4 * qt + j) for j in range(4)]
        allu = [u for (h, qt) in items for u in units_of(h, qt)]

        def st_tile(h_, qt_, kt, bank):
            b_ = h_ % 2
            ks = slice(kt * 128, (kt + 1) * 128)
            qs_ = slice(qt_ * T, (qt_ + 1) * T)
            P.mm(PS(bank), V(kt_[b_][:, ks], hkt[b_]), V(qnt[b_][:, qs_], hqn[b_]), start=True, stop=False)
            P.mm(PS(bank), V(krt[:, ks], hkr), V(qrt[b_][:, qs_], hqr[b_]), start=False, stop=True)

        def st_unit(ui):
            h_, qt_, kind, k0 = allu[ui]
            base = 2 * (ui % 2)
            st_tile(h_, qt_, k0, base)
            if kind == "pair":
                st_tile(h_, qt_, k0 + 1, base + 1)
        load_head(0)
        st_unit(0)
        p2 = 0
        for ui, (h, qt, kind, k0) in enumerate(allu):
            b = h % 2
            if kind == "pair" and k0 == 0 and qt == 1 and h + 1 < 8:
                load_head(h + 1)
            qs = slice(qt * T, (qt + 1) * T)
            nk = 4 * qt + 4
            ob_ = 4 + (qt % 2)
            db_ = 6 + (qt % 2)
            da = qt % 2
            base = 2 * (ui % 2)
            if ui + 1 < len(allu):
                st_unit(ui + 1)
            tiles = []
            if kind == "pair":
                pb_ = p2 % 3; p2 += 1
                src = V(psall[:, base * 512:(base + 2) * 512], [bh[base], bh[base + 1]])
                P.act(V(pt2[pb_], hpt2[pb_]), src, AF.Exp)
                tiles = [(k0, V(pt2[pb_][:, 0:T], hpt2[pb_])), (k0 + 1, V(pt2[pb_][:, T:2 * T], hpt2[pb_]))]
            else:
                j = k0 - 4 * qt
                bk = banks[base]
                P.act(V(pd[j][0:64, 128 * j:T], hpd[j]), V(bk[0:64, 128 * j:T], bh[base]), AF.Exp)
                P.act(V(pd[j][64:128, 128 * j + 64:T], hpd[j]), V(bk[64:128, 128 * j + 64:T], bh[base]), AF.Exp)
                tiles = [(k0, V(pd[j], hpd[j]))]
            for kt, pv_ in tiles:
                P.mm(PS(ob_), V(vt[b][:, kt, :], hvt[b]), pv_, start=(kt == 0), stop=(kt == nk - 1))
                if kt % 2 == 1:
                    P.mm(PS(db_), onesb, pv_, start=(kt == 1), stop=False)
                elif kt == 0:
                    P.copy(V(dacc[da], hdacc[da]), pv_)
                else:
                    P.tt(V(dacc[da], hdacc[da]), V(dacc[da], hdacc[da]), pv_, ALU.add)
            if kind == "diag" and k0 == nk - 1:
                P.copy(V(dhi, hdhi), V(dacc[da], hdacc[da]))
                P.tt(V(dlo, hdlo), V(dacc[da], hdacc[da]), V(dhi, hdhi), ALU.subtract)
                P.mm(PS(db_), onesb, V(dhi, hdhi), start=False, stop=False)
                P.mm(PS(db_), onesb, V(dlo, hdlo), start=False, stop=True)
                P.act(V(rd, hrd), PS(db_), AF.Ln)
                P.act(V(rd, hrd), V(rd, hrd), AF.Exp, scale=-1.0)
                o = qt % 2
                P.tt(V(oo[o], hoo[o]), PS(ob_), V(rd, hrd), ALU.mult)
                P.dma("pool", V(o_d[h * 128:(h + 1) * 128, qs], dh("o", qt)), V(oo[o], hoo[o]))
        if do_barrier:
            barrier()
        return A.off
    def phase_G(l):
        xsrc = xT_d if l == 0 else xres_d
        xsn = "xT" if l == 0 else "xres"
        A = Arena(big, ARENA0, 103000)
        wgate = A.alloc(8 * 3072, BF16).rearrange("p (k n) -> p k n", k=8); hwg = H()
        wbr = [A.alloc(8 * 1024, BF16).rearrange("p (k n) -> p k n", k=8) for _ in range(3)]; hwb = [H() for _ in range(3)]
        wo = A.alloc(8 * 1024, BF16).rearrange("p (k n) -> p k n", k=8); hwo = H()
        kp = lambda d: d.rearrange("(k p) n -> p k n", p=128)
        for b3 in range(3):
            wload(wgate[:, :, b3 * 1024:(b3 + 1) * 1024], kp(w_in_d[l])[:, :, OFF_GATE + b3 * 1024:OFF_GATE + (b3 + 1) * 1024], hwg)
        for b3, wd in enumerate((ssd_wo_d, conv_wo_d, mla_wo_d)):
            wload(wbr[b3], kp(wd[l]), hwb[b3])
        wload(wo, kp(wout_d[l]), hwo)
        TG = 256; NTG = S // TG
        fmg = lambda d, t: d.rearrange("(k p) s -> p k s", p=128)[:, :, t * TG:(t + 1) * TG]
        xt3 = [A.alloc(8 * TG, F32).rearrange("p (k s) -> p k s", k=8) for _ in range(3)]; hxt3 = [H(), H(), H()]
        ut2 = [A.alloc(8 * TG, BF16).rearrange("p (k s) -> p k s", k=8) for _ in range(2)]; hut2 = [H(), H()]
        sq = A.alloc(8 * TG, BF16).rearrange("p (k s) -> p k s", k=8); hsq = H()
        rs = A.alloc(TG, F32); hrs = H()
        br2 = [[A.alloc(8 * TG, BF16).rearrange("p (k s) -> p k s", k=8) for _ in range(3)] for _ in range(2)]
        hbr2 = [[H() for _ in range(3)] for _ in range(2)]
        mg = A.alloc(8 * TG, BF16).rearrange("p (k s) -> p k s", k=8); hmg = H()
        gt_ = [A.alloc(TG, F32) for _ in range(2)]; hgt = [H(), H()]
        macc = A.alloc(TG, F32); hmacc = H()
        srcs = [(ys_d, "ys"), (yc_d, "yc"), (o_d, "o")]

        def prep(t):
            b = t % 2; b3x = t % 3
            P.dma("sp", V(xt3[b3x], hxt3[b3x]), V(fmg(xsrc, t), dh(xsn, t // 2)))
            for b3, (d_, nm) in enumerate(srcs):
                P.dma("sp", V(br2[b][b3], hbr2[b][b3]), V(fmg(d_, t), dall(nm)))
            xnorm(xt3[b3x], hxt3[b3x], C_MIXG, ut2[b], hut2[b], sq, hsq, rs, hrs, 0, width=TG)
        prep(0)
        for t in range(NTG):
            b = t % 2; b3x = t % 3
            xt = xt3[b3x]; hxt = hxt3[b3x]; ut = ut2[b]; hut = hut2[b]; br = br2[b]; hbr = hbr2[b]
            if t + 1 < NTG:
                prep(t + 1)
            gi = 0
            for m in range(8):
                for b3 in range(3):
                    gb = 1 + gi % 2; yb = 3 + gi % 2; g2 = gi % 2; gi += 1
                    for k in range(8):
                        P.mm(PS(gb, TG), V(wgate[:, k, b3 * 1024 + m * 128:b3 * 1024 + (m + 1) * 128], hwg), V(ut[:, k, :], hut), start=(k == 0), stop=(k == 7))
                    for k in range(8):
                        P.mm(PS(yb, TG), V(wbr[b3][:, k, m * 128:(m + 1) * 128], hwb[b3]), V(br[b3][:, k, :], hbr[b3]), start=(k == 0), stop=(k == 7))
                    P.act(V(gt_[g2], hgt[g2]), PS(gb, TG), AF.Sigmoid, bias=pcol(C_GATEB + b3 * 8 + m))
                    if b3 == 0:
                        P.tt(V(macc, hmacc), V(gt_[g2], hgt[g2]), PS(yb, TG), ALU.mult)
                    else:
                        P.tt(V(gt_[g2], hgt[g2]), V(gt_[g2], hgt[g2]), PS(yb, TG), ALU.mult)
                        if b3 == 1:
                            P.tt(V(macc, hmacc), V(macc, hmacc), V(gt_[g2], hgt[g2]), ALU.add)
                        else:
                            P.tt(V(mg[:, m, :], hmg), V(macc, hmacc), V(gt_[g2], hgt[g2]), ALU.add)
            for n in range(8):
                ob_ = 5 + n % 2
                for m in range(8):
                    P.mm(PS(ob_, TG), V(wo[:, m, n * 128:(n + 1) * 128], hwo), V(mg[:, m, :], hmg), start=(m == 0), stop=(m == 7))
                P.tt(V(xt[:, n, :], hxt), V(xt[:, n, :], hxt), PS(ob_, TG), ALU.add)
            P.dma("pool", V(fmg(xres_d, t), dh("xres", t // 2)), V(xt, hxt))
        barrier()

    def phase_X(l):
        A = Arena(big, ARENA0, 103000)
        kp = lambda d: d.rearrange("(k p) n -> p k n", p=128)
        wkv = A.alloc(8 * 2048, BF16).rearrange("p (k n) -> p k n", k=8); hwkv = H()
        wload(wkv[:, :, 0:1024], kp(xwkv_d[l])[:, :, 0:1024], hwkv)
        wload(wkv[:, :, 1024:2048], kp(xwkv_d[l])[:, :, 1024:2048], hwkv)
        mt = A.alloc(8 * 256, F32).rearrange("p (k s) -> p k s", k=8); hmt = H()
        mn = A.alloc(8 * 256, BF16).rearrange("p (k s) -> p k s", k=8); hmn = H()
        sqm = A.alloc(8 * 256, BF16).rearrange("p (k s) -> p k s", k=8); hsqm = H()
        rsm = A.alloc(256, F32); hrsm = H()
        KX = A.alloc(8 * 256, BF16).rearrange("p (c s) -> p c s", c=8); hKX = H()
        VX = A.alloc(2 * 1024, BF16).rearrange("p (m d) -> p m d", m=2); hVX = H()
        sq2 = A.alloc(2 * 256, BF16).rearrange("p (c s) -> p c s", c=2); hsq2 = H()
        P.dma("sp", V(mt, hmt), V(memT_d.rearrange("(k p) s -> p k s", p=128), H()))
        xnorm(mt, hmt, C_MEMG, mn, hmn, sqm, hsqm, rsm, hrsm, 0, nk=8, width=256)
        for hx in range(4):
            for c in range(2):
                for k in range(8):
                    P.mm(PS(1 + c, 256), V(wkv[:, k, hx * 256 + c * 128:hx * 256 + (c + 1) * 128], hwkv), V(mn[:, k, :], hmn), start=(k == 0), stop=(k == 7))
                P.act(V(sq2[:, c, :], hsq2), PS(1 + c, 256), AF.Square)
            for c in range(2):
                P.mm(PS(3, 256), onesb, V(sq2[:, c, :], hsq2), start=(c == 0), stop=(c == 1))
            P.act(V(rsm, hrsm), PS(3, 256), AF.Ln, bias=epsv, scale=1.0 / 256)
            P.act(V(rsm, hrsm), V(rsm, hrsm), AF.Exp, scale=-0.5)
            for c in range(2):
                P.stt(V(KX[:, hx * 2 + c, :], hKX), PS(1 + c, 256), pcol(C_XKG + c), V(rsm, hrsm), ALU.mult, ALU.mult)
        for m in range(2):
            for half in range(2):
                for k in range(8):
                    P.mm(PS(4 + half), V(mn[:, k, m * 128:(m + 1) * 128], hmn), V(wkv[:, k, 1024 + half * 512:1024 + (half + 1) * 512], hwkv), start=(k == 0), stop=(k == 7))
                P.copy(V(VX[:, m, half * 512:(half + 1) * 512], hVX), PS(4 + half), eng="act")
        barrier()
        A2 = Arena(big, ARENA0, 103000)
        KX2 = A2.alloc(8 * 256, BF16).rearrange("p (c s) -> p c s", c=8); hKX2 = H()
        VX2 = A2.alloc(2 * 1024, BF16).rearrange("p (m d) -> p m d", m=2); hVX2 = H()
        P.copy(V(KX2, hKX2), V(KX, hKX)); P.copy(V(VX2, hVX2), V(VX, hVX))
        barrier()
        A = A2
        wq_ = A.alloc(8 * 1024, BF16).rearrange("p (k n) -> p k n", k=8); hwq = H()
        wo_ = A.alloc(8 * 1024, BF16).rearrange("p (k n) -> p k n", k=8); hwo = H()
        wload(wq_, kp(xwq_d[l]), hwq); wload(wo_, kp(xwo_d[l]), hwo)
        xt = [A.alloc(8 * T, F32).rearrange("p (k s) -> p k s", k=8) for _ in range(3)]; hxt = [H(), H(), H()]
        ht2 = [A.alloc(8 * T, BF16).rearrange("p (k s) -> p k s", k=8) for _ in range(2)]; hht2 = [H(), H()]
        sq = A.alloc(8 * T, BF16).rearrange("p (k s) -> p k s", k=8); hsq = H()
        rs = A.alloc(T, F32); hrs = H()
        sqq = A.alloc(2 * T, BF16).rearrange("p (c s) -> p c s", c=2); hsqq = H()
        rsq = A.alloc(T, F32); hrsq = H()
        qx = A.alloc(2 * T, BF16).rearrange("p (c s) -> p c s", c=2); hqx = H()
        pt = [A.alloc(T, BF16) for _ in range(2)]; hpt = [H(), H()]
        rd = A.alloc(T, F32); hrd = H()
        ox = A.alloc(8 * T, BF16).rearrange("p (k s) -> p k s", k=8); hox = H()
        def prep(t):
            b = t % 2; b3 = t % 3
            P.dma("sp", V(xt[b3], hxt[b3]), V(fmview(xres_d, t), dh("xres", t)))
            xnorm(xt[b3], hxt[b3], C_XATG, ht2[b], hht2[b], sq, hsq, rs, hrs, 0)
        prep(0)
        for t in range(NT):
            b = t % 2
            ht = ht2[b]; hht = hht2[b]
            b = t % 3
            if t + 1 < NT:
                prep(t + 1)
            for hx in range(4):
                for c in range(2):
                    for k in range(8):
                        P.mm(PS(1 + c), V(wq_[:, k, hx * 256 + c * 128:hx * 256 + (c + 1) * 128], hwq), V(ht[:, k, :], hht), start=(k == 0), stop=(k == 7))
                    P.act(V(sqq[:, c, :], hsqq), PS(1 + c), AF.Square)
                for c in range(2):
                    P.mm(PS(3), onesb, V(sqq[:, c, :], hsqq), start=(c == 0), stop=(c == 1))
                P.act(V(rsq, hrsq), PS(3), AF.Ln, bias=epsv, scale=1.0 / 256)
                P.act(V(rsq, hrsq), V(rsq, hrsq), AF.Exp, scale=-0.5)
                for c in range(2):
                    P.stt(V(qx[:, c, :], hqx), PS(1 + c), V(gsc[:, 3 + c:4 + c], h_gsc), V(rsq, hrsq), ALU.mult, ALU.mult)
                for m in range(2):
                    for c in range(2):
                        P.mm(PS(4 + m), V(KX2[:, hx * 2 + c, m * 128:(m + 1) * 128], hKX2), V(qx[:, c, :], hqx), start=(c == 0), stop=(c == 1))
                    P.act(V(pt[m], hpt[m]), PS(4 + m), AF.Exp)
                for m in range(2):
                    P.mm(PS(3), onesb, V(pt[m], hpt[m]), start=(m == 0), stop=(m == 1))
                P.act(V(rd, hrd), PS(3), AF.Ln)
                P.act(V(rd, hrd), V(rd, hrd), AF.Exp, scale=-1.0)
                for c2 in range(2):
                    for m in range(2):
                        P.mm(PS(6 + c2), V(VX2[:, m, hx * 256 + c2 * 128:hx * 256 + (c2 + 1) * 128], hVX2), V(pt[m], hpt[m]), start=(m == 0), stop=(m == 1))
                    P.tt(V(ox[:, hx * 2 + c2, :], hox), PS(6 + c2), V(rd, hrd), ALU.mult)
            for n in range(8):
                ob_ = 1 + n % 2
                for k in range(8):
                    P.mm(PS(ob_), V(wo_[:, k, n * 128:(n + 1) * 128], hwo), V(ox[:, k, :], hox), start=(k == 0), stop=(k == 7))
                P.tt(V(xt[b][:, n, :], hxt[b]), V(xt[b][:, n, :], hxt[b]), PS(ob_), ALU.add)
            P.dma("pool", V(fmview(xres_d, t), dh("xres", t)), V(xt[b], hxt[b]))
        barrier()

    def phase_F(l, last):
        kp = lambda d: d.rearrange("(k p) n -> p k n", p=128)
        toks = []
        for hh in range(2):
            A = Arena(big, ARENA0, 103000)
            w1g = A.alloc(8 * 1408, BF16).rearrange("p (k n) -> p k n", k=8); hw1g = H()
            w1u = A.alloc(8 * 1408, BF16).rearrange("p (k n) -> p k n", k=8); hw1u = H()
            w2 = A.alloc(11 * 1024, BF16).rearrange("p (k n) -> p k n", k=11); hw2 = H()
            wload(w1g, kp(fwi_d[l])[:, :, hh * 1408:(hh + 1) * 1408], hw1g)
            wload(w1u, kp(fwi_d[l])[:, :, 2816 + hh * 1408:2816 + (hh + 1) * 1408], hw1u)
            wload(w2, fwo_d[l, hh * 1408:(hh + 1) * 1408, :].rearrange("(k p) n -> p k n", p=128), hw2)
            xt = [A.alloc(8 * T, F32).rearrange("p (k s) -> p k s", k=8) for _ in range(3)]; hxt = [H(), H(), H()]
            pt_ = A.alloc(8 * T, F32).rearrange("p (k s) -> p k s", k=8); hpt_ = H()
            ht2 = [A.alloc(8 * T, BF16).rearrange("p (k s) -> p k s", k=8) for _ in range(2)]; hht2 = [H(), H()]
            sq = A.alloc(8 * T, BF16).rearrange("p (k s) -> p k s", k=8); hsq = H()
            rs = A.alloc(T, F32); hrs = H()
            ac = A.alloc(11 * T, BF16).rearrange("p (k s) -> p k s", k=11); hac = H()
            sg = [A.alloc(T, F32) for _ in range(2)]; hsg = [H(), H()]
            def prep(t):
                b = t % 2; b3 = t % 3
                P.dma("sp", V(xt[b3], hxt[b3]), V(fmview(xres_d, t), dh("xres", t)))
                xnorm(xt[b3], hxt[b3], C_FFNG, ht2[b], hht2[b], sq, hsq, rs, hrs, 0)
            prep(0)
            for t in range(NT):
                b = t % 2
                ht = ht2[b]; hht = hht2[b]
                b = t % 3
                if t + 1 < NT:
                    prep(t + 1)
                if hh == 1:
                    P.dma("sp", V(pt_, hpt_), V(fmview(fp_d, t), dh("fp", t)))
                for j in range(11):
                    s2 = j % 2
                    for k in range(8):
                        P.mm(PS(1 + s2), V(w1g[:, k, j * 128:(j + 1) * 128], hw1g), V(ht[:, k, :], hht), start=(k == 0), stop=(k == 7))
                    for k in range(8):
                        P.mm(PS(3 + s2), V(w1u[:, k, j * 128:(j + 1) * 128], hw1u), V(ht[:, k, :], hht), start=(k == 0), stop=(k == 7))
                    P.act(V(sg[s2], hsg[s2]), PS(1 + s2), AF.Silu)
                    P.tt(V(ac[:, j, :], hac), V(sg[s2], hsg[s2]), PS(3 + s2), ALU.mult)
                for n in range(8):
                    ob_ = 5 + n % 2
                    for j in range(11):
                        P.mm(PS(ob_), V(w2[:, j, n * 128:(n + 1) * 128], hw2), V(ac[:, j, :], hac), start=(j == 0), stop=(j == 10))
                    if hh == 0:
                        P.copy(V(xt[b][:, n, :], hxt[b]), PS(ob_), eng="act")
                    else:
                        P.tt(V(xt[b][:, n, :], hxt[b]), V(xt[b][:, n, :], hxt[b]), PS(ob_), ALU.add)
                        P.tt(V(xt[b][:, n, :], hxt[b]), V(xt[b][:, n, :], hxt[b]), V(pt_[:, n, :], hpt_), ALU.add)
                if hh == 0:
                    P.dma("pool", V(fmview(fp_d, t), dh("fp", t)), V(xt[b], hxt[b]))
                else:
                    dst = out_d if last else xres_d
                    tk = P.dma("pool", V(fmview(dst, t), dh("out" if last else "xres", t)), V(xt[b], hxt[b]))
                    toks.append(tk)
            barrier()
        return toks

    def phase_M2C(l):
        la = P.capture()
        end = phase_M2(l, do_barrier=False)
        lb = P.capture()
        phase_C(l, base=end, bk=(6, 7), do_barrier=False)
        P.interleave(la, lb)
        barrier()

    phases = {"A": phase_A, "S": phase_S, "C": phase_C, "M1": phase_M1, "M2": phase_M2, "G": phase_G, "X": phase_X, "M2C": phase_M2C}
    order = ["A", "S", "C", "M1", "M2", "G", "X", "F"]
    P.marks = []
    for l in range(L):
        for ph in order:
            if only is not None and ph not in only:
                continue
            P.marks.append((l, ph, P.eng["dve"].n))
            if ph == "F":
                final_toks = phase_F(l, l == L - 1)
            else:
                phases[ph](l)
    P.finish(final_toks)
    return P


from concourse.bass_utils import run_bass_kernel_spmd


def _fm(v, nch):
    return np.ascontiguousarray(np.asarray(v, np.float32).reshape(nch, 128).T)


def _consts():
    c = np.zeros((128, NCST), np.float32)
    i = np.arange(128)
    c[:, K_TRI:K_TRI + 128] = (i[:, None] <= i[None, :])
    c[:, K_GT:K_GT + 128] = (i[:, None] > i[None, :])
    c[:, K_ID:K_ID + 128] = np.eye(128)
    f = np.arange(32)
    inv = (10000.0 ** (-(2 * f).astype(np.float32) / np.float32(64))).astype(np.float32)
    c[0:32, K_INV] = inv; c[32:64, K_INV] = inv
    q = np.arange(512)
    for j in range(4):
        c[:, K_MASK + j * 512:K_MASK + (j + 1) * 512] = ((2 * j + i[:, None] // 64) <= (q[None, :] // 64))
    return c


def _pack_params(inp):
    L = 4
    pp = np.zeros((L, 128, NPP), np.float32)
    pb = np.zeros((L, NPB), np.float32)
    for l in range(L):
        p = pp[l]
        p[:, C_MIXG:C_MIXG + 8] = _fm(inp["mix_norm_g"][l], 8)
        p[:, C_XATG:C_XATG + 8] = _fm(inp["xattn_norm_g"][l], 8)
        p[:, C_FFNG:C_FFNG + 8] = _fm(inp["ffn_norm_g"][l], 8)
        p[:, C_MEMG:C_MEMG + 8] = _fm(inp["mem_norm_g"][l], 8)
        p[:, C_SCW:C_SCW + 64] = np.asarray(inp["ssd_conv_w"][l]).T.reshape(16, 128, 4).transpose(1, 0, 2).reshape(128, 64)
        p[:, C_SCB:C_SCB + 16] = _fm(inp["ssd_conv_b"][l], 16)
        p[:, C_CDW:C_CDW + 248] = np.asarray(inp["conv_dw_w"][l]).T.reshape(8, 128, 31).transpose(1, 0, 2).reshape(128, 248)
        p[:, C_CDB:C_CDB + 8] = _fm(inp["conv_dw_b"][l], 8)
        p[:, C_LNG:C_LNG + 8] = _fm(inp["conv_ln_g"][l], 8)
        p[:, C_LNB:C_LNB + 8] = _fm(inp["conv_ln_b"][l], 8)
        p[:, C_QAG:C_QAG + 3] = _fm(inp["mla_q_a_g"][l], 3)
        p[:, C_KVAG:C_KVAG + 2] = _fm(inp["mla_kv_a_g"][l], 2)
        for (cn, cr, crs, g) in ((C_QNN, C_QNR, C_QNRS, inp["mla_q_norm_g"][l]), (C_KNN, C_KNR, C_KNRS, inp["mla_k_norm_g"][l])):
            g = np.asarray(g, np.float32)
            p[:, cn] = g[0:128]
            p[0:64, cr] = g[128:192]
            p[0:64, crs] = np.concatenate([g[160:192], g[128:160]])
        p[:, C_GATEB:C_GATEB + 24] = np.asarray(inp["gate_b"][l], np.float32).reshape(24, 128).T
        p[:, C_XQG:C_XQG + 2] = _fm(inp["xattn_q_norm_g"][l], 2)
        p[:, C_XKG:C_XKG + 2] = _fm(inp["xattn_k_norm_g"][l], 2)
        pb[l, B_DTB:B_DTB + 16] = inp["ssd_dt_bias"][l]
        pb[l, B_ALOG:B_ALOG + 16] = inp["ssd_a_log"][l]
        pb[l, B_DSK:B_DSK + 16] = inp["ssd_d"][l]
        pb[l, B_SNG:B_SNG + 1024] = inp["ssd_norm_g"][l]
    return pp, pb


WNAMES = ["w_in", "ssd_w_out", "conv_w_out", "mla_w_q_b", "mla_w_kv_b", "mla_w_o", "w_out", "xattn_w_q", "xattn_w_kv",
          "xattn_w_o", "ffn_w_in", "ffn_w_out"]


def make_in_maps(inp, cores):
    pp, pb = _pack_params(inp)
    cst = _consts()
    shared = {n: np.ascontiguousarray(np.asarray(inp[n], np.float32)) for n in WNAMES}
    shared.update(cst=cst, pp=pp, pb=pb)
    maps = []
    for b in cores:
        m = dict(shared)
        m["xT"] = np.ascontiguousarray(np.asarray(inp["x"][b], np.float32).T)
        m["memT"] = np.ascontiguousarray(np.asarray(inp["mem"][b], np.float32).T)
        m["pos"] = np.ascontiguousarray(np.asarray(inp["positions"][b], np.int32)[None, :])
        maps.append(m)
    return maps


_CACHE = {}


def kernel(**inputs):
    if "P" not in _CACHE:
        _CACHE["P"] = build(L=4)
    P = _CACHE["P"]
    maps = make_in_maps(inputs, list(range(8)))
    res = run_bass_kernel_spmd(P.nc, maps, core_ids=list(range(8)))
    out = np.stack([np.ascontiguousarray(r["outT"].T) for r in res.results], axis=0)
    return out.astype(np.float32)
```
